# Optimizing a Trainium2 kernel written in Bass

```python
import jax, jax.numpy as jnp
from jax import lax
import numpy as np

D_MODEL = 1024
BATCH = 8
SEQ = 4096
DEPTH = 4

N_MIXERS = 3
N_MLA_LAYERS = (DEPTH + 2) // 3
N_DIL_LAYERS = (DEPTH + 1) // 3
N_RWKV_LAYERS = DEPTH // 3

MLA_HEADS = 16
MLA_Q_LORA = 384
MLA_KV_LORA = 256
MLA_NOPE = 64
MLA_ROPE = 32
MLA_V = 64
ROPE_THETA = 10000.0
Q_BLOCK = 128

DIL_GROUPS = ((128, 1), (512, 4), (2048, 16))
DIL_HEADS = 16
DIL_HEAD_DIM = 64
DIL_BLOCK = 128

RWKV_HEAD = 64
RWKV_HEADS = D_MODEL // RWKV_HEAD
RWKV_DECAY_LORA = 64
RWKV_A_LORA = 64
RWKV_GATE_LORA = 160
RWKV_GN_EPS = 64e-5

D_FF = 2752
CONV_WIDTH = 3

ALPHA = (2 * DEPTH) ** 0.25
BETA = (8 * DEPTH) ** -0.25

LN_EPS = 1e-5
RMS_EPS = 1e-6
NEG_INF = -1e30

kernel_name = "hybrid_mla_dilated_rwkv7_convffn_deepnorm"


def layer_norm(x, g, b):
    xf = x.astype(jnp.float32)
    mu = jnp.mean(xf, axis=-1, keepdims=True)
    var = jnp.mean(jnp.square(xf - mu), axis=-1, keepdims=True)
    return ((xf - mu) * lax.rsqrt(var + LN_EPS) * g + b).astype(x.dtype)


def rms_norm(x, g):
    xf = x.astype(jnp.float32)
    y = xf * lax.rsqrt(jnp.mean(jnp.square(xf), axis=-1, keepdims=True) + RMS_EPS)
    return (y * g).astype(x.dtype)


def rope_tables(positions):
    inv_freq = ROPE_THETA ** (-jnp.arange(0, MLA_ROPE, 2, dtype=jnp.float32) / MLA_ROPE)
    ang = positions.astype(jnp.float32)[..., None] * inv_freq
    return jnp.cos(ang), jnp.sin(ang)


def apply_rope(t, cos, sin):
    half = t.shape[-1] // 2
    t1, t2 = t[..., :half], t[..., half:]
    cos = cos.astype(t.dtype)
    sin = sin.astype(t.dtype)
    return jnp.concatenate([t1 * cos - t2 * sin, t2 * cos + t1 * sin], axis=-1)


def causal_block_attention(q, k, v, scale):
    B, S, H, Dk = q.shape
    nb = S // Q_BLOCK
    qb = jnp.moveaxis(q.reshape(B, nb, Q_BLOCK, H, Dk), 1, 0)
    key_pos = jnp.arange(S, dtype=jnp.int32)

    def one_block(args):
        q_blk, i = args
        s = jnp.einsum('bqhd,bkhd->bhqk', q_blk, k, preferred_element_type=jnp.float32) * scale
        q_pos = i * Q_BLOCK + jnp.arange(Q_BLOCK, dtype=jnp.int32)
        s = jnp.where(key_pos[None, :] <= q_pos[:, None], s, NEG_INF)
        p = jax.nn.softmax(s, axis=-1)
        return jnp.einsum('bhqk,bkhd->bqhd', p.astype(v.dtype), v)

    o = lax.map(one_block, (qb, jnp.arange(nb, dtype=jnp.int32)))
    return jnp.moveaxis(o, 0, 1).reshape(B, S, H, v.shape[-1])


def mla_mixer(x, positions, w_down, q_norm, kv_norm, w_uq, w_ukv, w_o):
    B, S, _ = x.shape
    lat = x @ w_down
    c_q = rms_norm(lat[..., :MLA_Q_LORA], q_norm)
    c_kv = rms_norm(lat[..., MLA_Q_LORA:MLA_Q_LORA + MLA_KV_LORA], kv_norm)
    k_pe = lat[..., MLA_Q_LORA + MLA_KV_LORA:]
    q = (c_q @ w_uq).reshape(B, S, MLA_HEADS, MLA_NOPE + MLA_ROPE)
    kv = (c_kv @ w_ukv).reshape(B, S, MLA_HEADS, MLA_NOPE + MLA_V)
    k_nope, v = kv[..., :MLA_NOPE], kv[..., MLA_NOPE:]
    cos, sin = rope_tables(positions)
    q_pe = apply_rope(q[..., MLA_NOPE:], cos[:, :, None], sin[:, :, None])
    k_pe = apply_rope(k_pe, cos, sin)
    q = jnp.concatenate([q[..., :MLA_NOPE], q_pe], axis=-1)
    k = jnp.concatenate([k_nope, jnp.broadcast_to(k_pe[:, :, None], (B, S, MLA_HEADS, MLA_ROPE))], axis=-1)
    o = causal_block_attention(q, k, v, (MLA_NOPE + MLA_ROPE) ** -0.5)
    return o.reshape(B, S, MLA_HEADS * MLA_V) @ w_o


def dilated_group_attention(q, k, v, window, dilation):
    B, S, H, Dh = q.shape
    L = window // dilation
    Bk = DIL_BLOCK
    unit = dilation * Bk
    Sp = -(-S // unit) * unit
    nb = Sp // unit

    def split(t):
        return jnp.pad(t, ((0, 0), (0, Sp - S), (0, 0), (0, 0))).reshape(B, nb, Bk, dilation, H, Dh)

    def with_prev(t):
        prev = jnp.pad(t, ((0, 0), (1, 0), (0, 0), (0, 0), (0, 0), (0, 0)))[:, :-1]
        return jnp.concatenate([prev, t], axis=2)

    qb = split(q)
    kc = with_prev(split(k))
    vc = with_prev(split(v))
    s = jnp.einsum('bnqrhd,bnkrhd->bnrhqk', qb, kc, preferred_element_type=jnp.float32) * Dh ** -0.5
    i = jnp.arange(Bk)[:, None]
    j = jnp.arange(2 * Bk)[None, :]
    dist = i + Bk - j
    blk = jnp.arange(nb)[:, None, None]
    valid = (dist >= 0) & (dist <= L) & ((blk > 0) | (j >= Bk))
    s = jnp.where(valid[None, :, None, None], s, NEG_INF)
    lse = jax.nn.logsumexp(s, axis=-1)
    p = jnp.exp(s - lse[..., None])
    o = jnp.einsum('bnrhqk,bnkrhd->bnqrhd', p.astype(v.dtype), vc)
    o = o.reshape(B, Sp, H, Dh)[:, :S]
    lse = jnp.transpose(lse, (0, 1, 4, 2, 3)).reshape(B, Sp, H)[:, :S]
    return o, lse


def dilated_mixer(x, w_qkv, w_o):
    B, S, _ = x.shape
    qkv = (x @ w_qkv).reshape(B, S, len(DIL_GROUPS), 3, DIL_HEADS, DIL_HEAD_DIM)
    outs, lses = [], []
    for g, (window, dilation) in enumerate(DIL_GROUPS):
        o, lse = dilated_group_attention(qkv[:, :, g, 0], qkv[:, :, g, 1], qkv[:, :, g, 2], window, dilation)
        outs.append(o)
        lses.append(lse)
    wts = jax.nn.softmax(jnp.stack(lses), axis=0)
    o = jnp.einsum('gbsh,gbshd->bshd', wts, jnp.stack(outs).astype(jnp.float32)).astype(x.dtype)
    return o.reshape(B, S, DIL_HEADS * DIL_HEAD_DIM) @ w_o


def rwkv7_mixer(x, mu, w_rkv, w0, w1, w2, a0, a1, a2, g1, g2, k_k, k_a, r_k, ln_w, ln_b, w_o):
    B, T, C = x.shape
    H, N = RWKV_HEADS, RWKV_HEAD
    f32 = jnp.float32
    xx = jnp.pad(x, ((0, 0), (1, 0), (0, 0)))[:, :-1] - x
    xs = x[None] + xx[None] * mu[:, None, None, :]
    rkv = jnp.einsum('jbtc,jcd->jbtd', xs[:3], w_rkv)
    r, k, v = rkv[0], rkv[1], rkv[2]
    w_log = -jax.nn.softplus(-(w0 + jnp.tanh(xs[3] @ w1) @ w2).astype(f32)) - 0.5
    decay = jnp.exp(-jnp.exp(w_log))
    a = jax.nn.sigmoid(a0 + (xs[4] @ a1) @ a2)
    g = jax.nn.sigmoid(xs[5] @ g1) @ g2
    kk = (k * k_k).reshape(B, T, H, N).astype(f32)
    kk = kk / jnp.maximum(jnp.linalg.norm(kk, axis=-1, keepdims=True), 1e-12)
    k = k * (1.0 + (a - 1.0) * k_a)

    def to_time(t):
        return jnp.moveaxis(t.reshape(B, T, H, N).astype(f32), 1, 0)

    def step(S, inp):
        r_t, w_t, k_t, v_t, kk_t, a_t = inp
        s_kk = jnp.einsum('bhij,bhj->bhi', S, kk_t)
        S = S * w_t[:, :, None, :] - s_kk[..., None] * (kk_t * a_t)[:, :, None, :] + v_t[..., None] * k_t[:, :, None, :]
        return S, jnp.einsum('bhij,bhj->bhi', S, r_t)

    S0 = jnp.zeros((B, H, N, N), f32)
    _, out = lax.scan(step, S0, (to_time(r), to_time(decay), to_time(k), to_time(v), jnp.moveaxis(kk, 1, 0), to_time(a)))
    out = jnp.moveaxis(out, 0, 1)
    mean = jnp.mean(out, axis=-1, keepdims=True)
    var = jnp.mean(jnp.square(out - mean), axis=-1, keepdims=True)
    out = ((out - mean) * lax.rsqrt(var + RWKV_GN_EPS)).reshape(B, T, C) * ln_w + ln_b
    rh = r.reshape(B, T, H, N).astype(f32)
    kh = k.reshape(B, T, H, N).astype(f32)
    bonus = jnp.sum(rh * kh * r_k, axis=-1, keepdims=True) * v.reshape(B, T, H, N).astype(f32)
    y = ((out + bonus.reshape(B, T, C)) * g).astype(x.dtype)
    return y @ w_o


def conv_ffn(x, w_in, conv_w, conv_b, w_out):
    h = x @ w_in
    a, b = h[..., :D_FF], h[..., D_FF:]
    a = lax.conv_general_dilated(a, conv_w[:, None, :].astype(a.dtype), window_strides=(1,),
                                 padding=[(CONV_WIDTH - 1, 0)],
                                 dimension_numbers=('NWC', 'WIO', 'NWC'),
                                 feature_group_count=D_FF) + conv_b
    return (jax.nn.silu(a) * b) @ w_out


def setup_inputs(seed: int = 0) -> dict:
    key = jax.random.key(seed)
    ks_all = jax.random.split(key, 48)
    ks = iter([ks_all[i] for i in range(48)])
    f32 = jnp.float32

    def nrm(shape, scale):
        return jax.random.normal(next(ks), shape, f32) * scale

    D = D_MODEL
    NA, NB, NC = N_MLA_LAYERS, N_DIL_LAYERS, N_RWKV_LAYERS
    x = nrm((BATCH, SEQ, D), 1.0)
    offsets = jax.random.randint(next(ks), (BATCH, 1), 0, 1024, dtype=jnp.int32)
    positions = offsets + jnp.arange(SEQ, dtype=jnp.int32)[None, :]
    ln_g = 1.0 + nrm((DEPTH, 2, D), 0.05)
    ln_b = nrm((DEPTH, 2, D), 0.01)
    mla_w_down = nrm((NA, D, MLA_Q_LORA + MLA_KV_LORA + MLA_ROPE), D ** -0.5)
    mla_q_norm = 1.0 + nrm((NA, MLA_Q_LORA), 0.05)
    mla_kv_norm = 1.0 + nrm((NA, MLA_KV_LORA), 0.05)
    mla_w_uq = nrm((NA, MLA_Q_LORA, MLA_HEADS * (MLA_NOPE + MLA_ROPE)), MLA_Q_LORA ** -0.5)
    mla_w_ukv = nrm((NA, MLA_KV_LORA, MLA_HEADS * (MLA_NOPE + MLA_V)), MLA_KV_LORA ** -0.5)
    mla_w_o = nrm((NA, MLA_HEADS * MLA_V, D), (MLA_HEADS * MLA_V) ** -0.5 * BETA)
    dil_w_qkv = nrm((NB, D, len(DIL_GROUPS) * 3 * DIL_HEADS * DIL_HEAD_DIM), D ** -0.5)
    dil_w_o = nrm((NB, DIL_HEADS * DIL_HEAD_DIM, D), (DIL_HEADS * DIL_HEAD_DIM) ** -0.5 * BETA)
    rwkv_mu = jax.random.uniform(next(ks), (NC, 6, D), f32)
    rwkv_w_rkv = nrm((NC, 3, D, D), D ** -0.5)
    rwkv_w0 = jnp.linspace(-6.0, -1.0, D, dtype=f32)[None, :] + nrm((NC, D), 0.1)
    rwkv_w1 = nrm((NC, D, RWKV_DECAY_LORA), D ** -0.5)
    rwkv_w2 = nrm((NC, RWKV_DECAY_LORA, D), 0.1 * RWKV_DECAY_LORA ** -0.5)
    rwkv_a0 = nrm((NC, D), 0.1)
    rwkv_a1 = nrm((NC, D, RWKV_A_LORA), D ** -0.5)
    rwkv_a2 = nrm((NC, RWKV_A_LORA, D), 0.1 * RWKV_A_LORA ** -0.5)
    rwkv_g1 = nrm((NC, D, RWKV_GATE_LORA), D ** -0.5)
    rwkv_g2 = nrm((NC, RWKV_GATE_LORA, D), RWKV_GATE_LORA ** -0.5)
    rwkv_k_k = 0.85 + nrm((NC, D), 0.05)
    rwkv_k_a = 1.0 + nrm((NC, D), 0.05)
    rwkv_r_k = nrm((NC, RWKV_HEADS, RWKV_HEAD), 0.1)
    rwkv_ln_w = 1.0 + nrm((NC, D), 0.05)
    rwkv_ln_b = nrm((NC, D), 0.01)
    rwkv_w_o = nrm((NC, D, D), D ** -0.5 * BETA)
    ffn_w_in = nrm((DEPTH, D, 2 * D_FF), D ** -0.5)
    ffn_conv_w = nrm((DEPTH, CONV_WIDTH, D_FF), CONV_WIDTH ** -0.5)
    ffn_conv_b = nrm((DEPTH, D_FF), 0.01)
    ffn_w_out = nrm((DEPTH, D_FF, D), D_FF ** -0.5 * BETA)
    return {"x": x, "positions": positions, "ln_g": ln_g, "ln_b": ln_b,
            "mla_w_down": mla_w_down, "mla_q_norm": mla_q_norm, "mla_kv_norm": mla_kv_norm,
            "mla_w_uq": mla_w_uq, "mla_w_ukv": mla_w_ukv, "mla_w_o": mla_w_o,
            "dil_w_qkv": dil_w_qkv, "dil_w_o": dil_w_o,
            "rwkv_mu": rwkv_mu, "rwkv_w_rkv": rwkv_w_rkv, "rwkv_w0": rwkv_w0, "rwkv_w1": rwkv_w1,
            "rwkv_w2": rwkv_w2, "rwkv_a0": rwkv_a0, "rwkv_a1": rwkv_a1, "rwkv_a2": rwkv_a2,
            "rwkv_g1": rwkv_g1, "rwkv_g2": rwkv_g2, "rwkv_k_k": rwkv_k_k, "rwkv_k_a": rwkv_k_a,
            "rwkv_r_k": rwkv_r_k, "rwkv_ln_w": rwkv_ln_w, "rwkv_ln_b": rwkv_ln_b, "rwkv_w_o": rwkv_w_o,
            "ffn_w_in": ffn_w_in, "ffn_conv_w": ffn_conv_w, "ffn_conv_b": ffn_conv_b, "ffn_w_out": ffn_w_out}


def reference(x, positions, ln_g, ln_b, mla_w_down, mla_q_norm, mla_kv_norm, mla_w_uq, mla_w_ukv, mla_w_o,
              dil_w_qkv, dil_w_o, rwkv_mu, rwkv_w_rkv, rwkv_w0, rwkv_w1, rwkv_w2, rwkv_a0, rwkv_a1, rwkv_a2,
              rwkv_g1, rwkv_g2, rwkv_k_k, rwkv_k_a, rwkv_r_k, rwkv_ln_w, rwkv_ln_b, rwkv_w_o,
              ffn_w_in, ffn_conv_w, ffn_conv_b, ffn_w_out):
    ia = ib = ic = 0
    for i in range(DEPTH):
        kind = i % N_MIXERS
        if kind == 0:
            h = mla_mixer(x, positions, mla_w_down[ia], mla_q_norm[ia], mla_kv_norm[ia],
                          mla_w_uq[ia], mla_w_ukv[ia], mla_w_o[ia])
            ia += 1
        elif kind == 1:
            h = dilated_mixer(x, dil_w_qkv[ib], dil_w_o[ib])
            ib += 1
        else:
            h = rwkv7_mixer(x, rwkv_mu[ic], rwkv_w_rkv[ic], rwkv_w0[ic], rwkv_w1[ic], rwkv_w2[ic],
                            rwkv_a0[ic], rwkv_a1[ic], rwkv_a2[ic], rwkv_g1[ic], rwkv_g2[ic],
                            rwkv_k_k[ic], rwkv_k_a[ic], rwkv_r_k[ic], rwkv_ln_w[ic], rwkv_ln_b[ic], rwkv_w_o[ic])
            ic += 1
        x = layer_norm(ALPHA * x + h, ln_g[i, 0], ln_b[i, 0])
        h = conv_ffn(x, ffn_w_in[i], ffn_conv_w[i], ffn_conv_b[i], ffn_w_out[i])
        x = layer_norm(ALPHA * x + h, ln_g[i, 1], ln_b[i, 1])
    return x
```

```python
import numpy as np
from contextlib import ExitStack
import concourse.bass as bass
import concourse.mybir as mybir
from concourse.bass_utils import run_bass_kernel_spmd

F32 = mybir.dt.float32
BF16 = mybir.dt.bfloat16
I32 = mybir.dt.int32
AF = mybir.ActivationFunctionType
ALU = mybir.AluOpType

T = 4096
D = 1024
DEPTH = 4
NB = T // 128
ALPHA = (2 * DEPTH) ** 0.25
LN_EPS = 1e-5
RMS_EPS = 1e-6
D_FF = 2752
NCH = 22


class _Eng:
    def __init__(self, name, eng, sem):
        self.name = name
        self.eng = eng
        self.sem = sem
        self.count = 0
        self.waited = {}


class Sched:
    def __init__(self, nc, stack, n_dma_sems=16):
        self.nc = nc
        mk = lambda n: stack.enter_context(nc.semaphore(n))
        self.pe = _Eng("pe", nc.tensor, mk("s_pe"))
        self.act = _Eng("act", nc.scalar, mk("s_act"))
        self.dve = _Eng("dve", nc.vector, mk("s_dve"))
        self.pool = _Eng("pool", nc.gpsimd, mk("s_pool"))
        self.sp = _Eng("sp", nc.sync, None)
        self.q = {"sp": [mk(f"s_dsp{i}") for i in range(n_dma_sems)],
                  "pool": [mk(f"s_dpl{i}") for i in range(n_dma_sems)]}
        self.qeng = {"sp": self.sp, "pool": self.pool}
        self.dma_cnt = {"sp": 0, "pool": 0}
        self.dma_last = {}
        self.last_write = {}
        self.readers = {}
        self.n_ops = 0
        self.n_waits = 0

    def _wait(self, E, tok):
        sem, val, src = tok
        if src == "pe" and E.name == "pe":
            return
        k = id(sem)
        if E.waited.get(k, 0) >= val:
            return
        E.eng.wait_ge(sem, val)
        E.waited[k] = val
        self.n_waits += 1

    def _deps(self, E, reads, writes):
        for r in reads:
            t = self.last_write.get(r)
            if t is not None:
                self._wait(E, t)
        for w in writes:
            t = self.last_write.get(w)
            if t is not None:
                self._wait(E, t)
            for t in self.readers.get(w, ()):
                self._wait(E, t)

    def _commit(self, tok, reads, writes):
        for r in reads:
            self.readers.setdefault(r, []).append(tok)
        for w in writes:
            self.last_write[w] = tok
            self.readers[w] = []

    def op(self, E, fn, reads=(), writes=()):
        self._deps(E, reads, writes)
        ins = fn(E.eng)
        E.count += 1
        ins.then_inc(E.sem, 1)
        tok = (E.sem, E.count, E.name)
        self._commit(tok, reads, writes)
        self.n_ops += 1
        return tok

    def dma(self, qname, fn, reads=(), writes=()):
        E = self.qeng[qname]
        pool = self.q[qname]
        i = self.dma_cnt[qname]
        self.dma_cnt[qname] = i + 1
        slot = i % len(pool)
        prev = self.dma_last.get((qname, slot))
        if prev is not None:
            self._wait(E, prev)
        self._deps(E, reads, writes)
        ins = fn(E.eng)
        ins.then_inc(pool[slot], 16)
        tok = (pool[slot], 16 * (i // len(pool) + 1), "dma_" + qname)
        self.dma_last[(qname, slot)] = tok
        self._commit(tok, reads, writes)
        self.n_ops += 1
        return tok

    def barrier(self):
        toks = [(E.sem, E.count, E.name) for E in (self.pe, self.act, self.dve, self.pool) if E.count]
        toks += list(self.dma_last.values())
        for E in (self.pe, self.act, self.dve, self.pool, self.sp):
            for t in toks:
                if t[2] == E.name:
                    continue
                self._wait(E, t)
        self.last_write = {}
        self.readers = {}


class Builder:
    def __init__(self, plan, debug_out=False):
        self.plan = plan
        nc = bass.Bass("TRN2", target_bir_lowering=False)
        self.nc = nc
        self.din = {}

    def nm(self, n):
        self._uid = getattr(self, "_uid", 0) + 1
        return f"{n}_{self._uid}"

    def dram_in(self, name, shape, dt=F32):
        t = self.nc.dram_tensor(name, list(shape), dt, kind="ExternalInput").ap()
        self.din[name] = t
        return t

    def build(self):
        nc = self.nc
        plan = self.plan
        self.x_in = self.dram_in("x", [T, D])
        self.out = nc.dram_tensor("out", [T, D], F32, kind="ExternalOutput").ap()
        self.ln_g = self.dram_in("ln_g", [DEPTH, 2, D])
        self.ln_b = self.dram_in("ln_b", [DEPTH, 2, D])
        self.ffn_w_in = self.dram_in("ffn_w_in", [DEPTH, D, 2 * D_FF])
        self.ffn_w_out = self.dram_in("ffn_w_out", [DEPTH, D_FF, D])
        self.ffn_cw = self.dram_in("ffn_cw", [DEPTH, 128, NCH, 4])
        kinds = {k for k, _ in plan}
        if "dil" in kinds or "rwkv" in kinds:
            self.oT_d = nc.dram_tensor("oT_d", [8, 128, T], BF16, kind="Internal").ap()
        if "dil" in kinds:
            self.dil_wqkv = self.dram_in("dil_w_qkv", [D, 9216])
            self.dil_wo = self.dram_in("dil_w_o", [D, D])
        if "rwkv" in kinds:
            self.rw_mu = self.dram_in("rw_mu", [128, 8, 6])
            self.rw_wrkv = self.dram_in("rwkv_w_rkv", [3, D, D])
            self.rw_l1 = self.dram_in("rw_l1", [D, 288])
            self.rw_w2 = self.dram_in("rwkv_w2", [64, D])
            self.rw_a2 = self.dram_in("rwkv_a2", [64, D])
            self.rw_g2 = self.dram_in("rwkv_g2", [160, D])
            self.rw_vec = self.dram_in("rw_vec", [128, 8, 8])
            self.rw_wo = self.dram_in("rwkv_w_o", [D, D])
        if "mla" in kinds:
            self.pos = self.dram_in("positions", [T], I32)
            self.rope_c = self.dram_in("rope_c", [96, 2])
            self.mla_wd = self.dram_in("mla_wd", [2, D, 768])
            self.mla_qn = self.dram_in("mla_qn", [2, 128, 3])
            self.mla_kvn = self.dram_in("mla_kvn", [2, 128, 2])
            self.mla_wuq = self.dram_in("mla_wuq", [2, 384, 2048])
            self.mla_wukv = self.dram_in("mla_wukv", [2, 256, 2048])
            self.mla_wo = self.dram_in("mla_wo", [2, D, D])

        with ExitStack() as st:
            self.st = st
            S = self.S = Sched(nc, st)
            gsb = lambda n, shp, dt: st.enter_context(nc.sbuf_tensor(self.nm(n), shp, dt))
            self.actT = gsb("actT", [128, 8, T], BF16)
            self.ident = gsb("ident", [128, 128], BF16)
            self.lng = gsb("lng", [128, D], F32)
            self.lnb = gsb("lnb", [128, D], F32)
            self.ones_bf = gsb("ones_bf", [128, 128], BF16)
            self.ones_f = gsb("ones_f", [128, 128], F32)
            self.tri = gsb("tri", [128, 128], BF16)
            self.ep_idx = 0
            self.cur_src = self.x_in

            S.op(S.pool, lambda e: e.memset(self.ident[:], 1.0), writes=["ident"])
            S.op(S.pool, lambda e: e.affine_select(out=self.ident[:], in_=self.ident[:], pattern=[[1, 128]],
                                                   compare_op=ALU.is_equal, fill=0.0, base=0,
                                                   channel_multiplier=-1),
                 reads=["ident"], writes=["ident"])
            S.op(S.pool, lambda e: e.memset(self.ones_bf[:], 1.0), writes=["ones_bf"])
            S.op(S.pool, lambda e: e.memset(self.ones_f[:], 1.0), writes=["ones_f"])
            S.op(S.pool, lambda e: e.memset(self.tri[:], 1.0), writes=["tri"])
            S.op(S.pool, lambda e: e.affine_select(out=self.tri[:], in_=self.tri[:], pattern=[[1, 128]],
                                                   compare_op=ALU.is_ge, fill=0.0, base=0,
                                                   channel_multiplier=-1),
                 reads=["tri"], writes=["tri"])
            self.init_phase()
            for step in plan:
                kind, L = step
                if kind == "ffn":
                    self.ffn_phase(L)
                elif kind == "mla":
                    self.mla_phase(L)
                elif kind == "dil":
                    self.dil_phase(L)
                elif kind == "rwkv":
                    self.rwkv_phase(L)
                elif kind == "copy":
                    self.copy_phase()
                else:
                    raise ValueError(kind)
            S.barrier()
        return nc

    def transposes_to_actT(self, m, xb, pT, res_xb):
        S = self.S
        for k in range(8):
            S.op(S.pe, lambda e, k=k: e.transpose(out=pT[:, k, :], in_=xb[:, k * 128:(k + 1) * 128],
                                                  identity=self.ident[:]),
                 reads=[res_xb, "ident"], writes=["pT"])
        S.op(S.dve, lambda e: e.tensor_copy(out=self.actT[:, :, m * 128:(m + 1) * 128], in_=pT[:]),
             reads=["pT"], writes=[("actT", m)])

    def alloc_epi(self, ph):
        nc = self.nc
        sb = lambda n, shp, dt: ph.enter_context(nc.sbuf_tensor(self.nm(n), shp, dt))
        self.xr = [sb(f"xr{i}", [128, D], F32) for i in range(2)]
        self.z = [sb(f"z{i}", [128, D], F32) for i in range(2)]
        self.xb = [sb(f"xb{i}", [128, D], BF16) for i in range(2)]
        self.st6 = [sb(f"st6{i}", [128, 2, 6], F32) for i in range(2)]
        self.mv = [sb(f"mv{i}", [128, 8], F32) for i in range(2)]

    def init_phase(self):
        nc, S = self.nc, self.S
        with ExitStack() as ph:
            self.alloc_epi(ph)
            pT = ph.enter_context(nc.psum_tensor(self.nm("pT_i"), [128, 8, 128], BF16))
            for m in range(NB):
                b = m % 2
                S.dma("sp", lambda e: e.dma_start(out=self.xr[b][:], in_=self.x_in[m * 128:(m + 1) * 128, :]),
                      writes=[("xr", b)])
                S.op(S.act, lambda e: e.copy(out=self.xb[b][:], in_=self.xr[b][:]),
                     reads=[("xr", b)], writes=[("xb", b)])
                self.transposes_to_actT(m, self.xb[b], pT, ("xb", b))
            S.barrier()

    def copy_phase(self):
        S = self.S
        ph = ExitStack()
        self.alloc_epi(ph)
        for m in range(NB):
            b = m % 2
            S.dma("sp", lambda e: e.dma_start(out=self.xr[b][:], in_=self.cur_src[m * 128:(m + 1) * 128, :]),
                  reads=[("xres", m)], writes=[("xr", b)])
            S.dma("sp", lambda e: e.dma_start(out=self.out[m * 128:(m + 1) * 128, :], in_=self.xr[b][:]),
                  reads=[("xr", b)], writes=[("xres", m)])
        S.barrier()
        ph.close()
        self.cur_src = self.out

    def load_ln(self, L, which):
        S = self.S
        S.dma("sp", lambda e: e.dma_start(out=self.lng[:], in_=self.ln_g[L, which, :].partition_broadcast(128)),
              writes=["lng"])
        S.dma("sp", lambda e: e.dma_start(out=self.lnb[:], in_=self.ln_b[L, which, :].partition_broadcast(128)),
              writes=["lnb"])

    def prefetch_xr(self, m):
        S = self.S
        b = self.ep_idx % 2
        src = self.cur_src
        S.dma("sp", lambda e: e.dma_start(out=self.xr[b][:], in_=src[m * 128:(m + 1) * 128, :]),
              reads=[("xres", m)], writes=[("xr", b)])

    def epilogue(self, m, py, py_res, pT):
        S = self.S
        b = self.ep_idx % 2
        self.ep_idx += 1
        xr, z, xb, st6, mv = self.xr[b], self.z[b], self.xb[b], self.st6[b], self.mv[b]
        rz, rmv = ("z", b), ("mv", b)
        S.op(S.dve, lambda e: e.scalar_tensor_tensor(out=z[:], in0=xr[:], scalar=float(ALPHA), in1=py,
                                                     op0=ALU.mult, op1=ALU.add),
             reads=[("xr", b), py_res], writes=[rz])
        for c in range(2):
            S.op(S.dve, lambda e, c=c: e.bn_stats(out=st6[:, c, :], in_=z[:, c * 512:(c + 1) * 512]),
                 reads=[rz], writes=[("st6", b, c)])
        S.op(S.dve, lambda e: e.bn_aggr(out=mv[:, 0:2], in_=st6[:].rearrange("p a b -> p (a b)")),
             reads=[("st6", b, 0), ("st6", b, 1)], writes=[rmv])
        S.op(S.dve, lambda e: e.tensor_scalar(out=mv[:, 2:3], in0=mv[:, 1:2], scalar1=float(LN_EPS), scalar2=None,
                                              op0=ALU.add), reads=[rmv], writes=[rmv])
        S.op(S.act, lambda e: e.activation(out=mv[:, 3:4], in_=mv[:, 2:3], func=AF.Sqrt), reads=[rmv], writes=[rmv])
        S.op(S.dve, lambda e: e.reciprocal(out=mv[:, 4:5], in_=mv[:, 3:4]), reads=[rmv], writes=[rmv])
        S.op(S.dve, lambda e: e.scalar_tensor_tensor(out=mv[:, 5:6], in0=mv[:, 0:1], scalar=-1.0, in1=mv[:, 4:5],
                                                     op0=ALU.mult, op1=ALU.mult), reads=[rmv], writes=[rmv])
        S.op(S.act, lambda e: e.activation(out=z[:], in_=z[:], func=AF.Identity, bias=mv[:, 5:6], scale=mv[:, 4:5]),
             reads=[rmv, rz], writes=[rz])
        S.op(S.pool, lambda e: e.tensor_tensor(out=z[:], in0=z[:], in1=self.lng[:], op=ALU.mult),
             reads=[rz, "lng"], writes=[rz])
        S.op(S.pool, lambda e: e.tensor_tensor(out=z[:], in0=z[:], in1=self.lnb[:], op=ALU.add),
             reads=[rz, "lnb"], writes=[rz])
        S.dma("pool", lambda e: e.dma_start(out=self.out[m * 128:(m + 1) * 128, :], in_=z[:]),
              reads=[rz], writes=[("xres", m)])
        S.op(S.act, lambda e: e.copy(out=xb[:], in_=z[:]), reads=[rz], writes=[("xb", b)])
        self.transposes_to_actT(m, xb, pT, ("xb", b))

    def ffn_phase(self, L):
        nc, S = self.nc, self.S
        actT = self.actT
        with ExitStack() as ph:
            sb = lambda n, shp, dt: ph.enter_context(nc.sbuf_tensor(self.nm(n), shp, dt))
            ps = lambda n, shp, dt: ph.enter_context(nc.psum_tensor(self.nm(n), shp, dt))
            self.alloc_epi(ph)
            w_out = sb("f_wout", [128, NCH, D], BF16)
            cw = sb("f_cw", [128, NCH, 4], F32)
            halo = sb("f_halo", [128, NCH, 2], F32)
            g = sb("f_g", [128, NCH, 512], BF16)
            wab = [sb(f"f_wab{i}", [128, 8, 256], BF16) for i in range(3)]
            asb = [sb(f"f_a{i}", [128, 514], F32) for i in range(3)]
            tt = [sb(f"f_t{i}", [128, 512], F32) for i in range(3)]
            pa = [ps(f"f_pa{i}", [128, 512], F32) for i in range(2)]
            pb = [ps(f"f_pb{i}", [128, 512], F32) for i in range(2)]
            py = ps("f_py", [128, D], F32)
            pT = ps("f_pT", [128, 8, 128], BF16)

            self.load_ln(L, 1)
            S.dma("sp", lambda e: e.dma_start(out=cw[:], in_=self.ffn_cw[L]), writes=["cw"])
            S.dma("pool", lambda e: e.dma_start(
                out=w_out[:, 0:21, :], in_=self.ffn_w_out[L, 0:21 * 128, :].rearrange("(c p) n -> p c n", p=128)),
                writes=["wout"])
            S.dma("pool", lambda e: e.dma_start(out=w_out[0:64, 21, :], in_=self.ffn_w_out[L, 21 * 128:D_FF, :]),
                  writes=["wout21"])
            S.op(S.pool, lambda e: e.memset(halo[:], 0.0), writes=[("halo", c) for c in range(NCH)])
            w_in = self.ffn_w_in[L].rearrange("(k p) n -> p k n", p=128)
            it = 0
            for j in range(T // 512):
                tok = slice(j * 512, (j + 1) * 512)
                act_res = [("actT", 4 * j + q) for q in range(4)]
                for c in range(NCH):
                    wc = 128 if c < NCH - 1 else 64
                    r3, r2 = it % 3, it % 2
                    it += 1
                    wb_, a_, t_ = wab[r3], asb[r3], tt[r3]
                    S.dma("pool", lambda e: e.dma_start(out=wb_[:, :, 0:wc], in_=w_in[:, :, c * 128:c * 128 + wc]),
                          writes=[("wa", r3)])
                    S.dma("pool", lambda e: e.dma_start(out=wb_[:, :, 128:128 + wc],
                                                        in_=w_in[:, :, D_FF + c * 128:D_FF + c * 128 + wc]),
                          writes=[("wb", r3)])
                    for k in range(8):
                        S.op(S.pe, lambda e, k=k: e.matmul(pa[r2][0:wc, :], lhsT=wb_[:, k, 0:wc], rhs=actT[:, k, tok],
                                                           start=(k == 0), stop=(k == 7)),
                             reads=[("wa", r3)] + act_res, writes=[("pa", r2)])
                    for k in range(8):
                        S.op(S.pe, lambda e, k=k: e.matmul(pb[r2][0:wc, :], lhsT=wb_[:, k, 128:128 + wc],
                                                           rhs=actT[:, k, tok], start=(k == 0), stop=(k == 7)),
                             reads=[("wb", r3)] + act_res, writes=[("pb", r2)])
                    ra, rt = ("a", r3), ("t", r3)
                    S.op(S.pool, lambda e: e.tensor_copy(out=a_[0:wc, 0:2], in_=halo[0:wc, c, :]),
                         reads=[("halo", c)], writes=[ra])
                    S.op(S.act, lambda e: e.copy(out=a_[0:wc, 2:514], in_=pa[r2][0:wc, :]),
                         reads=[("pa", r2), ra], writes=[ra])
                    S.op(S.pool, lambda e: e.tensor_copy(out=halo[0:wc, c, :], in_=a_[0:wc, 512:514]),
                         reads=[ra], writes=[("halo", c)])
                    S.op(S.dve, lambda e: e.tensor_scalar(out=t_[0:wc, :], in0=a_[0:wc, 2:514],
                                                          scalar1=cw[0:wc, c, 2:3], scalar2=cw[0:wc, c, 3:4],
                                                          op0=ALU.mult, op1=ALU.add),
                         reads=[ra, "cw"], writes=[rt])
                    S.op(S.dve, lambda e: e.scalar_tensor_tensor(out=t_[0:wc, :], in0=a_[0:wc, 1:513],
                                                                 scalar=cw[0:wc, c, 1:2], in1=t_[0:wc, :],
                                                                 op0=ALU.mult, op1=ALU.add),
                         reads=[ra, rt], writes=[rt])
                    S.op(S.dve, lambda e: e.scalar_tensor_tensor(out=t_[0:wc, :], in0=a_[0:wc, 0:512],
                                                                 scalar=cw[0:wc, c, 0:1], in1=t_[0:wc, :],
                                                                 op0=ALU.mult, op1=ALU.add),
                         reads=[ra, rt], writes=[rt])
                    S.op(S.act, lambda e: e.activation(out=t_[0:wc, :], in_=t_[0:wc, :], func=AF.Silu),
                         reads=[rt], writes=[rt])
                    S.op(S.dve, lambda e: e.tensor_tensor(out=g[0:wc, c, :], in0=t_[0:wc, :], in1=pb[r2][0:wc, :],
                                                          op=ALU.mult),
                         reads=[rt, ("pb", r2)], writes=[("g", c)])
                for mm in range(4):
                    m = 4 * j + mm
                    self.prefetch_xr(m)
                    for n in range(2):
                        for c in range(NCH):
                            wc = 128 if c < NCH - 1 else 64
                            S.op(S.pe, lambda e, n=n, c=c, wc=wc: e.matmul(
                                py[:, n * 512:(n + 1) * 512], lhsT=g[0:wc, c, mm * 128:(mm + 1) * 128],
                                rhs=w_out[0:wc, c, n * 512:(n + 1) * 512], start=(c == 0), stop=(c == NCH - 1)),
                                 reads=[("g", c), "wout", "wout21"], writes=["py"])
                    self.epilogue(m, py[:], "py", pT)
            S.barrier()
        self.cur_src = self.out


    def out_proj_phase(self, L, w_dram):
        nc, S = self.nc, self.S
        with ExitStack() as ph:
            sb = lambda n, shp, dt: ph.enter_context(nc.sbuf_tensor(self.nm(n), shp, dt))
            ps = lambda n, shp, dt: ph.enter_context(nc.psum_tensor(self.nm(n), shp, dt))
            self.alloc_epi(ph)
            wo = sb("o_w", [128, 8, D], BF16)
            py = [ps(f"o_py{i}", [128, D], F32) for i in range(2)]
            pT = ps("o_pT", [128, 8, 128], BF16)
            self.load_ln(L, 0)
            S.dma("pool", lambda e: e.dma_start(out=wo[:], in_=w_dram.rearrange("(k p) n -> p k n", p=128)),
                  writes=["wo"])
            for m in range(NB):
                self.prefetch_xr(m)
                p_ = py[m % 2]
                for n in range(2):
                    for k in range(8):
                        S.op(S.pe, lambda e, n=n, k=k: e.matmul(
                            p_[:, n * 512:(n + 1) * 512], lhsT=self.actT[:, k, m * 128:(m + 1) * 128],
                            rhs=wo[:, k, n * 512:(n + 1) * 512], start=(k == 0), stop=(k == 7)),
                             reads=[("actT", m), "wo"], writes=[("py", m % 2)])
                self.epilogue(m, p_[:], ("py", m % 2), pT)
            S.barrier()
        self.cur_src = self.out

    def mla_phase(self, L):
        nc, S = self.nc, self.S
        actT = self.actT
        ia = L // 3
        SCALE = 96.0 ** -0.5
        TWO_PI = 2.0 * np.pi
        with ExitStack() as ml:
            msb = lambda n, shp, dt: ml.enter_context(nc.sbuf_tensor(self.nm(n), shp, dt))
            cqn = msb("m_cqn", [128, 3, T], BF16)
            ckvn = msb("m_ckvn", [128, 2, T], BF16)
            KT = msb("m_KT", [96, T], BF16)
            cosT = msb("m_cos", [96, T], BF16)
            sinS = msb("m_sin", [96, T], BF16)
            rc = msb("m_rc", [96, 2], F32)
            with ExitStack() as ph:
                sb = lambda n, shp, dt: ph.enter_context(nc.sbuf_tensor(self.nm(n), shp, dt))
                HT = T // 2
                posi = sb("m_posi", [96, HT], I32)
                ang = sb("m_ang", [96, HT], F32)
                tmp = sb("m_tmp", [96, HT], F32)
                yi = sb("m_yi", [96, HT], I32)
                msk = sb("m_msk", [96, HT], F32)
                S.dma("sp", lambda e: e.dma_start(out=rc[:], in_=self.rope_c), writes=["rc"])

                def sin_turns():
                    S.op(S.dve, lambda e: e.tensor_copy(out=yi[:], in_=tmp[:]), reads=["tmp"], writes=["yi"])
                    S.op(S.dve, lambda e: e.tensor_copy(out=msk[:], in_=yi[:]), reads=["yi"], writes=["msk"])
                    S.op(S.dve, lambda e: e.tensor_tensor(out=tmp[:], in0=tmp[:], in1=msk[:], op=ALU.subtract),
                         reads=["tmp", "msk"], writes=["tmp"])
                    S.op(S.dve, lambda e: e.tensor_scalar(out=msk[:], in0=tmp[:], scalar1=0.5, scalar2=None,
                                                          op0=ALU.is_gt), reads=["tmp"], writes=["msk"])
                    S.op(S.dve, lambda e: e.tensor_tensor(out=tmp[:], in0=tmp[:], in1=msk[:], op=ALU.subtract),
                         reads=["tmp", "msk"], writes=["tmp"])
                    S.op(S.dve, lambda e: e.tensor_scalar(out=msk[:], in0=tmp[:], scalar1=-0.5, scalar2=None,
                                                          op0=ALU.is_lt), reads=["tmp"], writes=["msk"])
                    S.op(S.dve, lambda e: e.tensor_tensor(out=tmp[:], in0=tmp[:], in1=msk[:], op=ALU.add),
                         reads=["tmp", "msk"], writes=["tmp"])
                    S.op(S.act, lambda e: e.activation(out=tmp[:], in_=tmp[:], func=AF.Sin, scale=6.28318),
                         reads=["tmp"], writes=["tmp"])

                for hh in range(2):
                    cs = slice(hh * HT, (hh + 1) * HT)
                    S.dma("sp", lambda e: e.dma_start(out=posi[:], in_=self.pos[cs].partition_broadcast(96)),
                          writes=["posi"])
                    S.op(S.dve, lambda e: e.tensor_copy(out=ang[:], in_=posi[:]), reads=["posi"], writes=["ang"])
                    S.op(S.dve, lambda e: e.tensor_scalar(out=ang[:], in0=ang[:], scalar1=rc[:, 0:1], scalar2=None,
                                                          op0=ALU.mult), reads=["ang", "rc"], writes=["ang"])
                    S.op(S.dve, lambda e: e.tensor_copy(out=tmp[:], in_=ang[:]), reads=["ang"], writes=["tmp"])
                    sin_turns()
                    S.op(S.dve, lambda e: e.tensor_scalar(out=sinS[64:96, cs], in0=tmp[64:96, :],
                                                          scalar1=rc[64:96, 1:2], scalar2=None, op0=ALU.mult),
                         reads=["tmp", "rc"], writes=["sinS"])
                    S.op(S.dve, lambda e: e.tensor_scalar(out=tmp[:], in0=ang[:], scalar1=0.25, scalar2=None,
                                                          op0=ALU.add), reads=["ang", "sinS"], writes=["tmp"])
                    sin_turns()
                    S.op(S.dve, lambda e: e.tensor_copy(out=cosT[64:96, cs], in_=tmp[64:96, :]),
                         reads=["tmp"], writes=["cosT"])
                S.barrier()
            with ExitStack() as ph:
                sb = lambda n, shp, dt: ph.enter_context(nc.sbuf_tensor(self.nm(n), shp, dt))
                ps = lambda n, shp, dt: ph.enter_context(nc.psum_tensor(self.nm(n), shp, dt))
                wd = sb("m_wd", [128, 8, 768], BF16)
                gq = sb("m_gq", [128, 3], F32)
                gkv = sb("m_gkv", [128, 2], F32)
                raw = [sb(f"m_raw{i}", [128, 5, 512], F32) for i in range(2)]
                sq = [sb(f"m_sq{i}", [128, 5, 512], BF16) for i in range(2)]
                rs = [sb(f"m_rs{i}", [128, 2, 512], F32) for i in range(2)]
                t1 = [sb(f"m_t1{i}", [96, 512], F32) for i in range(2)]
                t2 = [sb(f"m_t2{i}", [96, 512], F32) for i in range(2)]
                p_lat = [ps(f"m_plat{i}", [128, 512], F32) for i in range(2)]
                p_ss = [ps(f"m_pss{i}", [128, 512], F32) for i in range(2)]
                p_kA = ps("m_pkA", [96, 512], F32)
                p_kB = ps("m_pkB", [96, 512], F32)
                S.dma("pool", lambda e: e.dma_start(out=wd[:], in_=self.mla_wd[ia].rearrange("(k p) n -> p k n", p=128)),
                      writes=["wd"])
                S.dma("sp", lambda e: e.dma_start(out=gq[:], in_=self.mla_qn[ia]), writes=["gq"])
                S.dma("sp", lambda e: e.dma_start(out=gkv[:], in_=self.mla_kvn[ia]), writes=["gkv"])
                it = 0
                for j in range(T // 512):
                    tok = slice(j * 512, (j + 1) * 512)
                    ares = [("actT", 4 * j + q) for q in range(4)]
                    b = j % 2
                    for c in range(5):
                        pl = p_lat[it % 2]
                        rpl = ("plat", it % 2)
                        it += 1
                        for k in range(8):
                            S.op(S.pe, lambda e, k=k: e.matmul(pl[:], lhsT=wd[:, k, c * 128:(c + 1) * 128],
                                                               rhs=actT[:, k, tok], start=(k == 0), stop=(k == 7)),
                                 reads=["wd"] + ares, writes=[rpl])
                        S.op(S.act, lambda e: e.copy(out=raw[b][:, c, :], in_=pl[:]), reads=[rpl], writes=[("raw", b, c)])
                        S.op(S.act, lambda e: e.activation(out=sq[b][:, c, :], in_=pl[:], func=AF.Square),
                             reads=[rpl], writes=[("sq", b, c)])
                    for which, (c0, c1, dim) in enumerate([(0, 3, 384.0), (3, 5, 256.0)]):
                        for c in range(c0, c1):
                            S.op(S.pe, lambda e, c=c: e.matmul(p_ss[which][:], lhsT=self.ones_bf[:], rhs=sq[b][:, c, :],
                                                               start=(c == c0), stop=(c == c1 - 1)),
                                 reads=[("sq", b, c), "ones_bf"], writes=[("pss", which)])
                        rr = ("rs", b, which)
                        S.op(S.dve, lambda e: e.tensor_scalar(out=rs[b][:, which, :], in0=p_ss[which][:],
                                                              scalar1=1.0 / dim, scalar2=float(RMS_EPS),
                                                              op0=ALU.mult, op1=ALU.add),
                             reads=[("pss", which)], writes=[rr])
                        S.op(S.act, lambda e: e.activation(out=rs[b][:, which, :], in_=rs[b][:, which, :], func=AF.Sqrt),
                             reads=[rr], writes=[rr])
                        S.op(S.dve, lambda e: e.reciprocal(out=rs[b][:, which, :], in_=rs[b][:, which, :]),
                             reads=[rr], writes=[rr])
                        for c in range(c0, c1):
                            dst = cqn[:, c, tok] if which == 0 else ckvn[:, c - 3, tok]
                            gsc = gq[:, c:c + 1] if which == 0 else gkv[:, c - 3:c - 2]
                            S.op(S.dve, lambda e: e.scalar_tensor_tensor(out=dst, in0=raw[b][:, c, :], scalar=gsc,
                                                                         in1=rs[b][:, which, :], op0=ALU.mult,
                                                                         op1=ALU.mult),
                                 reads=[("raw", b, c), rr, "gq", "gkv"], writes=[("cn", c, j)])
                    for k in range(8):
                        S.op(S.pe, lambda e, k=k: e.matmul(p_kA[:], lhsT=wd[:, k, 640:736], rhs=actT[:, k, tok],
                                                           start=(k == 0), stop=(k == 7)),
                             reads=["wd"] + ares, writes=["pkA"])
                    for k in range(8):
                        S.op(S.pe, lambda e, k=k: e.matmul(p_kB[:], lhsT=wd[:, k, 672:768], rhs=actT[:, k, tok],
                                                           start=(k == 0), stop=(k == 7)),
                             reads=["wd"] + ares, writes=["pkB"])
                    S.op(S.dve, lambda e: e.tensor_tensor(out=t1[b][64:96, :], in0=p_kA[64:96, :], in1=cosT[64:96, tok],
                                                          op=ALU.mult), reads=["pkA"], writes=[("t1", b)])
                    S.op(S.dve, lambda e: e.tensor_tensor(out=t2[b][64:96, :], in0=p_kB[64:96, :], in1=sinS[64:96, tok],
                                                          op=ALU.mult), reads=["pkB"], writes=[("t2", b)])
                    S.op(S.pool, lambda e: e.tensor_tensor(out=KT[64:96, tok], in0=t1[b][64:96, :], in1=t2[b][64:96, :],
                                                           op=ALU.add), reads=[("t1", b), ("t2", b)], writes=[("KTpe", j)])
                S.barrier()
            with ExitStack() as ph:
                sb = lambda n, shp, dt: ph.enter_context(nc.sbuf_tensor(self.nm(n), shp, dt))
                ps = lambda n, shp, dt: ph.enter_context(nc.psum_tensor(self.nm(n), shp, dt))
                wuq = sb("m_wuq", [128, 3, 2048], BF16)
                wukv = sb("m_wukv", [128, 2, 2048], BF16)
                Vx = [sb(f"m_Vx{i}", [128, 32, 128], BF16) for i in range(2)]
                QT = [sb(f"m_QT{i}", [96, 512], BF16) for i in range(2)]
                pt = [sb(f"m_pt{i}", [128, 512], BF16) for i in range(3)]
                t1 = [sb(f"m_u1{i}", [96, 512], F32) for i in range(2)]
                t2 = [sb(f"m_u2{i}", [96, 512], F32) for i in range(2)]
                rec = sb("m_rec", [128, 512], F32)
                bcs = sb("m_bcs", [128, 512], F32)
                p_p = [ps(f"m_pp{i}", [128, 512], F32) for i in range(2)]
                p_qA = ps("m_pqA", [96, 512], F32)
                p_qB = ps("m_pqB", [96, 512], F32)
                p_s = [ps(f"m_ps{i}", [128, 512], F32) for i in range(2)]
                p_o = ps("m_po", [128, 512], F32)
                p_bc = ps("m_pbc", [128, 512], F32)
                S.dma("pool", lambda e: e.dma_start(out=wuq[:], in_=self.mla_wuq[ia].rearrange("(k p) n -> p k n", p=128)),
                      writes=["wuq"])
                S.dma("pool", lambda e: e.dma_start(out=wukv[:], in_=self.mla_wukv[ia].rearrange("(k p) n -> p k n", p=128)),
                      writes=["wukv"])
                S.op(S.pool, lambda e: e.memset(Vx[0][:], 1.0), writes=[("Vx", 0)])
                S.op(S.pool, lambda e: e.memset(Vx[1][:], 1.0), writes=[("Vx", 1)])
                ipp = 0
                iq = 0
                ipt = 0
                for h in range(16):
                    hl, ch = h % 2, h // 2
                    r0, d0 = hl * 64, 64 - hl * 64
                    vx = Vx[hl]
                    for j in range(T // 512):
                        tok = slice(j * 512, (j + 1) * 512)
                        pp, rpp = p_p[ipp % 2], ("pp", ipp % 2)
                        ipp += 1
                        for k in range(2):
                            S.op(S.pe, lambda e, k=k: e.matmul(pp[0:64, :], lhsT=wukv[:, k, h * 64:(h + 1) * 64],
                                                               rhs=ckvn[:, k, tok], start=(k == 0), stop=(k == 1)),
                                 reads=["wukv"], writes=[rpp])
                        S.op(S.act, lambda e: e.copy(out=KT[0:64, tok], in_=pp[0:64, :]), reads=[rpp], writes=[("KT", j)])
                    for j in range(4):
                        pp, rpp = p_p[ipp % 2], ("pp", ipp % 2)
                        ipp += 1
                        ppv = pp[:].rearrange("p (b d) -> p b d", d=64)
                        for bb in range(8):
                            blk = j * 8 + bb
                            for k in range(2):
                                S.op(S.pe, lambda e, k=k: e.matmul(
                                    ppv[:, bb, :], lhsT=ckvn[:, k, blk * 128:(blk + 1) * 128],
                                    rhs=wukv[:, k, 1024 + h * 64:1024 + (h + 1) * 64], start=(k == 0), stop=(k == 1)),
                                     reads=["wukv"], writes=[rpp])
                        S.op(S.dve, lambda e: e.tensor_copy(out=vx[:, j * 8:(j + 1) * 8, r0:r0 + 64], in_=ppv),
                             reads=[rpp], writes=[("Vx", hl)])
                    for qt in range(T // 512):
                        tok = slice(qt * 512, (qt + 1) * 512)
                        qT, rq = QT[iq % 2], ("QT", iq % 2)
                        u1, u2 = t1[iq % 2], t2[iq % 2]
                        ru1, ru2 = ("u1", iq % 2), ("u2", iq % 2)
                        iq += 1
                        for k in range(3):
                            S.op(S.pe, lambda e, k=k: e.matmul(p_qA[:], lhsT=wuq[:, k, h * 128:h * 128 + 96],
                                                               rhs=cqn[:, k, tok], start=(k == 0), stop=(k == 2)),
                                 reads=["wuq"], writes=["pqA"])
                        for k in range(3):
                            S.op(S.pe, lambda e, k=k: e.matmul(p_qB[:], lhsT=wuq[:, k, h * 128 + 32:h * 128 + 128],
                                                               rhs=cqn[:, k, tok], start=(k == 0), stop=(k == 2)),
                                 reads=["wuq"], writes=["pqB"])
                        S.op(S.act, lambda e: e.copy(out=qT[0:64, :], in_=p_qA[0:64, :]), reads=["pqA"], writes=[rq])
                        S.op(S.dve, lambda e: e.tensor_tensor(out=u1[64:96, :], in0=p_qA[64:96, :], in1=cosT[64:96, tok],
                                                              op=ALU.mult), reads=["pqA"], writes=[ru1])
                        S.op(S.dve, lambda e: e.tensor_tensor(out=u2[64:96, :], in0=p_qB[64:96, :], in1=sinS[64:96, tok],
                                                              op=ALU.mult), reads=["pqB"], writes=[ru2])
                        S.op(S.pool, lambda e: e.tensor_tensor(out=qT[64:96, :], in0=u1[64:96, :], in1=u2[64:96, :],
                                                               op=ALU.add), reads=[ru1, ru2], writes=[rq])
                        nkb = 4 * qt + 4
                        for kb in range(nkb):
                            n0 = max(0, kb - 4 * qt) * 128
                            psb, rps = p_s[ipt % 2], ("ps", ipt % 2)
                            ptb, rpt = pt[ipt % 3], ("pt", ipt % 3)
                            ipt += 1
                            S.op(S.pe, lambda e: e.matmul(psb[:, n0:512], lhsT=KT[0:96, kb * 128:(kb + 1) * 128],
                                                          rhs=qT[0:96, n0:512], start=True, stop=True),
                                 reads=[("KT", kb // 4), rq], writes=[rps])
                            S.op(S.act, lambda e: e.activation(out=ptb[:, n0:512], in_=psb[:, n0:512], func=AF.Exp,
                                                               scale=float(SCALE)), reads=[rps], writes=[rpt])
                            if kb >= 4 * qt:
                                S.op(S.pool, lambda e: e.tensor_tensor(out=ptb[:, n0:n0 + 128], in0=ptb[:, n0:n0 + 128],
                                                                       in1=self.tri[:], op=ALU.mult),
                                     reads=[rpt, "tri"], writes=[rpt])
                            S.op(S.pe, lambda e: e.matmul(p_o[:, n0:512], lhsT=vx[:, kb, :], rhs=ptb[:, n0:512],
                                                          start=(kb == 0), stop=(kb == nkb - 1)),
                                 reads=[("Vx", hl), rpt], writes=["po"])
                        S.op(S.dve, lambda e: e.reciprocal(out=rec[d0:d0 + 1, :], in_=p_o[d0:d0 + 1, :]),
                             reads=["po"], writes=["rec"])
                        S.op(S.pe, lambda e: e.matmul(p_bc[:], lhsT=self.ones_f[d0:d0 + 1, :], rhs=rec[d0:d0 + 1, :],
                                                      start=True, stop=True), reads=["rec", "ones_f"], writes=["pbc"])
                        S.op(S.act, lambda e: e.copy(out=bcs[r0:r0 + 64, :], in_=p_bc[r0:r0 + 64, :]),
                             reads=["pbc"], writes=["bcs"])
                        S.op(S.dve, lambda e: e.tensor_tensor(out=actT[r0:r0 + 64, ch, tok], in0=p_o[r0:r0 + 64, :],
                                                              in1=bcs[r0:r0 + 64, :], op=ALU.mult),
                             reads=["po", "bcs"], writes=[("actT", 4 * qt + q) for q in range(4)])
                S.barrier()
        self.out_proj_phase(L, self.mla_wo[ia])


    def load_oT(self):
        S = self.S
        for c in range(8):
            S.dma("sp", lambda e, c=c: e.dma_start(out=self.actT[:, c, :], in_=self.oT_d[c]),
                  reads=[("oTd", c)], writes=[("actT", m) for m in range(NB)])

    def dil_phase(self, L):
        nc, S = self.nc, self.S
        actT = self.actT
        DIL = (1, 4, 16)
        with ExitStack() as ph:
            sb = lambda n, shp, dt: ph.enter_context(nc.sbuf_tensor(self.nm(n), shp, dt))
            ps = lambda n, shp, dt: ph.enter_context(nc.psum_tensor(self.nm(n), shp, dt))
            wd = sb("d_w", [128, 8, 9, 128], BF16)
            QK = [[sb(f"d_qk{g}{i}", [128, T], BF16) for i in range(2)] for g in range(3)]
            Vx = [sb(f"d_vx{g}", [128, 32, 192], BF16) for g in range(3)]
            osb = sb("d_osb", [128, T], BF16)
            mask2 = sb("d_mask2", [128, 256], BF16)
            pt = [sb(f"d_pt{i}", [128, 256], BF16) for i in range(3)]
            rec = sb("d_rec", [128, 512], F32)
            bcs = sb("d_bcs", [128, 512], F32)
            p_o = ps("d_po", [128, 2048], F32)
            p_s = [ps(f"d_ps{i}", [128, 512], F32) for i in range(2)]
            p_bc = ps("d_pbc", [128, 512], F32)
            p_p = ps("d_pp", [128, 512], F32)
            S.op(S.pool, lambda e: e.memset(mask2[:], 1.0), writes=["mask2"])
            S.op(S.pool, lambda e: e.affine_select(out=mask2[:, 0:128], in_=mask2[:, 0:128], pattern=[[-1, 128]],
                                                   compare_op=ALU.is_ge, fill=0.0, base=0, channel_multiplier=1),
                 reads=["mask2"], writes=["mask2"])
            S.op(S.pool, lambda e: e.affine_select(out=mask2[:, 128:256], in_=mask2[:, 128:256], pattern=[[1, 128]],
                                                   compare_op=ALU.is_ge, fill=0.0, base=0, channel_multiplier=-1),
                 reads=["mask2"], writes=["mask2"])
            for g in range(3):
                S.op(S.pool, lambda e, g=g: e.memset(Vx[g][:], 1.0), writes=[("Vx", g)])
            wsrc = self.dil_wqkv.rearrange("(k p) (c h d) -> p k c (h d)", p=128, c=9, h=16)
            ips = 0
            for hp in range(8):
                for c9 in range(9):
                    S.dma("pool", lambda e: e.dma_start(out=wd[:, :, c9, :], in_=wsrc[:, :, c9, hp * 128:(hp + 1) * 128]),
                          writes=[("wd", c9)])
                for j in range(T // 512):
                    tok = slice(j * 512, (j + 1) * 512)
                    ares = [("actT", 4 * j + q) for q in range(4)]
                    for g in range(3):
                        for qk in range(2):
                            for k in range(8):
                                S.op(S.pe, lambda e, k=k: e.matmul(p_p[:], lhsT=wd[:, k, g * 3 + qk, :],
                                                                   rhs=actT[:, k, tok], start=(k == 0), stop=(k == 7)),
                                     reads=[("wd", g * 3 + qk)] + ares, writes=["pp"])
                            eng = S.act if (g * 2 + qk) % 2 == 0 else S.dve
                            if eng is S.act:
                                S.op(S.act, lambda e: e.copy(out=QK[g][qk][:, tok], in_=p_p[:]),
                                     reads=["pp"], writes=[("QK", g, qk, j)])
                            else:
                                S.op(S.dve, lambda e: e.tensor_copy(out=QK[g][qk][:, tok], in_=p_p[:]),
                                     reads=["pp"], writes=[("QK", g, qk, j)])
                ppv = p_p[:].rearrange("p (b d) -> p b d", d=128)
                for g in range(3):
                    d = DIL[g]
                    for b4 in range(8):
                        for bb in range(4):
                            blk = b4 * 4 + bb
                            n, r = blk // d, blk % d
                            t0 = n * 128 * d + r
                            for k in range(8):
                                S.op(S.pe, lambda e, k=k: e.matmul(
                                    ppv[:, bb, :], lhsT=actT[:, k, t0:t0 + 127 * d + 1:d], rhs=wd[:, k, g * 3 + 2, :],
                                    start=(k == 0), stop=(k == 7)),
                                     reads=[("wd", g * 3 + 2)] + [("actT", m) for m in range(n * d, (n + 1) * d)], writes=["pp"])
                        S.op(S.act, lambda e: e.copy(out=Vx[g][:, b4 * 4:(b4 + 1) * 4, 0:64], in_=ppv[:, :, 0:64]),
                             reads=["pp"], writes=[("Vx", g)])
                        S.op(S.dve, lambda e: e.tensor_copy(out=Vx[g][:, b4 * 4:(b4 + 1) * 4, 128:192],
                                                            in_=ppv[:, :, 64:128]),
                             reads=["pp"], writes=[("Vx", g)])
                for hl in range(2):
                    r0, d0 = hl * 64, 64 - hl * 64
                    rows = slice(r0, r0 + 64)
                    vsl = slice(0, 128) if hl == 0 else slice(64, 192)
                    for U in range(2):
                        S.op(S.dve, lambda e: e.memset(p_o[:], 0.0), writes=["po"])
                        for g in range(3):
                            d = DIL[g]
                            Kt, Qt = QK[g][1], QK[g][0]
                            for qb in range(16):
                                blk = U * 16 + qb
                                n, r = blk // d, blk % d
                                t0 = n * 128 * d + r
                                qsl = slice(t0, t0 + 127 * d + 1, d)
                                osl = slice(t0 - U * 2048, t0 - U * 2048 + 127 * d + 1, d)
                                psb = p_s[(ips // 2) % 2]
                                half = (ips % 2) * 256
                                rps = ("ps", (ips // 2) % 2, ips % 2)
                                ptb, rpt = pt[ips % 3], ("pt", ips % 3)
                                ips += 1
                                c0 = 0 if n > 0 else 128
                                qres = [("QK", g, 0, (t0 // 512) + q) for q in range(max(1, (128 * d) // 512))]
                                if n > 0:
                                    tp = t0 - 128 * d
                                    S.op(S.pe, lambda e: e.matmul(psb[:, half:half + 128], lhsT=Kt[rows, tp:tp + 127 * d + 1:d],
                                                                  rhs=Qt[rows, qsl], start=True, stop=True),
                                         reads=[("QKall",)], writes=[rps])
                                S.op(S.pe, lambda e: e.matmul(psb[:, half + 128:half + 256], lhsT=Kt[rows, qsl],
                                                              rhs=Qt[rows, qsl], start=True, stop=True),
                                     reads=[("QKall",)], writes=[rps])
                                S.op(S.act, lambda e: e.activation(out=ptb[:, c0:256], in_=psb[:, half + c0:half + 256],
                                                                   func=AF.Exp, scale=0.125), reads=[rps], writes=[rpt])
                                S.op(S.pool, lambda e: e.tensor_tensor(out=ptb[:, c0:256], in0=ptb[:, c0:256],
                                                                       in1=mask2[:, c0:256], op=ALU.mult),
                                     reads=[rpt, "mask2"], writes=[rpt])
                                if n > 0:
                                    S.op(S.pe, lambda e: e.matmul(p_o[:, osl], lhsT=Vx[g][:, blk - d, vsl], rhs=ptb[:, 0:128],
                                                                  start=False, stop=False, skip_group_check=True),
                                         reads=[("Vx", g), rpt, "po"], writes=["po"])
                                S.op(S.pe, lambda e: e.matmul(p_o[:, osl], lhsT=Vx[g][:, blk, vsl], rhs=ptb[:, 128:256],
                                                              start=False, stop=False, skip_group_check=True),
                                     reads=[("Vx", g), rpt, "po"], writes=["po"])
                        for qq in range(4):
                            cs = slice(qq * 512, (qq + 1) * 512)
                            tok = slice(U * 2048 + qq * 512, U * 2048 + (qq + 1) * 512)
                            S.op(S.dve, lambda e: e.reciprocal(out=rec[d0:d0 + 1, :], in_=p_o[d0:d0 + 1, cs]),
                                 reads=["po"], writes=["rec"])
                            S.op(S.pe, lambda e: e.matmul(p_bc[:], lhsT=self.ones_f[d0:d0 + 1, :], rhs=rec[d0:d0 + 1, :],
                                                          start=True, stop=True), reads=["rec", "ones_f"], writes=["pbc"])
                            S.op(S.act, lambda e: e.copy(out=bcs[rows, :], in_=p_bc[rows, :]), reads=["pbc"], writes=["bcs"])
                            S.op(S.dve, lambda e: e.tensor_tensor(out=osb[rows, tok], in0=p_o[rows, cs], in1=bcs[rows, :],
                                                                  op=ALU.mult), reads=["po", "bcs"], writes=["osb"])
                S.dma("sp", lambda e: e.dma_start(out=self.oT_d[hp], in_=osb[:]), reads=["osb"], writes=[("oTd", hp)])
            S.barrier()
            self.load_oT()
            S.barrier()
        self.out_proj_phase(L, self.dil_wo)


    def rwkv_phase(self, L):
        nc, S = self.nc, self.S
        actT = self.actT
        RT = 256
        NT_ = T // RT
        GN_EPS = 64e-5
        DEC = -float(np.exp(-0.5))
        with ExitStack() as ph:
            sb = lambda n, shp, dt: ph.enter_context(nc.sbuf_tensor(self.nm(n), shp, dt))
            ps = lambda n, shp, dt: ph.enter_context(nc.psum_tensor(self.nm(n), shp, dt))
            mu = sb("r_mu", [128, 8, 6], F32)
            omu = sb("r_omu", [128, 8, 6], F32)
            vec = sb("r_vec", [128, 8, 8], F32)
            stage = sb("r_stage", [128, 8, 160], F32)
            l1a = sb("r_l1a", [128, 8, 288], BF16)
            l1b = sb("r_l1b", [128, 8, 288], BF16)
            hwa = sb("r_hwa", [128, T], BF16)
            hg = sb("r_hg", [128, 2, T], BF16)
            w2 = sb("r_w2", [128, D], BF16)
            g2 = sb("r_g2", [128, 2, D], BF16)
            wa = [sb(f"r_wa{i}", [128, 8, 128], BF16) for i in range(3)]
            wb = [sb(f"r_wb{i}", [128, 8, 128], BF16) for i in range(3)]
            f32t = lambda n: sb(n, [128, RT], F32)
            r_, k_, v_, lw, a_, g_, kk, km, be, Lc, eL, eLn, eLp, eD, tA, tB = [f32t(f"r_f{i}") for i in range(16)]
            LC = sb("r_LC", [128, 4], F32)
            eLC = sb("r_eLC", [128, 4], F32)
            rmask = sb("r_rmask", [128, RT], F32)
            blk1 = sb("r_blk1", [128, 128], F32)
            bdn = ["kap", "rt", "kt", "bt", "vf", "kb", "bb"]
            BD = {n: sb("r_bd_" + n, [128, 4, 128], F32) for n in bdn}
            MkvT, AkrT, AbrT, Y = [sb(f"r_m{i}", [128, 4, 128], F32) for i in range(4)]
            X = [sb(f"r_X{i}", [128, 4, 128], F32) for i in range(2)]
            XT = [sb(f"r_XT{i}", [128, 4, 128], F32) for i in range(2)]
            Vtok, Ktok, Btok = [sb(f"r_tk{i}", [128, 4, 128], F32) for i in range(3)]
            Wsb = sb("r_Wsb", [128, 128], F32)
            nU = sb("r_nU", [128, 128], F32)
            Abd = sb("r_Abd", [128, 128], F32)
            Osb = sb("r_Osb", [128, 128], F32)
            On = sb("r_On", [128, 4, 128], F32)
            ysb = sb("r_ysb", [128, RT], F32)
            osb = [sb(f"r_osb{i}", [128, RT], BF16) for i in range(2)]
            SU4, UI4, SL4, ID4 = [sb(f"r_msk{i}", [128, 4, 128], F32) for i in range(4)]
            identf = sb("r_identf", [128, 128], F32)
            st6 = sb("r_st6", [128, 6], F32)
            mv = sb("r_mv", [128, 8], F32)
            pP = [ps(f"r_pP{i}", [128, 512], F32) for i in range(2)]
            pX = [ps(f"r_pX{i}", [128, 512], F32) for i in range(5)]
            pS = ps("r_pS", [128, 512], F32)

            def aff(t, pattern, cm, op, base=0):
                S.op(S.pool, lambda e: e.affine_select(out=t, in_=t, pattern=pattern, compare_op=op, fill=0.0,
                                                       base=base, channel_multiplier=cm),
                     reads=["cst"], writes=["cst"])
            S.op(S.pool, lambda e: e.memset(identf[:], 1.0), writes=["cst"])
            aff(identf[:], [[1, 128]], -1, ALU.is_equal)
            for t4 in (SU4, UI4, SL4, ID4):
                S.op(S.pool, lambda e, t4=t4: e.memset(t4[:], 0.0), reads=["cst"], writes=["cst"])
            S.op(S.pool, lambda e: e.memset(blk1[:], 0.0), reads=["cst"], writes=["cst"])
            for hb in range(2):
                rs_ = slice(hb * 64, hb * 64 + 64)
                S.op(S.pool, lambda e: e.memset(blk1[rs_, rs_], 1.0), reads=["cst"], writes=["cst"])
                for c in range(4):
                    for t4 in (SU4, UI4, SL4, ID4):
                        S.op(S.pool, lambda e, t4=t4: e.memset(t4[rs_, c, rs_], 1.0), reads=["cst"], writes=["cst"])
                    aff(SU4[rs_, c, rs_], [[1, 64]], -1, ALU.is_gt)
                    aff(UI4[rs_, c, rs_], [[1, 64]], -1, ALU.is_ge)
                    aff(SL4[rs_, c, rs_], [[-1, 64]], 1, ALU.is_gt)
                    aff(ID4[rs_, c, rs_], [[1, 64]], -1, ALU.is_equal)
            S.op(S.pool, lambda e: e.memset(rmask[:], 1.0), reads=["cst"], writes=["cst"])
            for c in range(4):
                S.op(S.pool, lambda e, c=c: e.memset(rmask[:, c * 64:c * 64 + 1], 0.0), reads=["cst"], writes=["cst"])
            for n in bdn:
                S.op(S.pool, lambda e, n=n: e.memset(BD[n][:], 0.0), writes=[("bd", n)])
            S.op(S.pool, lambda e: e.memset(On[:], 0.0), writes=["On"])

            S.dma("sp", lambda e: e.dma_start(out=mu[:], in_=self.rw_mu), writes=["mu"])
            S.dma("sp", lambda e: e.dma_start(out=vec[:], in_=self.rw_vec), writes=["vec"])
            S.op(S.dve, lambda e: e.tensor_scalar(out=omu[:], in0=mu[:], scalar1=-1.0, scalar2=1.0, op0=ALU.mult,
                                                  op1=ALU.add), reads=["mu"], writes=["omu"])
            S.op(S.dve, lambda e: e.tensor_scalar(out=vec[:, :, 4:5], in0=vec[:, :, 3:4], scalar1=-1.0, scalar2=1.0,
                                                  op0=ALU.mult, op1=ALU.add), reads=["vec"], writes=["vec"])
            S.dma("pool", lambda e: e.dma_start(out=w2[0:64, :], in_=self.rw_w2), writes=["w2"])
            S.dma("pool", lambda e: e.dma_start(out=w2[64:128, :], in_=self.rw_a2), writes=["a2"])
            S.dma("pool", lambda e: e.dma_start(out=g2[:, 0, :], in_=self.rw_g2[0:128, :]), writes=["g2a"])
            S.dma("pool", lambda e: e.dma_start(out=g2[0:32, 1, :], in_=self.rw_g2[128:160, :]), writes=["g2b"])

            def scaled_weights(src3, ncols, dsts_a, dsts_b, jcols):
                S.dma("sp", lambda e: e.dma_start(out=stage[:, :, 0:ncols], in_=src3), writes=["stage"])
                for (c0, c1, j), da, db in zip(jcols, dsts_a, dsts_b):
                    for k in range(8):
                        S.op(S.pool, lambda e, k=k: e.tensor_scalar(out=da[:, k, :], in0=stage[:, k, c0:c1],
                                                                    scalar1=omu[:, k, j:j + 1], scalar2=None,
                                                                    op0=ALU.mult),
                             reads=["stage", "omu"], writes=["wsc"])
                        S.op(S.pool, lambda e, k=k: e.tensor_scalar(out=db[:, k, :], in0=stage[:, k, c0:c1],
                                                                    scalar1=mu[:, k, j:j + 1], scalar2=None,
                                                                    op0=ALU.mult),
                             reads=["stage", "mu"], writes=["wsc"])

            def proj(pout, la, lb, j, M=128, r0=0):
                t0 = j * RT
                ares = [("actT", m) for m in range(max(0, 2 * j - 1), 2 * j + 2)]
                for k in range(8):
                    S.op(S.pe, lambda e, k=k: e.matmul(pout[r0:r0 + M, 0:RT], lhsT=la(k), rhs=actT[:, k, t0:t0 + RT],
                                                       start=(k == 0), stop=False),
                         reads=["wsc"] + ares, writes=["pP"])
                c0 = 1 if j == 0 else 0
                for k in range(8):
                    S.op(S.pe, lambda e, k=k: e.matmul(pout[r0:r0 + M, c0:RT], lhsT=lb(k),
                                                       rhs=actT[:, k, t0 - 1 + c0:t0 + RT - 1],
                                                       start=False, stop=(k == 7)),
                         reads=["wsc"] + ares, writes=["pP"])

            l1src = self.rw_l1.rearrange("(k p) n -> p k n", p=128)
            scaled_weights(l1src[:, :, 0:128], 128, [l1a[:, :, 0:64], l1a[:, :, 64:128]],
                           [l1b[:, :, 0:64], l1b[:, :, 64:128]], [(0, 64, 3), (64, 128, 4)])
            scaled_weights(l1src[:, :, 128:288], 160, [l1a[:, :, 128:288]], [l1b[:, :, 128:288]], [(0, 160, 5)])
            for j in range(NT_):
                tok = slice(j * RT, (j + 1) * RT)
                p0 = pP[j % 2]
                proj(p0, lambda k: l1a[:, k, 0:128], lambda k: l1b[:, k, 0:128], j)
                S.op(S.act, lambda e: e.activation(out=hwa[0:64, tok], in_=p0[0:64, 0:RT], func=AF.Tanh),
                     reads=["pP"], writes=["hwa"])
                S.op(S.act, lambda e: e.copy(out=hwa[64:128, tok], in_=p0[64:128, 0:RT]), reads=["pP"], writes=["hwa"])
                proj(p0, lambda k: l1a[:, k, 128:256], lambda k: l1b[:, k, 128:256], j)
                S.op(S.act, lambda e: e.activation(out=hg[:, 0, tok], in_=p0[:, 0:RT], func=AF.Sigmoid),
                     reads=["pP"], writes=["hg"])
                proj(p0, lambda k: l1a[:, k, 256:288], lambda k: l1b[:, k, 256:288], j, M=32)
                S.op(S.act, lambda e: e.activation(out=hg[0:32, 1, tok], in_=p0[0:32, 0:RT], func=AF.Sigmoid),
                     reads=["pP"], writes=["hg"])

            wsrc = self.rw_wrkv.rearrange("j (k p) n -> j p k n", p=128)
            H0, H1 = slice(0, 64), slice(64, 128)

            def dv(fn, reads, writes, eng=None):
                S.op(eng or S.dve, fn, reads=reads, writes=writes)

            import os
            STOP = int(os.environ.get("RW_STOP", "9"))
            for hp in range(8 if STOP > 0 else 0):
                cols = slice(hp * 128, (hp + 1) * 128)
                for jj in range(3):
                    scaled_weights(wsrc[jj][:, :, cols], 128, [wa[jj]], [wb[jj]], [(0, 128, jj)])
                S.op(S.pool, lambda e: e.memset(Abd[:], 0.0), reads=["Abd"], writes=["Abd"])
                vcol = lambda i: vec[:, hp, i:i + 1]
                for j in range(NT_):
                    tok = slice(j * RT, (j + 1) * RT)
                    F = "F"
                    for jj, dst in enumerate((r_, k_, v_)):
                        p0 = pP[jj % 2]
                        proj(p0, lambda k: wa[jj][:, k, :], lambda k: wb[jj][:, k, :], j)
                        S.op(S.act, lambda e: e.copy(out=dst[:], in_=p0[:, 0:RT]), reads=["pP"], writes=[F])
                    p0 = pP[1]
                    S.op(S.pe, lambda e: e.matmul(p0[:, 0:RT], lhsT=w2[0:64, cols], rhs=hwa[0:64, tok], start=True,
                                                  stop=True), reads=["w2", "hwa"], writes=["pP"])
                    S.op(S.act, lambda e: e.activation(out=lw[:], in_=p0[:, 0:RT], func=AF.Sigmoid, bias=vcol(0)),
                         reads=["pP", "vec"], writes=[F])
                    S.op(S.pe, lambda e: e.matmul(p0[:, 0:RT], lhsT=w2[64:128, cols], rhs=hwa[64:128, tok], start=True,
                                                  stop=True), reads=["a2", "hwa"], writes=["pP"])
                    S.op(S.act, lambda e: e.activation(out=a_[:], in_=p0[:, 0:RT], func=AF.Sigmoid, bias=vcol(1)),
                         reads=["pP", "vec"], writes=[F])
                    S.op(S.pe, lambda e: e.matmul(p0[:, 0:RT], lhsT=g2[:, 0, cols], rhs=hg[:, 0, tok], start=True,
                                                  stop=False), reads=["g2a", "hg"], writes=["pP"])
                    S.op(S.pe, lambda e: e.matmul(p0[:, 0:RT], lhsT=g2[0:32, 1, cols], rhs=hg[0:32, 1, tok], start=False,
                                                  stop=True), reads=["g2b", "hg"], writes=["pP"])
                    S.op(S.act, lambda e: e.copy(out=g_[:], in_=p0[:, 0:RT]), reads=["pP"], writes=[F])
                    dv(lambda e: e.tensor_scalar(out=lw[:], in0=lw[:], scalar1=DEC, scalar2=None, op0=ALU.mult), [F], [F])
                    dv(lambda e: e.tensor_scalar(out=kk[:], in0=k_[:], scalar1=vcol(2), scalar2=None, op0=ALU.mult),
                       [F, "vec"], [F])
                    dv(lambda e: e.tensor_tensor(out=tA[:], in0=kk[:], in1=kk[:], op=ALU.mult), [F], [F], S.pool)
                    S.op(S.pe, lambda e: e.matmul(pP[0][:, 0:RT], lhsT=blk1[:], rhs=tA[:], start=True, stop=True),
                         reads=[F, "cst"], writes=["pP"])
                    S.op(S.act, lambda e: e.activation(out=tA[:], in_=pP[0][:, 0:RT], func=AF.Sqrt), reads=["pP"], writes=[F])
                    dv(lambda e: e.tensor_scalar(out=tA[:], in0=tA[:], scalar1=1e-12, scalar2=None, op0=ALU.max), [F], [F])
                    dv(lambda e: e.reciprocal(out=tA[:], in_=tA[:]), [F], [F])
                    dv(lambda e: e.tensor_tensor(out=kk[:], in0=kk[:], in1=tA[:], op=ALU.mult), [F], [F])
                    dv(lambda e: e.tensor_scalar(out=tB[:], in0=a_[:], scalar1=vcol(3), scalar2=vcol(4), op0=ALU.mult,
                                                 op1=ALU.add), [F, "vec"], [F])
                    dv(lambda e: e.tensor_tensor(out=km[:], in0=k_[:], in1=tB[:], op=ALU.mult), [F], [F])
                    dv(lambda e: e.tensor_tensor(out=be[:], in0=kk[:], in1=a_[:], op=ALU.mult), [F], [F], S.pool)
                    dv(lambda e: e.scalar_tensor_tensor(out=tB[:], in0=r_[:], scalar=vcol(5), in1=km[:], op0=ALU.mult,
                                                        op1=ALU.mult), [F, "vec"], [F])
                    S.op(S.pe, lambda e: e.matmul(pP[0][:, 0:RT], lhsT=blk1[:], rhs=tB[:], start=True, stop=True),
                         reads=[F, "cst"], writes=["pP"])
                    dv(lambda e: e.tensor_tensor(out=tA[:], in0=pP[0][:, 0:RT], in1=v_[:], op=ALU.mult), ["pP", F], [F])
                    dv(lambda e: e.tensor_tensor_scan(out=Lc[:], data0=rmask[:], data1=lw[:], initial=0.0,
                                                      op0=ALU.mult, op1=ALU.add), [F, "cst"], [F])
                    S.op(S.act, lambda e: e.activation(out=eL[:], in_=Lc[:], func=AF.Exp), reads=[F], writes=[F])
                    S.op(S.act, lambda e: e.activation(out=eLn[:], in_=Lc[:], func=AF.Exp, scale=-1.0), reads=[F], writes=[F])
                    dv(lambda e: e.tensor_tensor(out=eLp[:], in0=Lc[:], in1=lw[:], op=ALU.subtract), [F], [F], S.pool)
                    S.op(S.act, lambda e: e.activation(out=eLp[:], in_=eLp[:], func=AF.Exp), reads=[F], writes=[F])
                    L3 = Lc[:].rearrange("p (c t) -> p c t", t=64)
                    dv(lambda e: e.tensor_copy(out=LC[:], in_=L3[:, :, 63]), [F], [F])
                    S.op(S.act, lambda e: e.activation(out=eLC[:], in_=LC[:], func=AF.Exp), reads=[F], writes=[F])
                    for c in range(4):
                        S.op(S.act, lambda e, c=c: e.activation(out=eD[:, c * 64:(c + 1) * 64], in_=Lc[:, c * 64:(c + 1) * 64],
                                                                func=AF.Exp, scale=-1.0, bias=LC[:, c:c + 1]),
                             reads=[F], writes=[F])
                    prods = [("kap", kk, eLp), ("rt", r_, eL), ("kt", km, eLn), ("bt", be, eLn), ("kb", km, eD),
                             ("bb", be, eD)]
                    ie = 0
                    for n, x0, x1 in prods:
                        for hs in (H0, H1):
                            eng = S.dve if ie % 2 == 0 else S.pool
                            ie += 1
                            dv(lambda e: e.tensor_tensor(out=BD[n][hs, :, hs],
                                                         in0=x0[hs, :].rearrange("p (c t) -> p c t", t=64),
                                                         in1=x1[hs, :].rearrange("p (c t) -> p c t", t=64), op=ALU.mult),
                               [F], [("bd", n)], eng)
                    for hs in (H0, H1):
                        S.op(S.act, lambda e: e.copy(out=BD["vf"][hs, :, hs], in_=v_[hs, :].rearrange("p (c t) -> p c t", t=64)),
                             reads=[F], writes=[("bd", "vf")])
                    if STOP < 2:
                        continue
                    for c in range(4):
                        cs = slice(c * 128, (c + 1) * 128)
                        gm = [(0, "kt", "kap"), (1, "kt", "rt"), (2, "bt", "kap"), (3, "bt", "rt"), (4, "kap", "bt")]
                        for pi, ln_, rn_ in gm:
                            S.op(S.pe, lambda e: e.matmul(pX[pi][:, cs], lhsT=BD[ln_][:, c, :], rhs=BD[rn_][:, c, :],
                                                          start=True, stop=True),
                                 reads=[("bd", ln_), ("bd", rn_)], writes=[("pX", pi)])
                    f4 = lambda t: t[:].rearrange("p c t -> p (c t)")
                    dv(lambda e: e.tensor_tensor(out=f4(MkvT), in0=pX[0][:], in1=f4(SU4), op=ALU.mult), [("pX", 0), "cst"], ["MkvT"])
                    dv(lambda e: e.tensor_tensor(out=f4(AkrT), in0=pX[1][:], in1=f4(UI4), op=ALU.mult), [("pX", 1), "cst"], ["AkrT"])
                    dv(lambda e: e.tensor_tensor(out=f4(X[0]), in0=pX[2][:], in1=f4(SU4), op=ALU.mult), [("pX", 2), "cst"], [("X", 0)])
                    dv(lambda e: e.tensor_tensor(out=f4(AbrT), in0=pX[3][:], in1=f4(UI4), op=ALU.mult), [("pX", 3), "cst"], ["AbrT"])
                    dv(lambda e: e.tensor_tensor(out=f4(XT[0]), in0=pX[4][:], in1=f4(SL4), op=ALU.mult), [("pX", 4), "cst"], [("XT", 0)])
                    if STOP < 3:
                        continue
                    dv(lambda e: e.tensor_tensor(out=f4(Y), in0=f4(ID4), in1=f4(X[0]), op=ALU.subtract),
                       [("X", 0), "cst"], ["Y"], S.pool)
                    cur = 0
                    for lvl in range(int(os.environ.get('RW_LVL', '5'))):
                        nxt = 1 - cur
                        last = (lvl == 4)
                        for c in range(4):
                            cs = slice(c * 128, (c + 1) * 128)
                            if not last:
                                S.op(S.pe, lambda e: e.matmul(pX[0][:, cs], lhsT=XT[cur][:, c, :], rhs=X[cur][:, c, :],
                                                              start=True, stop=True),
                                     reads=[("X", cur), ("XT", cur)], writes=[("pX", 0)])
                            S.op(S.pe, lambda e: e.matmul(pX[2][:, cs], lhsT=X[cur][:, c, :], rhs=XT[cur][:, c, :],
                                                          start=True, stop=True),
                                 reads=[("X", cur), ("XT", cur)], writes=[("pX", 2)])
                        if not last:
                            dv(lambda e: e.tensor_copy(out=f4(X[nxt]), in_=pX[0][:]), [("pX", 0)], [("X", nxt)])
                        S.op(S.act, lambda e: e.copy(out=f4(XT[nxt]), in_=pX[2][:]), reads=[("pX", 2)], writes=[("XT", nxt)])
                        for c in range(4):
                            cs = slice(c * 128, (c + 1) * 128)
                            S.op(S.pe, lambda e: e.matmul(pX[1][:, cs], lhsT=XT[nxt][:, c, :],
                                                          rhs=Y[:, c, :], start=True, stop=True),
                                 reads=[("XT", nxt), "Y"], writes=[("pX", 1)])
                        dv(lambda e: e.tensor_tensor(out=f4(Y), in0=f4(Y), in1=pX[1][:], op=ALU.add),
                           [("pX", 1), "Y"], ["Y"])
                        cur = nxt
                    for pi, n, dst in ((3, "vf", Vtok), (4, "kb", Ktok), (1, "bb", Btok)):
                        for c in range(4):
                            S.op(S.pe, lambda e: e.transpose(out=pX[pi][:, c * 128:(c + 1) * 128], in_=BD[n][:, c, :],
                                                             identity=identf[:]),
                                 reads=[("bd", n), "cst"], writes=[("pX", pi)])
                        S.op(S.act, lambda e: e.copy(out=f4(dst), in_=pX[pi][:]), reads=[("pX", pi)], writes=[("tok", n)])
                    if STOP < 4:
                        continue
                    for c in range(4):
                        S.op(S.pe, lambda e: e.matmul(pX[0][:, 0:128], lhsT=BD["kap"][:, c, :], rhs=Abd[:], start=True, stop=False),
                             reads=[("bd", "kap"), "Abd"], writes=["pW"])
                        S.op(S.pe, lambda e: e.matmul(pX[0][:, 0:128], lhsT=MkvT[:, c, :], rhs=Vtok[:, c, :], start=False, stop=True),
                             reads=["MkvT", ("tok", "vf")], writes=["pW"])
                        S.op(S.act, lambda e: e.copy(out=Wsb[:], in_=pX[0][:, 0:128]), reads=["pW"], writes=["Wsb"])
                        S.op(S.pe, lambda e: e.matmul(pX[1][:, 0:128], lhsT=Y[:, c, :], rhs=Wsb[:], start=True, stop=True),
                             reads=["Y", "Wsb"], writes=["pU"])
                        dv(lambda e: e.tensor_scalar(out=nU[:], in0=pX[1][:, 0:128], scalar1=-1.0, scalar2=None, op0=ALU.mult),
                           ["pU"], ["nU"])
                        S.op(S.pe, lambda e: e.matmul(pX[2][:, 0:128], lhsT=BD["rt"][:, c, :], rhs=Abd[:], start=True, stop=False),
                             reads=[("bd", "rt"), "Abd"], writes=["pO"])
                        S.op(S.pe, lambda e: e.matmul(pX[2][:, 0:128], lhsT=AkrT[:, c, :], rhs=Vtok[:, c, :], start=False, stop=False),
                             reads=["AkrT", ("tok", "vf")], writes=["pO"])
                        S.op(S.pe, lambda e: e.matmul(pX[2][:, 0:128], lhsT=AbrT[:, c, :], rhs=nU[:], start=False, stop=True),
                             reads=["AbrT", "nU"], writes=["pO"])
                        S.op(S.pe, lambda e: e.matmul(pX[3][:, 0:128], lhsT=Ktok[:, c, :], rhs=Vtok[:, c, :], start=True, stop=False),
                             reads=[("tok", "kb"), ("tok", "vf")], writes=["pA"])
                        S.op(S.pe, lambda e: e.matmul(pX[3][:, 0:128], lhsT=Btok[:, c, :], rhs=nU[:], start=False, stop=True),
                             reads=[("tok", "bb"), "nU"], writes=["pA"])
                        dv(lambda e: e.scalar_tensor_tensor(out=Abd[:], in0=Abd[:], scalar=eLC[:, c:c + 1], in1=pX[3][:, 0:128],
                                                            op0=ALU.mult, op1=ALU.add), ["pA", "Abd", F], ["Abd"])
                        if int(os.environ.get("RW_SEQ", "9")) < 1:
                            continue
                        S.op(S.act, lambda e: e.copy(out=Osb[:], in_=pX[2][:, 0:128]), reads=["pO"], writes=["Osb"])
                        for hs in (H0, H1):
                            dv(lambda e: e.bn_stats(out=st6[hs, :], in_=Osb[hs, hs]), ["Osb"], ["st6"])
                        dv(lambda e: e.bn_aggr(out=mv[:, 0:2], in_=st6[:]), ["st6"], ["mv"])
                        dv(lambda e: e.tensor_scalar(out=mv[:, 2:3], in0=mv[:, 1:2], scalar1=GN_EPS, scalar2=None, op0=ALU.add),
                           ["mv"], ["mv"])
                        S.op(S.act, lambda e: e.activation(out=mv[:, 3:4], in_=mv[:, 2:3], func=AF.Sqrt), reads=["mv"], writes=["mv"])
                        dv(lambda e: e.reciprocal(out=mv[:, 4:5], in_=mv[:, 3:4]), ["mv"], ["mv"])
                        dv(lambda e: e.scalar_tensor_tensor(out=mv[:, 5:6], in0=mv[:, 0:1], scalar=-1.0, in1=mv[:, 4:5],
                                                            op0=ALU.mult, op1=ALU.mult), ["mv"], ["mv"])
                        for hs in (H0, H1):
                            S.op(S.act, lambda e: e.activation(out=On[hs, c, hs], in_=Osb[hs, hs], func=AF.Identity,
                                                               bias=mv[hs, 5:6], scale=mv[hs, 4:5]),
                                 reads=["mv", "Osb"], writes=["On"])
                        if int(os.environ.get("RW_SEQ", "9")) < 2:
                            continue
                        S.op(S.pe, lambda e: e.transpose(out=pP[1][:, c * 128:(c + 1) * 128], in_=On[:, c, :], identity=identf[:]),
                             reads=["On", "cst"], writes=["pP"])
                    if int(os.environ.get("RW_SEQ", "9")) < 3:
                        continue
                    for hs in (H0, H1):
                        dv(lambda e: e.tensor_scalar(out=ysb[hs, :].rearrange("p (c t) -> p c t", t=64),
                                                     in0=pP[1][:].rearrange("p (c t) -> p c t", t=128)[hs, :, hs],
                                                     scalar1=vec[hs, hp, 6:7], scalar2=vec[hs, hp, 7:8], op0=ALU.mult,
                                                     op1=ALU.add), ["pP", "vec"], ["ysb"])
                    dv(lambda e: e.tensor_tensor(out=ysb[:], in0=ysb[:], in1=tA[:], op=ALU.add), ["ysb", F], ["ysb"], S.pool)
                    ob = osb[j % 2]
                    dv(lambda e: e.tensor_tensor(out=ob[:], in0=ysb[:], in1=g_[:], op=ALU.mult), ["ysb", F], [("osb", j % 2)], S.pool)
                    S.dma("sp", lambda e: e.dma_start(out=self.oT_d[hp][:, tok], in_=ob[:]), reads=[("osb", j % 2)],
                          writes=[("oTd", hp)])
            S.barrier()
            self.load_oT()
            S.barrier()
        self.out_proj_phase(L, self.rw_wo)


def host_layout(inp):
    out = {}
    cw = np.zeros((DEPTH, NCH * 128, 4), np.float32)
    cw[:, :D_FF, 0:3] = np.transpose(inp["ffn_conv_w"], (0, 2, 1))
    cw[:, :D_FF, 3] = inp["ffn_conv_b"]
    out["ffn_cw"] = np.ascontiguousarray(cw.reshape(DEPTH, NCH, 128, 4).transpose(0, 2, 1, 3))
    for k in ("ln_g", "ln_b", "ffn_w_in", "ffn_w_out"):
        out[k] = np.ascontiguousarray(inp[k], dtype=np.float32)
    for k in ("dil_w_qkv", "dil_w_o"):
        out[k] = np.ascontiguousarray(inp[k][0], dtype=np.float32)
    fm = lambda v: np.ascontiguousarray(np.asarray(v, np.float32).reshape(8, 128).T)
    out["rw_mu"] = np.ascontiguousarray(inp["rwkv_mu"][0].reshape(6, 8, 128).transpose(2, 1, 0))
    ka = inp["rwkv_k_a"][0]
    vecs = [inp["rwkv_w0"][0], inp["rwkv_a0"][0], inp["rwkv_k_k"][0], ka, None, inp["rwkv_r_k"][0].reshape(-1),
            inp["rwkv_ln_w"][0], inp["rwkv_ln_b"][0]]
    rv = np.zeros((128, 8, 8), np.float32)
    for i, v in enumerate(vecs):
        if v is not None:
            rv[:, :, i] = fm(v)
    out["rw_vec"] = rv
    out["rwkv_w_rkv"] = np.ascontiguousarray(inp["rwkv_w_rkv"][0], dtype=np.float32)
    out["rw_l1"] = np.ascontiguousarray(np.concatenate([inp["rwkv_w1"][0], inp["rwkv_a1"][0], inp["rwkv_g1"][0]], axis=1))
    for k in ("rwkv_w2", "rwkv_a2", "rwkv_g2", "rwkv_w_o"):
        out[k] = np.ascontiguousarray(inp[k][0], dtype=np.float32)
    wd = inp["mla_w_down"]
    out["mla_wd"] = np.ascontiguousarray(np.concatenate(
        [wd[:, :, 0:640], wd[:, :, 0:64], wd[:, :, 640:672], wd[:, :, 656:672], wd[:, :, 640:656]], axis=2))
    out["mla_qn"] = np.ascontiguousarray(inp["mla_q_norm"].reshape(-1, 3, 128).transpose(0, 2, 1))
    out["mla_kvn"] = np.ascontiguousarray(inp["mla_kv_norm"].reshape(-1, 2, 128).transpose(0, 2, 1))
    wq = inp["mla_w_uq"].reshape(-1, 384, 16, 96)
    out["mla_wuq"] = np.ascontiguousarray(np.concatenate(
        [wq[..., 0:96], wq[..., 80:96], wq[..., 64:80]], axis=3).reshape(-1, 384, 2048))
    wkv = inp["mla_w_ukv"].reshape(-1, 256, 16, 128)
    out["mla_wukv"] = np.ascontiguousarray(np.concatenate(
        [wkv[..., 0:64].reshape(-1, 256, 1024), wkv[..., 64:128].reshape(-1, 256, 1024)], axis=2))
    out["mla_wo"] = np.ascontiguousarray(inp["mla_w_o"], dtype=np.float32)
    rc = np.zeros((96, 2), np.float32)
    invf = (10000.0 ** (-np.arange(0, 32, 2, dtype=np.float32) / np.float32(32))).astype(np.float32)
    rc[64:80, 0] = invf / np.float32(2 * np.pi)
    rc[80:96, 0] = invf / np.float32(2 * np.pi)
    rc[64:80, 1] = -1.0
    rc[80:96, 1] = 1.0
    out["rope_c"] = rc
    return out


DEFAULT_PLAN = [("mla", 0), ("ffn", 0), ("dil", 1), ("ffn", 1), ("rwkv", 2), ("ffn", 2), ("mla", 3), ("ffn", 3)]
_CACHE = {}


def run(inputs, plan, n_cores=8, trace=False):
    key = tuple(plan)
    if key not in _CACHE:
        b = Builder(plan)
        nc = b.build()
        _CACHE[key] = (b, nc)
    b, nc = _CACHE[key]
    shared = host_layout(inputs)
    in_maps = []
    for c in range(n_cores):
        d = {"x": np.ascontiguousarray(inputs["x"][c], dtype=np.float32),
             "positions": np.ascontiguousarray(inputs["positions"][c], dtype=np.int32)}
        d.update(shared)
        d = {k: v for k, v in d.items() if k in b.din}
        in_maps.append(d)
    res = run_bass_kernel_spmd(nc, in_maps, core_ids=list(range(n_cores)), trace=trace)
    return np.stack([r["out"] for r in res.results], axis=0), res


def kernel(**inputs):
    out, _ = run(inputs, DEFAULT_PLAN)
    return out.astype(np.float32)
```

```python
import numpy as np
from contextlib import ExitStack
import concourse.bass as bass
import concourse.mybir as mybir
from concourse.bass_utils import run_bass_kernel_spmd

F32 = mybir.dt.float32
BF16 = mybir.dt.bfloat16
I32 = mybir.dt.int32
AF = mybir.ActivationFunctionType
ALU = mybir.AluOpType

T = 4096
D = 1024
DEPTH = 4
NB = T // 128
ALPHA = (2 * DEPTH) ** 0.25
LN_EPS = 1e-5
RMS_EPS = 1e-6
D_FF = 2752
NCH = 22


class _Eng:
    def __init__(self, name, eng, sem):
        self.name = name
        self.eng = eng
        self.sem = sem
        self.count = 0
        self.waited = {}


class Sched:
    def __init__(self, nc, stack, n_dma_sems=16):
        self.nc = nc
        mk = lambda n: stack.enter_context(nc.semaphore(n))
        self.pe = _Eng("pe", nc.tensor, mk("s_pe"))
        self.act = _Eng("act", nc.scalar, mk("s_act"))
        self.dve = _Eng("dve", nc.vector, mk("s_dve"))
        self.pool = _Eng("pool", nc.gpsimd, mk("s_pool"))
        self.sp = _Eng("sp", nc.sync, None)
        self.q = {"sp": [mk(f"s_dsp{i}") for i in range(n_dma_sems)],
                  "pool": [mk(f"s_dpl{i}") for i in range(n_dma_sems)]}
        self.qeng = {"sp": self.sp, "pool": self.pool}
        self.dma_cnt = {"sp": 0, "pool": 0}
        self.dma_last = {}
        self.last_write = {}
        self.readers = {}
        self.n_ops = 0
        self.n_waits = 0

    def _wait(self, E, tok):
        sem, val, src = tok
        if src == "pe" and E.name == "pe":
            return
        k = id(sem)
        if E.waited.get(k, 0) >= val:
            return
        E.eng.wait_ge(sem, val)
        E.waited[k] = val
        self.n_waits += 1

    def _deps(self, E, reads, writes):
        for r in reads:
            t = self.last_write.get(r)
            if t is not None:
                self._wait(E, t)
        for w in writes:
            t = self.last_write.get(w)
            if t is not None:
                self._wait(E, t)
            for t in self.readers.get(w, ()):
                self._wait(E, t)

    def _commit(self, tok, reads, writes):
        for r in reads:
            self.readers.setdefault(r, []).append(tok)
        for w in writes:
            self.last_write[w] = tok
            self.readers[w] = []

    def op(self, E, fn, reads=(), writes=()):
        self._deps(E, reads, writes)
        ins = fn(E.eng)
        E.count += 1
        ins.then_inc(E.sem, 1)
        tok = (E.sem, E.count, E.name)
        self._commit(tok, reads, writes)
        self.n_ops += 1
        return tok

    def dma(self, qname, fn, reads=(), writes=()):
        E = self.qeng[qname]
        pool = self.q[qname]
        i = self.dma_cnt[qname]
        self.dma_cnt[qname] = i + 1
        slot = i % len(pool)
        prev = self.dma_last.get((qname, slot))
        if prev is not None:
            self._wait(E, prev)
        self._deps(E, reads, writes)
        ins = fn(E.eng)
        ins.then_inc(pool[slot], 16)
        tok = (pool[slot], 16 * (i // len(pool) + 1), "dma_" + qname)
        self.dma_last[(qname, slot)] = tok
        self._commit(tok, reads, writes)
        self.n_ops += 1
        return tok

    def barrier(self):
        toks = [(E.sem, E.count, E.name) for E in (self.pe, self.act, self.dve, self.pool) if E.count]
        toks += list(self.dma_last.values())
        for E in (self.pe, self.act, self.dve, self.pool, self.sp):
            for t in toks:
                if t[2] == E.name:
                    continue
                self._wait(E, t)
        self.last_write = {}
        self.readers = {}


class Builder:
    def __init__(self, plan, debug_out=False):
        self.plan = plan
        nc = bass.Bass("TRN2", target_bir_lowering=False)
        self.nc = nc
        self.din = {}

    def nm(self, n):
        self._uid = getattr(self, "_uid", 0) + 1
        return f"{n}_{self._uid}"

    def dram_in(self, name, shape, dt=F32):
        t = self.nc.dram_tensor(name, list(shape), dt, kind="ExternalInput").ap()
        self.din[name] = t
        return t

    def build(self):
        nc = self.nc
        plan = self.plan
        self.x_in = self.dram_in("x", [T, D])
        self.out = nc.dram_tensor("out", [T, D], F32, kind="ExternalOutput").ap()
        self.ln_g = self.dram_in("ln_g", [DEPTH, 2, D])
        self.ln_b = self.dram_in("ln_b", [DEPTH, 2, D])
        self.ffn_w_in = self.dram_in("ffn_w_in", [DEPTH, D, 2 * D_FF])
        self.ffn_w_out = self.dram_in("ffn_w_out", [DEPTH, D_FF, D])
        self.ffn_cw = self.dram_in("ffn_cw", [DEPTH, 128, NCH, 4])
        kinds = {k for k, _ in plan}
        if "dil" in kinds or "rwkv" in kinds:
            self.oT_d = nc.dram_tensor("oT_d", [8, 128, T], BF16, kind="Internal").ap()
        if "dil" in kinds:
            self.dil_wqkv = self.dram_in("dil_w_qkv", [D, 9216])
            self.dil_wo = self.dram_in("dil_w_o", [D, D])
        if "rwkv" in kinds:
            self.rw_mu = self.dram_in("rw_mu", [128, 8, 6])
            self.rw_wrkv = self.dram_in("rwkv_w_rkv", [3, D, D])
            self.rw_l1 = self.dram_in("rw_l1", [D, 288])
            self.rw_w2 = self.dram_in("rwkv_w2", [64, D])
            self.rw_a2 = self.dram_in("rwkv_a2", [64, D])
            self.rw_g2 = self.dram_in("rwkv_g2", [160, D])
            self.rw_vec = self.dram_in("rw_vec", [128, 8, 8])
            self.rw_wo = self.dram_in("rwkv_w_o", [D, D])
        if "mla" in kinds:
            self.pos = self.dram_in("positions", [T], I32)
            self.rope_c = self.dram_in("rope_c", [96, 2])
            self.mla_wd = self.dram_in("mla_wd", [2, D, 768])
            self.mla_qn = self.dram_in("mla_qn", [2, 128, 3])
            self.mla_kvn = self.dram_in("mla_kvn", [2, 128, 2])
            self.mla_wuq = self.dram_in("mla_wuq", [2, 384, 2048])
            self.mla_wukv = self.dram_in("mla_wukv", [2, 256, 2048])
            self.mla_wo = self.dram_in("mla_wo", [2, D, D])

        with ExitStack() as st:
            self.st = st
            S = self.S = Sched(nc, st)
            gsb = lambda n, shp, dt: st.enter_context(nc.sbuf_tensor(self.nm(n), shp, dt))
            self.actT = gsb("actT", [128, 8, T], BF16)
            self.ident = gsb("ident", [128, 128], BF16)
            self.lng = gsb("lng", [128, D], F32)
            self.lnb = gsb("lnb", [128, D], F32)
            self.ones_bf = gsb("ones_bf", [128, 128], BF16)
            self.ones_f = gsb("ones_f", [128, 128], F32)
            self.tri = gsb("tri", [128, 128], BF16)
            self.ep_idx = 0
            self.cur_src = self.x_in

            S.op(S.pool, lambda e: e.memset(self.ident[:], 1.0), writes=["ident"])
            S.op(S.pool, lambda e: e.affine_select(out=self.ident[:], in_=self.ident[:], pattern=[[1, 128]],
                                                   compare_op=ALU.is_equal, fill=0.0, base=0,
                                                   channel_multiplier=-1),
                 reads=["ident"], writes=["ident"])
            S.op(S.pool, lambda e: e.memset(self.ones_bf[:], 1.0), writes=["ones_bf"])
            S.op(S.pool, lambda e: e.memset(self.ones_f[:], 1.0), writes=["ones_f"])
            S.op(S.pool, lambda e: e.memset(self.tri[:], 1.0), writes=["tri"])
            S.op(S.pool, lambda e: e.affine_select(out=self.tri[:], in_=self.tri[:], pattern=[[1, 128]],
                                                   compare_op=ALU.is_ge, fill=0.0, base=0,
                                                   channel_multiplier=-1),
                 reads=["tri"], writes=["tri"])
            self.init_phase()
            for step in plan:
                kind, L = step
                if kind == "ffn":
                    self.ffn_phase(L)
                elif kind == "mla":
                    self.mla_phase(L)
                elif kind == "dil":
                    self.dil_phase(L)
                elif kind == "rwkv":
                    self.rwkv_phase(L)
                elif kind == "copy":
                    self.copy_phase()
                else:
                    raise ValueError(kind)
            S.barrier()
        return nc

    def transposes_to_actT(self, m, xb, pT, res_xb):
        S = self.S
        for k in range(8):
            S.op(S.pe, lambda e, k=k: e.transpose(out=pT[:, k, :], in_=xb[:, k * 128:(k + 1) * 128],
                                                  identity=self.ident[:]),
                 reads=[res_xb, "ident"], writes=["pT"])
        S.op(S.dve, lambda e: e.tensor_copy(out=self.actT[:, :, m * 128:(m + 1) * 128], in_=pT[:]),
             reads=["pT"], writes=[("actT", m)])

    def alloc_epi(self, ph):
        nc = self.nc
        sb = lambda n, shp, dt: ph.enter_context(nc.sbuf_tensor(self.nm(n), shp, dt))
        self.xr = [sb(f"xr{i}", [128, D], F32) for i in range(2)]
        self.z = [sb(f"z{i}", [128, D], F32) for i in range(2)]
        self.xb = [sb(f"xb{i}", [128, D], BF16) for i in range(2)]
        self.st6 = [sb(f"st6{i}", [128, 2, 6], F32) for i in range(2)]
        self.mv = [sb(f"mv{i}", [128, 8], F32) for i in range(2)]

    def init_phase(self):
        nc, S = self.nc, self.S
        with ExitStack() as ph:
            self.alloc_epi(ph)
            pT = ph.enter_context(nc.psum_tensor(self.nm("pT_i"), [128, 8, 128], BF16))
            for m in range(NB):
                b = m % 2
                S.dma("sp", lambda e: e.dma_start(out=self.xr[b][:], in_=self.x_in[m * 128:(m + 1) * 128, :]),
                      writes=[("xr", b)])
                S.op(S.act, lambda e: e.copy(out=self.xb[b][:], in_=self.xr[b][:]),
                     reads=[("xr", b)], writes=[("xb", b)])
                self.transposes_to_actT(m, self.xb[b], pT, ("xb", b))
            S.barrier()

    def copy_phase(self):
        S = self.S
        ph = ExitStack()
        self.alloc_epi(ph)
        for m in range(NB):
            b = m % 2
            S.dma("sp", lambda e: e.dma_start(out=self.xr[b][:], in_=self.cur_src[m * 128:(m + 1) * 128, :]),
                  reads=[("xres", m)], writes=[("xr", b)])
            S.dma("sp", lambda e: e.dma_start(out=self.out[m * 128:(m + 1) * 128, :], in_=self.xr[b][:]),
                  reads=[("xr", b)], writes=[("xres", m)])
        S.barrier()
        ph.close()
        self.cur_src = self.out

    def load_ln(self, L, which):
        S = self.S
        S.dma("sp", lambda e: e.dma_start(out=self.lng[:], in_=self.ln_g[L, which, :].partition_broadcast(128)),
              writes=["lng"])
        S.dma("sp", lambda e: e.dma_start(out=self.lnb[:], in_=self.ln_b[L, which, :].partition_broadcast(128)),
              writes=["lnb"])

    def prefetch_xr(self, m):
        S = self.S
        b = self.ep_idx % 2
        src = self.cur_src
        S.dma("sp", lambda e: e.dma_start(out=self.xr[b][:], in_=src[m * 128:(m + 1) * 128, :]),
              reads=[("xres", m)], writes=[("xr", b)])

    def epilogue(self, m, py, py_res, pT):
        S = self.S
        prev_tr = getattr(self, "pending_tr", None)
        self.pending_tr = None
        b = self.ep_idx % 2
        self.ep_idx += 1
        xr, z, xb, st6, mv = self.xr[b], self.z[b], self.xb[b], self.st6[b], self.mv[b]
        rz, rmv = ("z", b), ("mv", b)
        S.op(S.dve, lambda e: e.scalar_tensor_tensor(out=z[:], in0=xr[:], scalar=float(ALPHA), in1=py,
                                                     op0=ALU.mult, op1=ALU.add),
             reads=[("xr", b), py_res], writes=[rz])
        if prev_tr is not None:
            self.transposes_to_actT(*prev_tr)
        for c in range(2):
            S.op(S.dve, lambda e, c=c: e.bn_stats(out=st6[:, c, :], in_=z[:, c * 512:(c + 1) * 512]),
                 reads=[rz], writes=[("st6", b, c)])
        S.op(S.dve, lambda e: e.bn_aggr(out=mv[:, 0:2], in_=st6[:].rearrange("p a b -> p (a b)")),
             reads=[("st6", b, 0), ("st6", b, 1)], writes=[rmv])
        S.op(S.dve, lambda e: e.tensor_scalar(out=mv[:, 2:3], in0=mv[:, 1:2], scalar1=float(LN_EPS), scalar2=None,
                                              op0=ALU.add), reads=[rmv], writes=[rmv])
        S.op(S.act, lambda e: e.activation(out=mv[:, 3:4], in_=mv[:, 2:3], func=AF.Sqrt), reads=[rmv], writes=[rmv])
        S.op(S.dve, lambda e: e.reciprocal(out=mv[:, 4:5], in_=mv[:, 3:4]), reads=[rmv], writes=[rmv])
        S.op(S.dve, lambda e: e.scalar_tensor_tensor(out=mv[:, 5:6], in0=mv[:, 0:1], scalar=-1.0, in1=mv[:, 4:5],
                                                     op0=ALU.mult, op1=ALU.mult), reads=[rmv], writes=[rmv])
        S.op(S.act, lambda e: e.activation(out=z[:], in_=z[:], func=AF.Identity, bias=mv[:, 5:6], scale=mv[:, 4:5]),
             reads=[rmv, rz], writes=[rz])
        S.op(S.pool, lambda e: e.tensor_tensor(out=z[:], in0=z[:], in1=self.lng[:], op=ALU.mult),
             reads=[rz, "lng"], writes=[rz])
        S.op(S.pool, lambda e: e.tensor_tensor(out=z[:], in0=z[:], in1=self.lnb[:], op=ALU.add),
             reads=[rz, "lnb"], writes=[rz])
        S.dma("pool", lambda e: e.dma_start(out=self.out[m * 128:(m + 1) * 128, :], in_=z[:]),
              reads=[rz], writes=[("xres", m)])
        S.op(S.act, lambda e: e.copy(out=xb[:], in_=z[:]), reads=[rz], writes=[("xb", b)])
        self.pending_tr = (m, xb, pT, ("xb", b))

    def flush_tr(self):
        if getattr(self, "pending_tr", None) is not None:
            self.transposes_to_actT(*self.pending_tr)
            self.pending_tr = None

    def ffn_phase(self, L):
        nc, S = self.nc, self.S
        actT = self.actT
        with ExitStack() as ph:
            sb = lambda n, shp, dt: ph.enter_context(nc.sbuf_tensor(self.nm(n), shp, dt))
            ps = lambda n, shp, dt: ph.enter_context(nc.psum_tensor(self.nm(n), shp, dt))
            self.alloc_epi(ph)
            w_out = sb("f_wout", [128, NCH, D], BF16)
            cw = sb("f_cw", [128, NCH, 4], F32)
            halo = sb("f_halo", [128, NCH, 2], F32)
            g = sb("f_g", [128, NCH, 512], BF16)
            wab = [sb(f"f_wab{i}", [128, 8, 256], BF16) for i in range(3)]
            asb = [sb(f"f_a{i}", [128, 514], F32) for i in range(3)]
            tt = [sb(f"f_t{i}", [128, 512], F32) for i in range(3)]
            pab = [ps(f"f_pab{i}", [128, 512], F32) for i in range(3)]
            py = [ps(f"f_py{i}", [128, D], F32) for i in range(2)]
            pT = ps("f_pT", [128, 8, 128], BF16)

            self.load_ln(L, 1)
            S.dma("sp", lambda e: e.dma_start(out=cw[:], in_=self.ffn_cw[L]), writes=["cw"])
            S.dma("pool", lambda e: e.dma_start(
                out=w_out[:, 0:21, :], in_=self.ffn_w_out[L, 0:21 * 128, :].rearrange("(c p) n -> p c n", p=128)),
                writes=["wout"])
            S.dma("pool", lambda e: e.dma_start(out=w_out[0:64, 21, :], in_=self.ffn_w_out[L, 21 * 128:D_FF, :]),
                  writes=["wout21"])
            S.op(S.pool, lambda e: e.memset(halo[:], 0.0), writes=[("halo", c) for c in range(NCH)])
            w_in = self.ffn_w_in[L].rearrange("(k p) n -> p k n", p=128)
            NJ = T // 512
            NIT = NJ * NCH

            def load_w(i):
                if i >= NIT:
                    return
                c = i % NCH
                wc = 128 if c < NCH - 1 else 64
                r3 = i % 3
                wb_ = wab[r3]
                S.dma("pool", lambda e: e.dma_start(out=wb_[:, :, 0:wc], in_=w_in[:, :, c * 128:c * 128 + wc]),
                      writes=[("wa", r3)])
                S.dma("pool", lambda e: e.dma_start(out=wb_[:, :, 128:128 + wc],
                                                    in_=w_in[:, :, D_FF + c * 128:D_FF + c * 128 + wc]),
                      writes=[("wb", r3)])

            load_w(0)
            load_w(1)
            for i in range(NIT):
                j, c = i // NCH, i % NCH
                tok = slice(j * 512, (j + 1) * 512)
                act_res = [("actT", 4 * j + q) for q in range(4)]
                wc = 128 if c < NCH - 1 else 64
                r3 = i % 3
                ia_, ib_ = (2 * i) % 3, (2 * i + 1) % 3
                pa_, pb_ = pab[ia_], pab[ib_]
                rpa, rpb = ("pab", ia_), ("pab", ib_)
                wb_, a_, t_ = wab[r3], asb[r3], tt[r3]
                load_w(i + 2)
                for k in range(8):
                    S.op(S.pe, lambda e, k=k: e.matmul(pa_[0:wc, :], lhsT=wb_[:, k, 0:wc], rhs=actT[:, k, tok],
                                                       start=(k == 0), stop=(k == 7)),
                         reads=[("wa", r3)] + act_res, writes=[rpa])
                for k in range(8):
                    S.op(S.pe, lambda e, k=k: e.matmul(pb_[0:wc, :], lhsT=wb_[:, k, 128:128 + wc],
                                                       rhs=actT[:, k, tok], start=(k == 0), stop=(k == 7)),
                         reads=[("wb", r3)] + act_res, writes=[rpb])
                if c == 1:
                    self.flush_tr()
                ra, rt = ("a", r3), ("t", r3)
                S.op(S.act, lambda e: e.copy(out=a_[0:wc, 0:2], in_=halo[0:wc, c, :]),
                     reads=[("halo", c)], writes=[ra])
                S.op(S.act, lambda e: e.copy(out=a_[0:wc, 2:514], in_=pa_[0:wc, :]),
                     reads=[rpa, ra], writes=[ra])
                S.op(S.act, lambda e: e.copy(out=halo[0:wc, c, :], in_=a_[0:wc, 512:514]),
                     reads=[ra], writes=[("halo", c)])
                S.op(S.dve, lambda e: e.tensor_scalar(out=t_[0:wc, :], in0=a_[0:wc, 2:514],
                                                      scalar1=cw[0:wc, c, 2:3], scalar2=cw[0:wc, c, 3:4],
                                                      op0=ALU.mult, op1=ALU.add),
                     reads=[ra, "cw"], writes=[rt])
                S.op(S.dve, lambda e: e.scalar_tensor_tensor(out=t_[0:wc, :], in0=a_[0:wc, 1:513],
                                                             scalar=cw[0:wc, c, 1:2], in1=t_[0:wc, :],
                                                             op0=ALU.mult, op1=ALU.add),
                     reads=[ra, rt], writes=[rt])
                S.op(S.dve, lambda e: e.scalar_tensor_tensor(out=t_[0:wc, :], in0=a_[0:wc, 0:512],
                                                             scalar=cw[0:wc, c, 0:1], in1=t_[0:wc, :],
                                                             op0=ALU.mult, op1=ALU.add),
                     reads=[ra, rt], writes=[rt])
                S.op(S.act, lambda e: e.activation(out=t_[0:wc, :], in_=t_[0:wc, :], func=AF.Silu),
                     reads=[rt], writes=[rt])
                S.op(S.dve, lambda e: e.tensor_tensor(out=g[0:wc, c, :], in0=t_[0:wc, :], in1=pb_[0:wc, :],
                                                      op=ALU.mult),
                     reads=[rt, rpb], writes=[("g", c)])
                if c < NCH - 1:
                    continue
                for mm in range(4):
                    m = 4 * j + mm
                    self.prefetch_xr(m)
                    p_ = py[m % 2]
                    for n in range(2):
                        for cc in range(NCH):
                            wcc = 128 if cc < NCH - 1 else 64
                            S.op(S.pe, lambda e, n=n, cc=cc, wcc=wcc: e.matmul(
                                p_[:, n * 512:(n + 1) * 512], lhsT=g[0:wcc, cc, mm * 128:(mm + 1) * 128],
                                rhs=w_out[0:wcc, cc, n * 512:(n + 1) * 512], start=(cc == 0), stop=(cc == NCH - 1)),
                                 reads=[("g", cc), "wout", "wout21"], writes=[("py", m % 2)])
                    self.epilogue(m, p_[:], ("py", m % 2), pT)
            self.flush_tr()
            S.barrier()
        self.cur_src = self.out


    def out_proj_phase(self, L, w_dram):
        nc, S = self.nc, self.S
        with ExitStack() as ph:
            sb = lambda n, shp, dt: ph.enter_context(nc.sbuf_tensor(self.nm(n), shp, dt))
            ps = lambda n, shp, dt: ph.enter_context(nc.psum_tensor(self.nm(n), shp, dt))
            self.alloc_epi(ph)
            wo = sb("o_w", [128, 8, D], BF16)
            py = [ps(f"o_py{i}", [128, D], F32) for i in range(2)]
            pT = ps("o_pT", [128, 8, 128], BF16)
            self.load_ln(L, 0)
            S.dma("pool", lambda e: e.dma_start(out=wo[:], in_=w_dram.rearrange("(k p) n -> p k n", p=128)),
                  writes=["wo"])
            for m in range(NB):
                self.prefetch_xr(m)
                p_ = py[m % 2]
                for n in range(2):
                    for k in range(8):
                        S.op(S.pe, lambda e, n=n, k=k: e.matmul(
                            p_[:, n * 512:(n + 1) * 512], lhsT=self.actT[:, k, m * 128:(m + 1) * 128],
                            rhs=wo[:, k, n * 512:(n + 1) * 512], start=(k == 0), stop=(k == 7)),
                             reads=[("actT", m), "wo"], writes=[("py", m % 2)])
                self.epilogue(m, p_[:], ("py", m % 2), pT)
            self.flush_tr()
            S.barrier()
        self.cur_src = self.out

    def mla_phase(self, L):
        nc, S = self.nc, self.S
        actT = self.actT
        ia = L // 3
        SCALE = 96.0 ** -0.5
        TWO_PI = 2.0 * np.pi
        with ExitStack() as ml:
            msb = lambda n, shp, dt: ml.enter_context(nc.sbuf_tensor(self.nm(n), shp, dt))
            cqn = msb("m_cqn", [128, 3, T], BF16)
            ckvn = msb("m_ckvn", [128, 2, T], BF16)
            KT = msb("m_KT", [96, T], BF16)
            cosT = msb("m_cos", [96, T], BF16)
            sinS = msb("m_sin", [96, T], BF16)
            rc = msb("m_rc", [96, 2], F32)
            with ExitStack() as ph:
                sb = lambda n, shp, dt: ph.enter_context(nc.sbuf_tensor(self.nm(n), shp, dt))
                HT = T // 2
                posi = sb("m_posi", [96, HT], I32)
                ang = sb("m_ang", [96, HT], F32)
                tmp = sb("m_tmp", [96, HT], F32)
                yi = sb("m_yi", [96, HT], I32)
                msk = sb("m_msk", [96, HT], F32)
                S.dma("sp", lambda e: e.dma_start(out=rc[:], in_=self.rope_c), writes=["rc"])

                def sin_turns():
                    S.op(S.dve, lambda e: e.tensor_copy(out=yi[:], in_=tmp[:]), reads=["tmp"], writes=["yi"])
                    S.op(S.dve, lambda e: e.tensor_copy(out=msk[:], in_=yi[:]), reads=["yi"], writes=["msk"])
                    S.op(S.dve, lambda e: e.tensor_tensor(out=tmp[:], in0=tmp[:], in1=msk[:], op=ALU.subtract),
                         reads=["tmp", "msk"], writes=["tmp"])
                    S.op(S.dve, lambda e: e.tensor_scalar(out=msk[:], in0=tmp[:], scalar1=0.5, scalar2=None,
                                                          op0=ALU.is_gt), reads=["tmp"], writes=["msk"])
                    S.op(S.dve, lambda e: e.tensor_tensor(out=tmp[:], in0=tmp[:], in1=msk[:], op=ALU.subtract),
                         reads=["tmp", "msk"], writes=["tmp"])
                    S.op(S.dve, lambda e: e.tensor_scalar(out=msk[:], in0=tmp[:], scalar1=-0.5, scalar2=None,
                                                          op0=ALU.is_lt), reads=["tmp"], writes=["msk"])
                    S.op(S.dve, lambda e: e.tensor_tensor(out=tmp[:], in0=tmp[:], in1=msk[:], op=ALU.add),
                         reads=["tmp", "msk"], writes=["tmp"])
                    S.op(S.act, lambda e: e.activation(out=tmp[:], in_=tmp[:], func=AF.Sin, scale=6.28318),
                         reads=["tmp"], writes=["tmp"])

                for hh in range(2):
                    cs = slice(hh * HT, (hh + 1) * HT)
                    S.dma("sp", lambda e: e.dma_start(out=posi[:], in_=self.pos[cs].partition_broadcast(96)),
                          writes=["posi"])
                    S.op(S.dve, lambda e: e.tensor_copy(out=ang[:], in_=posi[:]), reads=["posi"], writes=["ang"])
                    S.op(S.dve, lambda e: e.tensor_scalar(out=ang[:], in0=ang[:], scalar1=rc[:, 0:1], scalar2=None,
                                                          op0=ALU.mult), reads=["ang", "rc"], writes=["ang"])
                    S.op(S.dve, lambda e: e.tensor_copy(out=tmp[:], in_=ang[:]), reads=["ang"], writes=["tmp"])
                    sin_turns()
                    S.op(S.dve, lambda e: e.tensor_scalar(out=sinS[64:96, cs], in0=tmp[64:96, :],
                                                          scalar1=rc[64:96, 1:2], scalar2=None, op0=ALU.mult),
                         reads=["tmp", "rc"], writes=["sinS"])
                    S.op(S.dve, lambda e: e.tensor_scalar(out=tmp[:], in0=ang[:], scalar1=0.25, scalar2=None,
                                                          op0=ALU.add), reads=["ang", "sinS"], writes=["tmp"])
                    sin_turns()
                    S.op(S.dve, lambda e: e.tensor_copy(out=cosT[64:96, cs], in_=tmp[64:96, :]),
                         reads=["tmp"], writes=["cosT"])
                S.barrier()
            with ExitStack() as ph:
                sb = lambda n, shp, dt: ph.enter_context(nc.sbuf_tensor(self.nm(n), shp, dt))
                ps = lambda n, shp, dt: ph.enter_context(nc.psum_tensor(self.nm(n), shp, dt))
                wd = sb("m_wd", [128, 8, 768], BF16)
                gq = sb("m_gq", [128, 3], F32)
                gkv = sb("m_gkv", [128, 2], F32)
                raw = [sb(f"m_raw{i}", [128, 5, 512], F32) for i in range(2)]
                sq = [sb(f"m_sq{i}", [128, 5, 512], BF16) for i in range(2)]
                rs = [sb(f"m_rs{i}", [128, 2, 512], F32) for i in range(2)]
                t1 = [sb(f"m_t1{i}", [96, 512], F32) for i in range(2)]
                t2 = [sb(f"m_t2{i}", [96, 512], F32) for i in range(2)]
                p_lat = [ps(f"m_plat{i}", [128, 512], F32) for i in range(2)]
                p_ss = [ps(f"m_pss{i}", [128, 512], F32) for i in range(2)]
                p_kA = ps("m_pkA", [96, 512], F32)
                p_kB = ps("m_pkB", [96, 512], F32)
                S.dma("pool", lambda e: e.dma_start(out=wd[:], in_=self.mla_wd[ia].rearrange("(k p) n -> p k n", p=128)),
                      writes=["wd"])
                S.dma("sp", lambda e: e.dma_start(out=gq[:], in_=self.mla_qn[ia]), writes=["gq"])
                S.dma("sp", lambda e: e.dma_start(out=gkv[:], in_=self.mla_kvn[ia]), writes=["gkv"])
                it = 0
                for j in range(T // 512):
                    tok = slice(j * 512, (j + 1) * 512)
                    ares = [("actT", 4 * j + q) for q in range(4)]
                    b = j % 2
                    for c in range(5):
                        pl = p_lat[it % 2]
                        rpl = ("plat", it % 2)
                        it += 1
                        for k in range(8):
                            S.op(S.pe, lambda e, k=k: e.matmul(pl[:], lhsT=wd[:, k, c * 128:(c + 1) * 128],
                                                               rhs=actT[:, k, tok], start=(k == 0), stop=(k == 7)),
                                 reads=["wd"] + ares, writes=[rpl])
                        S.op(S.act, lambda e: e.copy(out=raw[b][:, c, :], in_=pl[:]), reads=[rpl], writes=[("raw", b, c)])
                        S.op(S.act, lambda e: e.activation(out=sq[b][:, c, :], in_=pl[:], func=AF.Square),
                             reads=[rpl], writes=[("sq", b, c)])
                    for which, (c0, c1, dim) in enumerate([(0, 3, 384.0), (3, 5, 256.0)]):
                        for c in range(c0, c1):
                            S.op(S.pe, lambda e, c=c: e.matmul(p_ss[which][:], lhsT=self.ones_bf[:], rhs=sq[b][:, c, :],
                                                               start=(c == c0), stop=(c == c1 - 1)),
                                 reads=[("sq", b, c), "ones_bf"], writes=[("pss", which)])
                        rr = ("rs", b, which)
                        S.op(S.dve, lambda e: e.tensor_scalar(out=rs[b][:, which, :], in0=p_ss[which][:],
                                                              scalar1=1.0 / dim, scalar2=float(RMS_EPS),
                                                              op0=ALU.mult, op1=ALU.add),
                             reads=[("pss", which)], writes=[rr])
                        S.op(S.act, lambda e: e.activation(out=rs[b][:, which, :], in_=rs[b][:, which, :], func=AF.Sqrt),
                             reads=[rr], writes=[rr])
                        S.op(S.dve, lambda e: e.reciprocal(out=rs[b][:, which, :], in_=rs[b][:, which, :]),
                             reads=[rr], writes=[rr])
                        for c in range(c0, c1):
                            dst = cqn[:, c, tok] if which == 0 else ckvn[:, c - 3, tok]
                            gsc = gq[:, c:c + 1] if which == 0 else gkv[:, c - 3:c - 2]
                            S.op(S.dve, lambda e: e.scalar_tensor_tensor(out=dst, in0=raw[b][:, c, :], scalar=gsc,
                                                                         in1=rs[b][:, which, :], op0=ALU.mult,
                                                                         op1=ALU.mult),
                                 reads=[("raw", b, c), rr, "gq", "gkv"], writes=[("cn", c, j)])
                    for k in range(8):
                        S.op(S.pe, lambda e, k=k: e.matmul(p_kA[:], lhsT=wd[:, k, 640:736], rhs=actT[:, k, tok],
                                                           start=(k == 0), stop=(k == 7)),
                             reads=["wd"] + ares, writes=["pkA"])
                    for k in range(8):
                        S.op(S.pe, lambda e, k=k: e.matmul(p_kB[:], lhsT=wd[:, k, 672:768], rhs=actT[:, k, tok],
                                                           start=(k == 0), stop=(k == 7)),
                             reads=["wd"] + ares, writes=["pkB"])
                    S.op(S.dve, lambda e: e.tensor_tensor(out=t1[b][64:96, :], in0=p_kA[64:96, :], in1=cosT[64:96, tok],
                                                          op=ALU.mult), reads=["pkA"], writes=[("t1", b)])
                    S.op(S.dve, lambda e: e.tensor_tensor(out=t2[b][64:96, :], in0=p_kB[64:96, :], in1=sinS[64:96, tok],
                                                          op=ALU.mult), reads=["pkB"], writes=[("t2", b)])
                    S.op(S.pool, lambda e: e.tensor_tensor(out=KT[64:96, tok], in0=t1[b][64:96, :], in1=t2[b][64:96, :],
                                                           op=ALU.add), reads=[("t1", b), ("t2", b)], writes=[("KTpe", j)])
                S.barrier()
            with ExitStack() as ph:
                sb = lambda n, shp, dt: ph.enter_context(nc.sbuf_tensor(self.nm(n), shp, dt))
                ps = lambda n, shp, dt: ph.enter_context(nc.psum_tensor(self.nm(n), shp, dt))
                wuq = sb("m_wuq", [128, 3, 2048], BF16)
                wukv = sb("m_wukv", [128, 2, 2048], BF16)
                Vx = [sb(f"m_Vx{i}", [128, 32, 128], BF16) for i in range(2)]
                QT = [sb(f"m_QT{i}", [96, 512], BF16) for i in range(2)]
                pt = [sb(f"m_pt{i}", [128, 512], BF16) for i in range(4)]
                t1 = [sb(f"m_u1{i}", [96, 512], F32) for i in range(2)]
                t2 = [sb(f"m_u2{i}", [96, 512], F32) for i in range(2)]
                rec = sb("m_rec", [128, 512], F32)
                bcs = sb("m_bcs", [128, 512], F32)
                p_p = [ps(f"m_pp{i}", [128, 512], F32) for i in range(2)]
                p_s = [ps(f"m_ps{i}", [128, 512], F32) for i in range(3)]
                p_o = [ps(f"m_po{i}", [128, 512], F32) for i in range(2)]
                p_bc = ps("m_pbc", [128, 512], F32)
                KTb = sb("m_KTb", [96, T], BF16)
                KTs = [KT, KTb]
                S.dma("pool", lambda e: e.dma_start(out=wuq[:], in_=self.mla_wuq[ia].rearrange("(k p) n -> p k n", p=128)),
                      writes=["wuq"])
                S.dma("pool", lambda e: e.dma_start(out=wukv[:], in_=self.mla_wukv[ia].rearrange("(k p) n -> p k n", p=128)),
                      writes=["wukv"])
                S.op(S.pool, lambda e: e.memset(Vx[0][:], 1.0), writes=[("Vx", 0)])
                S.op(S.pool, lambda e: e.memset(Vx[1][:], 1.0), writes=[("Vx", 1)])
                S.op(S.pool, lambda e: e.tensor_copy(out=KTb[64:96, :], in_=KT[64:96, :]), writes=["KTb_pe"])
                cnt = {"pp": 0}

                def kv_proj(h):
                    hl = h % 2
                    kt = KTs[hl]
                    vx = Vx[hl]
                    r0 = hl * 64
                    for j in range(T // 512):
                        tok = slice(j * 512, (j + 1) * 512)
                        pp, rpp = p_p[cnt["pp"] % 2], ("pp", cnt["pp"] % 2)
                        cnt["pp"] += 1
                        for k in range(2):
                            S.op(S.pe, lambda e, k=k: e.matmul(pp[0:64, :], lhsT=wukv[:, k, h * 64:(h + 1) * 64],
                                                               rhs=ckvn[:, k, tok], start=(k == 0), stop=(k == 1)),
                                 reads=["wukv"], writes=[rpp])
                        S.op(S.dve, lambda e: e.tensor_copy(out=kt[0:64, tok], in_=pp[0:64, :]), reads=[rpp],
                             writes=[("KT", hl, j)])
                    for j in range(4):
                        pp, rpp = p_p[cnt["pp"] % 2], ("pp", cnt["pp"] % 2)
                        cnt["pp"] += 1
                        ppv = pp[:].rearrange("p (b d) -> p b d", d=64)
                        for bb in range(8):
                            blk = j * 8 + bb
                            for k in range(2):
                                S.op(S.pe, lambda e, k=k: e.matmul(
                                    ppv[:, bb, :], lhsT=ckvn[:, k, blk * 128:(blk + 1) * 128],
                                    rhs=wukv[:, k, 1024 + h * 64:1024 + (h + 1) * 64], start=(k == 0), stop=(k == 1)),
                                     reads=["wukv"], writes=[rpp])
                        S.op(S.dve, lambda e: e.tensor_copy(out=vx[:, j * 8:(j + 1) * 8, r0:r0 + 64], in_=ppv),
                             reads=[rpp], writes=[("Vx", hl, j)])

                def prep_q(h, qt, iq):
                    tok = slice(qt * 512, (qt + 1) * 512)
                    qT, rq = QT[iq % 2], ("QT", iq % 2)
                    u1, u2 = t1[iq % 2], t2[iq % 2]
                    ru1, ru2 = ("u1", iq % 2), ("u2", iq % 2)
                    pA, pB = p_p[0], p_p[1]
                    for k in range(3):
                        S.op(S.pe, lambda e, k=k: e.matmul(pA[0:96, :], lhsT=wuq[:, k, h * 128:h * 128 + 96],
                                                           rhs=cqn[:, k, tok], start=(k == 0), stop=(k == 2)),
                             reads=["wuq"], writes=[("pp", 0)])
                    for k in range(3):
                        S.op(S.pe, lambda e, k=k: e.matmul(pB[0:96, :], lhsT=wuq[:, k, h * 128 + 32:h * 128 + 128],
                                                           rhs=cqn[:, k, tok], start=(k == 0), stop=(k == 2)),
                             reads=["wuq"], writes=[("pp", 1)])
                    S.op(S.dve, lambda e: e.tensor_copy(out=qT[0:64, :], in_=pA[0:64, :]), reads=[("pp", 0)], writes=[rq])
                    S.op(S.dve, lambda e: e.tensor_tensor(out=u1[64:96, :], in0=pA[64:96, :], in1=cosT[64:96, tok],
                                                          op=ALU.mult), reads=[("pp", 0)], writes=[ru1])
                    S.op(S.dve, lambda e: e.tensor_tensor(out=u2[64:96, :], in0=pB[64:96, :], in1=sinS[64:96, tok],
                                                          op=ALU.mult), reads=[("pp", 1)], writes=[ru2])
                    S.op(S.dve, lambda e: e.tensor_tensor(out=qT[64:96, :], in0=u1[64:96, :], in1=u2[64:96, :],
                                                          op=ALU.add), reads=[ru1, ru2], writes=[rq])

                def fin_q1(h, qt, iq):
                    hl = h % 2
                    d0 = 64 - hl * 64
                    po, rpo = p_o[iq % 2], ("po", iq % 2)
                    S.op(S.act, lambda e: e.copy(out=rec[d0:d0 + 1, :], in_=po[d0:d0 + 1, :]), reads=[rpo], writes=["rec"])

                def fin_q2(h, qt, iq):
                    hl, ch = h % 2, h // 2
                    r0, d0 = hl * 64, 64 - hl * 64
                    tok = slice(qt * 512, (qt + 1) * 512)
                    po, rpo = p_o[iq % 2], ("po", iq % 2)
                    S.op(S.pe, lambda e: e.matmul(p_bc[:], lhsT=self.ones_f[d0:d0 + 1, :], rhs=rec[d0:d0 + 1, :],
                                                  start=True, stop=True), reads=["rec", "ones_f"], writes=["pbc"])
                    S.op(S.dve, lambda e: e.reciprocal(out=bcs[r0:r0 + 64, :], in_=p_bc[r0:r0 + 64, :]),
                         reads=["pbc"], writes=["bcs"])
                    S.op(S.dve, lambda e: e.tensor_tensor(out=actT[r0:r0 + 64, ch, tok], in0=po[r0:r0 + 64, :],
                                                          in1=bcs[r0:r0 + 64, :], op=ALU.mult),
                         reads=[rpo, "bcs"], writes=[("actT", 4 * qt + q) for q in range(4)])

                items = []
                iq = 0
                for h in range(16):
                    for qt in range(T // 512):
                        nkb = 4 * qt + 4
                        for kb in range(nkb):
                            items.append(dict(h=h, qt=qt, kb=kb, nkb=nkb, iq=iq, pre=[], post=[]))
                        iq += 1
                first = {}
                for i, it in enumerate(items):
                    first.setdefault((it["h"], it["qt"]), i)
                for (h, qt), i in first.items():
                    lo = 0 if i == 0 else i - items[i - 1]["nkb"]
                    items[max(lo, i - 6)]["pre"].append(lambda h=h, qt=qt, iq=items[i]["iq"]: prep_q(h, qt, iq))
                    if qt == 0 and h > 0:
                        items[first[(h - 1, 5)]]["pre"].insert(0, lambda h=h: kv_proj(h))
                for i, it in enumerate(items):
                    if it["kb"] == it["nkb"] - 1:
                        it["post"].append(lambda it=it: fin_q1(it["h"], it["qt"], it["iq"]))
                        items[min(len(items) - 1, i + 3)]["post"].append(
                            lambda it=it: fin_q2(it["h"], it["qt"], it["iq"]))
                kv_proj(0)

                def stA(i, it):
                    h, qt, kb = it["h"], it["qt"], it["kb"]
                    hl = h % 2
                    n0 = max(0, kb - 4 * qt) * 128
                    qT, rq = QT[it["iq"] % 2], ("QT", it["iq"] % 2)
                    S.op(S.pe, lambda e: e.matmul(p_s[i % 3][:, n0:512], lhsT=KTs[hl][0:96, kb * 128:(kb + 1) * 128],
                                                  rhs=qT[0:96, n0:512], start=True, stop=True),
                         reads=[("KT", hl, kb // 4), rq, "KTb_pe"], writes=[("ps", i % 3)])

                def stB(i, it):
                    qt, kb = it["qt"], it["kb"]
                    n0 = max(0, kb - 4 * qt) * 128
                    ptb, rpt = pt[i % 4], ("pt", i % 4)
                    S.op(S.act, lambda e: e.activation(out=ptb[:, n0:512], in_=p_s[i % 3][:, n0:512], func=AF.Exp,
                                                       scale=float(SCALE)), reads=[("ps", i % 3)], writes=[rpt])
                    if kb >= 4 * qt:
                        S.op(S.dve, lambda e: e.tensor_tensor(out=ptb[:, n0:n0 + 128], in0=ptb[:, n0:n0 + 128],
                                                              in1=self.tri[:], op=ALU.mult),
                             reads=[rpt, "tri"], writes=[rpt])

                def stC(i, it):
                    h, qt, kb, nkb = it["h"], it["qt"], it["kb"], it["nkb"]
                    hl = h % 2
                    n0 = max(0, kb - 4 * qt) * 128
                    po, rpo = p_o[it["iq"] % 2], ("po", it["iq"] % 2)
                    S.op(S.pe, lambda e: e.matmul(po[:, n0:512], lhsT=Vx[hl][:, kb, :], rhs=pt[i % 4][:, n0:512],
                                                  start=(kb == 0), stop=(kb == nkb - 1)),
                         reads=[("Vx", hl, kb // 8), ("pt", i % 4)], writes=[rpo])

                n = len(items)
                for t_ in range(n + 3):
                    if t_ < n:
                        for f in items[t_]["pre"]:
                            f()
                        stA(t_, items[t_])
                    if 0 <= t_ - 1 < n:
                        stB(t_ - 1, items[t_ - 1])
                    if 0 <= t_ - 3 < n:
                        stC(t_ - 3, items[t_ - 3])
                        for f in items[t_ - 3]["post"]:
                            f()
                S.barrier()
        self.out_proj_phase(L, self.mla_wo[ia])


    def load_oT(self):
        S = self.S
        for c in range(8):
            S.dma("sp", lambda e, c=c: e.dma_start(out=self.actT[:, c, :], in_=self.oT_d[c]),
                  reads=[("oTd", c)], writes=[("actT", m) for m in range(NB)])

    def dil_phase(self, L):
        nc, S = self.nc, self.S
        actT = self.actT
        DIL = (1, 4, 16)
        with ExitStack() as ph:
            sb = lambda n, shp, dt: ph.enter_context(nc.sbuf_tensor(self.nm(n), shp, dt))
            ps = lambda n, shp, dt: ph.enter_context(nc.psum_tensor(self.nm(n), shp, dt))
            wd = sb("d_w", [128, 8, 9, 128], BF16)
            QK = [[sb(f"d_qk{g}{i}", [128, T], BF16) for i in range(2)] for g in range(3)]
            Vx = [sb(f"d_vx{g}", [128, 32, 192], BF16) for g in range(3)]
            osb = sb("d_osb", [128, T], BF16)
            mask2 = sb("d_mask2", [128, 256], BF16)
            pt = [sb(f"d_pt{i}", [128, 256], BF16) for i in range(3)]
            rec = sb("d_rec", [128, 512], F32)
            bcs = sb("d_bcs", [128, 512], F32)
            p_o = ps("d_po", [128, 2048], F32)
            p_s = [ps(f"d_ps{i}", [128, 512], F32) for i in range(2)]
            p_bc = ps("d_pbc", [128, 512], F32)
            p_p = ps("d_pp", [128, 512], F32)
            S.op(S.pool, lambda e: e.memset(mask2[:], 1.0), writes=["mask2"])
            S.op(S.pool, lambda e: e.affine_select(out=mask2[:, 0:128], in_=mask2[:, 0:128], pattern=[[-1, 128]],
                                                   compare_op=ALU.is_ge, fill=0.0, base=0, channel_multiplier=1),
                 reads=["mask2"], writes=["mask2"])
            S.op(S.pool, lambda e: e.affine_select(out=mask2[:, 128:256], in_=mask2[:, 128:256], pattern=[[1, 128]],
                                                   compare_op=ALU.is_ge, fill=0.0, base=0, channel_multiplier=-1),
                 reads=["mask2"], writes=["mask2"])
            for g in range(3):
                S.op(S.pool, lambda e, g=g: e.memset(Vx[g][:], 1.0), writes=[("Vx", g)])
            wsrc = self.dil_wqkv.rearrange("(k p) (c h d) -> p k c (h d)", p=128, c=9, h=16)
            ips = 0
            for hp in range(8):
                for c9 in range(9):
                    S.dma("pool", lambda e: e.dma_start(out=wd[:, :, c9, :], in_=wsrc[:, :, c9, hp * 128:(hp + 1) * 128]),
                          writes=[("wd", c9)])
                for j in range(T // 512):
                    tok = slice(j * 512, (j + 1) * 512)
                    ares = [("actT", 4 * j + q) for q in range(4)]
                    for g in range(3):
                        for qk in range(2):
                            for k in range(8):
                                S.op(S.pe, lambda e, k=k: e.matmul(p_p[:], lhsT=wd[:, k, g * 3 + qk, :],
                                                                   rhs=actT[:, k, tok], start=(k == 0), stop=(k == 7)),
                                     reads=[("wd", g * 3 + qk)] + ares, writes=["pp"])
                            eng = S.act if (g * 2 + qk) % 2 == 0 else S.dve
                            if eng is S.act:
                                S.op(S.act, lambda e: e.copy(out=QK[g][qk][:, tok], in_=p_p[:]),
                                     reads=["pp"], writes=[("QK", g, qk, j)])
                            else:
                                S.op(S.dve, lambda e: e.tensor_copy(out=QK[g][qk][:, tok], in_=p_p[:]),
                                     reads=["pp"], writes=[("QK", g, qk, j)])
                ppv = p_p[:].rearrange("p (b d) -> p b d", d=128)
                for g in range(3):
                    d = DIL[g]
                    for b4 in range(8):
                        for bb in range(4):
                            blk = b4 * 4 + bb
                            n, r = blk // d, blk % d
                            t0 = n * 128 * d + r
                            for k in range(8):
                                S.op(S.pe, lambda e, k=k: e.matmul(
                                    ppv[:, bb, :], lhsT=actT[:, k, t0:t0 + 127 * d + 1:d], rhs=wd[:, k, g * 3 + 2, :],
                                    start=(k == 0), stop=(k == 7)),
                                     reads=[("wd", g * 3 + 2)] + [("actT", m) for m in range(n * d, (n + 1) * d)], writes=["pp"])
                        S.op(S.act, lambda e: e.copy(out=Vx[g][:, b4 * 4:(b4 + 1) * 4, 0:64], in_=ppv[:, :, 0:64]),
                             reads=["pp"], writes=[("Vx", g)])
                        S.op(S.dve, lambda e: e.tensor_copy(out=Vx[g][:, b4 * 4:(b4 + 1) * 4, 128:192],
                                                            in_=ppv[:, :, 64:128]),
                             reads=["pp"], writes=[("Vx", g)])
                for hl in range(2):
                    r0, d0 = hl * 64, 64 - hl * 64
                    rows = slice(r0, r0 + 64)
                    vsl = slice(0, 128) if hl == 0 else slice(64, 192)
                    for U in range(2):
                        S.op(S.dve, lambda e: e.memset(p_o[:], 0.0), writes=["po"])
                        for g in range(3):
                            d = DIL[g]
                            Kt, Qt = QK[g][1], QK[g][0]
                            for qb in range(16):
                                blk = U * 16 + qb
                                n, r = blk // d, blk % d
                                t0 = n * 128 * d + r
                                qsl = slice(t0, t0 + 127 * d + 1, d)
                                osl = slice(t0 - U * 2048, t0 - U * 2048 + 127 * d + 1, d)
                                psb = p_s[(ips // 2) % 2]
                                half = (ips % 2) * 256
                                rps = ("ps", (ips // 2) % 2, ips % 2)
                                ptb, rpt = pt[ips % 3], ("pt", ips % 3)
                                ips += 1
                                c0 = 0 if n > 0 else 128
                                qres = [("QK", g, 0, (t0 // 512) + q) for q in range(max(1, (128 * d) // 512))]
                                if n > 0:
                                    tp = t0 - 128 * d
                                    S.op(S.pe, lambda e: e.matmul(psb[:, half:half + 128], lhsT=Kt[rows, tp:tp + 127 * d + 1:d],
                                                                  rhs=Qt[rows, qsl], start=True, stop=True),
                                         reads=[("QKall",)], writes=[rps])
                                S.op(S.pe, lambda e: e.matmul(psb[:, half + 128:half + 256], lhsT=Kt[rows, qsl],
                                                              rhs=Qt[rows, qsl], start=True, stop=True),
                                     reads=[("QKall",)], writes=[rps])
                                S.op(S.act, lambda e: e.activation(out=ptb[:, c0:256], in_=psb[:, half + c0:half + 256],
                                                                   func=AF.Exp, scale=0.125), reads=[rps], writes=[rpt])
                                S.op(S.pool, lambda e: e.tensor_tensor(out=ptb[:, c0:256], in0=ptb[:, c0:256],
                                                                       in1=mask2[:, c0:256], op=ALU.mult),
                                     reads=[rpt, "mask2"], writes=[rpt])
                                if n > 0:
                                    S.op(S.pe, lambda e: e.matmul(p_o[:, osl], lhsT=Vx[g][:, blk - d, vsl], rhs=ptb[:, 0:128],
                                                                  start=False, stop=False, skip_group_check=True),
                                         reads=[("Vx", g), rpt, "po"], writes=["po"])
                                S.op(S.pe, lambda e: e.matmul(p_o[:, osl], lhsT=Vx[g][:, blk, vsl], rhs=ptb[:, 128:256],
                                                              start=False, stop=False, skip_group_check=True),
                                     reads=[("Vx", g), rpt, "po"], writes=["po"])
                        for qq in range(4):
                            cs = slice(qq * 512, (qq + 1) * 512)
                            tok = slice(U * 2048 + qq * 512, U * 2048 + (qq + 1) * 512)
                            S.op(S.dve, lambda e: e.reciprocal(out=rec[d0:d0 + 1, :], in_=p_o[d0:d0 + 1, cs]),
                                 reads=["po"], writes=["rec"])
                            S.op(S.pe, lambda e: e.matmul(p_bc[:], lhsT=self.ones_f[d0:d0 + 1, :], rhs=rec[d0:d0 + 1, :],
                                                          start=True, stop=True), reads=["rec", "ones_f"], writes=["pbc"])
                            S.op(S.act, lambda e: e.copy(out=bcs[rows, :], in_=p_bc[rows, :]), reads=["pbc"], writes=["bcs"])
                            S.op(S.dve, lambda e: e.tensor_tensor(out=osb[rows, tok], in0=p_o[rows, cs], in1=bcs[rows, :],
                                                                  op=ALU.mult), reads=["po", "bcs"], writes=["osb"])
                S.dma("sp", lambda e: e.dma_start(out=self.oT_d[hp], in_=osb[:]), reads=["osb"], writes=[("oTd", hp)])
            S.barrier()
            self.load_oT()
            S.barrier()
        self.out_proj_phase(L, self.dil_wo)


    def rwkv_phase(self, L):
        nc, S = self.nc, self.S
        actT = self.actT
        RT = 256
        NT_ = T // RT
        GN_EPS = 64e-5
        DEC = -float(np.exp(-0.5))
        with ExitStack() as ph:
            sb = lambda n, shp, dt: ph.enter_context(nc.sbuf_tensor(self.nm(n), shp, dt))
            ps = lambda n, shp, dt: ph.enter_context(nc.psum_tensor(self.nm(n), shp, dt))
            mu = sb("r_mu", [128, 8, 6], F32)
            omu = sb("r_omu", [128, 8, 6], F32)
            vec = sb("r_vec", [128, 8, 8], F32)
            stage = sb("r_stage", [128, 8, 160], F32)
            l1a = sb("r_l1a", [128, 8, 288], BF16)
            l1b = sb("r_l1b", [128, 8, 288], BF16)
            hwa = sb("r_hwa", [128, T], BF16)
            hg = sb("r_hg", [128, 2, T], BF16)
            w2 = sb("r_w2", [128, D], BF16)
            g2 = sb("r_g2", [128, 2, D], BF16)
            wa = [sb(f"r_wa{i}", [128, 8, 128], BF16) for i in range(3)]
            wb = [sb(f"r_wb{i}", [128, 8, 128], BF16) for i in range(3)]
            f32t = lambda n: sb(n, [128, RT], F32)
            r_, k_, v_, lw, a_, g_, kk, km, be, Lc, eL, eLn, eLp, eD, tA, tB = [f32t(f"r_f{i}") for i in range(16)]
            LC = sb("r_LC", [128, 4], F32)
            eLC = sb("r_eLC", [128, 4], F32)
            rmask = sb("r_rmask", [128, RT], F32)
            blk1 = sb("r_blk1", [128, 128], F32)
            bdn = ["kap", "rt", "kt", "bt", "vf", "kb", "bb"]
            BD = {n: sb("r_bd_" + n, [128, 4, 128], F32) for n in bdn}
            MkvT, AkrT, AbrT, Y = [sb(f"r_m{i}", [128, 4, 128], F32) for i in range(4)]
            X = [sb(f"r_X{i}", [128, 4, 128], F32) for i in range(2)]
            XT = [sb(f"r_XT{i}", [128, 4, 128], F32) for i in range(2)]
            Vtok, Ktok, Btok = [sb(f"r_tk{i}", [128, 4, 128], F32) for i in range(3)]
            Wsb = sb("r_Wsb", [128, 128], F32)
            nU = sb("r_nU", [128, 128], F32)
            Abd = sb("r_Abd", [128, 128], F32)
            Osb = sb("r_Osb", [128, 128], F32)
            On = sb("r_On", [128, 4, 128], F32)
            ysb = sb("r_ysb", [128, RT], F32)
            osb = [sb(f"r_osb{i}", [128, RT], BF16) for i in range(2)]
            SU4, UI4, SL4, ID4 = [sb(f"r_msk{i}", [128, 4, 128], F32) for i in range(4)]
            identf = sb("r_identf", [128, 128], F32)
            st6 = sb("r_st6", [128, 6], F32)
            mv = sb("r_mv", [128, 8], F32)
            pP = [ps(f"r_pP{i}", [128, 512], F32) for i in range(2)]
            pX = [ps(f"r_pX{i}", [128, 512], F32) for i in range(5)]
            pS = ps("r_pS", [128, 512], F32)

            def aff(t, pattern, cm, op, base=0):
                S.op(S.pool, lambda e: e.affine_select(out=t, in_=t, pattern=pattern, compare_op=op, fill=0.0,
                                                       base=base, channel_multiplier=cm),
                     reads=["cst"], writes=["cst"])
            S.op(S.pool, lambda e: e.memset(identf[:], 1.0), writes=["cst"])
            aff(identf[:], [[1, 128]], -1, ALU.is_equal)
            for t4 in (SU4, UI4, SL4, ID4):
                S.op(S.pool, lambda e, t4=t4: e.memset(t4[:], 0.0), reads=["cst"], writes=["cst"])
            S.op(S.pool, lambda e: e.memset(blk1[:], 0.0), reads=["cst"], writes=["cst"])
            for hb in range(2):
                rs_ = slice(hb * 64, hb * 64 + 64)
                S.op(S.pool, lambda e: e.memset(blk1[rs_, rs_], 1.0), reads=["cst"], writes=["cst"])
                for c in range(4):
                    for t4 in (SU4, UI4, SL4, ID4):
                        S.op(S.pool, lambda e, t4=t4: e.memset(t4[rs_, c, rs_], 1.0), reads=["cst"], writes=["cst"])
                    aff(SU4[rs_, c, rs_], [[1, 64]], -1, ALU.is_gt)
                    aff(UI4[rs_, c, rs_], [[1, 64]], -1, ALU.is_ge)
                    aff(SL4[rs_, c, rs_], [[-1, 64]], 1, ALU.is_gt)
                    aff(ID4[rs_, c, rs_], [[1, 64]], -1, ALU.is_equal)
            S.op(S.pool, lambda e: e.memset(rmask[:], 1.0), reads=["cst"], writes=["cst"])
            for c in range(4):
                S.op(S.pool, lambda e, c=c: e.memset(rmask[:, c * 64:c * 64 + 1], 0.0), reads=["cst"], writes=["cst"])
            for n in bdn:
                S.op(S.pool, lambda e, n=n: e.memset(BD[n][:], 0.0), writes=[("bd", n)])
            S.op(S.pool, lambda e: e.memset(On[:], 0.0), writes=["On"])

            S.dma("sp", lambda e: e.dma_start(out=mu[:], in_=self.rw_mu), writes=["mu"])
            S.dma("sp", lambda e: e.dma_start(out=vec[:], in_=self.rw_vec), writes=["vec"])
            S.op(S.dve, lambda e: e.tensor_scalar(out=omu[:], in0=mu[:], scalar1=-1.0, scalar2=1.0, op0=ALU.mult,
                                                  op1=ALU.add), reads=["mu"], writes=["omu"])
            S.op(S.dve, lambda e: e.tensor_scalar(out=vec[:, :, 4:5], in0=vec[:, :, 3:4], scalar1=-1.0, scalar2=1.0,
                                                  op0=ALU.mult, op1=ALU.add), reads=["vec"], writes=["vec"])
            S.dma("pool", lambda e: e.dma_start(out=w2[0:64, :], in_=self.rw_w2), writes=["w2"])
            S.dma("pool", lambda e: e.dma_start(out=w2[64:128, :], in_=self.rw_a2), writes=["a2"])
            S.dma("pool", lambda e: e.dma_start(out=g2[:, 0, :], in_=self.rw_g2[0:128, :]), writes=["g2a"])
            S.dma("pool", lambda e: e.dma_start(out=g2[0:32, 1, :], in_=self.rw_g2[128:160, :]), writes=["g2b"])

            def scaled_weights(src3, ncols, dsts_a, dsts_b, jcols):
                S.dma("sp", lambda e: e.dma_start(out=stage[:, :, 0:ncols], in_=src3), writes=["stage"])
                for (c0, c1, j), da, db in zip(jcols, dsts_a, dsts_b):
                    for k in range(8):
                        S.op(S.pool, lambda e, k=k: e.tensor_scalar(out=da[:, k, :], in0=stage[:, k, c0:c1],
                                                                    scalar1=omu[:, k, j:j + 1], scalar2=None,
                                                                    op0=ALU.mult),
                             reads=["stage", "omu"], writes=["wsc"])
                        S.op(S.pool, lambda e, k=k: e.tensor_scalar(out=db[:, k, :], in0=stage[:, k, c0:c1],
                                                                    scalar1=mu[:, k, j:j + 1], scalar2=None,
                                                                    op0=ALU.mult),
                             reads=["stage", "mu"], writes=["wsc"])

            def proj(pout, la, lb, j, M=128, r0=0):
                t0 = j * RT
                ares = [("actT", m) for m in range(max(0, 2 * j - 1), 2 * j + 2)]
                for k in range(8):
                    S.op(S.pe, lambda e, k=k: e.matmul(pout[r0:r0 + M, 0:RT], lhsT=la(k), rhs=actT[:, k, t0:t0 + RT],
                                                       start=(k == 0), stop=False),
                         reads=["wsc"] + ares, writes=["pP"])
                c0 = 1 if j == 0 else 0
                for k in range(8):
                    S.op(S.pe, lambda e, k=k: e.matmul(pout[r0:r0 + M, c0:RT], lhsT=lb(k),
                                                       rhs=actT[:, k, t0 - 1 + c0:t0 + RT - 1],
                                                       start=False, stop=(k == 7)),
                         reads=["wsc"] + ares, writes=["pP"])

            l1src = self.rw_l1.rearrange("(k p) n -> p k n", p=128)
            scaled_weights(l1src[:, :, 0:128], 128, [l1a[:, :, 0:64], l1a[:, :, 64:128]],
                           [l1b[:, :, 0:64], l1b[:, :, 64:128]], [(0, 64, 3), (64, 128, 4)])
            scaled_weights(l1src[:, :, 128:288], 160, [l1a[:, :, 128:288]], [l1b[:, :, 128:288]], [(0, 160, 5)])
            for j in range(NT_):
                tok = slice(j * RT, (j + 1) * RT)
                p0 = pP[j % 2]
                proj(p0, lambda k: l1a[:, k, 0:128], lambda k: l1b[:, k, 0:128], j)
                S.op(S.act, lambda e: e.activation(out=hwa[0:64, tok], in_=p0[0:64, 0:RT], func=AF.Tanh),
                     reads=["pP"], writes=["hwa"])
                S.op(S.act, lambda e: e.copy(out=hwa[64:128, tok], in_=p0[64:128, 0:RT]), reads=["pP"], writes=["hwa"])
                proj(p0, lambda k: l1a[:, k, 128:256], lambda k: l1b[:, k, 128:256], j)
                S.op(S.act, lambda e: e.activation(out=hg[:, 0, tok], in_=p0[:, 0:RT], func=AF.Sigmoid),
                     reads=["pP"], writes=["hg"])
                proj(p0, lambda k: l1a[:, k, 256:288], lambda k: l1b[:, k, 256:288], j, M=32)
                S.op(S.act, lambda e: e.activation(out=hg[0:32, 1, tok], in_=p0[0:32, 0:RT], func=AF.Sigmoid),
                     reads=["pP"], writes=["hg"])

            wsrc = self.rw_wrkv.rearrange("j (k p) n -> j p k n", p=128)
            H0, H1 = slice(0, 64), slice(64, 128)

            def dv(fn, reads, writes, eng=None):
                S.op(eng or S.dve, fn, reads=reads, writes=writes)

            import os
            STOP = int(os.environ.get("RW_STOP", "9"))
            for hp in range(8 if STOP > 0 else 0):
                cols = slice(hp * 128, (hp + 1) * 128)
                for jj in range(3):
                    scaled_weights(wsrc[jj][:, :, cols], 128, [wa[jj]], [wb[jj]], [(0, 128, jj)])
                S.op(S.pool, lambda e: e.memset(Abd[:], 0.0), reads=["Abd"], writes=["Abd"])
                vcol = lambda i: vec[:, hp, i:i + 1]
                for j in range(NT_):
                    tok = slice(j * RT, (j + 1) * RT)
                    F = "F"
                    for jj, dst in enumerate((r_, k_, v_)):
                        p0 = pP[jj % 2]
                        proj(p0, lambda k: wa[jj][:, k, :], lambda k: wb[jj][:, k, :], j)
                        S.op(S.act, lambda e: e.copy(out=dst[:], in_=p0[:, 0:RT]), reads=["pP"], writes=[F])
                    p0 = pP[1]
                    S.op(S.pe, lambda e: e.matmul(p0[:, 0:RT], lhsT=w2[0:64, cols], rhs=hwa[0:64, tok], start=True,
                                                  stop=True), reads=["w2", "hwa"], writes=["pP"])
                    S.op(S.act, lambda e: e.activation(out=lw[:], in_=p0[:, 0:RT], func=AF.Sigmoid, bias=vcol(0)),
                         reads=["pP", "vec"], writes=[F])
                    S.op(S.pe, lambda e: e.matmul(p0[:, 0:RT], lhsT=w2[64:128, cols], rhs=hwa[64:128, tok], start=True,
                                                  stop=True), reads=["a2", "hwa"], writes=["pP"])
                    S.op(S.act, lambda e: e.activation(out=a_[:], in_=p0[:, 0:RT], func=AF.Sigmoid, bias=vcol(1)),
                         reads=["pP", "vec"], writes=[F])
                    S.op(S.pe, lambda e: e.matmul(p0[:, 0:RT], lhsT=g2[:, 0, cols], rhs=hg[:, 0, tok], start=True,
                                                  stop=False), reads=["g2a", "hg"], writes=["pP"])
                    S.op(S.pe, lambda e: e.matmul(p0[:, 0:RT], lhsT=g2[0:32, 1, cols], rhs=hg[0:32, 1, tok], start=False,
                                                  stop=True), reads=["g2b", "hg"], writes=["pP"])
                    S.op(S.act, lambda e: e.copy(out=g_[:], in_=p0[:, 0:RT]), reads=["pP"], writes=[F])
                    dv(lambda e: e.tensor_scalar(out=lw[:], in0=lw[:], scalar1=DEC, scalar2=None, op0=ALU.mult), [F], [F])
                    dv(lambda e: e.tensor_scalar(out=kk[:], in0=k_[:], scalar1=vcol(2), scalar2=None, op0=ALU.mult),
                       [F, "vec"], [F])
                    dv(lambda e: e.tensor_tensor(out=tA[:], in0=kk[:], in1=kk[:], op=ALU.mult), [F], [F], S.pool)
                    S.op(S.pe, lambda e: e.matmul(pP[0][:, 0:RT], lhsT=blk1[:], rhs=tA[:], start=True, stop=True),
                         reads=[F, "cst"], writes=["pP"])
                    S.op(S.act, lambda e: e.activation(out=tA[:], in_=pP[0][:, 0:RT], func=AF.Sqrt), reads=["pP"], writes=[F])
                    dv(lambda e: e.tensor_scalar(out=tA[:], in0=tA[:], scalar1=1e-12, scalar2=None, op0=ALU.max), [F], [F])
                    dv(lambda e: e.reciprocal(out=tA[:], in_=tA[:]), [F], [F])
                    dv(lambda e: e.tensor_tensor(out=kk[:], in0=kk[:], in1=tA[:], op=ALU.mult), [F], [F])
                    dv(lambda e: e.tensor_scalar(out=tB[:], in0=a_[:], scalar1=vcol(3), scalar2=vcol(4), op0=ALU.mult,
                                                 op1=ALU.add), [F, "vec"], [F])
                    dv(lambda e: e.tensor_tensor(out=km[:], in0=k_[:], in1=tB[:], op=ALU.mult), [F], [F])
                    dv(lambda e: e.tensor_tensor(out=be[:], in0=kk[:], in1=a_[:], op=ALU.mult), [F], [F], S.pool)
                    dv(lambda e: e.scalar_tensor_tensor(out=tB[:], in0=r_[:], scalar=vcol(5), in1=km[:], op0=ALU.mult,
                                                        op1=ALU.mult), [F, "vec"], [F])
                    S.op(S.pe, lambda e: e.matmul(pP[0][:, 0:RT], lhsT=blk1[:], rhs=tB[:], start=True, stop=True),
                         reads=[F, "cst"], writes=["pP"])
                    dv(lambda e: e.tensor_tensor(out=tA[:], in0=pP[0][:, 0:RT], in1=v_[:], op=ALU.mult), ["pP", F], [F])
                    dv(lambda e: e.tensor_tensor_scan(out=Lc[:], data0=rmask[:], data1=lw[:], initial=0.0,
                                                      op0=ALU.mult, op1=ALU.add), [F, "cst"], [F])
                    S.op(S.act, lambda e: e.activation(out=eL[:], in_=Lc[:], func=AF.Exp), reads=[F], writes=[F])
                    S.op(S.act, lambda e: e.activation(out=eLn[:], in_=Lc[:], func=AF.Exp, scale=-1.0), reads=[F], writes=[F])
                    dv(lambda e: e.tensor_tensor(out=eLp[:], in0=Lc[:], in1=lw[:], op=ALU.subtract), [F], [F], S.pool)
                    S.op(S.act, lambda e: e.activation(out=eLp[:], in_=eLp[:], func=AF.Exp), reads=[F], writes=[F])
                    L3 = Lc[:].rearrange("p (c t) -> p c t", t=64)
                    dv(lambda e: e.tensor_copy(out=LC[:], in_=L3[:, :, 63]), [F], [F])
                    S.op(S.act, lambda e: e.activation(out=eLC[:], in_=LC[:], func=AF.Exp), reads=[F], writes=[F])
                    for c in range(4):
                        S.op(S.act, lambda e, c=c: e.activation(out=eD[:, c * 64:(c + 1) * 64], in_=Lc[:, c * 64:(c + 1) * 64],
                                                                func=AF.Exp, scale=-1.0, bias=LC[:, c:c + 1]),
                             reads=[F], writes=[F])
                    prods = [("kap", kk, eLp), ("rt", r_, eL), ("kt", km, eLn), ("bt", be, eLn), ("kb", km, eD),
                             ("bb", be, eD)]
                    ie = 0
                    for n, x0, x1 in prods:
                        for hs in (H0, H1):
                            eng = S.dve if ie % 2 == 0 else S.pool
                            ie += 1
                            dv(lambda e: e.tensor_tensor(out=BD[n][hs, :, hs],
                                                         in0=x0[hs, :].rearrange("p (c t) -> p c t", t=64),
                                                         in1=x1[hs, :].rearrange("p (c t) -> p c t", t=64), op=ALU.mult),
                               [F], [("bd", n)], eng)
                    for hs in (H0, H1):
                        S.op(S.act, lambda e: e.copy(out=BD["vf"][hs, :, hs], in_=v_[hs, :].rearrange("p (c t) -> p c t", t=64)),
                             reads=[F], writes=[("bd", "vf")])
                    if STOP < 2:
                        continue
                    for c in range(4):
                        cs = slice(c * 128, (c + 1) * 128)
                        gm = [(0, "kt", "kap"), (1, "kt", "rt"), (2, "bt", "kap"), (3, "bt", "rt"), (4, "kap", "bt")]
                        for pi, ln_, rn_ in gm:
                            S.op(S.pe, lambda e: e.matmul(pX[pi][:, cs], lhsT=BD[ln_][:, c, :], rhs=BD[rn_][:, c, :],
                                                          start=True, stop=True),
                                 reads=[("bd", ln_), ("bd", rn_)], writes=[("pX", pi)])
                    f4 = lambda t: t[:].rearrange("p c t -> p (c t)")
                    dv(lambda e: e.tensor_tensor(out=f4(MkvT), in0=pX[0][:], in1=f4(SU4), op=ALU.mult), [("pX", 0), "cst"], ["MkvT"])
                    dv(lambda e: e.tensor_tensor(out=f4(AkrT), in0=pX[1][:], in1=f4(UI4), op=ALU.mult), [("pX", 1), "cst"], ["AkrT"])
                    dv(lambda e: e.tensor_tensor(out=f4(X[0]), in0=pX[2][:], in1=f4(SU4), op=ALU.mult), [("pX", 2), "cst"], [("X", 0)])
                    dv(lambda e: e.tensor_tensor(out=f4(AbrT), in0=pX[3][:], in1=f4(UI4), op=ALU.mult), [("pX", 3), "cst"], ["AbrT"])
                    dv(lambda e: e.tensor_tensor(out=f4(XT[0]), in0=pX[4][:], in1=f4(SL4), op=ALU.mult), [("pX", 4), "cst"], [("XT", 0)])
                    if STOP < 3:
                        continue
                    dv(lambda e: e.tensor_tensor(out=f4(Y), in0=f4(ID4), in1=f4(X[0]), op=ALU.subtract),
                       [("X", 0), "cst"], ["Y"], S.pool)
                    cur = 0
                    for lvl in range(int(os.environ.get('RW_LVL', '5'))):
                        nxt = 1 - cur
                        last = (lvl == 4)
                        for c in range(4):
                            cs = slice(c * 128, (c + 1) * 128)
                            if not last:
                                S.op(S.pe, lambda e: e.matmul(pX[0][:, cs], lhsT=XT[cur][:, c, :], rhs=X[cur][:, c, :],
                                                              start=True, stop=True),
                                     reads=[("X", cur), ("XT", cur)], writes=[("pX", 0)])
                            S.op(S.pe, lambda e: e.matmul(pX[2][:, cs], lhsT=X[cur][:, c, :], rhs=XT[cur][:, c, :],
                                                          start=True, stop=True),
                                 reads=[("X", cur), ("XT", cur)], writes=[("pX", 2)])
                        if not last:
                            dv(lambda e: e.tensor_copy(out=f4(X[nxt]), in_=pX[0][:]), [("pX", 0)], [("X", nxt)])
                        S.op(S.act, lambda e: e.copy(out=f4(XT[nxt]), in_=pX[2][:]), reads=[("pX", 2)], writes=[("XT", nxt)])
                        for c in range(4):
                            cs = slice(c * 128, (c + 1) * 128)
                            S.op(S.pe, lambda e: e.matmul(pX[1][:, cs], lhsT=XT[nxt][:, c, :],
                                                          rhs=Y[:, c, :], start=True, stop=True),
                                 reads=[("XT", nxt), "Y"], writes=[("pX", 1)])
                        dv(lambda e: e.tensor_tensor(out=f4(Y), in0=f4(Y), in1=pX[1][:], op=ALU.add),
                           [("pX", 1), "Y"], ["Y"])
                        cur = nxt
                    for pi, n, dst in ((3, "vf", Vtok), (4, "kb", Ktok), (1, "bb", Btok)):
                        for c in range(4):
                            S.op(S.pe, lambda e: e.transpose(out=pX[pi][:, c * 128:(c + 1) * 128], in_=BD[n][:, c, :],
                                                             identity=identf[:]),
                                 reads=[("bd", n), "cst"], writes=[("pX", pi)])
                        S.op(S.act, lambda e: e.copy(out=f4(dst), in_=pX[pi][:]), reads=[("pX", pi)], writes=[("tok", n)])
                    if STOP < 4:
                        continue
                    for c in range(4):
                        S.op(S.pe, lambda e: e.matmul(pX[0][:, 0:128], lhsT=BD["kap"][:, c, :], rhs=Abd[:], start=True, stop=False),
                             reads=[("bd", "kap"), "Abd"], writes=["pW"])
                        S.op(S.pe, lambda e: e.matmul(pX[0][:, 0:128], lhsT=MkvT[:, c, :], rhs=Vtok[:, c, :], start=False, stop=True),
                             reads=["MkvT", ("tok", "vf")], writes=["pW"])
                        S.op(S.act, lambda e: e.copy(out=Wsb[:], in_=pX[0][:, 0:128]), reads=["pW"], writes=["Wsb"])
                        S.op(S.pe, lambda e: e.matmul(pX[1][:, 0:128], lhsT=Y[:, c, :], rhs=Wsb[:], start=True, stop=True),
                             reads=["Y", "Wsb"], writes=["pU"])
                        dv(lambda e: e.tensor_scalar(out=nU[:], in0=pX[1][:, 0:128], scalar1=-1.0, scalar2=None, op0=ALU.mult),
                           ["pU"], ["nU"])
                        S.op(S.pe, lambda e: e.matmul(pX[2][:, 0:128], lhsT=BD["rt"][:, c, :], rhs=Abd[:], start=True, stop=False),
                             reads=[("bd", "rt"), "Abd"], writes=["pO"])
                        S.op(S.pe, lambda e: e.matmul(pX[2][:, 0:128], lhsT=AkrT[:, c, :], rhs=Vtok[:, c, :], start=False, stop=False),
                             reads=["AkrT", ("tok", "vf")], writes=["pO"])
                        S.op(S.pe, lambda e: e.matmul(pX[2][:, 0:128], lhsT=AbrT[:, c, :], rhs=nU[:], start=False, stop=True),
                             reads=["AbrT", "nU"], writes=["pO"])
                        S.op(S.pe, lambda e: e.matmul(pX[3][:, 0:128], lhsT=Ktok[:, c, :], rhs=Vtok[:, c, :], start=True, stop=False),
                             reads=[("tok", "kb"), ("tok", "vf")], writes=["pA"])
                        S.op(S.pe, lambda e: e.matmul(pX[3][:, 0:128], lhsT=Btok[:, c, :], rhs=nU[:], start=False, stop=True),
                             reads=[("tok", "bb"), "nU"], writes=["pA"])
                        dv(lambda e: e.scalar_tensor_tensor(out=Abd[:], in0=Abd[:], scalar=eLC[:, c:c + 1], in1=pX[3][:, 0:128],
                                                            op0=ALU.mult, op1=ALU.add), ["pA", "Abd", F], ["Abd"])
                        if int(os.environ.get("RW_SEQ", "9")) < 1:
                            continue
                        S.op(S.act, lambda e: e.copy(out=Osb[:], in_=pX[2][:, 0:128]), reads=["pO"], writes=["Osb"])
                        for hs in (H0, H1):
                            dv(lambda e: e.bn_stats(out=st6[hs, :], in_=Osb[hs, hs]), ["Osb"], ["st6"])
                        dv(lambda e: e.bn_aggr(out=mv[:, 0:2], in_=st6[:]), ["st6"], ["mv"])
                        dv(lambda e: e.tensor_scalar(out=mv[:, 2:3], in0=mv[:, 1:2], scalar1=GN_EPS, scalar2=None, op0=ALU.add),
                           ["mv"], ["mv"])
                        S.op(S.act, lambda e: e.activation(out=mv[:, 3:4], in_=mv[:, 2:3], func=AF.Sqrt), reads=["mv"], writes=["mv"])
                        dv(lambda e: e.reciprocal(out=mv[:, 4:5], in_=mv[:, 3:4]), ["mv"], ["mv"])
                        dv(lambda e: e.scalar_tensor_tensor(out=mv[:, 5:6], in0=mv[:, 0:1], scalar=-1.0, in1=mv[:, 4:5],
                                                            op0=ALU.mult, op1=ALU.mult), ["mv"], ["mv"])
                        for hs in (H0, H1):
                            S.op(S.act, lambda e: e.activation(out=On[hs, c, hs], in_=Osb[hs, hs], func=AF.Identity,
                                                               bias=mv[hs, 5:6], scale=mv[hs, 4:5]),
                                 reads=["mv", "Osb"], writes=["On"])
                        if int(os.environ.get("RW_SEQ", "9")) < 2:
                            continue
                        S.op(S.pe, lambda e: e.transpose(out=pP[1][:, c * 128:(c + 1) * 128], in_=On[:, c, :], identity=identf[:]),
                             reads=["On", "cst"], writes=["pP"])
                    if int(os.environ.get("RW_SEQ", "9")) < 3:
                        continue
                    for hs in (H0, H1):
                        dv(lambda e: e.tensor_scalar(out=ysb[hs, :].rearrange("p (c t) -> p c t", t=64),
                                                     in0=pP[1][:].rearrange("p (c t) -> p c t", t=128)[hs, :, hs],
                                                     scalar1=vec[hs, hp, 6:7], scalar2=vec[hs, hp, 7:8], op0=ALU.mult,
                                                     op1=ALU.add), ["pP", "vec"], ["ysb"])
                    dv(lambda e: e.tensor_tensor(out=ysb[:], in0=ysb[:], in1=tA[:], op=ALU.add), ["ysb", F], ["ysb"], S.pool)
                    ob = osb[j % 2]
                    dv(lambda e: e.tensor_tensor(out=ob[:], in0=ysb[:], in1=g_[:], op=ALU.mult), ["ysb", F], [("osb", j % 2)], S.pool)
                    S.dma("sp", lambda e: e.dma_start(out=self.oT_d[hp][:, tok], in_=ob[:]), reads=[("osb", j % 2)],
                          writes=[("oTd", hp)])
            S.barrier()
            self.load_oT()
            S.barrier()
        self.out_proj_phase(L, self.rw_wo)


def host_layout(inp):
    out = {}
    cw = np.zeros((DEPTH, NCH * 128, 4), np.float32)
    cw[:, :D_FF, 0:3] = np.transpose(inp["ffn_conv_w"], (0, 2, 1))
    cw[:, :D_FF, 3] = inp["ffn_conv_b"]
    out["ffn_cw"] = np.ascontiguousarray(cw.reshape(DEPTH, NCH, 128, 4).transpose(0, 2, 1, 3))
    for k in ("ln_g", "ln_b", "ffn_w_in", "ffn_w_out"):
        out[k] = np.ascontiguousarray(inp[k], dtype=np.float32)
    for k in ("dil_w_qkv", "dil_w_o"):
        out[k] = np.ascontiguousarray(inp[k][0], dtype=np.float32)
    fm = lambda v: np.ascontiguousarray(np.asarray(v, np.float32).reshape(8, 128).T)
    out["rw_mu"] = np.ascontiguousarray(inp["rwkv_mu"][0].reshape(6, 8, 128).transpose(2, 1, 0))
    ka = inp["rwkv_k_a"][0]
    vecs = [inp["rwkv_w0"][0], inp["rwkv_a0"][0], inp["rwkv_k_k"][0], ka, None, inp["rwkv_r_k"][0].reshape(-1),
            inp["rwkv_ln_w"][0], inp["rwkv_ln_b"][0]]
    rv = np.zeros((128, 8, 8), np.float32)
    for i, v in enumerate(vecs):
        if v is not None:
            rv[:, :, i] = fm(v)
    out["rw_vec"] = rv
    out["rwkv_w_rkv"] = np.ascontiguousarray(inp["rwkv_w_rkv"][0], dtype=np.float32)
    out["rw_l1"] = np.ascontiguousarray(np.concatenate([inp["rwkv_w1"][0], inp["rwkv_a1"][0], inp["rwkv_g1"][0]], axis=1))
    for k in ("rwkv_w2", "rwkv_a2", "rwkv_g2", "rwkv_w_o"):
        out[k] = np.ascontiguousarray(inp[k][0], dtype=np.float32)
    wd = inp["mla_w_down"]
    out["mla_wd"] = np.ascontiguousarray(np.concatenate(
        [wd[:, :, 0:640], wd[:, :, 0:64], wd[:, :, 640:672], wd[:, :, 656:672], wd[:, :, 640:656]], axis=2))
    out["mla_qn"] = np.ascontiguousarray(inp["mla_q_norm"].reshape(-1, 3, 128).transpose(0, 2, 1))
    out["mla_kvn"] = np.ascontiguousarray(inp["mla_kv_norm"].reshape(-1, 2, 128).transpose(0, 2, 1))
    wq = inp["mla_w_uq"].reshape(-1, 384, 16, 96)
    out["mla_wuq"] = np.ascontiguousarray(np.concatenate(
        [wq[..., 0:96], wq[..., 80:96], wq[..., 64:80]], axis=3).reshape(-1, 384, 2048))
    wkv = inp["mla_w_ukv"].reshape(-1, 256, 16, 128)
    out["mla_wukv"] = np.ascontiguousarray(np.concatenate(
        [wkv[..., 0:64].reshape(-1, 256, 1024), wkv[..., 64:128].reshape(-1, 256, 1024)], axis=2))
    out["mla_wo"] = np.ascontiguousarray(inp["mla_w_o"], dtype=np.float32)
    rc = np.zeros((96, 2), np.float32)
    invf = (10000.0 ** (-np.arange(0, 32, 2, dtype=np.float32) / np.float32(32))).astype(np.float32)
    rc[64:80, 0] = invf / np.float32(2 * np.pi)
    rc[80:96, 0] = invf / np.float32(2 * np.pi)
    rc[64:80, 1] = -1.0
    rc[80:96, 1] = 1.0
    out["rope_c"] = rc
    return out


DEFAULT_PLAN = [("mla", 0), ("ffn", 0), ("dil", 1), ("ffn", 1), ("rwkv", 2), ("ffn", 2), ("mla", 3), ("ffn", 3)]
_CACHE = {}


def run(inputs, plan, n_cores=8, trace=False):
    key = tuple(plan)
    if key not in _CACHE:
        b = Builder(plan)
        nc = b.build()
        _CACHE[key] = (b, nc)
    b, nc = _CACHE[key]
    shared = host_layout(inputs)
    in_maps = []
    for c in range(n_cores):
        d = {"x": np.ascontiguousarray(inputs["x"][c], dtype=np.float32),
             "positions": np.ascontiguousarray(inputs["positions"][c], dtype=np.int32)}
        d.update(shared)
        d = {k: v for k, v in d.items() if k in b.din}
        in_maps.append(d)
    res = run_bass_kernel_spmd(nc, in_maps, core_ids=list(range(n_cores)), trace=trace)
    return np.stack([r["out"] for r in res.results], axis=0), res


def kernel(**inputs):
    out, _ = run(inputs, DEFAULT_PLAN)
    return out.astype(np.float32)
```

```python
import numpy as np
from contextlib import ExitStack
import concourse.bass as bass
import concourse.mybir as mybir
from concourse.bass_utils import run_bass_kernel_spmd

F32 = mybir.dt.float32
BF16 = mybir.dt.bfloat16
I32 = mybir.dt.int32
AF = mybir.ActivationFunctionType
ALU = mybir.AluOpType

T = 4096
D = 1024
DEPTH = 4
NB = T // 128
ALPHA = (2 * DEPTH) ** 0.25
LN_EPS = 1e-5
RMS_EPS = 1e-6
D_FF = 2752
NCH = 22


class _Eng:
    def __init__(self, name, eng, sem):
        self.name = name
        self.eng = eng
        self.sem = sem
        self.count = 0
        self.waited = {}


class Sched:
    def __init__(self, nc, stack, n_dma_sems=16):
        self.nc = nc
        mk = lambda n: stack.enter_context(nc.semaphore(n))
        self.pe = _Eng("pe", nc.tensor, mk("s_pe"))
        self.act = _Eng("act", nc.scalar, mk("s_act"))
        self.dve = _Eng("dve", nc.vector, mk("s_dve"))
        self.pool = _Eng("pool", nc.gpsimd, mk("s_pool"))
        self.sp = _Eng("sp", nc.sync, None)
        self.q = {"sp": [mk(f"s_dsp{i}") for i in range(n_dma_sems)],
                  "pool": [mk(f"s_dpl{i}") for i in range(n_dma_sems)]}
        self.qeng = {"sp": self.sp, "pool": self.pool}
        self.dma_cnt = {"sp": 0, "pool": 0}
        self.dma_last = {}
        self.last_write = {}
        self.readers = {}
        self.n_ops = 0
        self.n_waits = 0

    def _wait(self, E, tok):
        sem, val, src = tok
        if src == "pe" and E.name == "pe":
            return
        k = id(sem)
        if E.waited.get(k, 0) >= val:
            return
        E.eng.wait_ge(sem, val)
        E.waited[k] = val
        self.n_waits += 1

    def _deps(self, E, reads, writes):
        for r in reads:
            t = self.last_write.get(r)
            if t is not None:
                self._wait(E, t)
        for w in writes:
            t = self.last_write.get(w)
            if t is not None:
                self._wait(E, t)
            for t in self.readers.get(w, ()):
                self._wait(E, t)

    def _commit(self, tok, reads, writes):
        for r in reads:
            self.readers.setdefault(r, []).append(tok)
        for w in writes:
            self.last_write[w] = tok
            self.readers[w] = []

    def op(self, E, fn, reads=(), writes=()):
        self._deps(E, reads, writes)
        ins = fn(E.eng)
        E.count += 1
        ins.then_inc(E.sem, 1)
        tok = (E.sem, E.count, E.name)
        self._commit(tok, reads, writes)
        self.n_ops += 1
        return tok

    def dma(self, qname, fn, reads=(), writes=()):
        E = self.qeng[qname]
        pool = self.q[qname]
        i = self.dma_cnt[qname]
        self.dma_cnt[qname] = i + 1
        slot = i % len(pool)
        prev = self.dma_last.get((qname, slot))
        if prev is not None:
            self._wait(E, prev)
        self._deps(E, reads, writes)
        ins = fn(E.eng)
        ins.then_inc(pool[slot], 16)
        tok = (pool[slot], 16 * (i // len(pool) + 1), "dma_" + qname)
        self.dma_last[(qname, slot)] = tok
        self._commit(tok, reads, writes)
        self.n_ops += 1
        return tok

    def barrier(self):
        toks = [(E.sem, E.count, E.name) for E in (self.pe, self.act, self.dve, self.pool) if E.count]
        toks += list(self.dma_last.values())
        for E in (self.pe, self.act, self.dve, self.pool, self.sp):
            for t in toks:
                if t[2] == E.name:
                    continue
                self._wait(E, t)
        self.last_write = {}
        self.readers = {}


class Builder:
    def __init__(self, plan, debug_out=False):
        self.plan = plan
        nc = bass.Bass("TRN2", target_bir_lowering=False)
        self.nc = nc
        self.din = {}

    def nm(self, n):
        self._uid = getattr(self, "_uid", 0) + 1
        return f"{n}_{self._uid}"

    def dram_in(self, name, shape, dt=F32):
        t = self.nc.dram_tensor(name, list(shape), dt, kind="ExternalInput").ap()
        self.din[name] = t
        return t

    def build(self):
        nc = self.nc
        plan = self.plan
        self.x_in = self.dram_in("x", [T, D])
        self.out = nc.dram_tensor("out", [T, D], F32, kind="ExternalOutput").ap()
        self.ln_g = self.dram_in("ln_g", [DEPTH, 2, D])
        self.ln_b = self.dram_in("ln_b", [DEPTH, 2, D])
        self.ffn_w_in = self.dram_in("ffn_w_in", [DEPTH, D, 2 * D_FF])
        self.ffn_w_out = self.dram_in("ffn_w_out", [DEPTH, D_FF, D])
        self.ffn_cw = self.dram_in("ffn_cw", [DEPTH, 128, NCH, 4])
        kinds = {k for k, _ in plan}
        if "dil" in kinds or "rwkv" in kinds:
            self.oT_d = nc.dram_tensor("oT_d", [8, 128, T], BF16, kind="Internal").ap()
        if "dil" in kinds:
            self.dil_wqkv = self.dram_in("dil_w_qkv", [D, 9216])
            self.dil_wo = self.dram_in("dil_w_o", [D, D])
        if "rwkv" in kinds:
            self.rw_mu = self.dram_in("rw_mu", [128, 8, 6])
            self.rw_wrkv = self.dram_in("rwkv_w_rkv", [3, D, D])
            self.rw_l1 = self.dram_in("rw_l1", [D, 288])
            self.rw_w2 = self.dram_in("rwkv_w2", [64, D])
            self.rw_a2 = self.dram_in("rwkv_a2", [64, D])
            self.rw_g2 = self.dram_in("rwkv_g2", [160, D])
            self.rw_vec = self.dram_in("rw_vec", [128, 8, 8])
            self.rw_wo = self.dram_in("rwkv_w_o", [D, D])
        if "mla" in kinds:
            self.pos = self.dram_in("positions", [T], I32)
            self.rope_c = self.dram_in("rope_c", [96, 2])
            self.mla_wd = self.dram_in("mla_wd", [2, D, 768])
            self.mla_qn = self.dram_in("mla_qn", [2, 128, 3])
            self.mla_kvn = self.dram_in("mla_kvn", [2, 128, 2])
            self.mla_wuq = self.dram_in("mla_wuq", [2, 384, 2048])
            self.mla_wukv = self.dram_in("mla_wukv", [2, 256, 2048])
            self.mla_wo = self.dram_in("mla_wo", [2, D, D])

        with ExitStack() as st:
            self.st = st
            S = self.S = Sched(nc, st)
            gsb = lambda n, shp, dt: st.enter_context(nc.sbuf_tensor(self.nm(n), shp, dt))
            self.actT = gsb("actT", [128, 8, T], BF16)
            self.ident = gsb("ident", [128, 128], BF16)
            self.lng = gsb("lng", [128, D], F32)
            self.lnb = gsb("lnb", [128, D], F32)
            self.ones_bf = gsb("ones_bf", [128, 128], BF16)
            self.ones_f = gsb("ones_f", [128, 128], F32)
            self.tri = gsb("tri", [128, 128], BF16)
            self.ep_idx = 0
            self.cur_src = self.x_in

            S.op(S.pool, lambda e: e.memset(self.ident[:], 1.0), writes=["ident"])
            S.op(S.pool, lambda e: e.affine_select(out=self.ident[:], in_=self.ident[:], pattern=[[1, 128]],
                                                   compare_op=ALU.is_equal, fill=0.0, base=0,
                                                   channel_multiplier=-1),
                 reads=["ident"], writes=["ident"])
            S.op(S.pool, lambda e: e.memset(self.ones_bf[:], 1.0), writes=["ones_bf"])
            S.op(S.pool, lambda e: e.memset(self.ones_f[:], 1.0), writes=["ones_f"])
            S.op(S.pool, lambda e: e.memset(self.tri[:], 1.0), writes=["tri"])
            S.op(S.pool, lambda e: e.affine_select(out=self.tri[:], in_=self.tri[:], pattern=[[1, 128]],
                                                   compare_op=ALU.is_ge, fill=0.0, base=0,
                                                   channel_multiplier=-1),
                 reads=["tri"], writes=["tri"])
            self.init_phase()
            for step in plan:
                kind, L = step
                if kind == "ffn":
                    self.ffn_phase(L)
                elif kind == "mla":
                    self.mla_phase(L)
                elif kind == "dil":
                    self.dil_phase(L)
                elif kind == "rwkv":
                    self.rwkv_phase(L)
                elif kind == "copy":
                    self.copy_phase()
                else:
                    raise ValueError(kind)
            S.barrier()
        return nc

    def transposes_to_actT(self, m, xb, pT, res_xb):
        S = self.S
        for k in range(8):
            S.op(S.pe, lambda e, k=k: e.transpose(out=pT[:, k, :], in_=xb[:, k * 128:(k + 1) * 128],
                                                  identity=self.ident[:]),
                 reads=[res_xb, "ident"], writes=["pT"])
        S.op(S.dve, lambda e: e.tensor_copy(out=self.actT[:, :, m * 128:(m + 1) * 128], in_=pT[:]),
             reads=["pT"], writes=[("actT", m)])

    def alloc_epi(self, ph):
        nc = self.nc
        sb = lambda n, shp, dt: ph.enter_context(nc.sbuf_tensor(self.nm(n), shp, dt))
        self.xr = [sb(f"xr{i}", [128, D], F32) for i in range(2)]
        self.z = [sb(f"z{i}", [128, D], F32) for i in range(2)]
        self.xb = [sb(f"xb{i}", [128, D], BF16) for i in range(2)]
        self.st6 = [sb(f"st6{i}", [128, 2, 6], F32) for i in range(2)]
        self.mv = [sb(f"mv{i}", [128, 8], F32) for i in range(2)]

    def init_phase(self):
        nc, S = self.nc, self.S
        with ExitStack() as ph:
            self.alloc_epi(ph)
            pT = ph.enter_context(nc.psum_tensor(self.nm("pT_i"), [128, 8, 128], BF16))
            for m in range(NB):
                b = m % 2
                S.dma("sp", lambda e: e.dma_start(out=self.xr[b][:], in_=self.x_in[m * 128:(m + 1) * 128, :]),
                      writes=[("xr", b)])
                S.op(S.act, lambda e: e.copy(out=self.xb[b][:], in_=self.xr[b][:]),
                     reads=[("xr", b)], writes=[("xb", b)])
                self.transposes_to_actT(m, self.xb[b], pT, ("xb", b))
            S.barrier()

    def copy_phase(self):
        S = self.S
        ph = ExitStack()
        self.alloc_epi(ph)
        for m in range(NB):
            b = m % 2
            S.dma("sp", lambda e: e.dma_start(out=self.xr[b][:], in_=self.cur_src[m * 128:(m + 1) * 128, :]),
                  reads=[("xres", m)], writes=[("xr", b)])
            S.dma("sp", lambda e: e.dma_start(out=self.out[m * 128:(m + 1) * 128, :], in_=self.xr[b][:]),
                  reads=[("xr", b)], writes=[("xres", m)])
        S.barrier()
        ph.close()
        self.cur_src = self.out

    def load_ln(self, L, which):
        S = self.S
        S.dma("sp", lambda e: e.dma_start(out=self.lng[:], in_=self.ln_g[L, which, :].partition_broadcast(128)),
              writes=["lng"])
        S.dma("sp", lambda e: e.dma_start(out=self.lnb[:], in_=self.ln_b[L, which, :].partition_broadcast(128)),
              writes=["lnb"])

    def prefetch_xr(self, m):
        S = self.S
        b = self.ep_idx % 2
        src = self.cur_src
        S.dma("sp", lambda e: e.dma_start(out=self.xr[b][:], in_=src[m * 128:(m + 1) * 128, :]),
              reads=[("xres", m)], writes=[("xr", b)])

    def epilogue(self, m, py, py_res, pT):
        S = self.S
        prev_tr = getattr(self, "pending_tr", None)
        self.pending_tr = None
        b = self.ep_idx % 2
        self.ep_idx += 1
        xr, z, xb, st6, mv = self.xr[b], self.z[b], self.xb[b], self.st6[b], self.mv[b]
        rz, rmv = ("z", b), ("mv", b)
        S.op(S.dve, lambda e: e.scalar_tensor_tensor(out=z[:], in0=xr[:], scalar=float(ALPHA), in1=py,
                                                     op0=ALU.mult, op1=ALU.add),
             reads=[("xr", b), py_res], writes=[rz])
        if prev_tr is not None:
            self.transposes_to_actT(*prev_tr)
        for c in range(2):
            S.op(S.dve, lambda e, c=c: e.bn_stats(out=st6[:, c, :], in_=z[:, c * 512:(c + 1) * 512]),
                 reads=[rz], writes=[("st6", b, c)])
        S.op(S.dve, lambda e: e.bn_aggr(out=mv[:, 0:2], in_=st6[:].rearrange("p a b -> p (a b)")),
             reads=[("st6", b, 0), ("st6", b, 1)], writes=[rmv])
        S.op(S.dve, lambda e: e.tensor_scalar(out=mv[:, 2:3], in0=mv[:, 1:2], scalar1=float(LN_EPS), scalar2=None,
                                              op0=ALU.add), reads=[rmv], writes=[rmv])
        S.op(S.act, lambda e: e.activation(out=mv[:, 3:4], in_=mv[:, 2:3], func=AF.Sqrt), reads=[rmv], writes=[rmv])
        S.op(S.dve, lambda e: e.reciprocal(out=mv[:, 4:5], in_=mv[:, 3:4]), reads=[rmv], writes=[rmv])
        S.op(S.dve, lambda e: e.scalar_tensor_tensor(out=mv[:, 5:6], in0=mv[:, 0:1], scalar=-1.0, in1=mv[:, 4:5],
                                                     op0=ALU.mult, op1=ALU.mult), reads=[rmv], writes=[rmv])
        S.op(S.act, lambda e: e.activation(out=z[:], in_=z[:], func=AF.Identity, bias=mv[:, 5:6], scale=mv[:, 4:5]),
             reads=[rmv, rz], writes=[rz])
        S.op(S.pool, lambda e: e.tensor_tensor(out=z[:], in0=z[:], in1=self.lng[:], op=ALU.mult),
             reads=[rz, "lng"], writes=[rz])
        S.op(S.pool, lambda e: e.tensor_tensor(out=z[:], in0=z[:], in1=self.lnb[:], op=ALU.add),
             reads=[rz, "lnb"], writes=[rz])
        S.dma("pool", lambda e: e.dma_start(out=self.out[m * 128:(m + 1) * 128, :], in_=z[:]),
              reads=[rz], writes=[("xres", m)])
        S.op(S.act, lambda e: e.copy(out=xb[:], in_=z[:]), reads=[rz], writes=[("xb", b)])
        self.pending_tr = (m, xb, pT, ("xb", b))

    def flush_tr(self):
        if getattr(self, "pending_tr", None) is not None:
            self.transposes_to_actT(*self.pending_tr)
            self.pending_tr = None

    def ffn_phase(self, L):
        nc, S = self.nc, self.S
        actT = self.actT
        with ExitStack() as ph:
            sb = lambda n, shp, dt: ph.enter_context(nc.sbuf_tensor(self.nm(n), shp, dt))
            ps = lambda n, shp, dt: ph.enter_context(nc.psum_tensor(self.nm(n), shp, dt))
            self.alloc_epi(ph)
            w_out = sb("f_wout", [128, NCH, D], BF16)
            cw = sb("f_cw", [128, NCH, 4], F32)
            halo = sb("f_halo", [128, NCH, 2], F32)
            g = sb("f_g", [128, NCH, 512], BF16)
            wab = [sb(f"f_wab{i}", [128, 8, 256], BF16) for i in range(3)]
            asb = [sb(f"f_a{i}", [128, 514], F32) for i in range(3)]
            tt = [sb(f"f_t{i}", [128, 512], F32) for i in range(3)]
            pab = [ps(f"f_pab{i}", [128, 512], F32) for i in range(3)]
            py = [ps(f"f_py{i}", [128, D], F32) for i in range(2)]
            pT = ps("f_pT", [128, 8, 128], BF16)

            self.load_ln(L, 1)
            S.dma("sp", lambda e: e.dma_start(out=cw[:], in_=self.ffn_cw[L]), writes=["cw"])
            S.dma("pool", lambda e: e.dma_start(
                out=w_out[:, 0:21, :], in_=self.ffn_w_out[L, 0:21 * 128, :].rearrange("(c p) n -> p c n", p=128)),
                writes=["wout"])
            S.dma("pool", lambda e: e.dma_start(out=w_out[0:64, 21, :], in_=self.ffn_w_out[L, 21 * 128:D_FF, :]),
                  writes=["wout21"])
            S.op(S.pool, lambda e: e.memset(halo[:], 0.0), writes=[("halo", c) for c in range(NCH)])
            w_in = self.ffn_w_in[L].rearrange("(k p) n -> p k n", p=128)
            NJ = T // 512
            NIT = NJ * NCH

            def load_w(i):
                if i >= NIT:
                    return
                c = i % NCH
                wc = 128 if c < NCH - 1 else 64
                r3 = i % 3
                wb_ = wab[r3]
                S.dma("pool", lambda e: e.dma_start(out=wb_[:, :, 0:wc], in_=w_in[:, :, c * 128:c * 128 + wc]),
                      writes=[("wa", r3)])
                S.dma("pool", lambda e: e.dma_start(out=wb_[:, :, 128:128 + wc],
                                                    in_=w_in[:, :, D_FF + c * 128:D_FF + c * 128 + wc]),
                      writes=[("wb", r3)])

            load_w(0)
            load_w(1)
            for i in range(NIT):
                j, c = i // NCH, i % NCH
                tok = slice(j * 512, (j + 1) * 512)
                act_res = [("actT", 4 * j + q) for q in range(4)]
                wc = 128 if c < NCH - 1 else 64
                r3 = i % 3
                ia_, ib_ = (2 * i) % 3, (2 * i + 1) % 3
                pa_, pb_ = pab[ia_], pab[ib_]
                rpa, rpb = ("pab", ia_), ("pab", ib_)
                wb_, a_, t_ = wab[r3], asb[r3], tt[r3]
                load_w(i + 2)
                for k in range(8):
                    S.op(S.pe, lambda e, k=k: e.matmul(pa_[0:wc, :], lhsT=wb_[:, k, 0:wc], rhs=actT[:, k, tok],
                                                       start=(k == 0), stop=(k == 7)),
                         reads=[("wa", r3)] + act_res, writes=[rpa])
                for k in range(8):
                    S.op(S.pe, lambda e, k=k: e.matmul(pb_[0:wc, :], lhsT=wb_[:, k, 128:128 + wc],
                                                       rhs=actT[:, k, tok], start=(k == 0), stop=(k == 7)),
                         reads=[("wb", r3)] + act_res, writes=[rpb])
                if c == 1:
                    self.flush_tr()
                ra, rt = ("a", r3), ("t", r3)
                S.op(S.act, lambda e: e.copy(out=a_[0:wc, 0:2], in_=halo[0:wc, c, :]),
                     reads=[("halo", c)], writes=[ra])
                S.op(S.act, lambda e: e.copy(out=a_[0:wc, 2:514], in_=pa_[0:wc, :]),
                     reads=[rpa, ra], writes=[ra])
                S.op(S.act, lambda e: e.copy(out=halo[0:wc, c, :], in_=a_[0:wc, 512:514]),
                     reads=[ra], writes=[("halo", c)])
                S.op(S.dve, lambda e: e.tensor_scalar(out=t_[0:wc, :], in0=a_[0:wc, 2:514],
                                                      scalar1=cw[0:wc, c, 2:3], scalar2=cw[0:wc, c, 3:4],
                                                      op0=ALU.mult, op1=ALU.add),
                     reads=[ra, "cw"], writes=[rt])
                S.op(S.dve, lambda e: e.scalar_tensor_tensor(out=t_[0:wc, :], in0=a_[0:wc, 1:513],
                                                             scalar=cw[0:wc, c, 1:2], in1=t_[0:wc, :],
                                                             op0=ALU.mult, op1=ALU.add),
                     reads=[ra, rt], writes=[rt])
                S.op(S.dve, lambda e: e.scalar_tensor_tensor(out=t_[0:wc, :], in0=a_[0:wc, 0:512],
                                                             scalar=cw[0:wc, c, 0:1], in1=t_[0:wc, :],
                                                             op0=ALU.mult, op1=ALU.add),
                     reads=[ra, rt], writes=[rt])
                S.op(S.act, lambda e: e.activation(out=t_[0:wc, :], in_=t_[0:wc, :], func=AF.Silu),
                     reads=[rt], writes=[rt])
                S.op(S.dve, lambda e: e.tensor_tensor(out=g[0:wc, c, :], in0=t_[0:wc, :], in1=pb_[0:wc, :],
                                                      op=ALU.mult),
                     reads=[rt, rpb], writes=[("g", c)])
                if c < NCH - 1:
                    continue
                for mm in range(4):
                    m = 4 * j + mm
                    self.prefetch_xr(m)
                    p_ = py[m % 2]
                    for n in range(2):
                        for cc in range(NCH):
                            wcc = 128 if cc < NCH - 1 else 64
                            S.op(S.pe, lambda e, n=n, cc=cc, wcc=wcc: e.matmul(
                                p_[:, n * 512:(n + 1) * 512], lhsT=g[0:wcc, cc, mm * 128:(mm + 1) * 128],
                                rhs=w_out[0:wcc, cc, n * 512:(n + 1) * 512], start=(cc == 0), stop=(cc == NCH - 1)),
                                 reads=[("g", cc), "wout", "wout21"], writes=[("py", m % 2)])
                    self.epilogue(m, p_[:], ("py", m % 2), pT)
            self.flush_tr()
            S.barrier()
        self.cur_src = self.out


    def out_proj_phase(self, L, w_dram):
        nc, S = self.nc, self.S
        with ExitStack() as ph:
            sb = lambda n, shp, dt: ph.enter_context(nc.sbuf_tensor(self.nm(n), shp, dt))
            ps = lambda n, shp, dt: ph.enter_context(nc.psum_tensor(self.nm(n), shp, dt))
            self.alloc_epi(ph)
            wo = sb("o_w", [128, 8, D], BF16)
            py = [ps(f"o_py{i}", [128, D], F32) for i in range(2)]
            pT = ps("o_pT", [128, 8, 128], BF16)
            self.load_ln(L, 0)
            S.dma("pool", lambda e: e.dma_start(out=wo[:], in_=w_dram.rearrange("(k p) n -> p k n", p=128)),
                  writes=["wo"])
            for m in range(NB):
                self.prefetch_xr(m)
                p_ = py[m % 2]
                for n in range(2):
                    for k in range(8):
                        S.op(S.pe, lambda e, n=n, k=k: e.matmul(
                            p_[:, n * 512:(n + 1) * 512], lhsT=self.actT[:, k, m * 128:(m + 1) * 128],
                            rhs=wo[:, k, n * 512:(n + 1) * 512], start=(k == 0), stop=(k == 7)),
                             reads=[("actT", m), "wo"], writes=[("py", m % 2)])
                self.epilogue(m, p_[:], ("py", m % 2), pT)
            self.flush_tr()
            S.barrier()
        self.cur_src = self.out

    def mla_phase(self, L):
        nc, S = self.nc, self.S
        actT = self.actT
        ia = L // 3
        SCALE = 96.0 ** -0.5
        TWO_PI = 2.0 * np.pi
        with ExitStack() as ml:
            msb = lambda n, shp, dt: ml.enter_context(nc.sbuf_tensor(self.nm(n), shp, dt))
            cqn = msb("m_cqn", [128, 3, T], BF16)
            ckvn = msb("m_ckvn", [128, 2, T], BF16)
            KT = msb("m_KT", [96, T], BF16)
            cosT = msb("m_cos", [96, T], BF16)
            sinS = msb("m_sin", [96, T], BF16)
            rc = msb("m_rc", [96, 2], F32)
            with ExitStack() as ph:
                sb = lambda n, shp, dt: ph.enter_context(nc.sbuf_tensor(self.nm(n), shp, dt))
                HT = T // 2
                posi = sb("m_posi", [96, HT], I32)
                ang = sb("m_ang", [96, HT], F32)
                tmp = sb("m_tmp", [96, HT], F32)
                yi = sb("m_yi", [96, HT], I32)
                msk = sb("m_msk", [96, HT], F32)
                S.dma("sp", lambda e: e.dma_start(out=rc[:], in_=self.rope_c), writes=["rc"])

                def sin_turns():
                    S.op(S.dve, lambda e: e.tensor_copy(out=yi[:], in_=tmp[:]), reads=["tmp"], writes=["yi"])
                    S.op(S.dve, lambda e: e.tensor_copy(out=msk[:], in_=yi[:]), reads=["yi"], writes=["msk"])
                    S.op(S.dve, lambda e: e.tensor_tensor(out=tmp[:], in0=tmp[:], in1=msk[:], op=ALU.subtract),
                         reads=["tmp", "msk"], writes=["tmp"])
                    S.op(S.dve, lambda e: e.tensor_scalar(out=msk[:], in0=tmp[:], scalar1=0.5, scalar2=None,
                                                          op0=ALU.is_gt), reads=["tmp"], writes=["msk"])
                    S.op(S.dve, lambda e: e.tensor_tensor(out=tmp[:], in0=tmp[:], in1=msk[:], op=ALU.subtract),
                         reads=["tmp", "msk"], writes=["tmp"])
                    S.op(S.dve, lambda e: e.tensor_scalar(out=msk[:], in0=tmp[:], scalar1=-0.5, scalar2=None,
                                                          op0=ALU.is_lt), reads=["tmp"], writes=["msk"])
                    S.op(S.dve, lambda e: e.tensor_tensor(out=tmp[:], in0=tmp[:], in1=msk[:], op=ALU.add),
                         reads=["tmp", "msk"], writes=["tmp"])
                    S.op(S.act, lambda e: e.activation(out=tmp[:], in_=tmp[:], func=AF.Sin, scale=6.28318),
                         reads=["tmp"], writes=["tmp"])

                for hh in range(2):
                    cs = slice(hh * HT, (hh + 1) * HT)
                    S.dma("sp", lambda e: e.dma_start(out=posi[:], in_=self.pos[cs].partition_broadcast(96)),
                          writes=["posi"])
                    S.op(S.dve, lambda e: e.tensor_copy(out=ang[:], in_=posi[:]), reads=["posi"], writes=["ang"])
                    S.op(S.dve, lambda e: e.tensor_scalar(out=ang[:], in0=ang[:], scalar1=rc[:, 0:1], scalar2=None,
                                                          op0=ALU.mult), reads=["ang", "rc"], writes=["ang"])
                    S.op(S.dve, lambda e: e.tensor_copy(out=tmp[:], in_=ang[:]), reads=["ang"], writes=["tmp"])
                    sin_turns()
                    S.op(S.dve, lambda e: e.tensor_scalar(out=sinS[64:96, cs], in0=tmp[64:96, :],
                                                          scalar1=rc[64:96, 1:2], scalar2=None, op0=ALU.mult),
                         reads=["tmp", "rc"], writes=["sinS"])
                    S.op(S.dve, lambda e: e.tensor_scalar(out=tmp[:], in0=ang[:], scalar1=0.25, scalar2=None,
                                                          op0=ALU.add), reads=["ang", "sinS"], writes=["tmp"])
                    sin_turns()
                    S.op(S.dve, lambda e: e.tensor_copy(out=cosT[64:96, cs], in_=tmp[64:96, :]),
                         reads=["tmp"], writes=["cosT"])
                S.barrier()
            with ExitStack() as ph:
                sb = lambda n, shp, dt: ph.enter_context(nc.sbuf_tensor(self.nm(n), shp, dt))
                ps = lambda n, shp, dt: ph.enter_context(nc.psum_tensor(self.nm(n), shp, dt))
                wd = sb("m_wd", [128, 8, 768], BF16)
                gq = sb("m_gq", [128, 3], F32)
                gkv = sb("m_gkv", [128, 2], F32)
                raw = [sb(f"m_raw{i}", [128, 5, 512], F32) for i in range(2)]
                sq = [sb(f"m_sq{i}", [128, 5, 512], BF16) for i in range(2)]
                rs = [sb(f"m_rs{i}", [128, 2, 512], F32) for i in range(2)]
                t1 = [sb(f"m_t1{i}", [96, 512], F32) for i in range(2)]
                t2 = [sb(f"m_t2{i}", [96, 512], F32) for i in range(2)]
                p_lat = [ps(f"m_plat{i}", [128, 512], F32) for i in range(2)]
                p_ss = [ps(f"m_pss{i}", [128, 512], F32) for i in range(2)]
                p_kA = ps("m_pkA", [96, 512], F32)
                p_kB = ps("m_pkB", [96, 512], F32)
                S.dma("pool", lambda e: e.dma_start(out=wd[:], in_=self.mla_wd[ia].rearrange("(k p) n -> p k n", p=128)),
                      writes=["wd"])
                S.dma("sp", lambda e: e.dma_start(out=gq[:], in_=self.mla_qn[ia]), writes=["gq"])
                S.dma("sp", lambda e: e.dma_start(out=gkv[:], in_=self.mla_kvn[ia]), writes=["gkv"])
                it = 0
                for j in range(T // 512):
                    tok = slice(j * 512, (j + 1) * 512)
                    ares = [("actT", 4 * j + q) for q in range(4)]
                    b = j % 2
                    for c in range(5):
                        pl = p_lat[it % 2]
                        rpl = ("plat", it % 2)
                        it += 1
                        for k in range(8):
                            S.op(S.pe, lambda e, k=k: e.matmul(pl[:], lhsT=wd[:, k, c * 128:(c + 1) * 128],
                                                               rhs=actT[:, k, tok], start=(k == 0), stop=(k == 7)),
                                 reads=["wd"] + ares, writes=[rpl])
                        S.op(S.act, lambda e: e.copy(out=raw[b][:, c, :], in_=pl[:]), reads=[rpl], writes=[("raw", b, c)])
                        S.op(S.act, lambda e: e.activation(out=sq[b][:, c, :], in_=pl[:], func=AF.Square),
                             reads=[rpl], writes=[("sq", b, c)])
                    for which, (c0, c1, dim) in enumerate([(0, 3, 384.0), (3, 5, 256.0)]):
                        for c in range(c0, c1):
                            S.op(S.pe, lambda e, c=c: e.matmul(p_ss[which][:], lhsT=self.ones_bf[:], rhs=sq[b][:, c, :],
                                                               start=(c == c0), stop=(c == c1 - 1)),
                                 reads=[("sq", b, c), "ones_bf"], writes=[("pss", which)])
                        rr = ("rs", b, which)
                        S.op(S.dve, lambda e: e.tensor_scalar(out=rs[b][:, which, :], in0=p_ss[which][:],
                                                              scalar1=1.0 / dim, scalar2=float(RMS_EPS),
                                                              op0=ALU.mult, op1=ALU.add),
                             reads=[("pss", which)], writes=[rr])
                        S.op(S.act, lambda e: e.activation(out=rs[b][:, which, :], in_=rs[b][:, which, :], func=AF.Sqrt),
                             reads=[rr], writes=[rr])
                        S.op(S.dve, lambda e: e.reciprocal(out=rs[b][:, which, :], in_=rs[b][:, which, :]),
                             reads=[rr], writes=[rr])
                        for c in range(c0, c1):
                            dst = cqn[:, c, tok] if which == 0 else ckvn[:, c - 3, tok]
                            gsc = gq[:, c:c + 1] if which == 0 else gkv[:, c - 3:c - 2]
                            S.op(S.dve, lambda e: e.scalar_tensor_tensor(out=dst, in0=raw[b][:, c, :], scalar=gsc,
                                                                         in1=rs[b][:, which, :], op0=ALU.mult,
                                                                         op1=ALU.mult),
                                 reads=[("raw", b, c), rr, "gq", "gkv"], writes=[("cn", c, j)])
                    for k in range(8):
                        S.op(S.pe, lambda e, k=k: e.matmul(p_kA[:], lhsT=wd[:, k, 640:736], rhs=actT[:, k, tok],
                                                           start=(k == 0), stop=(k == 7)),
                             reads=["wd"] + ares, writes=["pkA"])
                    for k in range(8):
                        S.op(S.pe, lambda e, k=k: e.matmul(p_kB[:], lhsT=wd[:, k, 672:768], rhs=actT[:, k, tok],
                                                           start=(k == 0), stop=(k == 7)),
                             reads=["wd"] + ares, writes=["pkB"])
                    S.op(S.dve, lambda e: e.tensor_tensor(out=t1[b][64:96, :], in0=p_kA[64:96, :], in1=cosT[64:96, tok],
                                                          op=ALU.mult), reads=["pkA"], writes=[("t1", b)])
                    S.op(S.dve, lambda e: e.tensor_tensor(out=t2[b][64:96, :], in0=p_kB[64:96, :], in1=sinS[64:96, tok],
                                                          op=ALU.mult), reads=["pkB"], writes=[("t2", b)])
                    S.op(S.pool, lambda e: e.tensor_tensor(out=KT[64:96, tok], in0=t1[b][64:96, :], in1=t2[b][64:96, :],
                                                           op=ALU.add), reads=[("t1", b), ("t2", b)], writes=[("KTpe", j)])
                S.barrier()
            with ExitStack() as ph:
                sb = lambda n, shp, dt: ph.enter_context(nc.sbuf_tensor(self.nm(n), shp, dt))
                ps = lambda n, shp, dt: ph.enter_context(nc.psum_tensor(self.nm(n), shp, dt))
                wuq = sb("m_wuq", [128, 3, 2048], BF16)
                wukv = sb("m_wukv", [128, 2, 2048], BF16)
                Vx = [sb(f"m_Vx{i}", [128, 32, 128], BF16) for i in range(2)]
                QT = [sb(f"m_QT{i}", [96, 512], BF16) for i in range(2)]
                pt = [sb(f"m_pt{i}", [128, 512], BF16) for i in range(4)]
                t1 = [sb(f"m_u1{i}", [96, 512], F32) for i in range(2)]
                t2 = [sb(f"m_u2{i}", [96, 512], F32) for i in range(2)]
                rec = sb("m_rec", [128, 512], F32)
                bcs = sb("m_bcs", [128, 512], F32)
                p_p = [ps(f"m_pp{i}", [128, 512], F32) for i in range(2)]
                p_s = [ps(f"m_ps{i}", [128, 512], F32) for i in range(3)]
                p_o = [ps(f"m_po{i}", [128, 512], F32) for i in range(2)]
                p_bc = ps("m_pbc", [128, 512], F32)
                KTb = sb("m_KTb", [96, T], BF16)
                KTs = [KT, KTb]
                S.dma("pool", lambda e: e.dma_start(out=wuq[:], in_=self.mla_wuq[ia].rearrange("(k p) n -> p k n", p=128)),
                      writes=["wuq"])
                S.dma("pool", lambda e: e.dma_start(out=wukv[:], in_=self.mla_wukv[ia].rearrange("(k p) n -> p k n", p=128)),
                      writes=["wukv"])
                S.op(S.pool, lambda e: e.memset(Vx[0][:], 1.0), writes=[("Vx", 0)])
                S.op(S.pool, lambda e: e.memset(Vx[1][:], 1.0), writes=[("Vx", 1)])
                S.op(S.pool, lambda e: e.tensor_copy(out=KTb[64:96, :], in_=KT[64:96, :]), writes=["KTb_pe"])
                cnt = {"pp": 0}

                def kv_proj(h):
                    hl = h % 2
                    kt = KTs[hl]
                    vx = Vx[hl]
                    r0 = hl * 64
                    for j in range(T // 512):
                        tok = slice(j * 512, (j + 1) * 512)
                        pp, rpp = p_p[cnt["pp"] % 2], ("pp", cnt["pp"] % 2)
                        cnt["pp"] += 1
                        for k in range(2):
                            S.op(S.pe, lambda e, k=k: e.matmul(pp[0:64, :], lhsT=wukv[:, k, h * 64:(h + 1) * 64],
                                                               rhs=ckvn[:, k, tok], start=(k == 0), stop=(k == 1)),
                                 reads=["wukv"], writes=[rpp])
                        S.op(S.dve, lambda e: e.tensor_copy(out=kt[0:64, tok], in_=pp[0:64, :]), reads=[rpp],
                             writes=[("KT", hl, j)])
                    for j in range(4):
                        pp, rpp = p_p[cnt["pp"] % 2], ("pp", cnt["pp"] % 2)
                        cnt["pp"] += 1
                        ppv = pp[:].rearrange("p (b d) -> p b d", d=64)
                        for bb in range(8):
                            blk = j * 8 + bb
                            for k in range(2):
                                S.op(S.pe, lambda e, k=k: e.matmul(
                                    ppv[:, bb, :], lhsT=ckvn[:, k, blk * 128:(blk + 1) * 128],
                                    rhs=wukv[:, k, 1024 + h * 64:1024 + (h + 1) * 64], start=(k == 0), stop=(k == 1)),
                                     reads=["wukv"], writes=[rpp])
                        S.op(S.dve, lambda e: e.tensor_copy(out=vx[:, j * 8:(j + 1) * 8, r0:r0 + 64], in_=ppv),
                             reads=[rpp], writes=[("Vx", hl, j)])

                def prep_q(h, qt, iq):
                    tok = slice(qt * 512, (qt + 1) * 512)
                    qT, rq = QT[iq % 2], ("QT", iq % 2)
                    u1, u2 = t1[iq % 2], t2[iq % 2]
                    ru1, ru2 = ("u1", iq % 2), ("u2", iq % 2)
                    pA, pB = p_p[0], p_p[1]
                    for k in range(3):
                        S.op(S.pe, lambda e, k=k: e.matmul(pA[0:96, :], lhsT=wuq[:, k, h * 128:h * 128 + 96],
                                                           rhs=cqn[:, k, tok], start=(k == 0), stop=(k == 2)),
                             reads=["wuq"], writes=[("pp", 0)])
                    for k in range(3):
                        S.op(S.pe, lambda e, k=k: e.matmul(pB[0:96, :], lhsT=wuq[:, k, h * 128 + 32:h * 128 + 128],
                                                           rhs=cqn[:, k, tok], start=(k == 0), stop=(k == 2)),
                             reads=["wuq"], writes=[("pp", 1)])
                    S.op(S.dve, lambda e: e.tensor_copy(out=qT[0:64, :], in_=pA[0:64, :]), reads=[("pp", 0)], writes=[rq])
                    S.op(S.dve, lambda e: e.tensor_tensor(out=u1[64:96, :], in0=pA[64:96, :], in1=cosT[64:96, tok],
                                                          op=ALU.mult), reads=[("pp", 0)], writes=[ru1])
                    S.op(S.dve, lambda e: e.tensor_tensor(out=u2[64:96, :], in0=pB[64:96, :], in1=sinS[64:96, tok],
                                                          op=ALU.mult), reads=[("pp", 1)], writes=[ru2])
                    S.op(S.dve, lambda e: e.tensor_tensor(out=qT[64:96, :], in0=u1[64:96, :], in1=u2[64:96, :],
                                                          op=ALU.add), reads=[ru1, ru2], writes=[rq])

                def fin_q1(h, qt, iq):
                    hl = h % 2
                    d0 = 64 - hl * 64
                    po, rpo = p_o[iq % 2], ("po", iq % 2)
                    S.op(S.act, lambda e: e.copy(out=rec[d0:d0 + 1, :], in_=po[d0:d0 + 1, :]), reads=[rpo], writes=["rec"])

                def fin_q2(h, qt, iq):
                    hl, ch = h % 2, h // 2
                    r0, d0 = hl * 64, 64 - hl * 64
                    tok = slice(qt * 512, (qt + 1) * 512)
                    po, rpo = p_o[iq % 2], ("po", iq % 2)
                    S.op(S.pe, lambda e: e.matmul(p_bc[:], lhsT=self.ones_f[d0:d0 + 1, :], rhs=rec[d0:d0 + 1, :],
                                                  start=True, stop=True), reads=["rec", "ones_f"], writes=["pbc"])
                    S.op(S.dve, lambda e: e.reciprocal(out=bcs[r0:r0 + 64, :], in_=p_bc[r0:r0 + 64, :]),
                         reads=["pbc"], writes=["bcs"])
                    S.op(S.dve, lambda e: e.tensor_tensor(out=actT[r0:r0 + 64, ch, tok], in0=po[r0:r0 + 64, :],
                                                          in1=bcs[r0:r0 + 64, :], op=ALU.mult),
                         reads=[rpo, "bcs"], writes=[("actT", 4 * qt + q) for q in range(4)])

                items = []
                iq = 0
                for h in range(16):
                    for qt in range(T // 512):
                        nkb = 4 * qt + 4
                        for kb in range(nkb):
                            items.append(dict(h=h, qt=qt, kb=kb, nkb=nkb, iq=iq, pre=[], post=[]))
                        iq += 1
                first = {}
                for i, it in enumerate(items):
                    first.setdefault((it["h"], it["qt"]), i)
                for (h, qt), i in first.items():
                    lo = 0 if i == 0 else i - items[i - 1]["nkb"]
                    items[max(lo, i - 6)]["pre"].append(lambda h=h, qt=qt, iq=items[i]["iq"]: prep_q(h, qt, iq))
                    if qt == 0 and h > 0:
                        items[first[(h - 1, 5)]]["pre"].insert(0, lambda h=h: kv_proj(h))
                for i, it in enumerate(items):
                    if it["kb"] == it["nkb"] - 1:
                        it["post"].append(lambda it=it: fin_q1(it["h"], it["qt"], it["iq"]))
                        items[min(len(items) - 1, i + 3)]["post"].append(
                            lambda it=it: fin_q2(it["h"], it["qt"], it["iq"]))
                kv_proj(0)

                def stA(i, it):
                    h, qt, kb = it["h"], it["qt"], it["kb"]
                    hl = h % 2
                    n0 = max(0, kb - 4 * qt) * 128
                    qT, rq = QT[it["iq"] % 2], ("QT", it["iq"] % 2)
                    S.op(S.pe, lambda e: e.matmul(p_s[i % 3][:, n0:512], lhsT=KTs[hl][0:96, kb * 128:(kb + 1) * 128],
                                                  rhs=qT[0:96, n0:512], start=True, stop=True),
                         reads=[("KT", hl, kb // 4), rq, "KTb_pe"], writes=[("ps", i % 3)])

                def stB(i, it):
                    qt, kb = it["qt"], it["kb"]
                    n0 = max(0, kb - 4 * qt) * 128
                    ptb, rpt = pt[i % 4], ("pt", i % 4)
                    S.op(S.act, lambda e: e.activation(out=ptb[:, n0:512], in_=p_s[i % 3][:, n0:512], func=AF.Exp,
                                                       scale=float(SCALE)), reads=[("ps", i % 3)], writes=[rpt])
                    if kb >= 4 * qt:
                        S.op(S.dve, lambda e: e.tensor_tensor(out=ptb[:, n0:n0 + 128], in0=ptb[:, n0:n0 + 128],
                                                              in1=self.tri[:], op=ALU.mult),
                             reads=[rpt, "tri"], writes=[rpt])

                def stC(i, it):
                    h, qt, kb, nkb = it["h"], it["qt"], it["kb"], it["nkb"]
                    hl = h % 2
                    n0 = max(0, kb - 4 * qt) * 128
                    po, rpo = p_o[it["iq"] % 2], ("po", it["iq"] % 2)
                    S.op(S.pe, lambda e: e.matmul(po[:, n0:512], lhsT=Vx[hl][:, kb, :], rhs=pt[i % 4][:, n0:512],
                                                  start=(kb == 0), stop=(kb == nkb - 1)),
                         reads=[("Vx", hl, kb // 8), ("pt", i % 4)], writes=[rpo])

                n = len(items)
                for t_ in range(n + 3):
                    if t_ < n:
                        for f in items[t_]["pre"]:
                            f()
                        stA(t_, items[t_])
                    if 0 <= t_ - 1 < n:
                        stB(t_ - 1, items[t_ - 1])
                    if 0 <= t_ - 3 < n:
                        stC(t_ - 3, items[t_ - 3])
                        for f in items[t_ - 3]["post"]:
                            f()
                S.barrier()
        self.out_proj_phase(L, self.mla_wo[ia])


    def load_oT(self):
        S = self.S
        for c in range(8):
            S.dma("sp", lambda e, c=c: e.dma_start(out=self.actT[:, c, :], in_=self.oT_d[c]),
                  reads=[("oTd", c)], writes=[("actT", m) for m in range(NB)])

    def dil_phase(self, L):
        nc, S = self.nc, self.S
        actT = self.actT
        DIL = (1, 4, 16)
        with ExitStack() as ph:
            sb = lambda n, shp, dt: ph.enter_context(nc.sbuf_tensor(self.nm(n), shp, dt))
            ps = lambda n, shp, dt: ph.enter_context(nc.psum_tensor(self.nm(n), shp, dt))
            wd = sb("d_w", [128, 8, 9, 128], BF16)
            QK = [[sb(f"d_qk{g}{i}", [128, T], BF16) for i in range(2)] for g in range(3)]
            Vx = [sb(f"d_vx{g}", [128, 32, 192], BF16) for g in range(3)]
            osb = sb("d_osb", [128, T], BF16)
            mask2 = sb("d_mask2", [128, 256], BF16)
            pt = [sb(f"d_pt{i}", [128, 256], BF16) for i in range(4)]
            rec = sb("d_rec", [128, 512], F32)
            bcs = sb("d_bcs", [128, 512], F32)
            p_o = ps("d_po", [128, 2048], F32)
            p_s = [ps(f"d_ps{i}", [128, 512], F32) for i in range(2)]
            p_bc = ps("d_pbc", [128, 512], F32)
            p_p = ps("d_pp", [128, 512], F32)
            S.op(S.pool, lambda e: e.memset(mask2[:], 1.0), writes=["mask2"])
            S.op(S.pool, lambda e: e.affine_select(out=mask2[:, 0:128], in_=mask2[:, 0:128], pattern=[[-1, 128]],
                                                   compare_op=ALU.is_ge, fill=0.0, base=0, channel_multiplier=1),
                 reads=["mask2"], writes=["mask2"])
            S.op(S.pool, lambda e: e.affine_select(out=mask2[:, 128:256], in_=mask2[:, 128:256], pattern=[[1, 128]],
                                                   compare_op=ALU.is_ge, fill=0.0, base=0, channel_multiplier=-1),
                 reads=["mask2"], writes=["mask2"])
            for g in range(3):
                S.op(S.pool, lambda e, g=g: e.memset(Vx[g][:], 1.0), writes=[("Vx", g)])
            wsrc = self.dil_wqkv.rearrange("(k p) (c h d) -> p k c (h d)", p=128, c=9, h=16)
            pbank = [p_p, p_s[0], p_s[1]]
            cnt = {"pp": 0, "it": 0}

            def next_pp():
                i = cnt["pp"] % 3
                cnt["pp"] += 1
                return pbank[i], ("pb", i)

            def fin(hl, U, qq):
                r0, d0 = hl * 64, 64 - hl * 64
                rows = slice(r0, r0 + 64)
                cs = slice(qq * 512, (qq + 1) * 512)
                tok = slice(U * 2048 + qq * 512, U * 2048 + (qq + 1) * 512)
                S.op(S.act, lambda e: e.copy(out=rec[d0:d0 + 1, :], in_=p_o[d0:d0 + 1, cs]), reads=["po"], writes=["rec"])
                S.op(S.pe, lambda e: e.matmul(p_bc[:], lhsT=self.ones_f[d0:d0 + 1, :], rhs=rec[d0:d0 + 1, :],
                                              start=True, stop=True), reads=["rec", "ones_f"], writes=["pbc"])
                S.op(S.dve, lambda e: e.reciprocal(out=bcs[rows, :], in_=p_bc[rows, :]), reads=["pbc"], writes=["bcs"])
                S.op(S.dve, lambda e: e.tensor_tensor(out=osb[rows, tok], in0=p_o[rows, cs], in1=bcs[rows, :],
                                                      op=ALU.mult), reads=["po", "bcs"], writes=["osb"])

            for hp in range(8):
                for c9 in range(9):
                    S.dma("pool", lambda e: e.dma_start(out=wd[:, :, c9, :], in_=wsrc[:, :, c9, hp * 128:(hp + 1) * 128]),
                          writes=[("wd", c9)])
                for j in range(T // 512):
                    tok = slice(j * 512, (j + 1) * 512)
                    ares = [("actT", 4 * j + q) for q in range(4)]
                    for g in range(3):
                        for qk in range(2):
                            pp, rpp = next_pp()
                            for k in range(8):
                                S.op(S.pe, lambda e, k=k: e.matmul(pp[:], lhsT=wd[:, k, g * 3 + qk, :],
                                                                   rhs=actT[:, k, tok], start=(k == 0), stop=(k == 7)),
                                     reads=[("wd", g * 3 + qk)] + ares, writes=[rpp])
                            if (g * 2 + qk) % 2 == 0:
                                S.op(S.act, lambda e: e.copy(out=QK[g][qk][:, tok], in_=pp[:]),
                                     reads=[rpp], writes=[("QK", g, qk, j)])
                            else:
                                S.op(S.dve, lambda e: e.tensor_copy(out=QK[g][qk][:, tok], in_=pp[:]),
                                     reads=[rpp], writes=[("QK", g, qk, j)])
                for g in range(3):
                    d = DIL[g]
                    for b4 in range(8):
                        pp, rpp = next_pp()
                        ppv = pp[:].rearrange("p (b d) -> p b d", d=128)
                        for bb in range(4):
                            blk = b4 * 4 + bb
                            n, r = blk // d, blk % d
                            t0 = n * 128 * d + r
                            for k in range(8):
                                S.op(S.pe, lambda e, k=k: e.matmul(
                                    ppv[:, bb, :], lhsT=actT[:, k, t0:t0 + 127 * d + 1:d], rhs=wd[:, k, g * 3 + 2, :],
                                    start=(k == 0), stop=(k == 7)),
                                     reads=[("wd", g * 3 + 2)] + [("actT", m) for m in range(n * d, (n + 1) * d)], writes=[rpp])
                        S.op(S.act, lambda e: e.copy(out=Vx[g][:, b4 * 4:(b4 + 1) * 4, 0:64], in_=ppv[:, :, 0:64]),
                             reads=[rpp], writes=[("Vx", g)])
                        S.op(S.dve, lambda e: e.tensor_copy(out=Vx[g][:, b4 * 4:(b4 + 1) * 4, 128:192],
                                                            in_=ppv[:, :, 64:128]),
                             reads=[rpp], writes=[("Vx", g)])
                items = []
                for hl in range(2):
                    for U in range(2):
                        for g in range(3):
                            for qb in range(16):
                                items.append(dict(hl=hl, U=U, g=g, qb=qb, pre=[], preC=[], post=[]))
                        items[-48]["preC"].append(lambda: S.op(S.dve, lambda e: e.memset(p_o[:], 0.0), writes=["po"]))
                        for qq in range(4):
                            items[-1]["post"].append(lambda hl=hl, U=U, qq=qq: fin(hl, U, qq))

                def geom(it):
                    hl, U, g, qb = it["hl"], it["U"], it["g"], it["qb"]
                    d = DIL[g]
                    blk = U * 16 + qb
                    n, r = blk // d, blk % d
                    t0 = n * 128 * d + r
                    return hl, U, g, d, blk, n, t0

                def stA(i, it):
                    hl, U, g, d, blk, n, t0 = geom(it)
                    rows = slice(hl * 64, hl * 64 + 64)
                    Kt, Qt = QK[g][1], QK[g][0]
                    qsl = slice(t0, t0 + 127 * d + 1, d)
                    psb = p_s[i % 2]
                    half = 0
                    rps = ("ps", i % 2)
                    if n > 0:
                        tp = t0 - 128 * d
                        S.op(S.pe, lambda e: e.matmul(psb[:, half:half + 128], lhsT=Kt[rows, tp:tp + 127 * d + 1:d],
                                                      rhs=Qt[rows, qsl], start=True, stop=True),
                             reads=[("QKall",)], writes=[rps, ("pb", i % 2 + 1)])
                    S.op(S.pe, lambda e: e.matmul(psb[:, half + 128:half + 256], lhsT=Kt[rows, qsl],
                                                  rhs=Qt[rows, qsl], start=True, stop=True),
                         reads=[("QKall",)], writes=[rps, ("pb", i % 2 + 1)])

                def stB(i, it):
                    hl, U, g, d, blk, n, t0 = geom(it)
                    psb = p_s[i % 2]
                    half = 0
                    rps = ("ps", i % 2)
                    ptb, rpt = pt[i % 4], ("pt", i % 4)
                    c0 = 0 if n > 0 else 128
                    S.op(S.act, lambda e: e.activation(out=ptb[:, c0:256], in_=psb[:, half + c0:half + 256],
                                                       func=AF.Exp, scale=0.125), reads=[rps], writes=[rpt])
                    S.op(S.dve, lambda e: e.tensor_tensor(out=ptb[:, c0:256], in0=ptb[:, c0:256],
                                                          in1=mask2[:, c0:256], op=ALU.mult),
                         reads=[rpt, "mask2"], writes=[rpt])

                def stC(i, it):
                    hl, U, g, d, blk, n, t0 = geom(it)
                    vsl = slice(0, 128) if hl == 0 else slice(64, 192)
                    osl = slice(t0 - U * 2048, t0 - U * 2048 + 127 * d + 1, d)
                    ptb, rpt = pt[i % 4], ("pt", i % 4)
                    if n > 0:
                        S.op(S.pe, lambda e: e.matmul(p_o[:, osl], lhsT=Vx[g][:, blk - d, vsl], rhs=ptb[:, 0:128],
                                                      start=False, stop=False, skip_group_check=True),
                             reads=[("Vx", g), rpt, "po"], writes=["po"])
                    S.op(S.pe, lambda e: e.matmul(p_o[:, osl], lhsT=Vx[g][:, blk, vsl], rhs=ptb[:, 128:256],
                                                  start=False, stop=False, skip_group_check=True),
                         reads=[("Vx", g), rpt, "po"], writes=["po"])

                n_it = len(items)
                for t_ in range(n_it + 3):
                    if t_ < n_it:
                        for f in items[t_]["pre"]:
                            f()
                        stA(t_, items[t_])
                    if 0 <= t_ - 1 < n_it:
                        stB(t_ - 1, items[t_ - 1])
                    if 0 <= t_ - 3 < n_it:
                        for f in items[t_ - 3]["preC"]:
                            f()
                        stC(t_ - 3, items[t_ - 3])
                        for f in items[t_ - 3]["post"]:
                            f()
                S.dma("sp", lambda e: e.dma_start(out=self.oT_d[hp], in_=osb[:]), reads=["osb"], writes=[("oTd", hp)])
            S.barrier()
            self.load_oT()
            S.barrier()
        self.out_proj_phase(L, self.dil_wo)


    def rwkv_phase(self, L):
        nc, S = self.nc, self.S
        actT = self.actT
        RT = 256
        NT_ = T // RT
        GN_EPS = 64e-5
        DEC = -float(np.exp(-0.5))
        with ExitStack() as ph:
            sb = lambda n, shp, dt: ph.enter_context(nc.sbuf_tensor(self.nm(n), shp, dt))
            ps = lambda n, shp, dt: ph.enter_context(nc.psum_tensor(self.nm(n), shp, dt))
            mu = sb("r_mu", [128, 8, 6], F32)
            omu = sb("r_omu", [128, 8, 6], F32)
            vec = sb("r_vec", [128, 8, 8], F32)
            stage = sb("r_stage", [128, 8, 160], F32)
            l1a = sb("r_l1a", [128, 8, 288], BF16)
            l1b = sb("r_l1b", [128, 8, 288], BF16)
            hwa = sb("r_hwa", [128, T], BF16)
            hg = sb("r_hg", [128, 2, T], BF16)
            w2 = sb("r_w2", [128, D], BF16)
            g2 = sb("r_g2", [128, 2, D], BF16)
            wa = [sb(f"r_wa{i}", [128, 8, 128], BF16) for i in range(3)]
            wb = [sb(f"r_wb{i}", [128, 8, 128], BF16) for i in range(3)]
            f32t = lambda n: sb(n, [128, RT], F32)
            r_, k_, v_, lw, a_, g_, kk, km, be, Lc, eL, eLn, eLp, eD, tA, tB = [f32t(f"r_f{i}") for i in range(16)]
            LC = sb("r_LC", [128, 4], F32)
            eLC = sb("r_eLC", [128, 4], F32)
            rmask = sb("r_rmask", [128, RT], F32)
            blk1 = sb("r_blk1", [128, 128], F32)
            bdn = ["kap", "rt", "kt", "bt", "vf", "kb", "bb"]
            BD = {n: sb("r_bd_" + n, [128, 4, 128], F32) for n in bdn}
            MkvT, AkrT, AbrT, Y = [sb(f"r_m{i}", [128, 4, 128], F32) for i in range(4)]
            X = [sb(f"r_X{i}", [128, 4, 128], F32) for i in range(2)]
            XT = [sb(f"r_XT{i}", [128, 4, 128], F32) for i in range(2)]
            Vtok, Ktok, Btok = [sb(f"r_tk{i}", [128, 4, 128], F32) for i in range(3)]
            Wsb = sb("r_Wsb", [128, 128], F32)
            nU = sb("r_nU", [128, 128], F32)
            Abd = sb("r_Abd", [128, 128], F32)
            Osb4 = sb("r_Osb4", [128, 4, 128], F32)
            st64 = sb("r_st64", [128, 4, 6], F32)
            mv4 = sb("r_mv4", [128, 4, 8], F32)
            On = sb("r_On", [128, 4, 128], F32)
            ysb = sb("r_ysb", [128, RT], F32)
            osb = [sb(f"r_osb{i}", [128, RT], BF16) for i in range(2)]
            SU4, UI4, SL4, ID4 = [sb(f"r_msk{i}", [128, 4, 128], F32) for i in range(4)]
            identf = sb("r_identf", [128, 128], F32)
            st6 = sb("r_st6", [128, 6], F32)
            mv = sb("r_mv", [128, 8], F32)
            pP = [ps(f"r_pP{i}", [128, 512], F32) for i in range(2)]
            pX = [ps(f"r_pX{i}", [128, 512], F32) for i in range(5)]
            pS = ps("r_pS", [128, 512], F32)

            def aff(t, pattern, cm, op, base=0):
                S.op(S.pool, lambda e: e.affine_select(out=t, in_=t, pattern=pattern, compare_op=op, fill=0.0,
                                                       base=base, channel_multiplier=cm),
                     reads=["cst"], writes=["cst"])
            S.op(S.pool, lambda e: e.memset(identf[:], 1.0), writes=["cst"])
            aff(identf[:], [[1, 128]], -1, ALU.is_equal)
            for t4 in (SU4, UI4, SL4, ID4):
                S.op(S.pool, lambda e, t4=t4: e.memset(t4[:], 0.0), reads=["cst"], writes=["cst"])
            S.op(S.pool, lambda e: e.memset(blk1[:], 0.0), reads=["cst"], writes=["cst"])
            for hb in range(2):
                rs_ = slice(hb * 64, hb * 64 + 64)
                S.op(S.pool, lambda e: e.memset(blk1[rs_, rs_], 1.0), reads=["cst"], writes=["cst"])
                for c in range(4):
                    for t4 in (SU4, UI4, SL4, ID4):
                        S.op(S.pool, lambda e, t4=t4: e.memset(t4[rs_, c, rs_], 1.0), reads=["cst"], writes=["cst"])
                    aff(SU4[rs_, c, rs_], [[1, 64]], -1, ALU.is_gt)
                    aff(UI4[rs_, c, rs_], [[1, 64]], -1, ALU.is_ge)
                    aff(SL4[rs_, c, rs_], [[-1, 64]], 1, ALU.is_gt)
                    aff(ID4[rs_, c, rs_], [[1, 64]], -1, ALU.is_equal)
            S.op(S.pool, lambda e: e.memset(rmask[:], 1.0), reads=["cst"], writes=["cst"])
            for c in range(4):
                S.op(S.pool, lambda e, c=c: e.memset(rmask[:, c * 64:c * 64 + 1], 0.0), reads=["cst"], writes=["cst"])
            for n in bdn:
                S.op(S.pool, lambda e, n=n: e.memset(BD[n][:], 0.0), writes=[("bd", n)])
            S.op(S.pool, lambda e: e.memset(On[:], 0.0), writes=["On"])

            S.dma("sp", lambda e: e.dma_start(out=mu[:], in_=self.rw_mu), writes=["mu"])
            S.dma("sp", lambda e: e.dma_start(out=vec[:], in_=self.rw_vec), writes=["vec"])
            S.op(S.dve, lambda e: e.tensor_scalar(out=omu[:], in0=mu[:], scalar1=-1.0, scalar2=1.0, op0=ALU.mult,
                                                  op1=ALU.add), reads=["mu"], writes=["omu"])
            S.op(S.dve, lambda e: e.tensor_scalar(out=vec[:, :, 4:5], in0=vec[:, :, 3:4], scalar1=-1.0, scalar2=1.0,
                                                  op0=ALU.mult, op1=ALU.add), reads=["vec"], writes=["vec"])
            S.dma("pool", lambda e: e.dma_start(out=w2[0:64, :], in_=self.rw_w2), writes=["w2"])
            S.dma("pool", lambda e: e.dma_start(out=w2[64:128, :], in_=self.rw_a2), writes=["a2"])
            S.dma("pool", lambda e: e.dma_start(out=g2[:, 0, :], in_=self.rw_g2[0:128, :]), writes=["g2a"])
            S.dma("pool", lambda e: e.dma_start(out=g2[0:32, 1, :], in_=self.rw_g2[128:160, :]), writes=["g2b"])

            def scaled_weights(src3, ncols, dsts_a, dsts_b, jcols):
                S.dma("sp", lambda e: e.dma_start(out=stage[:, :, 0:ncols], in_=src3), writes=["stage"])
                for (c0, c1, j), da, db in zip(jcols, dsts_a, dsts_b):
                    for k in range(8):
                        S.op(S.pool, lambda e, k=k: e.tensor_scalar(out=da[:, k, :], in0=stage[:, k, c0:c1],
                                                                    scalar1=omu[:, k, j:j + 1], scalar2=None,
                                                                    op0=ALU.mult),
                             reads=["stage", "omu"], writes=["wsc"])
                        S.op(S.pool, lambda e, k=k: e.tensor_scalar(out=db[:, k, :], in0=stage[:, k, c0:c1],
                                                                    scalar1=mu[:, k, j:j + 1], scalar2=None,
                                                                    op0=ALU.mult),
                             reads=["stage", "mu"], writes=["wsc"])

            def proj(pout, la, lb, j, M=128, r0=0):
                t0 = j * RT
                ares = [("actT", m) for m in range(max(0, 2 * j - 1), 2 * j + 2)]
                for k in range(8):
                    S.op(S.pe, lambda e, k=k: e.matmul(pout[r0:r0 + M, 0:RT], lhsT=la(k), rhs=actT[:, k, t0:t0 + RT],
                                                       start=(k == 0), stop=False),
                         reads=["wsc"] + ares, writes=["pP"])
                c0 = 1 if j == 0 else 0
                for k in range(8):
                    S.op(S.pe, lambda e, k=k: e.matmul(pout[r0:r0 + M, c0:RT], lhsT=lb(k),
                                                       rhs=actT[:, k, t0 - 1 + c0:t0 + RT - 1],
                                                       start=False, stop=(k == 7)),
                         reads=["wsc"] + ares, writes=["pP"])

            l1src = self.rw_l1.rearrange("(k p) n -> p k n", p=128)
            scaled_weights(l1src[:, :, 0:128], 128, [l1a[:, :, 0:64], l1a[:, :, 64:128]],
                           [l1b[:, :, 0:64], l1b[:, :, 64:128]], [(0, 64, 3), (64, 128, 4)])
            scaled_weights(l1src[:, :, 128:288], 160, [l1a[:, :, 128:288]], [l1b[:, :, 128:288]], [(0, 160, 5)])
            for j in range(NT_):
                tok = slice(j * RT, (j + 1) * RT)
                p0 = pP[j % 2]
                proj(p0, lambda k: l1a[:, k, 0:128], lambda k: l1b[:, k, 0:128], j)
                S.op(S.act, lambda e: e.activation(out=hwa[0:64, tok], in_=p0[0:64, 0:RT], func=AF.Tanh),
                     reads=["pP"], writes=["hwa"])
                S.op(S.act, lambda e: e.copy(out=hwa[64:128, tok], in_=p0[64:128, 0:RT]), reads=["pP"], writes=["hwa"])
                proj(p0, lambda k: l1a[:, k, 128:256], lambda k: l1b[:, k, 128:256], j)
                S.op(S.act, lambda e: e.activation(out=hg[:, 0, tok], in_=p0[:, 0:RT], func=AF.Sigmoid),
                     reads=["pP"], writes=["hg"])
                proj(p0, lambda k: l1a[:, k, 256:288], lambda k: l1b[:, k, 256:288], j, M=32)
                S.op(S.act, lambda e: e.activation(out=hg[0:32, 1, tok], in_=p0[0:32, 0:RT], func=AF.Sigmoid),
                     reads=["pP"], writes=["hg"])

            wsrc = self.rw_wrkv.rearrange("j (k p) n -> j p k n", p=128)
            H0, H1 = slice(0, 64), slice(64, 128)

            def dv(fn, reads, writes, eng=None):
                S.op(eng or S.dve, fn, reads=reads, writes=writes)

            import os
            STOP = int(os.environ.get("RW_STOP", "9"))
            for hp in range(8 if STOP > 0 else 0):
                cols = slice(hp * 128, (hp + 1) * 128)
                for jj in range(3):
                    scaled_weights(wsrc[jj][:, :, cols], 128, [wa[jj]], [wb[jj]], [(0, 128, jj)])
                S.op(S.pool, lambda e: e.memset(Abd[:], 0.0), reads=["Abd"], writes=["Abd"])
                vcol = lambda i: vec[:, hp, i:i + 1]
                for j in range(NT_):
                    tok = slice(j * RT, (j + 1) * RT)
                    F = "F"
                    for jj, dst in enumerate((r_, k_, v_)):
                        p0 = pP[jj % 2]
                        proj(p0, lambda k: wa[jj][:, k, :], lambda k: wb[jj][:, k, :], j)
                        S.op(S.act, lambda e: e.copy(out=dst[:], in_=p0[:, 0:RT]), reads=["pP"], writes=[F])
                    p0 = pP[1]
                    S.op(S.pe, lambda e: e.matmul(p0[:, 0:RT], lhsT=w2[0:64, cols], rhs=hwa[0:64, tok], start=True,
                                                  stop=True), reads=["w2", "hwa"], writes=["pP"])
                    S.op(S.act, lambda e: e.activation(out=lw[:], in_=p0[:, 0:RT], func=AF.Sigmoid, bias=vcol(0)),
                         reads=["pP", "vec"], writes=[F])
                    S.op(S.pe, lambda e: e.matmul(p0[:, 0:RT], lhsT=w2[64:128, cols], rhs=hwa[64:128, tok], start=True,
                                                  stop=True), reads=["a2", "hwa"], writes=["pP"])
                    S.op(S.act, lambda e: e.activation(out=a_[:], in_=p0[:, 0:RT], func=AF.Sigmoid, bias=vcol(1)),
                         reads=["pP", "vec"], writes=[F])
                    S.op(S.pe, lambda e: e.matmul(p0[:, 0:RT], lhsT=g2[:, 0, cols], rhs=hg[:, 0, tok], start=True,
                                                  stop=False), reads=["g2a", "hg"], writes=["pP"])
                    S.op(S.pe, lambda e: e.matmul(p0[:, 0:RT], lhsT=g2[0:32, 1, cols], rhs=hg[0:32, 1, tok], start=False,
                                                  stop=True), reads=["g2b", "hg"], writes=["pP"])
                    S.op(S.act, lambda e: e.copy(out=g_[:], in_=p0[:, 0:RT]), reads=["pP"], writes=[F])
                    dv(lambda e: e.tensor_scalar(out=lw[:], in0=lw[:], scalar1=DEC, scalar2=None, op0=ALU.mult), [F], [F])
                    dv(lambda e: e.tensor_scalar(out=kk[:], in0=k_[:], scalar1=vcol(2), scalar2=None, op0=ALU.mult),
                       [F, "vec"], [F])
                    dv(lambda e: e.tensor_tensor(out=tA[:], in0=kk[:], in1=kk[:], op=ALU.mult), [F], [F], S.pool)
                    S.op(S.pe, lambda e: e.matmul(pP[0][:, 0:RT], lhsT=blk1[:], rhs=tA[:], start=True, stop=True),
                         reads=[F, "cst"], writes=["pP"])
                    S.op(S.act, lambda e: e.activation(out=tA[:], in_=pP[0][:, 0:RT], func=AF.Sqrt), reads=["pP"], writes=[F])
                    dv(lambda e: e.tensor_scalar(out=tA[:], in0=tA[:], scalar1=1e-12, scalar2=None, op0=ALU.max), [F], [F])
                    dv(lambda e: e.reciprocal(out=tA[:], in_=tA[:]), [F], [F])
                    dv(lambda e: e.tensor_tensor(out=kk[:], in0=kk[:], in1=tA[:], op=ALU.mult), [F], [F])
                    dv(lambda e: e.tensor_scalar(out=tB[:], in0=a_[:], scalar1=vcol(3), scalar2=vcol(4), op0=ALU.mult,
                                                 op1=ALU.add), [F, "vec"], [F])
                    dv(lambda e: e.tensor_tensor(out=km[:], in0=k_[:], in1=tB[:], op=ALU.mult), [F], [F])
                    dv(lambda e: e.tensor_tensor(out=be[:], in0=kk[:], in1=a_[:], op=ALU.mult), [F], [F], S.pool)
                    dv(lambda e: e.scalar_tensor_tensor(out=tB[:], in0=r_[:], scalar=vcol(5), in1=km[:], op0=ALU.mult,
                                                        op1=ALU.mult), [F, "vec"], [F])
                    S.op(S.pe, lambda e: e.matmul(pP[0][:, 0:RT], lhsT=blk1[:], rhs=tB[:], start=True, stop=True),
                         reads=[F, "cst"], writes=["pP"])
                    dv(lambda e: e.tensor_tensor(out=tA[:], in0=pP[0][:, 0:RT], in1=v_[:], op=ALU.mult), ["pP", F], [F])
                    dv(lambda e: e.tensor_tensor_scan(out=Lc[:], data0=rmask[:], data1=lw[:], initial=0.0,
                                                      op0=ALU.mult, op1=ALU.add), [F, "cst"], [F])
                    S.op(S.act, lambda e: e.activation(out=eL[:], in_=Lc[:], func=AF.Exp), reads=[F], writes=[F])
                    S.op(S.act, lambda e: e.activation(out=eLn[:], in_=Lc[:], func=AF.Exp, scale=-1.0), reads=[F], writes=[F])
                    dv(lambda e: e.tensor_tensor(out=eLp[:], in0=Lc[:], in1=lw[:], op=ALU.subtract), [F], [F], S.pool)
                    S.op(S.act, lambda e: e.activation(out=eLp[:], in_=eLp[:], func=AF.Exp), reads=[F], writes=[F])
                    L3 = Lc[:].rearrange("p (c t) -> p c t", t=64)
                    dv(lambda e: e.tensor_copy(out=LC[:], in_=L3[:, :, 63]), [F], [F])
                    S.op(S.act, lambda e: e.activation(out=eLC[:], in_=LC[:], func=AF.Exp), reads=[F], writes=[F])
                    for c in range(4):
                        S.op(S.act, lambda e, c=c: e.activation(out=eD[:, c * 64:(c + 1) * 64], in_=Lc[:, c * 64:(c + 1) * 64],
                                                                func=AF.Exp, scale=-1.0, bias=LC[:, c:c + 1]),
                             reads=[F], writes=[F])
                    prods = [("kap", kk, eLp), ("rt", r_, eL), ("kt", km, eLn), ("bt", be, eLn), ("kb", km, eD),
                             ("bb", be, eD)]
                    ie = 0
                    for n, x0, x1 in prods:
                        for hs in (H0, H1):
                            eng = S.dve if ie % 2 == 0 else S.pool
                            ie += 1
                            dv(lambda e: e.tensor_tensor(out=BD[n][hs, :, hs],
                                                         in0=x0[hs, :].rearrange("p (c t) -> p c t", t=64),
                                                         in1=x1[hs, :].rearrange("p (c t) -> p c t", t=64), op=ALU.mult),
                               [F], [("bd", n)], eng)
                    for hs in (H0, H1):
                        S.op(S.act, lambda e: e.copy(out=BD["vf"][hs, :, hs], in_=v_[hs, :].rearrange("p (c t) -> p c t", t=64)),
                             reads=[F], writes=[("bd", "vf")])
                    if STOP < 2:
                        continue
                    for c in range(4):
                        cs = slice(c * 128, (c + 1) * 128)
                        gm = [(0, "kt", "kap"), (1, "kt", "rt"), (2, "bt", "kap"), (3, "bt", "rt"), (4, "kap", "bt")]
                        for pi, ln_, rn_ in gm:
                            S.op(S.pe, lambda e: e.matmul(pX[pi][:, cs], lhsT=BD[ln_][:, c, :], rhs=BD[rn_][:, c, :],
                                                          start=True, stop=True),
                                 reads=[("bd", ln_), ("bd", rn_)], writes=[("pX", pi)])
                    f4 = lambda t: t[:].rearrange("p c t -> p (c t)")
                    dv(lambda e: e.tensor_tensor(out=f4(MkvT), in0=pX[0][:], in1=f4(SU4), op=ALU.mult), [("pX", 0), "cst"], ["MkvT"])
                    dv(lambda e: e.tensor_tensor(out=f4(AkrT), in0=pX[1][:], in1=f4(UI4), op=ALU.mult), [("pX", 1), "cst"], ["AkrT"])
                    dv(lambda e: e.tensor_tensor(out=f4(X[0]), in0=pX[2][:], in1=f4(SU4), op=ALU.mult), [("pX", 2), "cst"], [("X", 0)])
                    dv(lambda e: e.tensor_tensor(out=f4(AbrT), in0=pX[3][:], in1=f4(UI4), op=ALU.mult), [("pX", 3), "cst"], ["AbrT"])
                    dv(lambda e: e.tensor_tensor(out=f4(XT[0]), in0=pX[4][:], in1=f4(SL4), op=ALU.mult), [("pX", 4), "cst"], [("XT", 0)])
                    if STOP < 3:
                        continue
                    dv(lambda e: e.tensor_tensor(out=f4(Y), in0=f4(ID4), in1=f4(X[0]), op=ALU.subtract),
                       [("X", 0), "cst"], ["Y"], S.pool)
                    cur = 0
                    for lvl in range(int(os.environ.get('RW_LVL', '5'))):
                        nxt = 1 - cur
                        last = (lvl == 4)
                        for c in range(4):
                            cs = slice(c * 128, (c + 1) * 128)
                            if not last:
                                S.op(S.pe, lambda e: e.matmul(pX[0][:, cs], lhsT=XT[cur][:, c, :], rhs=X[cur][:, c, :],
                                                              start=True, stop=True),
                                     reads=[("X", cur), ("XT", cur)], writes=[("pX", 0)])
                            S.op(S.pe, lambda e: e.matmul(pX[2][:, cs], lhsT=X[cur][:, c, :], rhs=XT[cur][:, c, :],
                                                          start=True, stop=True),
                                 reads=[("X", cur), ("XT", cur)], writes=[("pX", 2)])
                        if not last:
                            dv(lambda e: e.tensor_copy(out=f4(X[nxt]), in_=pX[0][:]), [("pX", 0)], [("X", nxt)])
                        S.op(S.act, lambda e: e.copy(out=f4(XT[nxt]), in_=pX[2][:]), reads=[("pX", 2)], writes=[("XT", nxt)])
                        for c in range(4):
                            cs = slice(c * 128, (c + 1) * 128)
                            S.op(S.pe, lambda e: e.matmul(pX[1][:, cs], lhsT=XT[nxt][:, c, :],
                                                          rhs=Y[:, c, :], start=True, stop=True),
                                 reads=[("XT", nxt), "Y"], writes=[("pX", 1)])
                        dv(lambda e: e.tensor_tensor(out=f4(Y), in0=f4(Y), in1=pX[1][:], op=ALU.add),
                           [("pX", 1), "Y"], ["Y"])
                        cur = nxt
                    for pi, n, dst in ((3, "vf", Vtok), (4, "kb", Ktok), (1, "bb", Btok)):
                        for c in range(4):
                            S.op(S.pe, lambda e: e.transpose(out=pX[pi][:, c * 128:(c + 1) * 128], in_=BD[n][:, c, :],
                                                             identity=identf[:]),
                                 reads=[("bd", n), "cst"], writes=[("pX", pi)])
                        S.op(S.act, lambda e: e.copy(out=f4(dst), in_=pX[pi][:]), reads=[("pX", pi)], writes=[("tok", n)])
                    if STOP < 4:
                        continue
                    for c in range(4):
                        S.op(S.pe, lambda e: e.matmul(pX[0][:, 0:128], lhsT=BD["kap"][:, c, :], rhs=Abd[:], start=True, stop=False),
                             reads=[("bd", "kap"), "Abd"], writes=["pW"])
                        S.op(S.pe, lambda e: e.matmul(pX[0][:, 0:128], lhsT=MkvT[:, c, :], rhs=Vtok[:, c, :], start=False, stop=True),
                             reads=["MkvT", ("tok", "vf")], writes=["pW"])
                        S.op(S.act, lambda e: e.copy(out=Wsb[:], in_=pX[0][:, 0:128]), reads=["pW"], writes=["Wsb"])
                        S.op(S.pe, lambda e: e.matmul(pX[1][:, 0:128], lhsT=Y[:, c, :], rhs=Wsb[:], start=True, stop=True),
                             reads=["Y", "Wsb"], writes=["pU"])
                        dv(lambda e: e.tensor_scalar(out=nU[:], in0=pX[1][:, 0:128], scalar1=-1.0, scalar2=None, op0=ALU.mult),
                           ["pU"], ["nU"])
                        S.op(S.pe, lambda e: e.matmul(pX[3][:, 0:128], lhsT=Ktok[:, c, :], rhs=Vtok[:, c, :], start=True, stop=False),
                             reads=[("tok", "kb"), ("tok", "vf")], writes=["pA"])
                        S.op(S.pe, lambda e: e.matmul(pX[3][:, 0:128], lhsT=Btok[:, c, :], rhs=nU[:], start=False, stop=True),
                             reads=[("tok", "bb"), "nU"], writes=["pA"])
                        oc = pX[2][:, c * 128:(c + 1) * 128]
                        S.op(S.pe, lambda e: e.matmul(oc, lhsT=BD["rt"][:, c, :], rhs=Abd[:], start=True, stop=False),
                             reads=[("bd", "rt"), "Abd"], writes=["pO"])
                        S.op(S.pe, lambda e: e.matmul(oc, lhsT=AkrT[:, c, :], rhs=Vtok[:, c, :], start=False, stop=False),
                             reads=["AkrT", ("tok", "vf")], writes=["pO"])
                        S.op(S.pe, lambda e: e.matmul(oc, lhsT=AbrT[:, c, :], rhs=nU[:], start=False, stop=True),
                             reads=["AbrT", "nU"], writes=["pO"])
                        dv(lambda e: e.scalar_tensor_tensor(out=Abd[:], in0=Abd[:], scalar=eLC[:, c:c + 1], in1=pX[3][:, 0:128],
                                                            op0=ALU.mult, op1=ALU.add), ["pA", "Abd", F], ["Abd"])
                    S.op(S.act, lambda e: e.copy(out=f4(Osb4), in_=pX[2][:]), reads=["pO"], writes=["Osb"])
                    for c in range(4):
                        for hs in (H0, H1):
                            dv(lambda e: e.bn_stats(out=st64[hs, c, :], in_=Osb4[hs, c, hs]), ["Osb"], ["st6"])
                        dv(lambda e: e.bn_aggr(out=mv4[:, c, 0:2], in_=st64[:, c, :]), ["st6"], ["mv"])
                    dv(lambda e: e.tensor_scalar(out=mv4[:, :, 2:3], in0=mv4[:, :, 1:2], scalar1=GN_EPS, scalar2=None, op0=ALU.add),
                       ["mv"], ["mv"])
                    S.op(S.act, lambda e: e.activation(out=mv4[:, :, 3:4], in_=mv4[:, :, 2:3], func=AF.Sqrt), reads=["mv"], writes=["mv"])
                    dv(lambda e: e.reciprocal(out=mv4[:, :, 4:5], in_=mv4[:, :, 3:4]), ["mv"], ["mv"])
                    dv(lambda e: e.scalar_tensor_tensor(out=mv4[:, :, 5:6], in0=mv4[:, :, 0:1], scalar=-1.0, in1=mv4[:, :, 4:5],
                                                        op0=ALU.mult, op1=ALU.mult), ["mv"], ["mv"])
                    for c in range(4):
                        for hs in (H0, H1):
                            S.op(S.act, lambda e: e.activation(out=On[hs, c, hs], in_=Osb4[hs, c, hs], func=AF.Identity,
                                                               bias=mv4[hs, c, 5:6], scale=mv4[hs, c, 4:5]),
                                 reads=["mv", "Osb"], writes=["On"])
                    for c in range(4):
                        S.op(S.pe, lambda e: e.transpose(out=pP[1][:, c * 128:(c + 1) * 128], in_=On[:, c, :], identity=identf[:]),
                             reads=["On", "cst"], writes=["pP"])
                    for hs in (H0, H1):
                        dv(lambda e: e.tensor_scalar(out=ysb[hs, :].rearrange("p (c t) -> p c t", t=64),
                                                     in0=pP[1][:].rearrange("p (c t) -> p c t", t=128)[hs, :, hs],
                                                     scalar1=vec[hs, hp, 6:7], scalar2=vec[hs, hp, 7:8], op0=ALU.mult,
                                                     op1=ALU.add), ["pP", "vec"], ["ysb"])
                    dv(lambda e: e.tensor_tensor(out=ysb[:], in0=ysb[:], in1=tA[:], op=ALU.add), ["ysb", F], ["ysb"], S.pool)
                    ob = osb[j % 2]
                    dv(lambda e: e.tensor_tensor(out=ob[:], in0=ysb[:], in1=g_[:], op=ALU.mult), ["ysb", F], [("osb", j % 2)], S.pool)
                    S.dma("sp", lambda e: e.dma_start(out=self.oT_d[hp][:, tok], in_=ob[:]), reads=[("osb", j % 2)],
                          writes=[("oTd", hp)])
            S.barrier()
            self.load_oT()
            S.barrier()
        self.out_proj_phase(L, self.rw_wo)


def host_layout(inp):
    out = {}
    cw = np.zeros((DEPTH, NCH * 128, 4), np.float32)
    cw[:, :D_FF, 0:3] = np.transpose(inp["ffn_conv_w"], (0, 2, 1))
    cw[:, :D_FF, 3] = inp["ffn_conv_b"]
    out["ffn_cw"] = np.ascontiguousarray(cw.reshape(DEPTH, NCH, 128, 4).transpose(0, 2, 1, 3))
    for k in ("ln_g", "ln_b", "ffn_w_in", "ffn_w_out"):
        out[k] = np.ascontiguousarray(inp[k], dtype=np.float32)
    for k in ("dil_w_qkv", "dil_w_o"):
        out[k] = np.ascontiguousarray(inp[k][0], dtype=np.float32)
    fm = lambda v: np.ascontiguousarray(np.asarray(v, np.float32).reshape(8, 128).T)
    out["rw_mu"] = np.ascontiguousarray(inp["rwkv_mu"][0].reshape(6, 8, 128).transpose(2, 1, 0))
    ka = inp["rwkv_k_a"][0]
    vecs = [inp["rwkv_w0"][0], inp["rwkv_a0"][0], inp["rwkv_k_k"][0], ka, None, inp["rwkv_r_k"][0].reshape(-1),
            inp["rwkv_ln_w"][0], inp["rwkv_ln_b"][0]]
    rv = np.zeros((128, 8, 8), np.float32)
    for i, v in enumerate(vecs):
        if v is not None:
            rv[:, :, i] = fm(v)
    out["rw_vec"] = rv
    out["rwkv_w_rkv"] = np.ascontiguousarray(inp["rwkv_w_rkv"][0], dtype=np.float32)
    out["rw_l1"] = np.ascontiguousarray(np.concatenate([inp["rwkv_w1"][0], inp["rwkv_a1"][0], inp["rwkv_g1"][0]], axis=1))
    for k in ("rwkv_w2", "rwkv_a2", "rwkv_g2", "rwkv_w_o"):
        out[k] = np.ascontiguousarray(inp[k][0], dtype=np.float32)
    wd = inp["mla_w_down"]
    out["mla_wd"] = np.ascontiguousarray(np.concatenate(
        [wd[:, :, 0:640], wd[:, :, 0:64], wd[:, :, 640:672], wd[:, :, 656:672], wd[:, :, 640:656]], axis=2))
    out["mla_qn"] = np.ascontiguousarray(inp["mla_q_norm"].reshape(-1, 3, 128).transpose(0, 2, 1))
    out["mla_kvn"] = np.ascontiguousarray(inp["mla_kv_norm"].reshape(-1, 2, 128).transpose(0, 2, 1))
    wq = inp["mla_w_uq"].reshape(-1, 384, 16, 96)
    out["mla_wuq"] = np.ascontiguousarray(np.concatenate(
        [wq[..., 0:96], wq[..., 80:96], wq[..., 64:80]], axis=3).reshape(-1, 384, 2048))
    wkv = inp["mla_w_ukv"].reshape(-1, 256, 16, 128)
    out["mla_wukv"] = np.ascontiguousarray(np.concatenate(
        [wkv[..., 0:64].reshape(-1, 256, 1024), wkv[..., 64:128].reshape(-1, 256, 1024)], axis=2))
    out["mla_wo"] = np.ascontiguousarray(inp["mla_w_o"], dtype=np.float32)
    rc = np.zeros((96, 2), np.float32)
    invf = (10000.0 ** (-np.arange(0, 32, 2, dtype=np.float32) / np.float32(32))).astype(np.float32)
    rc[64:80, 0] = invf / np.float32(2 * np.pi)
    rc[80:96, 0] = invf / np.float32(2 * np.pi)
    rc[64:80, 1] = -1.0
    rc[80:96, 1] = 1.0
    out["rope_c"] = rc
    return out


DEFAULT_PLAN = [("mla", 0), ("ffn", 0), ("dil", 1), ("ffn", 1), ("rwkv", 2), ("ffn", 2), ("mla", 3), ("ffn", 3)]
_CACHE = {}


def run(inputs, plan, n_cores=8, trace=False):
    key = tuple(plan)
    if key not in _CACHE:
        b = Builder(plan)
        nc = b.build()
        _CACHE[key] = (b, nc)
    b, nc = _CACHE[key]
    shared = host_layout(inputs)
    in_maps = []
    for c in range(n_cores):
        d = {"x": np.ascontiguousarray(inputs["x"][c], dtype=np.float32),
             "positions": np.ascontiguousarray(inputs["positions"][c], dtype=np.int32)}
        d.update(shared)
        d = {k: v for k, v in d.items() if k in b.din}
        in_maps.append(d)
    res = run_bass_kernel_spmd(nc, in_maps, core_ids=list(range(n_cores)), trace=trace)
    return np.stack([r["out"] for r in res.results], axis=0), res


def kernel(**inputs):
    out, _ = run(inputs, DEFAULT_PLAN)
    return out.astype(np.float32)
```

```python
import numpy as np
from contextlib import ExitStack
import concourse.bass as bass
import concourse.mybir as mybir
from concourse.bass_utils import run_bass_kernel_spmd

F32 = mybir.dt.float32
BF16 = mybir.dt.bfloat16
I32 = mybir.dt.int32
AF = mybir.ActivationFunctionType
ALU = mybir.AluOpType

T = 4096
D = 1024
DEPTH = 4
NB = T // 128
ALPHA = (2 * DEPTH) ** 0.25
LN_EPS = 1e-5
RMS_EPS = 1e-6
D_FF = 2752
NCH = 22


class _Eng:
    def __init__(self, name, eng, sem):
        self.name = name
        self.eng = eng
        self.sem = sem
        self.count = 0
        self.waited = {}


class Sched:
    def __init__(self, nc, stack, n_dma_sems=16):
        self.nc = nc
        mk = lambda n: stack.enter_context(nc.semaphore(n))
        self.pe = _Eng("pe", nc.tensor, mk("s_pe"))
        self.act = _Eng("act", nc.scalar, mk("s_act"))
        self.dve = _Eng("dve", nc.vector, mk("s_dve"))
        self.pool = _Eng("pool", nc.gpsimd, mk("s_pool"))
        self.sp = _Eng("sp", nc.sync, None)
        self.q = {"sp": [mk(f"s_dsp{i}") for i in range(n_dma_sems)],
                  "pool": [mk(f"s_dpl{i}") for i in range(n_dma_sems)]}
        self.qeng = {"sp": self.sp, "pool": self.pool}
        self.dma_cnt = {"sp": 0, "pool": 0}
        self.dma_last = {}
        self.last_write = {}
        self.readers = {}
        self.n_ops = 0
        self.n_waits = 0

    def _wait(self, E, tok):
        sem, val, src = tok
        if src == "pe" and E.name == "pe":
            return
        k = id(sem)
        if E.waited.get(k, 0) >= val:
            return
        E.eng.wait_ge(sem, val)
        E.waited[k] = val
        self.n_waits += 1

    def _deps(self, E, reads, writes):
        for r in reads:
            t = self.last_write.get(r)
            if t is not None:
                self._wait(E, t)
        for w in writes:
            t = self.last_write.get(w)
            if t is not None:
                self._wait(E, t)
            for t in self.readers.get(w, ()):
                self._wait(E, t)

    def _commit(self, tok, reads, writes):
        for r in reads:
            self.readers.setdefault(r, []).append(tok)
        for w in writes:
            self.last_write[w] = tok
            self.readers[w] = []

    def op(self, E, fn, reads=(), writes=()):
        self._deps(E, reads, writes)
        ins = fn(E.eng)
        E.count += 1
        ins.then_inc(E.sem, 1)
        tok = (E.sem, E.count, E.name)
        self._commit(tok, reads, writes)
        self.n_ops += 1
        return tok

    def dma(self, qname, fn, reads=(), writes=()):
        E = self.qeng[qname]
        pool = self.q[qname]
        i = self.dma_cnt[qname]
        self.dma_cnt[qname] = i + 1
        slot = i % len(pool)
        prev = self.dma_last.get((qname, slot))
        if prev is not None:
            self._wait(E, prev)
        self._deps(E, reads, writes)
        ins = fn(E.eng)
        ins.then_inc(pool[slot], 16)
        tok = (pool[slot], 16 * (i // len(pool) + 1), "dma_" + qname)
        self.dma_last[(qname, slot)] = tok
        self._commit(tok, reads, writes)
        self.n_ops += 1
        return tok

    def barrier(self):
        toks = [(E.sem, E.count, E.name) for E in (self.pe, self.act, self.dve, self.pool) if E.count]
        toks += list(self.dma_last.values())
        for E in (self.pe, self.act, self.dve, self.pool, self.sp):
            for t in toks:
                if t[2] == E.name:
                    continue
                self._wait(E, t)
        self.last_write = {}
        self.readers = {}


class Builder:
    def __init__(self, plan, debug_out=False):
        self.plan = plan
        nc = bass.Bass("TRN2", target_bir_lowering=False)
        self.nc = nc
        self.din = {}

    def nm(self, n):
        self._uid = getattr(self, "_uid", 0) + 1
        return f"{n}_{self._uid}"

    def dram_in(self, name, shape, dt=F32):
        t = self.nc.dram_tensor(name, list(shape), dt, kind="ExternalInput").ap()
        self.din[name] = t
        return t

    def build(self):
        nc = self.nc
        plan = self.plan
        self.x_in = self.dram_in("x", [T, D])
        self.out = nc.dram_tensor("out", [T, D], F32, kind="ExternalOutput").ap()
        self.ln_g = self.dram_in("ln_g", [DEPTH, 2, D])
        self.ln_b = self.dram_in("ln_b", [DEPTH, 2, D])
        self.ffn_w_in = self.dram_in("ffn_w_in", [DEPTH, D, 2 * D_FF])
        self.ffn_w_out = self.dram_in("ffn_w_out", [DEPTH, D_FF, D])
        self.ffn_cw = self.dram_in("ffn_cw", [DEPTH, 128, NCH, 4])
        kinds = {k for k, _ in plan}
        if "dil" in kinds or "rwkv" in kinds:
            self.oT_d = nc.dram_tensor("oT_d", [8, 128, T], BF16, kind="Internal").ap()
            self.xT_d = nc.dram_tensor("xT_d", [8, 128, T], BF16, kind="Internal").ap()
        if "dil" in kinds:
            self.dil_wqkv = self.dram_in("dil_w_qkv", [D, 9216])
            self.dil_wo = self.dram_in("dil_w_o", [D, D])
        if "rwkv" in kinds:
            self.rw_mu = self.dram_in("rw_mu", [128, 8, 6])
            self.rw_wrkv = self.dram_in("rwkv_w_rkv", [3, D, D])
            self.rw_l1 = self.dram_in("rw_l1", [D, 288])
            self.rw_w2 = self.dram_in("rwkv_w2", [64, D])
            self.rw_a2 = self.dram_in("rwkv_a2", [64, D])
            self.rw_g2 = self.dram_in("rwkv_g2", [160, D])
            self.rw_vec = self.dram_in("rw_vec", [128, 8, 8])
            self.rw_wo = self.dram_in("rwkv_w_o", [D, D])
        if "mla" in kinds:
            self.pos = self.dram_in("positions", [T], I32)
            self.rope_c = self.dram_in("rope_c", [96, 2])
            self.mla_wd = self.dram_in("mla_wd", [2, D, 768])
            self.mla_qn = self.dram_in("mla_qn", [2, 128, 3])
            self.mla_kvn = self.dram_in("mla_kvn", [2, 128, 2])
            self.mla_wuq = self.dram_in("mla_wuq", [2, 384, 2048])
            self.mla_wukv = self.dram_in("mla_wukv", [2, 256, 2048])
            self.mla_wo = self.dram_in("mla_wo", [2, D, D])

        with ExitStack() as st:
            self.st = st
            S = self.S = Sched(nc, st)
            gsb = lambda n, shp, dt: st.enter_context(nc.sbuf_tensor(self.nm(n), shp, dt))
            self.actT = gsb("actT", [128, 8, T], BF16)
            self.ident = gsb("ident", [128, 128], BF16)
            self.lng = gsb("lng", [128, D], F32)
            self.lnb = gsb("lnb", [128, D], F32)
            self.ones_bf = gsb("ones_bf", [128, 128], BF16)
            self.ones_f = gsb("ones_f", [128, 128], F32)
            self.tri = gsb("tri", [128, 128], BF16)
            self.ep_idx = 0
            self.cur_src = self.x_in

            S.op(S.pool, lambda e: e.memset(self.ident[:], 1.0), writes=["ident"])
            S.op(S.pool, lambda e: e.affine_select(out=self.ident[:], in_=self.ident[:], pattern=[[1, 128]],
                                                   compare_op=ALU.is_equal, fill=0.0, base=0,
                                                   channel_multiplier=-1),
                 reads=["ident"], writes=["ident"])
            S.op(S.pool, lambda e: e.memset(self.ones_bf[:], 1.0), writes=["ones_bf"])
            S.op(S.pool, lambda e: e.memset(self.ones_f[:], 1.0), writes=["ones_f"])
            S.op(S.pool, lambda e: e.memset(self.tri[:], 1.0), writes=["tri"])
            S.op(S.pool, lambda e: e.affine_select(out=self.tri[:], in_=self.tri[:], pattern=[[1, 128]],
                                                   compare_op=ALU.is_ge, fill=0.0, base=0,
                                                   channel_multiplier=-1),
                 reads=["tri"], writes=["tri"])
            self.init_phase()
            for step in plan:
                kind, L = step
                if kind == "ffn":
                    self.ffn_phase(L)
                elif kind == "mla":
                    self.mla_phase(L)
                elif kind == "dil":
                    self.dil_phase(L)
                elif kind == "rwkv":
                    self.rwkv_phase2(L)
                elif kind == "copy":
                    self.copy_phase()
                else:
                    raise ValueError(kind)
            S.barrier()
        return nc

    def transposes_to_actT(self, m, xb, pT, res_xb):
        S = self.S
        for k in range(8):
            S.op(S.pe, lambda e, k=k: e.transpose(out=pT[:, k, :], in_=xb[:, k * 128:(k + 1) * 128],
                                                  identity=self.ident[:]),
                 reads=[res_xb, "ident"], writes=["pT"])
        S.op(S.dve, lambda e: e.tensor_copy(out=self.actT[:, :, m * 128:(m + 1) * 128], in_=pT[:]),
             reads=["pT"], writes=[("actT", m)])

    def alloc_epi(self, ph):
        nc = self.nc
        sb = lambda n, shp, dt: ph.enter_context(nc.sbuf_tensor(self.nm(n), shp, dt))
        self.xr = [sb(f"xr{i}", [128, D], F32) for i in range(2)]
        self.z = [sb(f"z{i}", [128, D], F32) for i in range(2)]
        self.xb = [sb(f"xb{i}", [128, D], BF16) for i in range(2)]
        self.st6 = [sb(f"st6{i}", [128, 2, 6], F32) for i in range(2)]
        self.mv = [sb(f"mv{i}", [128, 8], F32) for i in range(2)]

    def init_phase(self):
        nc, S = self.nc, self.S
        with ExitStack() as ph:
            self.alloc_epi(ph)
            pT = ph.enter_context(nc.psum_tensor(self.nm("pT_i"), [128, 8, 128], BF16))
            for m in range(NB):
                b = m % 2
                S.dma("sp", lambda e: e.dma_start(out=self.xr[b][:], in_=self.x_in[m * 128:(m + 1) * 128, :]),
                      writes=[("xr", b)])
                S.op(S.act, lambda e: e.copy(out=self.xb[b][:], in_=self.xr[b][:]),
                     reads=[("xr", b)], writes=[("xb", b)])
                self.transposes_to_actT(m, self.xb[b], pT, ("xb", b))
            S.barrier()

    def copy_phase(self):
        S = self.S
        ph = ExitStack()
        self.alloc_epi(ph)
        for m in range(NB):
            b = m % 2
            S.dma("sp", lambda e: e.dma_start(out=self.xr[b][:], in_=self.cur_src[m * 128:(m + 1) * 128, :]),
                  reads=[("xres", m)], writes=[("xr", b)])
            S.dma("sp", lambda e: e.dma_start(out=self.out[m * 128:(m + 1) * 128, :], in_=self.xr[b][:]),
                  reads=[("xr", b)], writes=[("xres", m)])
        S.barrier()
        ph.close()
        self.cur_src = self.out

    def load_ln(self, L, which):
        S = self.S
        S.dma("sp", lambda e: e.dma_start(out=self.lng[:], in_=self.ln_g[L, which, :].partition_broadcast(128)),
              writes=["lng"])
        S.dma("sp", lambda e: e.dma_start(out=self.lnb[:], in_=self.ln_b[L, which, :].partition_broadcast(128)),
              writes=["lnb"])

    def prefetch_xr(self, m):
        S = self.S
        b = self.ep_idx % 2
        src = self.cur_src
        S.dma("sp", lambda e: e.dma_start(out=self.xr[b][:], in_=src[m * 128:(m + 1) * 128, :]),
              reads=[("xres", m)], writes=[("xr", b)])

    def epilogue(self, m, py, py_res, pT):
        S = self.S
        prev_tr = getattr(self, "pending_tr", None)
        self.pending_tr = None
        b = self.ep_idx % 2
        self.ep_idx += 1
        xr, z, xb, st6, mv = self.xr[b], self.z[b], self.xb[b], self.st6[b], self.mv[b]
        rz, rmv = ("z", b), ("mv", b)
        S.op(S.dve, lambda e: e.scalar_tensor_tensor(out=z[:], in0=xr[:], scalar=float(ALPHA), in1=py,
                                                     op0=ALU.mult, op1=ALU.add),
             reads=[("xr", b), py_res], writes=[rz])
        if prev_tr is not None:
            self.transposes_to_actT(*prev_tr)
        for c in range(2):
            S.op(S.dve, lambda e, c=c: e.bn_stats(out=st6[:, c, :], in_=z[:, c * 512:(c + 1) * 512]),
                 reads=[rz], writes=[("st6", b, c)])
        S.op(S.dve, lambda e: e.bn_aggr(out=mv[:, 0:2], in_=st6[:].rearrange("p a b -> p (a b)")),
             reads=[("st6", b, 0), ("st6", b, 1)], writes=[rmv])
        S.op(S.dve, lambda e: e.tensor_scalar(out=mv[:, 2:3], in0=mv[:, 1:2], scalar1=float(LN_EPS), scalar2=None,
                                              op0=ALU.add), reads=[rmv], writes=[rmv])
        S.op(S.act, lambda e: e.activation(out=mv[:, 3:4], in_=mv[:, 2:3], func=AF.Sqrt), reads=[rmv], writes=[rmv])
        S.op(S.dve, lambda e: e.reciprocal(out=mv[:, 4:5], in_=mv[:, 3:4]), reads=[rmv], writes=[rmv])
        S.op(S.dve, lambda e: e.scalar_tensor_tensor(out=mv[:, 5:6], in0=mv[:, 0:1], scalar=-1.0, in1=mv[:, 4:5],
                                                     op0=ALU.mult, op1=ALU.mult), reads=[rmv], writes=[rmv])
        S.op(S.act, lambda e: e.activation(out=z[:], in_=z[:], func=AF.Identity, bias=mv[:, 5:6], scale=mv[:, 4:5]),
             reads=[rmv, rz], writes=[rz])
        S.op(S.pool, lambda e: e.tensor_tensor(out=z[:], in0=z[:], in1=self.lng[:], op=ALU.mult),
             reads=[rz, "lng"], writes=[rz])
        S.op(S.pool, lambda e: e.tensor_tensor(out=z[:], in0=z[:], in1=self.lnb[:], op=ALU.add),
             reads=[rz, "lnb"], writes=[rz])
        S.dma("pool", lambda e: e.dma_start(out=self.out[m * 128:(m + 1) * 128, :], in_=z[:]),
              reads=[rz], writes=[("xres", m)])
        S.op(S.act, lambda e: e.copy(out=xb[:], in_=z[:]), reads=[rz], writes=[("xb", b)])
        self.pending_tr = (m, xb, pT, ("xb", b))

    def flush_tr(self):
        if getattr(self, "pending_tr", None) is not None:
            self.transposes_to_actT(*self.pending_tr)
            self.pending_tr = None

    def ffn_phase(self, L):
        nc, S = self.nc, self.S
        actT = self.actT
        with ExitStack() as ph:
            sb = lambda n, shp, dt: ph.enter_context(nc.sbuf_tensor(self.nm(n), shp, dt))
            ps = lambda n, shp, dt: ph.enter_context(nc.psum_tensor(self.nm(n), shp, dt))
            self.alloc_epi(ph)
            w_out = sb("f_wout", [128, NCH, D], BF16)
            cw = sb("f_cw", [128, NCH, 4], F32)
            halo = sb("f_halo", [128, NCH, 2], F32)
            g = sb("f_g", [128, NCH, 512], BF16)
            wab = [sb(f"f_wab{i}", [128, 8, 256], BF16) for i in range(3)]
            asb = [sb(f"f_a{i}", [128, 514], F32) for i in range(3)]
            tt = [sb(f"f_t{i}", [128, 512], F32) for i in range(3)]
            pab = [ps(f"f_pab{i}", [128, 512], F32) for i in range(3)]
            py = [ps(f"f_py{i}", [128, D], F32) for i in range(2)]
            pT = ps("f_pT", [128, 8, 128], BF16)

            self.load_ln(L, 1)
            S.dma("sp", lambda e: e.dma_start(out=cw[:], in_=self.ffn_cw[L]), writes=["cw"])
            S.dma("pool", lambda e: e.dma_start(
                out=w_out[:, 0:21, :], in_=self.ffn_w_out[L, 0:21 * 128, :].rearrange("(c p) n -> p c n", p=128)),
                writes=["wout"])
            S.dma("pool", lambda e: e.dma_start(out=w_out[0:64, 21, :], in_=self.ffn_w_out[L, 21 * 128:D_FF, :]),
                  writes=["wout21"])
            S.op(S.pool, lambda e: e.memset(halo[:], 0.0), writes=[("halo", c) for c in range(NCH)])
            w_in = self.ffn_w_in[L].rearrange("(k p) n -> p k n", p=128)
            NJ = T // 512
            NIT = NJ * NCH

            def load_w(i):
                if i >= NIT:
                    return
                c = i % NCH
                wc = 128 if c < NCH - 1 else 64
                r3 = i % 3
                wb_ = wab[r3]
                S.dma("pool", lambda e: e.dma_start(out=wb_[:, :, 0:wc], in_=w_in[:, :, c * 128:c * 128 + wc]),
                      writes=[("wa", r3)])
                S.dma("pool", lambda e: e.dma_start(out=wb_[:, :, 128:128 + wc],
                                                    in_=w_in[:, :, D_FF + c * 128:D_FF + c * 128 + wc]),
                      writes=[("wb", r3)])

            load_w(0)
            load_w(1)
            for i in range(NIT):
                j, c = i // NCH, i % NCH
                tok = slice(j * 512, (j + 1) * 512)
                act_res = [("actT", 4 * j + q) for q in range(4)]
                wc = 128 if c < NCH - 1 else 64
                r3 = i % 3
                ia_, ib_ = (2 * i) % 3, (2 * i + 1) % 3
                pa_, pb_ = pab[ia_], pab[ib_]
                rpa, rpb = ("pab", ia_), ("pab", ib_)
                wb_, a_, t_ = wab[r3], asb[r3], tt[r3]
                load_w(i + 2)
                for k in range(8):
                    S.op(S.pe, lambda e, k=k: e.matmul(pa_[0:wc, :], lhsT=wb_[:, k, 0:wc], rhs=actT[:, k, tok],
                                                       start=(k == 0), stop=(k == 7)),
                         reads=[("wa", r3)] + act_res, writes=[rpa])
                for k in range(8):
                    S.op(S.pe, lambda e, k=k: e.matmul(pb_[0:wc, :], lhsT=wb_[:, k, 128:128 + wc],
                                                       rhs=actT[:, k, tok], start=(k == 0), stop=(k == 7)),
                         reads=[("wb", r3)] + act_res, writes=[rpb])
                if c == 1:
                    self.flush_tr()
                ra, rt = ("a", r3), ("t", r3)
                S.op(S.act, lambda e: e.copy(out=a_[0:wc, 0:2], in_=halo[0:wc, c, :]),
                     reads=[("halo", c)], writes=[ra])
                S.op(S.act, lambda e: e.copy(out=a_[0:wc, 2:514], in_=pa_[0:wc, :]),
                     reads=[rpa, ra], writes=[ra])
                S.op(S.act, lambda e: e.copy(out=halo[0:wc, c, :], in_=a_[0:wc, 512:514]),
                     reads=[ra], writes=[("halo", c)])
                S.op(S.dve, lambda e: e.tensor_scalar(out=t_[0:wc, :], in0=a_[0:wc, 2:514],
                                                      scalar1=cw[0:wc, c, 2:3], scalar2=cw[0:wc, c, 3:4],
                                                      op0=ALU.mult, op1=ALU.add),
                     reads=[ra, "cw"], writes=[rt])
                S.op(S.dve, lambda e: e.scalar_tensor_tensor(out=t_[0:wc, :], in0=a_[0:wc, 1:513],
                                                             scalar=cw[0:wc, c, 1:2], in1=t_[0:wc, :],
                                                             op0=ALU.mult, op1=ALU.add),
                     reads=[ra, rt], writes=[rt])
                S.op(S.dve, lambda e: e.scalar_tensor_tensor(out=t_[0:wc, :], in0=a_[0:wc, 0:512],
                                                             scalar=cw[0:wc, c, 0:1], in1=t_[0:wc, :],
                                                             op0=ALU.mult, op1=ALU.add),
                     reads=[ra, rt], writes=[rt])
                S.op(S.act, lambda e: e.activation(out=t_[0:wc, :], in_=t_[0:wc, :], func=AF.Silu),
                     reads=[rt], writes=[rt])
                S.op(S.dve, lambda e: e.tensor_tensor(out=g[0:wc, c, :], in0=t_[0:wc, :], in1=pb_[0:wc, :],
                                                      op=ALU.mult),
                     reads=[rt, rpb], writes=[("g", c)])
                if c < NCH - 1:
                    continue
                for mm in range(4):
                    m = 4 * j + mm
                    self.prefetch_xr(m)
                    p_ = py[m % 2]
                    for n in range(2):
                        for cc in range(NCH):
                            wcc = 128 if cc < NCH - 1 else 64
                            S.op(S.pe, lambda e, n=n, cc=cc, wcc=wcc: e.matmul(
                                p_[:, n * 512:(n + 1) * 512], lhsT=g[0:wcc, cc, mm * 128:(mm + 1) * 128],
                                rhs=w_out[0:wcc, cc, n * 512:(n + 1) * 512], start=(cc == 0), stop=(cc == NCH - 1)),
                                 reads=[("g", cc), "wout", "wout21"], writes=[("py", m % 2)])
                    self.epilogue(m, p_[:], ("py", m % 2), pT)
            self.flush_tr()
            S.barrier()
        self.cur_src = self.out


    def out_proj_phase(self, L, w_dram):
        nc, S = self.nc, self.S
        with ExitStack() as ph:
            sb = lambda n, shp, dt: ph.enter_context(nc.sbuf_tensor(self.nm(n), shp, dt))
            ps = lambda n, shp, dt: ph.enter_context(nc.psum_tensor(self.nm(n), shp, dt))
            self.alloc_epi(ph)
            wo = sb("o_w", [128, 8, D], BF16)
            py = [ps(f"o_py{i}", [128, D], F32) for i in range(2)]
            pT = ps("o_pT", [128, 8, 128], BF16)
            self.load_ln(L, 0)
            S.dma("pool", lambda e: e.dma_start(out=wo[:], in_=w_dram.rearrange("(k p) n -> p k n", p=128)),
                  writes=["wo"])
            for m in range(NB):
                self.prefetch_xr(m)
                p_ = py[m % 2]
                for n in range(2):
                    for k in range(8):
                        S.op(S.pe, lambda e, n=n, k=k: e.matmul(
                            p_[:, n * 512:(n + 1) * 512], lhsT=self.actT[:, k, m * 128:(m + 1) * 128],
                            rhs=wo[:, k, n * 512:(n + 1) * 512], start=(k == 0), stop=(k == 7)),
                             reads=[("actT", m), "wo"], writes=[("py", m % 2)])
                self.epilogue(m, p_[:], ("py", m % 2), pT)
            self.flush_tr()
            S.barrier()
        self.cur_src = self.out

    def mla_phase(self, L):
        nc, S = self.nc, self.S
        actT = self.actT
        ia = L // 3
        SCALE = 96.0 ** -0.5
        TWO_PI = 2.0 * np.pi
        with ExitStack() as ml:
            msb = lambda n, shp, dt: ml.enter_context(nc.sbuf_tensor(self.nm(n), shp, dt))
            cqn = msb("m_cqn", [128, 3, T], BF16)
            ckvn = msb("m_ckvn", [128, 2, T], BF16)
            KT = msb("m_KT", [96, T], BF16)
            cosT = msb("m_cos", [96, T], BF16)
            sinS = msb("m_sin", [96, T], BF16)
            rc = msb("m_rc", [96, 2], F32)
            with ExitStack() as ph:
                sb = lambda n, shp, dt: ph.enter_context(nc.sbuf_tensor(self.nm(n), shp, dt))
                HT = T // 2
                posi = sb("m_posi", [96, HT], I32)
                ang = sb("m_ang", [96, HT], F32)
                tmp = sb("m_tmp", [96, HT], F32)
                yi = sb("m_yi", [96, HT], I32)
                msk = sb("m_msk", [96, HT], F32)
                S.dma("sp", lambda e: e.dma_start(out=rc[:], in_=self.rope_c), writes=["rc"])

                def sin_turns():
                    S.op(S.dve, lambda e: e.tensor_copy(out=yi[:], in_=tmp[:]), reads=["tmp"], writes=["yi"])
                    S.op(S.dve, lambda e: e.tensor_copy(out=msk[:], in_=yi[:]), reads=["yi"], writes=["msk"])
                    S.op(S.dve, lambda e: e.tensor_tensor(out=tmp[:], in0=tmp[:], in1=msk[:], op=ALU.subtract),
                         reads=["tmp", "msk"], writes=["tmp"])
                    S.op(S.dve, lambda e: e.tensor_scalar(out=msk[:], in0=tmp[:], scalar1=0.5, scalar2=None,
                                                          op0=ALU.is_gt), reads=["tmp"], writes=["msk"])
                    S.op(S.dve, lambda e: e.tensor_tensor(out=tmp[:], in0=tmp[:], in1=msk[:], op=ALU.subtract),
                         reads=["tmp", "msk"], writes=["tmp"])
                    S.op(S.dve, lambda e: e.tensor_scalar(out=msk[:], in0=tmp[:], scalar1=-0.5, scalar2=None,
                                                          op0=ALU.is_lt), reads=["tmp"], writes=["msk"])
                    S.op(S.dve, lambda e: e.tensor_tensor(out=tmp[:], in0=tmp[:], in1=msk[:], op=ALU.add),
                         reads=["tmp", "msk"], writes=["tmp"])
                    S.op(S.act, lambda e: e.activation(out=tmp[:], in_=tmp[:], func=AF.Sin, scale=6.28318),
                         reads=["tmp"], writes=["tmp"])

                for hh in range(2):
                    cs = slice(hh * HT, (hh + 1) * HT)
                    S.dma("sp", lambda e: e.dma_start(out=posi[:], in_=self.pos[cs].partition_broadcast(96)),
                          writes=["posi"])
                    S.op(S.dve, lambda e: e.tensor_copy(out=ang[:], in_=posi[:]), reads=["posi"], writes=["ang"])
                    S.op(S.dve, lambda e: e.tensor_scalar(out=ang[:], in0=ang[:], scalar1=rc[:, 0:1], scalar2=None,
                                                          op0=ALU.mult), reads=["ang", "rc"], writes=["ang"])
                    S.op(S.dve, lambda e: e.tensor_copy(out=tmp[:], in_=ang[:]), reads=["ang"], writes=["tmp"])
                    sin_turns()
                    S.op(S.dve, lambda e: e.tensor_scalar(out=sinS[64:96, cs], in0=tmp[64:96, :],
                                                          scalar1=rc[64:96, 1:2], scalar2=None, op0=ALU.mult),
                         reads=["tmp", "rc"], writes=["sinS"])
                    S.op(S.dve, lambda e: e.tensor_scalar(out=tmp[:], in0=ang[:], scalar1=0.25, scalar2=None,
                                                          op0=ALU.add), reads=["ang", "sinS"], writes=["tmp"])
                    sin_turns()
                    S.op(S.dve, lambda e: e.tensor_copy(out=cosT[64:96, cs], in_=tmp[64:96, :]),
                         reads=["tmp"], writes=["cosT"])
                S.barrier()
            with ExitStack() as ph:
                sb = lambda n, shp, dt: ph.enter_context(nc.sbuf_tensor(self.nm(n), shp, dt))
                ps = lambda n, shp, dt: ph.enter_context(nc.psum_tensor(self.nm(n), shp, dt))
                wd = sb("m_wd", [128, 8, 768], BF16)
                gq = sb("m_gq", [128, 3], F32)
                gkv = sb("m_gkv", [128, 2], F32)
                raw = [sb(f"m_raw{i}", [128, 5, 512], F32) for i in range(2)]
                sq = [sb(f"m_sq{i}", [128, 5, 512], BF16) for i in range(2)]
                rs = [sb(f"m_rs{i}", [128, 2, 512], F32) for i in range(2)]
                t1 = [sb(f"m_t1{i}", [96, 512], F32) for i in range(2)]
                t2 = [sb(f"m_t2{i}", [96, 512], F32) for i in range(2)]
                p_lat = [ps(f"m_plat{i}", [128, 512], F32) for i in range(2)]
                p_ss = [ps(f"m_pss{i}", [128, 512], F32) for i in range(2)]
                p_kA = ps("m_pkA", [96, 512], F32)
                p_kB = ps("m_pkB", [96, 512], F32)
                S.dma("pool", lambda e: e.dma_start(out=wd[:], in_=self.mla_wd[ia].rearrange("(k p) n -> p k n", p=128)),
                      writes=["wd"])
                S.dma("sp", lambda e: e.dma_start(out=gq[:], in_=self.mla_qn[ia]), writes=["gq"])
                S.dma("sp", lambda e: e.dma_start(out=gkv[:], in_=self.mla_kvn[ia]), writes=["gkv"])
                it = 0
                for j in range(T // 512):
                    tok = slice(j * 512, (j + 1) * 512)
                    ares = [("actT", 4 * j + q) for q in range(4)]
                    b = j % 2
                    for c in range(5):
                        pl = p_lat[it % 2]
                        rpl = ("plat", it % 2)
                        it += 1
                        for k in range(8):
                            S.op(S.pe, lambda e, k=k: e.matmul(pl[:], lhsT=wd[:, k, c * 128:(c + 1) * 128],
                                                               rhs=actT[:, k, tok], start=(k == 0), stop=(k == 7)),
                                 reads=["wd"] + ares, writes=[rpl])
                        S.op(S.act, lambda e: e.copy(out=raw[b][:, c, :], in_=pl[:]), reads=[rpl], writes=[("raw", b, c)])
                        S.op(S.act, lambda e: e.activation(out=sq[b][:, c, :], in_=pl[:], func=AF.Square),
                             reads=[rpl], writes=[("sq", b, c)])
                    for which, (c0, c1, dim) in enumerate([(0, 3, 384.0), (3, 5, 256.0)]):
                        for c in range(c0, c1):
                            S.op(S.pe, lambda e, c=c: e.matmul(p_ss[which][:], lhsT=self.ones_bf[:], rhs=sq[b][:, c, :],
                                                               start=(c == c0), stop=(c == c1 - 1)),
                                 reads=[("sq", b, c), "ones_bf"], writes=[("pss", which)])
                        rr = ("rs", b, which)
                        S.op(S.dve, lambda e: e.tensor_scalar(out=rs[b][:, which, :], in0=p_ss[which][:],
                                                              scalar1=1.0 / dim, scalar2=float(RMS_EPS),
                                                              op0=ALU.mult, op1=ALU.add),
                             reads=[("pss", which)], writes=[rr])
                        S.op(S.act, lambda e: e.activation(out=rs[b][:, which, :], in_=rs[b][:, which, :], func=AF.Sqrt),
                             reads=[rr], writes=[rr])
                        S.op(S.dve, lambda e: e.reciprocal(out=rs[b][:, which, :], in_=rs[b][:, which, :]),
                             reads=[rr], writes=[rr])
                        for c in range(c0, c1):
                            dst = cqn[:, c, tok] if which == 0 else ckvn[:, c - 3, tok]
                            gsc = gq[:, c:c + 1] if which == 0 else gkv[:, c - 3:c - 2]
                            S.op(S.dve, lambda e: e.scalar_tensor_tensor(out=dst, in0=raw[b][:, c, :], scalar=gsc,
                                                                         in1=rs[b][:, which, :], op0=ALU.mult,
                                                                         op1=ALU.mult),
                                 reads=[("raw", b, c), rr, "gq", "gkv"], writes=[("cn", c, j)])
                    for k in range(8):
                        S.op(S.pe, lambda e, k=k: e.matmul(p_kA[:], lhsT=wd[:, k, 640:736], rhs=actT[:, k, tok],
                                                           start=(k == 0), stop=(k == 7)),
                             reads=["wd"] + ares, writes=["pkA"])
                    for k in range(8):
                        S.op(S.pe, lambda e, k=k: e.matmul(p_kB[:], lhsT=wd[:, k, 672:768], rhs=actT[:, k, tok],
                                                           start=(k == 0), stop=(k == 7)),
                             reads=["wd"] + ares, writes=["pkB"])
                    S.op(S.dve, lambda e: e.tensor_tensor(out=t1[b][64:96, :], in0=p_kA[64:96, :], in1=cosT[64:96, tok],
                                                          op=ALU.mult), reads=["pkA"], writes=[("t1", b)])
                    S.op(S.dve, lambda e: e.tensor_tensor(out=t2[b][64:96, :], in0=p_kB[64:96, :], in1=sinS[64:96, tok],
                                                          op=ALU.mult), reads=["pkB"], writes=[("t2", b)])
                    S.op(S.pool, lambda e: e.tensor_tensor(out=KT[64:96, tok], in0=t1[b][64:96, :], in1=t2[b][64:96, :],
                                                           op=ALU.add), reads=[("t1", b), ("t2", b)], writes=[("KTpe", j)])
                S.barrier()
            with ExitStack() as ph:
                sb = lambda n, shp, dt: ph.enter_context(nc.sbuf_tensor(self.nm(n), shp, dt))
                ps = lambda n, shp, dt: ph.enter_context(nc.psum_tensor(self.nm(n), shp, dt))
                wuq = sb("m_wuq", [128, 3, 2048], BF16)
                wukv = sb("m_wukv", [128, 2, 2048], BF16)
                Vx = [sb(f"m_Vx{i}", [128, 32, 128], BF16) for i in range(2)]
                QT = [sb(f"m_QT{i}", [96, 512], BF16) for i in range(2)]
                pt = [sb(f"m_pt{i}", [128, 512], BF16) for i in range(4)]
                t1 = [sb(f"m_u1{i}", [96, 512], F32) for i in range(2)]
                t2 = [sb(f"m_u2{i}", [96, 512], F32) for i in range(2)]
                rec = sb("m_rec", [128, 512], F32)
                bcs = sb("m_bcs", [128, 512], F32)
                p_p = [ps(f"m_pp{i}", [128, 512], F32) for i in range(2)]
                p_s = [ps(f"m_ps{i}", [128, 512], F32) for i in range(3)]
                p_o = [ps(f"m_po{i}", [128, 512], F32) for i in range(2)]
                p_bc = ps("m_pbc", [128, 512], F32)
                KTb = sb("m_KTb", [96, T], BF16)
                KTs = [KT, KTb]
                S.dma("pool", lambda e: e.dma_start(out=wuq[:], in_=self.mla_wuq[ia].rearrange("(k p) n -> p k n", p=128)),
                      writes=["wuq"])
                S.dma("pool", lambda e: e.dma_start(out=wukv[:], in_=self.mla_wukv[ia].rearrange("(k p) n -> p k n", p=128)),
                      writes=["wukv"])
                S.op(S.pool, lambda e: e.memset(Vx[0][:], 1.0), writes=[("Vx", 0)])
                S.op(S.pool, lambda e: e.memset(Vx[1][:], 1.0), writes=[("Vx", 1)])
                S.op(S.pool, lambda e: e.tensor_copy(out=KTb[64:96, :], in_=KT[64:96, :]), writes=["KTb_pe"])
                cnt = {"pp": 0}

                def kv_proj(h):
                    hl = h % 2
                    kt = KTs[hl]
                    vx = Vx[hl]
                    r0 = hl * 64
                    for j in range(T // 512):
                        tok = slice(j * 512, (j + 1) * 512)
                        pp, rpp = p_p[cnt["pp"] % 2], ("pp", cnt["pp"] % 2)
                        cnt["pp"] += 1
                        for k in range(2):
                            S.op(S.pe, lambda e, k=k: e.matmul(pp[0:64, :], lhsT=wukv[:, k, h * 64:(h + 1) * 64],
                                                               rhs=ckvn[:, k, tok], start=(k == 0), stop=(k == 1)),
                                 reads=["wukv"], writes=[rpp])
                        S.op(S.dve, lambda e: e.tensor_copy(out=kt[0:64, tok], in_=pp[0:64, :]), reads=[rpp],
                             writes=[("KT", hl, j)])
                    for j in range(4):
                        pp, rpp = p_p[cnt["pp"] % 2], ("pp", cnt["pp"] % 2)
                        cnt["pp"] += 1
                        ppv = pp[:].rearrange("p (b d) -> p b d", d=64)
                        for bb in range(8):
                            blk = j * 8 + bb
                            for k in range(2):
                                S.op(S.pe, lambda e, k=k: e.matmul(
                                    ppv[:, bb, :], lhsT=ckvn[:, k, blk * 128:(blk + 1) * 128],
                                    rhs=wukv[:, k, 1024 + h * 64:1024 + (h + 1) * 64], start=(k == 0), stop=(k == 1)),
                                     reads=["wukv"], writes=[rpp])
                        S.op(S.dve, lambda e: e.tensor_copy(out=vx[:, j * 8:(j + 1) * 8, r0:r0 + 64], in_=ppv),
                             reads=[rpp], writes=[("Vx", hl, j)])

                def prep_q(h, qt, iq):
                    tok = slice(qt * 512, (qt + 1) * 512)
                    qT, rq = QT[iq % 2], ("QT", iq % 2)
                    u1, u2 = t1[iq % 2], t2[iq % 2]
                    ru1, ru2 = ("u1", iq % 2), ("u2", iq % 2)
                    pA, pB = p_p[0], p_p[1]
                    for k in range(3):
                        S.op(S.pe, lambda e, k=k: e.matmul(pA[0:96, :], lhsT=wuq[:, k, h * 128:h * 128 + 96],
                                                           rhs=cqn[:, k, tok], start=(k == 0), stop=(k == 2)),
                             reads=["wuq"], writes=[("pp", 0)])
                    for k in range(3):
                        S.op(S.pe, lambda e, k=k: e.matmul(pB[0:96, :], lhsT=wuq[:, k, h * 128 + 32:h * 128 + 128],
                                                           rhs=cqn[:, k, tok], start=(k == 0), stop=(k == 2)),
                             reads=["wuq"], writes=[("pp", 1)])
                    S.op(S.dve, lambda e: e.tensor_copy(out=qT[0:64, :], in_=pA[0:64, :]), reads=[("pp", 0)], writes=[rq])
                    S.op(S.dve, lambda e: e.tensor_tensor(out=u1[64:96, :], in0=pA[64:96, :], in1=cosT[64:96, tok],
                                                          op=ALU.mult), reads=[("pp", 0)], writes=[ru1])
                    S.op(S.dve, lambda e: e.tensor_tensor(out=u2[64:96, :], in0=pB[64:96, :], in1=sinS[64:96, tok],
                                                          op=ALU.mult), reads=[("pp", 1)], writes=[ru2])
                    S.op(S.dve, lambda e: e.tensor_tensor(out=qT[64:96, :], in0=u1[64:96, :], in1=u2[64:96, :],
                                                          op=ALU.add), reads=[ru1, ru2], writes=[rq])

                def fin_q1(h, qt, iq):
                    hl = h % 2
                    d0 = 64 - hl * 64
                    po, rpo = p_o[iq % 2], ("po", iq % 2)
                    S.op(S.act, lambda e: e.copy(out=rec[d0:d0 + 1, :], in_=po[d0:d0 + 1, :]), reads=[rpo], writes=["rec"])

                def fin_q2(h, qt, iq):
                    hl, ch = h % 2, h // 2
                    r0, d0 = hl * 64, 64 - hl * 64
                    tok = slice(qt * 512, (qt + 1) * 512)
                    po, rpo = p_o[iq % 2], ("po", iq % 2)
                    S.op(S.pe, lambda e: e.matmul(p_bc[:], lhsT=self.ones_f[d0:d0 + 1, :], rhs=rec[d0:d0 + 1, :],
                                                  start=True, stop=True), reads=["rec", "ones_f"], writes=["pbc"])
                    S.op(S.dve, lambda e: e.reciprocal(out=bcs[r0:r0 + 64, :], in_=p_bc[r0:r0 + 64, :]),
                         reads=["pbc"], writes=["bcs"])
                    S.op(S.dve, lambda e: e.tensor_tensor(out=actT[r0:r0 + 64, ch, tok], in0=po[r0:r0 + 64, :],
                                                          in1=bcs[r0:r0 + 64, :], op=ALU.mult),
                         reads=[rpo, "bcs"], writes=[("actT", 4 * qt + q) for q in range(4)])

                items = []
                iq = 0
                for h in range(16):
                    for qt in range(T // 512):
                        nkb = 4 * qt + 4
                        for kb in range(nkb):
                            items.append(dict(h=h, qt=qt, kb=kb, nkb=nkb, iq=iq, pre=[], post=[]))
                        iq += 1
                first = {}
                for i, it in enumerate(items):
                    first.setdefault((it["h"], it["qt"]), i)
                for (h, qt), i in first.items():
                    lo = 0 if i == 0 else i - items[i - 1]["nkb"]
                    items[max(lo, i - 6)]["pre"].append(lambda h=h, qt=qt, iq=items[i]["iq"]: prep_q(h, qt, iq))
                    if qt == 0 and h > 0:
                        items[first[(h - 1, 5)]]["pre"].insert(0, lambda h=h: kv_proj(h))
                for i, it in enumerate(items):
                    if it["kb"] == it["nkb"] - 1:
                        it["post"].append(lambda it=it: fin_q1(it["h"], it["qt"], it["iq"]))
                        items[min(len(items) - 1, i + 3)]["post"].append(
                            lambda it=it: fin_q2(it["h"], it["qt"], it["iq"]))
                kv_proj(0)

                def stA(i, it):
                    h, qt, kb = it["h"], it["qt"], it["kb"]
                    hl = h % 2
                    n0 = max(0, kb - 4 * qt) * 128
                    qT, rq = QT[it["iq"] % 2], ("QT", it["iq"] % 2)
                    S.op(S.pe, lambda e: e.matmul(p_s[i % 3][:, n0:512], lhsT=KTs[hl][0:96, kb * 128:(kb + 1) * 128],
                                                  rhs=qT[0:96, n0:512], start=True, stop=True),
                         reads=[("KT", hl, kb // 4), rq, "KTb_pe"], writes=[("ps", i % 3)])

                def stB(i, it):
                    qt, kb = it["qt"], it["kb"]
                    n0 = max(0, kb - 4 * qt) * 128
                    ptb, rpt = pt[i % 4], ("pt", i % 4)
                    S.op(S.act, lambda e: e.activation(out=ptb[:, n0:512], in_=p_s[i % 3][:, n0:512], func=AF.Exp,
                                                       scale=float(SCALE)), reads=[("ps", i % 3)], writes=[rpt])
                    if kb >= 4 * qt:
                        S.op(S.dve, lambda e: e.tensor_tensor(out=ptb[:, n0:n0 + 128], in0=ptb[:, n0:n0 + 128],
                                                              in1=self.tri[:], op=ALU.mult),
                             reads=[rpt, "tri"], writes=[rpt])

                def stC(i, it):
                    h, qt, kb, nkb = it["h"], it["qt"], it["kb"], it["nkb"]
                    hl = h % 2
                    n0 = max(0, kb - 4 * qt) * 128
                    po, rpo = p_o[it["iq"] % 2], ("po", it["iq"] % 2)
                    S.op(S.pe, lambda e: e.matmul(po[:, n0:512], lhsT=Vx[hl][:, kb, :], rhs=pt[i % 4][:, n0:512],
                                                  start=(kb == 0), stop=(kb == nkb - 1)),
                         reads=[("Vx", hl, kb // 8), ("pt", i % 4)], writes=[rpo])

                n = len(items)
                for t_ in range(n + 3):
                    if t_ < n:
                        for f in items[t_]["pre"]:
                            f()
                        stA(t_, items[t_])
                    if 0 <= t_ - 1 < n:
                        stB(t_ - 1, items[t_ - 1])
                    if 0 <= t_ - 3 < n:
                        stC(t_ - 3, items[t_ - 3])
                        for f in items[t_ - 3]["post"]:
                            f()
                S.barrier()
        self.out_proj_phase(L, self.mla_wo[ia])


    def load_oT(self):
        S = self.S
        for c in range(8):
            S.dma("sp", lambda e, c=c: e.dma_start(out=self.actT[:, c, :], in_=self.oT_d[c]),
                  reads=[("oTd", c)], writes=[("actT", m) for m in range(NB)])

    def dil_phase(self, L):
        nc, S = self.nc, self.S
        actT = self.actT
        DIL = (1, 4, 16)
        with ExitStack() as ph:
            sb = lambda n, shp, dt: ph.enter_context(nc.sbuf_tensor(self.nm(n), shp, dt))
            ps = lambda n, shp, dt: ph.enter_context(nc.psum_tensor(self.nm(n), shp, dt))
            wd = sb("d_w", [128, 8, 9, 128], BF16)
            QK = [[sb(f"d_qk{g}{i}", [128, T], BF16) for i in range(2)] for g in range(3)]
            Vx = [sb(f"d_vx{g}", [128, 32, 192], BF16) for g in range(3)]
            osb = sb("d_osb", [128, T], BF16)
            mask2 = sb("d_mask2", [128, 256], BF16)
            pt = [sb(f"d_pt{i}", [128, 256], BF16) for i in range(4)]
            rec = sb("d_rec", [128, 512], F32)
            bcs = sb("d_bcs", [128, 512], F32)
            p_o = ps("d_po", [128, 2048], F32)
            p_s = [ps(f"d_ps{i}", [128, 512], F32) for i in range(2)]
            p_bc = ps("d_pbc", [128, 512], F32)
            p_p = ps("d_pp", [128, 512], F32)
            S.op(S.pool, lambda e: e.memset(mask2[:], 1.0), writes=["mask2"])
            S.op(S.pool, lambda e: e.affine_select(out=mask2[:, 0:128], in_=mask2[:, 0:128], pattern=[[-1, 128]],
                                                   compare_op=ALU.is_ge, fill=0.0, base=0, channel_multiplier=1),
                 reads=["mask2"], writes=["mask2"])
            S.op(S.pool, lambda e: e.affine_select(out=mask2[:, 128:256], in_=mask2[:, 128:256], pattern=[[1, 128]],
                                                   compare_op=ALU.is_ge, fill=0.0, base=0, channel_multiplier=-1),
                 reads=["mask2"], writes=["mask2"])
            for g in range(3):
                S.op(S.pool, lambda e, g=g: e.memset(Vx[g][:], 1.0), writes=[("Vx", g)])
            wsrc = self.dil_wqkv.rearrange("(k p) (c h d) -> p k c (h d)", p=128, c=9, h=16)
            pbank = [p_p, p_s[0], p_s[1]]
            cnt = {"pp": 0, "it": 0}

            def next_pp():
                i = cnt["pp"] % 3
                cnt["pp"] += 1
                return pbank[i], ("pb", i)

            def fin(hl, U, qq):
                r0, d0 = hl * 64, 64 - hl * 64
                rows = slice(r0, r0 + 64)
                cs = slice(qq * 512, (qq + 1) * 512)
                tok = slice(U * 2048 + qq * 512, U * 2048 + (qq + 1) * 512)
                S.op(S.act, lambda e: e.copy(out=rec[d0:d0 + 1, :], in_=p_o[d0:d0 + 1, cs]), reads=["po"], writes=["rec"])
                S.op(S.pe, lambda e: e.matmul(p_bc[:], lhsT=self.ones_f[d0:d0 + 1, :], rhs=rec[d0:d0 + 1, :],
                                              start=True, stop=True), reads=["rec", "ones_f"], writes=["pbc"])
                S.op(S.dve, lambda e: e.reciprocal(out=bcs[rows, :], in_=p_bc[rows, :]), reads=["pbc"], writes=["bcs"])
                S.op(S.dve, lambda e: e.tensor_tensor(out=osb[rows, tok], in0=p_o[rows, cs], in1=bcs[rows, :],
                                                      op=ALU.mult), reads=["po", "bcs"], writes=["osb"])

            for hp in range(8):
                for c9 in range(9):
                    S.dma("pool", lambda e: e.dma_start(out=wd[:, :, c9, :], in_=wsrc[:, :, c9, hp * 128:(hp + 1) * 128]),
                          writes=[("wd", c9)])
                for j in range(T // 512):
                    tok = slice(j * 512, (j + 1) * 512)
                    ares = [("actT", 4 * j + q) for q in range(4)]
                    for g in range(3):
                        for qk in range(2):
                            pp, rpp = next_pp()
                            for k in range(8):
                                S.op(S.pe, lambda e, k=k: e.matmul(pp[:], lhsT=wd[:, k, g * 3 + qk, :],
                                                                   rhs=actT[:, k, tok], start=(k == 0), stop=(k == 7)),
                                     reads=[("wd", g * 3 + qk)] + ares, writes=[rpp])
                            if (g * 2 + qk) % 2 == 0:
                                S.op(S.act, lambda e: e.copy(out=QK[g][qk][:, tok], in_=pp[:]),
                                     reads=[rpp], writes=[("QK", g, qk, j)])
                            else:
                                S.op(S.dve, lambda e: e.tensor_copy(out=QK[g][qk][:, tok], in_=pp[:]),
                                     reads=[rpp], writes=[("QK", g, qk, j)])
                for g in range(3):
                    d = DIL[g]
                    for b4 in range(8):
                        pp, rpp = next_pp()
                        ppv = pp[:].rearrange("p (b d) -> p b d", d=128)
                        for bb in range(4):
                            blk = b4 * 4 + bb
                            n, r = blk // d, blk % d
                            t0 = n * 128 * d + r
                            for k in range(8):
                                S.op(S.pe, lambda e, k=k: e.matmul(
                                    ppv[:, bb, :], lhsT=actT[:, k, t0:t0 + 127 * d + 1:d], rhs=wd[:, k, g * 3 + 2, :],
                                    start=(k == 0), stop=(k == 7)),
                                     reads=[("wd", g * 3 + 2)] + [("actT", m) for m in range(n * d, (n + 1) * d)], writes=[rpp])
                        S.op(S.act, lambda e: e.copy(out=Vx[g][:, b4 * 4:(b4 + 1) * 4, 0:64], in_=ppv[:, :, 0:64]),
                             reads=[rpp], writes=[("Vx", g)])
                        S.op(S.dve, lambda e: e.tensor_copy(out=Vx[g][:, b4 * 4:(b4 + 1) * 4, 128:192],
                                                            in_=ppv[:, :, 64:128]),
                             reads=[rpp], writes=[("Vx", g)])
                items = []
                for hl in range(2):
                    for U in range(2):
                        for g in range(3):
                            for qb in range(16):
                                items.append(dict(hl=hl, U=U, g=g, qb=qb, pre=[], preC=[], post=[]))
                        items[-48]["preC"].append(lambda: S.op(S.dve, lambda e: e.memset(p_o[:], 0.0), writes=["po"]))
                        for qq in range(4):
                            items[-1]["post"].append(lambda hl=hl, U=U, qq=qq: fin(hl, U, qq))

                def geom(it):
                    hl, U, g, qb = it["hl"], it["U"], it["g"], it["qb"]
                    d = DIL[g]
                    blk = U * 16 + qb
                    n, r = blk // d, blk % d
                    t0 = n * 128 * d + r
                    return hl, U, g, d, blk, n, t0

                def stA(i, it):
                    hl, U, g, d, blk, n, t0 = geom(it)
                    rows = slice(hl * 64, hl * 64 + 64)
                    Kt, Qt = QK[g][1], QK[g][0]
                    qsl = slice(t0, t0 + 127 * d + 1, d)
                    psb = p_s[i % 2]
                    half = 0
                    rps = ("ps", i % 2)
                    if n > 0:
                        tp = t0 - 128 * d
                        S.op(S.pe, lambda e: e.matmul(psb[:, half:half + 128], lhsT=Kt[rows, tp:tp + 127 * d + 1:d],
                                                      rhs=Qt[rows, qsl], start=True, stop=True),
                             reads=[("QKall",)], writes=[rps, ("pb", i % 2 + 1)])
                    S.op(S.pe, lambda e: e.matmul(psb[:, half + 128:half + 256], lhsT=Kt[rows, qsl],
                                                  rhs=Qt[rows, qsl], start=True, stop=True),
                         reads=[("QKall",)], writes=[rps, ("pb", i % 2 + 1)])

                def stB(i, it):
                    hl, U, g, d, blk, n, t0 = geom(it)
                    psb = p_s[i % 2]
                    half = 0
                    rps = ("ps", i % 2)
                    ptb, rpt = pt[i % 4], ("pt", i % 4)
                    c0 = 0 if n > 0 else 128
                    S.op(S.act, lambda e: e.activation(out=ptb[:, c0:256], in_=psb[:, half + c0:half + 256],
                                                       func=AF.Exp, scale=0.125), reads=[rps], writes=[rpt])
                    S.op(S.dve, lambda e: e.tensor_tensor(out=ptb[:, c0:256], in0=ptb[:, c0:256],
                                                          in1=mask2[:, c0:256], op=ALU.mult),
                         reads=[rpt, "mask2"], writes=[rpt])

                def stC(i, it):
                    hl, U, g, d, blk, n, t0 = geom(it)
                    vsl = slice(0, 128) if hl == 0 else slice(64, 192)
                    osl = slice(t0 - U * 2048, t0 - U * 2048 + 127 * d + 1, d)
                    ptb, rpt = pt[i % 4], ("pt", i % 4)
                    if n > 0:
                        S.op(S.pe, lambda e: e.matmul(p_o[:, osl], lhsT=Vx[g][:, blk - d, vsl], rhs=ptb[:, 0:128],
                                                      start=False, stop=False, skip_group_check=True),
                             reads=[("Vx", g), rpt, "po"], writes=["po"])
                    S.op(S.pe, lambda e: e.matmul(p_o[:, osl], lhsT=Vx[g][:, blk, vsl], rhs=ptb[:, 128:256],
                                                  start=False, stop=False, skip_group_check=True),
                         reads=[("Vx", g), rpt, "po"], writes=["po"])

                n_it = len(items)
                for t_ in range(n_it + 3):
                    if t_ < n_it:
                        for f in items[t_]["pre"]:
                            f()
                        stA(t_, items[t_])
                    if 0 <= t_ - 1 < n_it:
                        stB(t_ - 1, items[t_ - 1])
                    if 0 <= t_ - 3 < n_it:
                        for f in items[t_ - 3]["preC"]:
                            f()
                        stC(t_ - 3, items[t_ - 3])
                        for f in items[t_ - 3]["post"]:
                            f()
                S.dma("sp", lambda e: e.dma_start(out=self.oT_d[hp], in_=osb[:]), reads=["osb"], writes=[("oTd", hp)])
            S.barrier()
            self.load_oT()
            S.barrier()
        self.out_proj_phase(L, self.dil_wo)


    def rwkv_phase(self, L):
        nc, S = self.nc, self.S
        actT = self.actT
        RT = 256
        NT_ = T // RT
        GN_EPS = 64e-5
        DEC = -float(np.exp(-0.5))
        with ExitStack() as ph:
            sb = lambda n, shp, dt: ph.enter_context(nc.sbuf_tensor(self.nm(n), shp, dt))
            ps = lambda n, shp, dt: ph.enter_context(nc.psum_tensor(self.nm(n), shp, dt))
            mu = sb("r_mu", [128, 8, 6], F32)
            omu = sb("r_omu", [128, 8, 6], F32)
            vec = sb("r_vec", [128, 8, 8], F32)
            stage = sb("r_stage", [128, 8, 160], F32)
            l1a = sb("r_l1a", [128, 8, 288], BF16)
            l1b = sb("r_l1b", [128, 8, 288], BF16)
            hwa = sb("r_hwa", [128, T], BF16)
            hg = sb("r_hg", [128, 2, T], BF16)
            w2 = sb("r_w2", [128, D], BF16)
            g2 = sb("r_g2", [128, 2, D], BF16)
            wa = [sb(f"r_wa{i}", [128, 8, 128], BF16) for i in range(3)]
            wb = [sb(f"r_wb{i}", [128, 8, 128], BF16) for i in range(3)]
            f32t = lambda n: sb(n, [128, RT], F32)
            r_, k_, v_, lw, a_, g_, kk, km, be, Lc, eL, eLn, eLp, eD, tA, tB = [f32t(f"r_f{i}") for i in range(16)]
            LC = sb("r_LC", [128, 4], F32)
            eLC = sb("r_eLC", [128, 4], F32)
            rmask = sb("r_rmask", [128, RT], F32)
            blk1 = sb("r_blk1", [128, 128], F32)
            bdn = ["kap", "rt", "kt", "bt", "vf", "kb", "bb"]
            BD = {n: sb("r_bd_" + n, [128, 4, 128], F32) for n in bdn}
            MkvT, AkrT, AbrT, Y = [sb(f"r_m{i}", [128, 4, 128], F32) for i in range(4)]
            X = [sb(f"r_X{i}", [128, 4, 128], F32) for i in range(2)]
            XT = [sb(f"r_XT{i}", [128, 4, 128], F32) for i in range(2)]
            Vtok, Ktok, Btok = [sb(f"r_tk{i}", [128, 4, 128], F32) for i in range(3)]
            Wsb = sb("r_Wsb", [128, 128], F32)
            nU = sb("r_nU", [128, 128], F32)
            Abd = sb("r_Abd", [128, 128], F32)
            Osb4 = sb("r_Osb4", [128, 4, 128], F32)
            st64 = sb("r_st64", [128, 4, 6], F32)
            mv4 = sb("r_mv4", [128, 4, 8], F32)
            On = sb("r_On", [128, 4, 128], F32)
            ysb = sb("r_ysb", [128, RT], F32)
            osb = [sb(f"r_osb{i}", [128, RT], BF16) for i in range(2)]
            SU4, UI4, SL4, ID4 = [sb(f"r_msk{i}", [128, 4, 128], F32) for i in range(4)]
            identf = sb("r_identf", [128, 128], F32)
            st6 = sb("r_st6", [128, 6], F32)
            mv = sb("r_mv", [128, 8], F32)
            pP = [ps(f"r_pP{i}", [128, 512], F32) for i in range(2)]
            pX = [ps(f"r_pX{i}", [128, 512], F32) for i in range(5)]
            pS = ps("r_pS", [128, 512], F32)

            def aff(t, pattern, cm, op, base=0):
                S.op(S.pool, lambda e: e.affine_select(out=t, in_=t, pattern=pattern, compare_op=op, fill=0.0,
                                                       base=base, channel_multiplier=cm),
                     reads=["cst"], writes=["cst"])
            S.op(S.pool, lambda e: e.memset(identf[:], 1.0), writes=["cst"])
            aff(identf[:], [[1, 128]], -1, ALU.is_equal)
            for t4 in (SU4, UI4, SL4, ID4):
                S.op(S.pool, lambda e, t4=t4: e.memset(t4[:], 0.0), reads=["cst"], writes=["cst"])
            S.op(S.pool, lambda e: e.memset(blk1[:], 0.0), reads=["cst"], writes=["cst"])
            for hb in range(2):
                rs_ = slice(hb * 64, hb * 64 + 64)
                S.op(S.pool, lambda e: e.memset(blk1[rs_, rs_], 1.0), reads=["cst"], writes=["cst"])
                for c in range(4):
                    for t4 in (SU4, UI4, SL4, ID4):
                        S.op(S.pool, lambda e, t4=t4: e.memset(t4[rs_, c, rs_], 1.0), reads=["cst"], writes=["cst"])
                    aff(SU4[rs_, c, rs_], [[1, 64]], -1, ALU.is_gt)
                    aff(UI4[rs_, c, rs_], [[1, 64]], -1, ALU.is_ge)
                    aff(SL4[rs_, c, rs_], [[-1, 64]], 1, ALU.is_gt)
                    aff(ID4[rs_, c, rs_], [[1, 64]], -1, ALU.is_equal)
            S.op(S.pool, lambda e: e.memset(rmask[:], 1.0), reads=["cst"], writes=["cst"])
            for c in range(4):
                S.op(S.pool, lambda e, c=c: e.memset(rmask[:, c * 64:c * 64 + 1], 0.0), reads=["cst"], writes=["cst"])
            for n in bdn:
                S.op(S.pool, lambda e, n=n: e.memset(BD[n][:], 0.0), writes=[("bd", n)])
            S.op(S.pool, lambda e: e.memset(On[:], 0.0), writes=["On"])

            S.dma("sp", lambda e: e.dma_start(out=mu[:], in_=self.rw_mu), writes=["mu"])
            S.dma("sp", lambda e: e.dma_start(out=vec[:], in_=self.rw_vec), writes=["vec"])
            S.op(S.dve, lambda e: e.tensor_scalar(out=omu[:], in0=mu[:], scalar1=-1.0, scalar2=1.0, op0=ALU.mult,
                                                  op1=ALU.add), reads=["mu"], writes=["omu"])
            S.op(S.dve, lambda e: e.tensor_scalar(out=vec[:, :, 4:5], in0=vec[:, :, 3:4], scalar1=-1.0, scalar2=1.0,
                                                  op0=ALU.mult, op1=ALU.add), reads=["vec"], writes=["vec"])
            S.dma("pool", lambda e: e.dma_start(out=w2[0:64, :], in_=self.rw_w2), writes=["w2"])
            S.dma("pool", lambda e: e.dma_start(out=w2[64:128, :], in_=self.rw_a2), writes=["a2"])
            S.dma("pool", lambda e: e.dma_start(out=g2[:, 0, :], in_=self.rw_g2[0:128, :]), writes=["g2a"])
            S.dma("pool", lambda e: e.dma_start(out=g2[0:32, 1, :], in_=self.rw_g2[128:160, :]), writes=["g2b"])

            def scaled_weights(src3, ncols, dsts_a, dsts_b, jcols):
                S.dma("sp", lambda e: e.dma_start(out=stage[:, :, 0:ncols], in_=src3), writes=["stage"])
                for (c0, c1, j), da, db in zip(jcols, dsts_a, dsts_b):
                    for k in range(8):
                        S.op(S.pool, lambda e, k=k: e.tensor_scalar(out=da[:, k, :], in0=stage[:, k, c0:c1],
                                                                    scalar1=omu[:, k, j:j + 1], scalar2=None,
                                                                    op0=ALU.mult),
                             reads=["stage", "omu"], writes=["wsc"])
                        S.op(S.pool, lambda e, k=k: e.tensor_scalar(out=db[:, k, :], in0=stage[:, k, c0:c1],
                                                                    scalar1=mu[:, k, j:j + 1], scalar2=None,
                                                                    op0=ALU.mult),
                             reads=["stage", "mu"], writes=["wsc"])

            def proj(pout, la, lb, j, M=128, r0=0):
                t0 = j * RT
                ares = [("actT", m) for m in range(max(0, 2 * j - 1), 2 * j + 2)]
                for k in range(8):
                    S.op(S.pe, lambda e, k=k: e.matmul(pout[r0:r0 + M, 0:RT], lhsT=la(k), rhs=actT[:, k, t0:t0 + RT],
                                                       start=(k == 0), stop=False),
                         reads=["wsc"] + ares, writes=["pP"])
                c0 = 1 if j == 0 else 0
                for k in range(8):
                    S.op(S.pe, lambda e, k=k: e.matmul(pout[r0:r0 + M, c0:RT], lhsT=lb(k),
                                                       rhs=actT[:, k, t0 - 1 + c0:t0 + RT - 1],
                                                       start=False, stop=(k == 7)),
                         reads=["wsc"] + ares, writes=["pP"])

            l1src = self.rw_l1.rearrange("(k p) n -> p k n", p=128)
            scaled_weights(l1src[:, :, 0:128], 128, [l1a[:, :, 0:64], l1a[:, :, 64:128]],
                           [l1b[:, :, 0:64], l1b[:, :, 64:128]], [(0, 64, 3), (64, 128, 4)])
            scaled_weights(l1src[:, :, 128:288], 160, [l1a[:, :, 128:288]], [l1b[:, :, 128:288]], [(0, 160, 5)])
            for j in range(NT_):
                tok = slice(j * RT, (j + 1) * RT)
                p0 = pP[j % 2]
                proj(p0, lambda k: l1a[:, k, 0:128], lambda k: l1b[:, k, 0:128], j)
                S.op(S.act, lambda e: e.activation(out=hwa[0:64, tok], in_=p0[0:64, 0:RT], func=AF.Tanh),
                     reads=["pP"], writes=["hwa"])
                S.op(S.act, lambda e: e.copy(out=hwa[64:128, tok], in_=p0[64:128, 0:RT]), reads=["pP"], writes=["hwa"])
                proj(p0, lambda k: l1a[:, k, 128:256], lambda k: l1b[:, k, 128:256], j)
                S.op(S.act, lambda e: e.activation(out=hg[:, 0, tok], in_=p0[:, 0:RT], func=AF.Sigmoid),
                     reads=["pP"], writes=["hg"])
                proj(p0, lambda k: l1a[:, k, 256:288], lambda k: l1b[:, k, 256:288], j, M=32)
                S.op(S.act, lambda e: e.activation(out=hg[0:32, 1, tok], in_=p0[0:32, 0:RT], func=AF.Sigmoid),
                     reads=["pP"], writes=["hg"])

            wsrc = self.rw_wrkv.rearrange("j (k p) n -> j p k n", p=128)
            H0, H1 = slice(0, 64), slice(64, 128)

            def dv(fn, reads, writes, eng=None):
                S.op(eng or S.dve, fn, reads=reads, writes=writes)

            import os
            STOP = int(os.environ.get("RW_STOP", "9"))
            for hp in range(8 if STOP > 0 else 0):
                cols = slice(hp * 128, (hp + 1) * 128)
                for jj in range(3):
                    scaled_weights(wsrc[jj][:, :, cols], 128, [wa[jj]], [wb[jj]], [(0, 128, jj)])
                S.op(S.pool, lambda e: e.memset(Abd[:], 0.0), reads=["Abd"], writes=["Abd"])
                vcol = lambda i: vec[:, hp, i:i + 1]
                for j in range(NT_):
                    tok = slice(j * RT, (j + 1) * RT)
                    F = "F"
                    for jj, dst in enumerate((r_, k_, v_)):
                        p0 = pP[jj % 2]
                        proj(p0, lambda k: wa[jj][:, k, :], lambda k: wb[jj][:, k, :], j)
                        S.op(S.act, lambda e: e.copy(out=dst[:], in_=p0[:, 0:RT]), reads=["pP"], writes=[F])
                    p0 = pP[1]
                    S.op(S.pe, lambda e: e.matmul(p0[:, 0:RT], lhsT=w2[0:64, cols], rhs=hwa[0:64, tok], start=True,
                                                  stop=True), reads=["w2", "hwa"], writes=["pP"])
                    S.op(S.act, lambda e: e.activation(out=lw[:], in_=p0[:, 0:RT], func=AF.Sigmoid, bias=vcol(0)),
                         reads=["pP", "vec"], writes=[F])
                    S.op(S.pe, lambda e: e.matmul(p0[:, 0:RT], lhsT=w2[64:128, cols], rhs=hwa[64:128, tok], start=True,
                                                  stop=True), reads=["a2", "hwa"], writes=["pP"])
                    S.op(S.act, lambda e: e.activation(out=a_[:], in_=p0[:, 0:RT], func=AF.Sigmoid, bias=vcol(1)),
                         reads=["pP", "vec"], writes=[F])
                    S.op(S.pe, lambda e: e.matmul(p0[:, 0:RT], lhsT=g2[:, 0, cols], rhs=hg[:, 0, tok], start=True,
                                                  stop=False), reads=["g2a", "hg"], writes=["pP"])
                    S.op(S.pe, lambda e: e.matmul(p0[:, 0:RT], lhsT=g2[0:32, 1, cols], rhs=hg[0:32, 1, tok], start=False,
                                                  stop=True), reads=["g2b", "hg"], writes=["pP"])
                    S.op(S.act, lambda e: e.copy(out=g_[:], in_=p0[:, 0:RT]), reads=["pP"], writes=[F])
                    dv(lambda e: e.tensor_scalar(out=lw[:], in0=lw[:], scalar1=DEC, scalar2=None, op0=ALU.mult), [F], [F])
                    dv(lambda e: e.tensor_scalar(out=kk[:], in0=k_[:], scalar1=vcol(2), scalar2=None, op0=ALU.mult),
                       [F, "vec"], [F])
                    dv(lambda e: e.tensor_tensor(out=tA[:], in0=kk[:], in1=kk[:], op=ALU.mult), [F], [F], S.pool)
                    S.op(S.pe, lambda e: e.matmul(pP[0][:, 0:RT], lhsT=blk1[:], rhs=tA[:], start=True, stop=True),
                         reads=[F, "cst"], writes=["pP"])
                    S.op(S.act, lambda e: e.activation(out=tA[:], in_=pP[0][:, 0:RT], func=AF.Sqrt), reads=["pP"], writes=[F])
                    dv(lambda e: e.tensor_scalar(out=tA[:], in0=tA[:], scalar1=1e-12, scalar2=None, op0=ALU.max), [F], [F])
                    dv(lambda e: e.reciprocal(out=tA[:], in_=tA[:]), [F], [F])
                    dv(lambda e: e.tensor_tensor(out=kk[:], in0=kk[:], in1=tA[:], op=ALU.mult), [F], [F])
                    dv(lambda e: e.tensor_scalar(out=tB[:], in0=a_[:], scalar1=vcol(3), scalar2=vcol(4), op0=ALU.mult,
                                                 op1=ALU.add), [F, "vec"], [F])
                    dv(lambda e: e.tensor_tensor(out=km[:], in0=k_[:], in1=tB[:], op=ALU.mult), [F], [F])
                    dv(lambda e: e.tensor_tensor(out=be[:], in0=kk[:], in1=a_[:], op=ALU.mult), [F], [F], S.pool)
                    dv(lambda e: e.scalar_tensor_tensor(out=tB[:], in0=r_[:], scalar=vcol(5), in1=km[:], op0=ALU.mult,
                                                        op1=ALU.mult), [F, "vec"], [F])
                    S.op(S.pe, lambda e: e.matmul(pP[0][:, 0:RT], lhsT=blk1[:], rhs=tB[:], start=True, stop=True),
                         reads=[F, "cst"], writes=["pP"])
                    dv(lambda e: e.tensor_tensor(out=tA[:], in0=pP[0][:, 0:RT], in1=v_[:], op=ALU.mult), ["pP", F], [F])
                    dv(lambda e: e.tensor_tensor_scan(out=Lc[:], data0=rmask[:], data1=lw[:], initial=0.0,
                                                      op0=ALU.mult, op1=ALU.add), [F, "cst"], [F])
                    S.op(S.act, lambda e: e.activation(out=eL[:], in_=Lc[:], func=AF.Exp), reads=[F], writes=[F])
                    S.op(S.act, lambda e: e.activation(out=eLn[:], in_=Lc[:], func=AF.Exp, scale=-1.0), reads=[F], writes=[F])
                    dv(lambda e: e.tensor_tensor(out=eLp[:], in0=Lc[:], in1=lw[:], op=ALU.subtract), [F], [F], S.pool)
                    S.op(S.act, lambda e: e.activation(out=eLp[:], in_=eLp[:], func=AF.Exp), reads=[F], writes=[F])
                    L3 = Lc[:].rearrange("p (c t) -> p c t", t=64)
                    dv(lambda e: e.tensor_copy(out=LC[:], in_=L3[:, :, 63]), [F], [F])
                    S.op(S.act, lambda e: e.activation(out=eLC[:], in_=LC[:], func=AF.Exp), reads=[F], writes=[F])
                    for c in range(4):
                        S.op(S.act, lambda e, c=c: e.activation(out=eD[:, c * 64:(c + 1) * 64], in_=Lc[:, c * 64:(c + 1) * 64],
                                                                func=AF.Exp, scale=-1.0, bias=LC[:, c:c + 1]),
                             reads=[F], writes=[F])
                    prods = [("kap", kk, eLp), ("rt", r_, eL), ("kt", km, eLn), ("bt", be, eLn), ("kb", km, eD),
                             ("bb", be, eD)]
                    ie = 0
                    for n, x0, x1 in prods:
                        for hs in (H0, H1):
                            eng = S.dve if ie % 2 == 0 else S.pool
                            ie += 1
                            dv(lambda e: e.tensor_tensor(out=BD[n][hs, :, hs],
                                                         in0=x0[hs, :].rearrange("p (c t) -> p c t", t=64),
                                                         in1=x1[hs, :].rearrange("p (c t) -> p c t", t=64), op=ALU.mult),
                               [F], [("bd", n)], eng)
                    for hs in (H0, H1):
                        S.op(S.act, lambda e: e.copy(out=BD["vf"][hs, :, hs], in_=v_[hs, :].rearrange("p (c t) -> p c t", t=64)),
                             reads=[F], writes=[("bd", "vf")])
                    if STOP < 2:
                        continue
                    for c in range(4):
                        cs = slice(c * 128, (c + 1) * 128)
                        gm = [(0, "kt", "kap"), (1, "kt", "rt"), (2, "bt", "kap"), (3, "bt", "rt"), (4, "kap", "bt")]
                        for pi, ln_, rn_ in gm:
                            S.op(S.pe, lambda e: e.matmul(pX[pi][:, cs], lhsT=BD[ln_][:, c, :], rhs=BD[rn_][:, c, :],
                                                          start=True, stop=True),
                                 reads=[("bd", ln_), ("bd", rn_)], writes=[("pX", pi)])
                    f4 = lambda t: t[:].rearrange("p c t -> p (c t)")
                    dv(lambda e: e.tensor_tensor(out=f4(MkvT), in0=pX[0][:], in1=f4(SU4), op=ALU.mult), [("pX", 0), "cst"], ["MkvT"])
                    dv(lambda e: e.tensor_tensor(out=f4(AkrT), in0=pX[1][:], in1=f4(UI4), op=ALU.mult), [("pX", 1), "cst"], ["AkrT"])
                    dv(lambda e: e.tensor_tensor(out=f4(X[0]), in0=pX[2][:], in1=f4(SU4), op=ALU.mult), [("pX", 2), "cst"], [("X", 0)])
                    dv(lambda e: e.tensor_tensor(out=f4(AbrT), in0=pX[3][:], in1=f4(UI4), op=ALU.mult), [("pX", 3), "cst"], ["AbrT"])
                    dv(lambda e: e.tensor_tensor(out=f4(XT[0]), in0=pX[4][:], in1=f4(SL4), op=ALU.mult), [("pX", 4), "cst"], [("XT", 0)])
                    if STOP < 3:
                        continue
                    dv(lambda e: e.tensor_tensor(out=f4(Y), in0=f4(ID4), in1=f4(X[0]), op=ALU.subtract),
                       [("X", 0), "cst"], ["Y"], S.pool)
                    cur = 0
                    for lvl in range(int(os.environ.get('RW_LVL', '5'))):
                        nxt = 1 - cur
                        last = (lvl == 4)
                        for c in range(4):
                            cs = slice(c * 128, (c + 1) * 128)
                            if not last:
                                S.op(S.pe, lambda e: e.matmul(pX[0][:, cs], lhsT=XT[cur][:, c, :], rhs=X[cur][:, c, :],
                                                              start=True, stop=True),
                                     reads=[("X", cur), ("XT", cur)], writes=[("pX", 0)])
                            S.op(S.pe, lambda e: e.matmul(pX[2][:, cs], lhsT=X[cur][:, c, :], rhs=XT[cur][:, c, :],
                                                          start=True, stop=True),
                                 reads=[("X", cur), ("XT", cur)], writes=[("pX", 2)])
                        if not last:
                            dv(lambda e: e.tensor_copy(out=f4(X[nxt]), in_=pX[0][:]), [("pX", 0)], [("X", nxt)])
                        S.op(S.act, lambda e: e.copy(out=f4(XT[nxt]), in_=pX[2][:]), reads=[("pX", 2)], writes=[("XT", nxt)])
                        for c in range(4):
                            cs = slice(c * 128, (c + 1) * 128)
                            S.op(S.pe, lambda e: e.matmul(pX[1][:, cs], lhsT=XT[nxt][:, c, :],
                                                          rhs=Y[:, c, :], start=True, stop=True),
                                 reads=[("XT", nxt), "Y"], writes=[("pX", 1)])
                        dv(lambda e: e.tensor_tensor(out=f4(Y), in0=f4(Y), in1=pX[1][:], op=ALU.add),
                           [("pX", 1), "Y"], ["Y"])
                        cur = nxt
                    for pi, n, dst in ((3, "vf", Vtok), (4, "kb", Ktok), (1, "bb", Btok)):
                        for c in range(4):
                            S.op(S.pe, lambda e: e.transpose(out=pX[pi][:, c * 128:(c + 1) * 128], in_=BD[n][:, c, :],
                                                             identity=identf[:]),
                                 reads=[("bd", n), "cst"], writes=[("pX", pi)])
                        S.op(S.act, lambda e: e.copy(out=f4(dst), in_=pX[pi][:]), reads=[("pX", pi)], writes=[("tok", n)])
                    if STOP < 4:
                        continue
                    for c in range(4):
                        S.op(S.pe, lambda e: e.matmul(pX[0][:, 0:128], lhsT=BD["kap"][:, c, :], rhs=Abd[:], start=True, stop=False),
                             reads=[("bd", "kap"), "Abd"], writes=["pW"])
                        S.op(S.pe, lambda e: e.matmul(pX[0][:, 0:128], lhsT=MkvT[:, c, :], rhs=Vtok[:, c, :], start=False, stop=True),
                             reads=["MkvT", ("tok", "vf")], writes=["pW"])
                        S.op(S.act, lambda e: e.copy(out=Wsb[:], in_=pX[0][:, 0:128]), reads=["pW"], writes=["Wsb"])
                        S.op(S.pe, lambda e: e.matmul(pX[1][:, 0:128], lhsT=Y[:, c, :], rhs=Wsb[:], start=True, stop=True),
                             reads=["Y", "Wsb"], writes=["pU"])
                        dv(lambda e: e.tensor_scalar(out=nU[:], in0=pX[1][:, 0:128], scalar1=-1.0, scalar2=None, op0=ALU.mult),
                           ["pU"], ["nU"])
                        S.op(S.pe, lambda e: e.matmul(pX[3][:, 0:128], lhsT=Ktok[:, c, :], rhs=Vtok[:, c, :], start=True, stop=False),
                             reads=[("tok", "kb"), ("tok", "vf")], writes=["pA"])
                        S.op(S.pe, lambda e: e.matmul(pX[3][:, 0:128], lhsT=Btok[:, c, :], rhs=nU[:], start=False, stop=True),
                             reads=[("tok", "bb"), "nU"], writes=["pA"])
                        oc = pX[2][:, c * 128:(c + 1) * 128]
                        S.op(S.pe, lambda e: e.matmul(oc, lhsT=BD["rt"][:, c, :], rhs=Abd[:], start=True, stop=False),
                             reads=[("bd", "rt"), "Abd"], writes=["pO"])
                        S.op(S.pe, lambda e: e.matmul(oc, lhsT=AkrT[:, c, :], rhs=Vtok[:, c, :], start=False, stop=False),
                             reads=["AkrT", ("tok", "vf")], writes=["pO"])
                        S.op(S.pe, lambda e: e.matmul(oc, lhsT=AbrT[:, c, :], rhs=nU[:], start=False, stop=True),
                             reads=["AbrT", "nU"], writes=["pO"])
                        dv(lambda e: e.scalar_tensor_tensor(out=Abd[:], in0=Abd[:], scalar=eLC[:, c:c + 1], in1=pX[3][:, 0:128],
                                                            op0=ALU.mult, op1=ALU.add), ["pA", "Abd", F], ["Abd"])
                    S.op(S.act, lambda e: e.copy(out=f4(Osb4), in_=pX[2][:]), reads=["pO"], writes=["Osb"])
                    for c in range(4):
                        for hs in (H0, H1):
                            dv(lambda e: e.bn_stats(out=st64[hs, c, :], in_=Osb4[hs, c, hs]), ["Osb"], ["st6"])
                        dv(lambda e: e.bn_aggr(out=mv4[:, c, 0:2], in_=st64[:, c, :]), ["st6"], ["mv"])
                    dv(lambda e: e.tensor_scalar(out=mv4[:, :, 2:3], in0=mv4[:, :, 1:2], scalar1=GN_EPS, scalar2=None, op0=ALU.add),
                       ["mv"], ["mv"])
                    S.op(S.act, lambda e: e.activation(out=mv4[:, :, 3:4], in_=mv4[:, :, 2:3], func=AF.Sqrt), reads=["mv"], writes=["mv"])
                    dv(lambda e: e.reciprocal(out=mv4[:, :, 4:5], in_=mv4[:, :, 3:4]), ["mv"], ["mv"])
                    dv(lambda e: e.scalar_tensor_tensor(out=mv4[:, :, 5:6], in0=mv4[:, :, 0:1], scalar=-1.0, in1=mv4[:, :, 4:5],
                                                        op0=ALU.mult, op1=ALU.mult), ["mv"], ["mv"])
                    for c in range(4):
                        for hs in (H0, H1):
                            S.op(S.act, lambda e: e.activation(out=On[hs, c, hs], in_=Osb4[hs, c, hs], func=AF.Identity,
                                                               bias=mv4[hs, c, 5:6], scale=mv4[hs, c, 4:5]),
                                 reads=["mv", "Osb"], writes=["On"])
                    for c in range(4):
                        S.op(S.pe, lambda e: e.transpose(out=pP[1][:, c * 128:(c + 1) * 128], in_=On[:, c, :], identity=identf[:]),
                             reads=["On", "cst"], writes=["pP"])
                    for hs in (H0, H1):
                        dv(lambda e: e.tensor_scalar(out=ysb[hs, :].rearrange("p (c t) -> p c t", t=64),
                                                     in0=pP[1][:].rearrange("p (c t) -> p c t", t=128)[hs, :, hs],
                                                     scalar1=vec[hs, hp, 6:7], scalar2=vec[hs, hp, 7:8], op0=ALU.mult,
                                                     op1=ALU.add), ["pP", "vec"], ["ysb"])
                    dv(lambda e: e.tensor_tensor(out=ysb[:], in0=ysb[:], in1=tA[:], op=ALU.add), ["ysb", F], ["ysb"], S.pool)
                    ob = osb[j % 2]
                    dv(lambda e: e.tensor_tensor(out=ob[:], in0=ysb[:], in1=g_[:], op=ALU.mult), ["ysb", F], [("osb", j % 2)], S.pool)
                    S.dma("sp", lambda e: e.dma_start(out=self.oT_d[hp][:, tok], in_=ob[:]), reads=[("osb", j % 2)],
                          writes=[("oTd", hp)])
            S.barrier()
            self.load_oT()
            S.barrier()
        self.out_proj_phase(L, self.rw_wo)


    def rwkv_phase2(self, L):
        import os
        nc, S = self.nc, self.S
        actT = self.actT
        RT = 256
        NCk = RT // 64
        NT_ = T // RT
        GN_EPS = 64e-5
        DEC = -float(np.exp(-0.5))
        H0, H1 = slice(0, 64), slice(64, 128)
        with ExitStack() as ph:
            sb = lambda n, shp, dt: ph.enter_context(nc.sbuf_tensor(self.nm(n), shp, dt))
            ps = lambda n, shp, dt: ph.enter_context(nc.psum_tensor(self.nm(n), shp, dt))
            PB = [[ps(f"r_P{l}{i}", [128, 512], F32) for i in range(4)] for l in range(2)]
            mu = sb("r_mu", [128, 8, 6], F32)
            omu = sb("r_omu", [128, 8, 6], F32)
            vec = sb("r_vec", [128, 8, 8], F32)
            hwa = sb("r_hwa", [128, T], BF16)
            hg = sb("r_hg", [128, 2, T], BF16)
            w2 = sb("r_w2", [128, D], BF16)
            g2 = sb("r_g2", [128, 2, D], BF16)
            stage = sb("r_stage", [128, 8, 128], F32)
            rmask = sb("r_rmask", [128, RT], BF16)
            blk1 = sb("r_blk1", [128, 128], F32)
            identf = sb("r_identf", [128, 128], F32)
            SUm, UIm, SLm, IDm = [sb(f"r_msk{i}", [128, NCk, 128], BF16) for i in range(4)]

            def dv(fn, reads, writes, eng=None):
                S.op(eng or S.dve, fn, reads=reads, writes=writes)

            def aff(t, pattern, cm, op):
                S.op(S.pool, lambda e: e.affine_select(out=t, in_=t, pattern=pattern, compare_op=op, fill=0.0,
                                                       base=0, channel_multiplier=cm), reads=["cst"], writes=["cst"])
            S.op(S.pool, lambda e: e.memset(identf[:], 1.0), writes=["cst"])
            aff(identf[:], [[1, 128]], -1, ALU.is_equal)
            for t4 in (SUm, UIm, SLm, IDm):
                S.op(S.pool, lambda e, t4=t4: e.memset(t4[:], 0.0), reads=["cst"], writes=["cst"])
            S.op(S.pool, lambda e: e.memset(blk1[:], 0.0), reads=["cst"], writes=["cst"])
            for hb in range(2):
                rs_ = slice(hb * 64, hb * 64 + 64)
                S.op(S.pool, lambda e: e.memset(blk1[rs_, rs_], 1.0), reads=["cst"], writes=["cst"])
                for c in range(NCk):
                    for t4 in (SUm, UIm, SLm, IDm):
                        S.op(S.pool, lambda e, t4=t4: e.memset(t4[rs_, c, rs_], 1.0), reads=["cst"], writes=["cst"])
                    aff(SUm[rs_, c, rs_], [[1, 64]], -1, ALU.is_gt)
                    aff(UIm[rs_, c, rs_], [[1, 64]], -1, ALU.is_ge)
                    aff(SLm[rs_, c, rs_], [[-1, 64]], 1, ALU.is_gt)
                    aff(IDm[rs_, c, rs_], [[1, 64]], -1, ALU.is_equal)
            S.op(S.pool, lambda e: e.memset(rmask[:], 1.0), reads=["cst"], writes=["cst"])
            for c in range(NCk):
                S.op(S.pool, lambda e, c=c: e.memset(rmask[:, c * 64:c * 64 + 1], 0.0), reads=["cst"], writes=["cst"])

            S.dma("sp", lambda e: e.dma_start(out=mu[:], in_=self.rw_mu), writes=["mu"])
            S.dma("sp", lambda e: e.dma_start(out=vec[:], in_=self.rw_vec), writes=["vec"])
            dv(lambda e: e.tensor_scalar(out=omu[:], in0=mu[:], scalar1=-1.0, scalar2=1.0, op0=ALU.mult, op1=ALU.add),
               ["mu"], ["omu"])
            dv(lambda e: e.tensor_scalar(out=vec[:, :, 4:5], in0=vec[:, :, 3:4], scalar1=-1.0, scalar2=1.0,
                                         op0=ALU.mult, op1=ALU.add), ["vec"], ["vec"])
            S.dma("pool", lambda e: e.dma_start(out=w2[0:64, :], in_=self.rw_w2), writes=["w2"])
            S.dma("pool", lambda e: e.dma_start(out=w2[64:128, :], in_=self.rw_a2), writes=["a2"])
            S.dma("pool", lambda e: e.dma_start(out=g2[:, 0, :], in_=self.rw_g2[0:128, :]), writes=["g2a"])
            S.dma("pool", lambda e: e.dma_start(out=g2[0:32, 1, :], in_=self.rw_g2[128:160, :]), writes=["g2b"])

            def scaled_weights(src3, ncols, dsts_a, dsts_b, jcols, wres, stage=stage):
                S.dma("sp", lambda e: e.dma_start(out=stage[:, :, 0:ncols], in_=src3), writes=["stage"])
                for (c0, c1, j), da, db in zip(jcols, dsts_a, dsts_b):
                    for k in range(8):
                        S.op(S.pool, lambda e, k=k: e.tensor_scalar(out=da[:, k, :], in0=stage[:, k, c0:c1],
                                                                    scalar1=omu[:, k, j:j + 1], scalar2=None,
                                                                    op0=ALU.mult),
                             reads=["stage", "omu"], writes=[wres])
                        S.op(S.pool, lambda e, k=k: e.tensor_scalar(out=db[:, k, :], in0=stage[:, k, c0:c1],
                                                                    scalar1=mu[:, k, j:j + 1], scalar2=None,
                                                                    op0=ALU.mult),
                             reads=["stage", "mu"], writes=[wres])

            def proj(pout, pres, la, lb, t0, n_tok, wres, M=128, xs=None, xres=None):
                if xs is not None:
                    for k in range(8):
                        S.op(S.pe, lambda e, k=k: e.matmul(pout[0:M, 0:n_tok], lhsT=la(k), rhs=xs[:, k, 1:n_tok + 1],
                                                           start=(k == 0), stop=False),
                             reads=[wres, xres], writes=[pres])
                    for k in range(8):
                        S.op(S.pe, lambda e, k=k: e.matmul(pout[0:M, 0:n_tok], lhsT=lb(k), rhs=xs[:, k, 0:n_tok],
                                                           start=False, stop=(k == 7)),
                             reads=[wres, xres], writes=[pres])
                    return
                ares = [("actT", m) for m in range(max(0, t0 // 128 - 1), (t0 + n_tok - 1) // 128 + 1)]
                for k in range(8):
                    S.op(S.pe, lambda e, k=k: e.matmul(pout[0:M, 0:n_tok], lhsT=la(k), rhs=actT[:, k, t0:t0 + n_tok],
                                                       start=(k == 0), stop=False),
                         reads=[wres] + ares, writes=[pres])
                c0 = 1 if t0 == 0 else 0
                for k in range(8):
                    S.op(S.pe, lambda e, k=k: e.matmul(pout[0:M, c0:n_tok], lhsT=lb(k),
                                                       rhs=actT[:, k, t0 - 1 + c0:t0 + n_tok - 1],
                                                       start=False, stop=(k == 7)),
                         reads=[wres] + ares, writes=[pres])

            with ExitStack() as ph1:
                sb1 = lambda n, shp, dt: ph1.enter_context(nc.sbuf_tensor(self.nm(n), shp, dt))
                l1a = sb1("r_l1a", [128, 8, 288], BF16)
                l1b = sb1("r_l1b", [128, 8, 288], BF16)
                stage1 = sb1("r_stage1", [128, 8, 160], F32)
                l1src = self.rw_l1.rearrange("(k p) n -> p k n", p=128)
                scaled_weights(l1src[:, :, 0:128], 128, [l1a[:, :, 0:64], l1a[:, :, 64:128]],
                               [l1b[:, :, 0:64], l1b[:, :, 64:128]], [(0, 64, 3), (64, 128, 4)], "l1", stage=stage1)
                scaled_weights(l1src[:, :, 128:288], 160, [l1a[:, :, 128:288]], [l1b[:, :, 128:288]], [(0, 160, 5)], "l1",
                               stage=stage1)
                R1T = 256
                for j in range(T // R1T):
                    tok = slice(j * R1T, (j + 1) * R1T)
                    b0, b1, b2 = PB[0][j % 2], PB[0][2 + j % 2], PB[1][j % 2]
                    r0_, r1_, r2_ = ("P", 0, j % 2), ("P", 0, 2 + j % 2), ("P", 1, j % 2)
                    proj(b0, r0_, lambda k: l1a[:, k, 0:128], lambda k: l1b[:, k, 0:128], j * R1T, R1T, "l1")
                    S.op(S.act, lambda e: e.activation(out=hwa[0:64, tok], in_=b0[0:64, 0:R1T], func=AF.Tanh),
                         reads=[r0_], writes=["hwa"])
                    S.op(S.act, lambda e: e.copy(out=hwa[64:128, tok], in_=b0[64:128, 0:R1T]), reads=[r0_], writes=["hwa"])
                    proj(b1, r1_, lambda k: l1a[:, k, 128:256], lambda k: l1b[:, k, 128:256], j * R1T, R1T, "l1")
                    S.op(S.act, lambda e: e.activation(out=hg[:, 0, tok], in_=b1[:, 0:R1T], func=AF.Sigmoid),
                         reads=[r1_], writes=["hg"])
                    proj(b2, r2_, lambda k: l1a[:, k, 256:288], lambda k: l1b[:, k, 256:288], j * R1T, R1T, "l1", M=32)
                    S.op(S.act, lambda e: e.activation(out=hg[0:32, 1, tok], in_=b2[0:32, 0:R1T], func=AF.Sigmoid),
                         reads=[r2_], writes=["hg"])
                for c in range(8):
                    S.dma("sp", lambda e, c=c: e.dma_start(out=self.xT_d[c], in_=actT[:, c, :]), writes=[("xTd", c)])
                S.barrier()

            flat = actT[:].rearrange("p k t -> p (k t)")
            arena = {"off": 0}

            def carve(shape, dt):
                n = int(np.prod(shape[1:]))
                nb = n * (2 if dt == F32 else 1)
                assert arena["off"] + nb <= 8 * T, "lane-1 arena overflow"
                v = flat[:, arena["off"]:arena["off"] + nb]
                arena["off"] += nb
                if dt == F32:
                    v = v.bitcast(F32)
                if len(shape) == 3:
                    v = v.rearrange("p (a b) -> p a b", b=shape[2])
                return v

            class _T:
                def __init__(self, ap):
                    self.ap = ap

                def __getitem__(self, key):
                    return self.ap[key]

            bdn = ["kap", "rt", "kt", "bt", "vf", "kb", "bb"]
            lanes = []
            for l in range(2):
                Bf = {}

                def lb_(name, shape, dt, l=l, small=False):
                    if l == 0 or small:
                        return sb(f"r{l}_{name}", shape, dt)
                    return _T(carve(shape, dt))
                for n in ["r_", "k_", "v_", "lw", "a_", "g_", "kk", "km", "be", "Lc", "eL", "eLn", "eLp", "eD", "tA", "tB", "ysb"]:
                    Bf[n] = lb_(n, [128, RT], F32)
                Bf["LC"] = lb_("LC", [128, NCk], F32, small=True)
                Bf["eLC"] = lb_("eLC", [128, NCk], F32, small=True)
                Bf["BD"] = {n: lb_("bd_" + n, [128, NCk, 128], F32) for n in bdn}
                for n in ["MkvT", "AkrT", "AbrT", "Y", "X0", "X1", "XT0", "XT1", "Vtok", "Ktok", "Btok", "Osb4", "On"]:
                    Bf[n] = lb_(n, [128, NCk, 128], F32, small=(n in ("Osb4", "On", "Btok")))
                for n in ["Wsb", "nU", "Abd"]:
                    Bf[n] = lb_(n, [128, 128], F32, small=True)
                Bf["osb"] = [lb_(f"osb{i}", [128, RT], BF16, small=True) for i in range(2)]
                Bf["st64"] = lb_("st64", [128, NCk, 6], F32, small=True)
                Bf["mv4"] = lb_("mv4", [128, NCk, 8], F32, small=True)
                Bf["wa"] = [lb_(f"wa{i}", [128, 8, 128], BF16) for i in range(3)]
                Bf["wb"] = [lb_(f"wb{i}", [128, 8, 128], BF16) for i in range(3)]
                Bf["xs"] = lb_("xs", [128, 8, RT + 1], BF16, small=True)
                for n in bdn:
                    S.op(S.pool, lambda e, n=n: e.memset(Bf["BD"][n][:], 0.0), writes=[(l, "bd", n)])
                S.op(S.pool, lambda e: e.memset(Bf["On"][:], 0.0), writes=[(l, "On")])
                lanes.append(Bf)

            wsrc = self.rw_wrkv.rearrange("j (k p) n -> j p k n", p=128)
            f4 = lambda t: t[:].rearrange("p c t -> p (c t)")
            W_ = NCk * 128

            xsrc = self.xT_d.rearrange("c p t -> p c t")

            def load_xs(l, j):
                xs = lanes[l]["xs"]
                t0 = j * RT
                if j == 0:
                    S.op(S.pool, lambda e: e.memset(xs[:, :, 0:1], 0.0), writes=[(l, "xs")])
                    S.dma("sp", lambda e: e.dma_start(out=xs[:, :, 1:RT + 1], in_=xsrc[:, :, 0:RT]),
                          reads=[("xTd", c) for c in range(8)], writes=[(l, "xs")])
                else:
                    S.dma("sp", lambda e: e.dma_start(out=xs[:, :, :], in_=xsrc[:, :, t0 - 1:t0 + RT]),
                          reads=[("xTd", c) for c in range(8)], writes=[(l, "xs")])

            def unit(l, hp, j):
                Bf = lanes[l]
                P = PB[l]
                pr = lambda i: ("P", l, i)
                R = lambda n: (l, n)
                F = (l, "F")
                cols = slice(hp * 128, (hp + 1) * 128)
                tok = slice(j * RT, (j + 1) * RT)
                vcol = lambda i: vec[:, hp, i:i + 1]
                r_, k_, v_, lw, a_, g_, kk, km, be, Lc, eL, eLn, eLp, eD, tA, tB, ysb = [
                    Bf[n] for n in ["r_", "k_", "v_", "lw", "a_", "g_", "kk", "km", "be", "Lc", "eL", "eLn", "eLp", "eD", "tA", "tB", "ysb"]]
                LC, eLC, BD = Bf["LC"], Bf["eLC"], Bf["BD"]
                MkvT, AkrT, AbrT, Y, Vtok, Ktok, Btok, Osb4, On = [Bf[n] for n in ["MkvT", "AkrT", "AbrT", "Y", "Vtok", "Ktok", "Btok", "Osb4", "On"]]
                X, XT = [Bf["X0"], Bf["X1"]], [Bf["XT0"], Bf["XT1"]]
                Wsb, nU, Abd, st64, mv4 = Bf["Wsb"], Bf["nU"], Bf["Abd"], Bf["st64"], Bf["mv4"]
                wa, wb = Bf["wa"], Bf["wb"]
                wres = R("wsc")
                for jj, dst in enumerate((r_, k_, v_)):
                    proj(P[jj], pr(jj), lambda k: wa[jj][:, k, :], lambda k: wb[jj][:, k, :], j * RT, RT, wres,
                         xs=Bf["xs"], xres=R("xs"))
                    S.op(S.act, lambda e: e.copy(out=dst[:], in_=P[jj][:, 0:RT]), reads=[pr(jj)], writes=[F])
                if j + 1 < NT_:
                    load_xs(l, j + 1)
                yield
                S.op(S.pe, lambda e: e.matmul(P[3][:, 0:RT], lhsT=w2[0:64, cols], rhs=hwa[0:64, tok], start=True, stop=True),
                     reads=["w2", "hwa"], writes=[pr(3)])
                S.op(S.act, lambda e: e.activation(out=lw[:], in_=P[3][:, 0:RT], func=AF.Sigmoid, bias=vcol(0)),
                     reads=[pr(3), "vec"], writes=[F])
                S.op(S.pe, lambda e: e.matmul(P[0][:, 0:RT], lhsT=w2[64:128, cols], rhs=hwa[64:128, tok], start=True, stop=True),
                     reads=["a2", "hwa"], writes=[pr(0)])
                S.op(S.act, lambda e: e.activation(out=a_[:], in_=P[0][:, 0:RT], func=AF.Sigmoid, bias=vcol(1)),
                     reads=[pr(0), "vec"], writes=[F])
                S.op(S.pe, lambda e: e.matmul(P[1][:, 0:RT], lhsT=g2[:, 0, cols], rhs=hg[:, 0, tok], start=True, stop=False),
                     reads=["g2a", "hg"], writes=[pr(1)])
                S.op(S.pe, lambda e: e.matmul(P[1][:, 0:RT], lhsT=g2[0:32, 1, cols], rhs=hg[0:32, 1, tok], start=False, stop=True),
                     reads=["g2b", "hg"], writes=[pr(1)])
                S.op(S.act, lambda e: e.copy(out=g_[:], in_=P[1][:, 0:RT]), reads=[pr(1)], writes=[F])
                yield
                dv(lambda e: e.tensor_scalar(out=lw[:], in0=lw[:], scalar1=DEC, scalar2=None, op0=ALU.mult), [F], [F])
                dv(lambda e: e.tensor_scalar(out=kk[:], in0=k_[:], scalar1=vcol(2), scalar2=None, op0=ALU.mult), [F, "vec"], [F])
                dv(lambda e: e.tensor_tensor(out=tA[:], in0=kk[:], in1=kk[:], op=ALU.mult), [F], [F], S.pool)
                S.op(S.pe, lambda e: e.matmul(P[2][:, 0:RT], lhsT=blk1[:], rhs=tA[:], start=True, stop=True),
                     reads=[F, "cst"], writes=[pr(2)])
                yield
                S.op(S.act, lambda e: e.activation(out=tA[:], in_=P[2][:, 0:RT], func=AF.Sqrt), reads=[pr(2)], writes=[F])
                dv(lambda e: e.tensor_scalar(out=tA[:], in0=tA[:], scalar1=1e-12, scalar2=None, op0=ALU.max), [F], [F])
                dv(lambda e: e.reciprocal(out=tA[:], in_=tA[:]), [F], [F])
                dv(lambda e: e.tensor_tensor(out=kk[:], in0=kk[:], in1=tA[:], op=ALU.mult), [F], [F])
                yield
                dv(lambda e: e.tensor_scalar(out=tB[:], in0=a_[:], scalar1=vcol(3), scalar2=vcol(4), op0=ALU.mult, op1=ALU.add),
                   [F, "vec"], [F])
                dv(lambda e: e.tensor_tensor(out=km[:], in0=k_[:], in1=tB[:], op=ALU.mult), [F], [F])
                dv(lambda e: e.tensor_tensor(out=be[:], in0=kk[:], in1=a_[:], op=ALU.mult), [F], [F], S.pool)
                dv(lambda e: e.scalar_tensor_tensor(out=tB[:], in0=r_[:], scalar=vcol(5), in1=km[:], op0=ALU.mult, op1=ALU.mult),
                   [F, "vec"], [F])
                S.op(S.pe, lambda e: e.matmul(P[3][:, 0:RT], lhsT=blk1[:], rhs=tB[:], start=True, stop=True),
                     reads=[F, "cst"], writes=[pr(3)])
                yield
                dv(lambda e: e.tensor_tensor(out=tA[:], in0=P[3][:, 0:RT], in1=v_[:], op=ALU.mult), [pr(3), F], [R("tA")])
                dv(lambda e: e.tensor_tensor_scan(out=Lc[:], data0=rmask[:], data1=lw[:], initial=0.0, op0=ALU.mult, op1=ALU.add),
                   [F, "cst"], [F])
                S.op(S.act, lambda e: e.activation(out=eL[:], in_=Lc[:], func=AF.Exp), reads=[F], writes=[F])
                S.op(S.act, lambda e: e.activation(out=eLn[:], in_=Lc[:], func=AF.Exp, scale=-1.0), reads=[F], writes=[F])
                dv(lambda e: e.tensor_tensor(out=eLp[:], in0=Lc[:], in1=lw[:], op=ALU.subtract), [F], [F], S.pool)
                S.op(S.act, lambda e: e.activation(out=eLp[:], in_=eLp[:], func=AF.Exp), reads=[F], writes=[F])
                L3 = Lc[:].rearrange("p (c t) -> p c t", t=64)
                dv(lambda e: e.tensor_copy(out=LC[:], in_=L3[:, :, 63]), [F], [F])
                S.op(S.act, lambda e: e.activation(out=eLC[:], in_=LC[:], func=AF.Exp), reads=[F], writes=[R("eLC")])
                for c in range(NCk):
                    S.op(S.act, lambda e, c=c: e.activation(out=eD[:, c * 64:(c + 1) * 64], in_=Lc[:, c * 64:(c + 1) * 64],
                                                            func=AF.Exp, scale=-1.0, bias=LC[:, c:c + 1]), reads=[F], writes=[F])
                yield
                prods = [("kap", kk, eLp), ("rt", r_, eL), ("kt", km, eLn), ("bt", be, eLn), ("kb", km, eD), ("bb", be, eD)]
                ie = 0
                for n, x0, x1 in prods:
                    for hs in (H0, H1):
                        eng = S.dve if ie % 2 == 0 else S.pool
                        ie += 1
                        dv(lambda e: e.tensor_tensor(out=BD[n][hs, :, hs], in0=x0[hs, :].rearrange("p (c t) -> p c t", t=64),
                                                     in1=x1[hs, :].rearrange("p (c t) -> p c t", t=64), op=ALU.mult),
                           [F], [R(("bd", n))], eng)
                for hs in (H0, H1):
                    S.op(S.act, lambda e: e.copy(out=BD["vf"][hs, :, hs], in_=v_[hs, :].rearrange("p (c t) -> p c t", t=64)),
                         reads=[F], writes=[R(("bd", "vf"))])
                yield
                gm = [(0, "kt", "kap"), (1, "kt", "rt"), (2, "bt", "kap"), (3, "bt", "rt")]
                for c in range(NCk):
                    for pi, ln_, rn_ in gm:
                        S.op(S.pe, lambda e: e.matmul(P[pi][:, c * 128:(c + 1) * 128], lhsT=BD[ln_][:, c, :],
                                                      rhs=BD[rn_][:, c, :], start=True, stop=True),
                             reads=[R(("bd", ln_)), R(("bd", rn_))], writes=[pr(pi)])
                yield
                dv(lambda e: e.tensor_tensor(out=f4(MkvT), in0=P[0][:, 0:W_], in1=f4(SUm), op=ALU.mult), [pr(0), "cst"], [R("MkvT")])
                dv(lambda e: e.tensor_tensor(out=f4(X[0]), in0=P[2][:, 0:W_], in1=f4(SUm), op=ALU.mult), [pr(2), "cst"], [R(("X", 0))])
                for c in range(NCk):
                    S.op(S.pe, lambda e: e.matmul(P[0][:, c * 128:(c + 1) * 128], lhsT=BD["kap"][:, c, :],
                                                  rhs=BD["bt"][:, c, :], start=True, stop=True),
                         reads=[R(("bd", "kap")), R(("bd", "bt"))], writes=[pr(0)])
                dv(lambda e: e.tensor_tensor(out=f4(AkrT), in0=P[1][:, 0:W_], in1=f4(UIm), op=ALU.mult), [pr(1), "cst"], [R("AkrT")])
                dv(lambda e: e.tensor_tensor(out=f4(AbrT), in0=P[3][:, 0:W_], in1=f4(UIm), op=ALU.mult), [pr(3), "cst"], [R("AbrT")])
                dv(lambda e: e.tensor_tensor(out=f4(Y), in0=f4(IDm), in1=f4(X[0]), op=ALU.subtract), [R(("X", 0)), "cst"], [R("Y")], S.pool)
                yield
                dv(lambda e: e.tensor_tensor(out=f4(XT[0]), in0=P[0][:, 0:W_], in1=f4(SLm), op=ALU.mult), [pr(0), "cst"], [R(("XT", 0))])
                yield
                cur = 0
                for lvl in range(5):
                    nxt = 1 - cur
                    last = (lvl == 4)
                    for c in range(NCk):
                        cs = slice(c * 128, (c + 1) * 128)
                        if not last:
                            S.op(S.pe, lambda e: e.matmul(P[0][:, cs], lhsT=XT[cur][:, c, :], rhs=X[cur][:, c, :], start=True, stop=True),
                                 reads=[R(("X", cur)), R(("XT", cur))], writes=[pr(0)])
                        S.op(S.pe, lambda e: e.matmul(P[1][:, cs], lhsT=X[cur][:, c, :], rhs=XT[cur][:, c, :], start=True, stop=True),
                             reads=[R(("X", cur)), R(("XT", cur))], writes=[pr(1)])
                    yield
                    if not last:
                        dv(lambda e: e.tensor_copy(out=f4(X[nxt]), in_=P[0][:, 0:W_]), [pr(0)], [R(("X", nxt))])
                    S.op(S.act, lambda e: e.copy(out=f4(XT[nxt]), in_=P[1][:, 0:W_]), reads=[pr(1)], writes=[R(("XT", nxt))])
                    for c in range(NCk):
                        cs = slice(c * 128, (c + 1) * 128)
                        S.op(S.pe, lambda e: e.matmul(P[2][:, cs], lhsT=XT[nxt][:, c, :], rhs=Y[:, c, :], start=True, stop=True),
                             reads=[R(("XT", nxt)), R("Y")], writes=[pr(2)])
                    yield
                    dv(lambda e: e.tensor_tensor(out=f4(Y), in0=f4(Y), in1=P[2][:, 0:W_], op=ALU.add), [pr(2), R("Y")], [R("Y")])
                    cur = nxt
                for pi, n, dst in ((3, "vf", Vtok), (0, "kb", Ktok), (1, "bb", Btok)):
                    for c in range(NCk):
                        S.op(S.pe, lambda e: e.transpose(out=P[pi][:, c * 128:(c + 1) * 128], in_=BD[n][:, c, :], identity=identf[:]),
                             reads=[R(("bd", n)), "cst"], writes=[pr(pi)])
                    S.op(S.act, lambda e: e.copy(out=f4(dst), in_=P[pi][:, 0:W_]), reads=[pr(pi)], writes=[R(("tok", n))])
                yield
                for c in range(NCk):
                    S.op(S.pe, lambda e: e.matmul(P[3][:, 0:128], lhsT=BD["kap"][:, c, :], rhs=Abd[:], start=True, stop=False),
                         reads=[R(("bd", "kap")), R("Abd")], writes=[pr(3)])
                    S.op(S.pe, lambda e: e.matmul(P[3][:, 0:128], lhsT=MkvT[:, c, :], rhs=Vtok[:, c, :], start=False, stop=True),
                         reads=[R("MkvT"), R(("tok", "vf"))], writes=[pr(3)])
                    S.op(S.act, lambda e: e.copy(out=Wsb[:], in_=P[3][:, 0:128]), reads=[pr(3)], writes=[R("Wsb")])
                    yield
                    S.op(S.pe, lambda e: e.matmul(P[0][:, 0:128], lhsT=Y[:, c, :], rhs=Wsb[:], start=True, stop=True),
                         reads=[R("Y"), R("Wsb")], writes=[pr(0)])
                    dv(lambda e: e.tensor_scalar(out=nU[:], in0=P[0][:, 0:128], scalar1=-1.0, scalar2=None, op0=ALU.mult),
                       [pr(0)], [R("nU")])
                    yield
                    S.op(S.pe, lambda e: e.matmul(P[2][:, 0:128], lhsT=Ktok[:, c, :], rhs=Vtok[:, c, :], start=True, stop=False),
                         reads=[R(("tok", "kb")), R(("tok", "vf"))], writes=[pr(2)])
                    S.op(S.pe, lambda e: e.matmul(P[2][:, 0:128], lhsT=Btok[:, c, :], rhs=nU[:], start=False, stop=True),
                         reads=[R(("tok", "bb")), R("nU")], writes=[pr(2)])
                    oc = P[1][:, c * 128:(c + 1) * 128]
                    S.op(S.pe, lambda e: e.matmul(oc, lhsT=BD["rt"][:, c, :], rhs=Abd[:], start=True, stop=False),
                         reads=[R(("bd", "rt")), R("Abd")], writes=[pr(1)])
                    S.op(S.pe, lambda e: e.matmul(oc, lhsT=AkrT[:, c, :], rhs=Vtok[:, c, :], start=False, stop=False),
                         reads=[R("AkrT"), R(("tok", "vf"))], writes=[pr(1)])
                    S.op(S.pe, lambda e: e.matmul(oc, lhsT=AbrT[:, c, :], rhs=nU[:], start=False, stop=True),
                         reads=[R("AbrT"), R("nU")], writes=[pr(1)])
                    dv(lambda e: e.scalar_tensor_tensor(out=Abd[:], in0=Abd[:], scalar=eLC[:, c:c + 1], in1=P[2][:, 0:128],
                                                        op0=ALU.mult, op1=ALU.add), [pr(2), R("Abd"), R("eLC")], [R("Abd")])
                    yield
                S.op(S.act, lambda e: e.copy(out=f4(Osb4), in_=P[1][:, 0:W_]), reads=[pr(1)], writes=[R("Osb")])
                for c in range(NCk):
                    for hs in (H0, H1):
                        dv(lambda e: e.bn_stats(out=st64[hs, c, :], in_=Osb4[hs, c, hs]), [R("Osb")], [R("st6")])
                    dv(lambda e: e.bn_aggr(out=mv4[:, c, 0:2], in_=st64[:, c, :]), [R("st6")], [R("mv")])
                dv(lambda e: e.tensor_scalar(out=mv4[:, :, 2:3], in0=mv4[:, :, 1:2], scalar1=GN_EPS, scalar2=None, op0=ALU.add),
                   [R("mv")], [R("mv")])
                yield
                S.op(S.act, lambda e: e.activation(out=mv4[:, :, 3:4], in_=mv4[:, :, 2:3], func=AF.Sqrt), reads=[R("mv")], writes=[R("mv")])
                dv(lambda e: e.reciprocal(out=mv4[:, :, 4:5], in_=mv4[:, :, 3:4]), [R("mv")], [R("mv")])
                dv(lambda e: e.scalar_tensor_tensor(out=mv4[:, :, 5:6], in0=mv4[:, :, 0:1], scalar=-1.0, in1=mv4[:, :, 4:5],
                                                    op0=ALU.mult, op1=ALU.mult), [R("mv")], [R("mv")])
                yield
                for c in range(NCk):
                    for hs in (H0, H1):
                        S.op(S.act, lambda e: e.activation(out=On[hs, c, hs], in_=Osb4[hs, c, hs], func=AF.Identity,
                                                           bias=mv4[hs, c, 5:6], scale=mv4[hs, c, 4:5]),
                             reads=[R("mv"), R("Osb")], writes=[R("On")])
                for c in range(NCk):
                    S.op(S.pe, lambda e: e.transpose(out=P[3][:, c * 128:(c + 1) * 128], in_=On[:, c, :], identity=identf[:]),
                         reads=[R("On"), "cst"], writes=[pr(3)])
                yield
                for hs in (H0, H1):
                    dv(lambda e: e.tensor_scalar(out=ysb[hs, :].rearrange("p (c t) -> p c t", t=64),
                                                 in0=P[3][:, 0:W_].rearrange("p (c t) -> p c t", t=128)[hs, :, hs],
                                                 scalar1=vec[hs, hp, 6:7], scalar2=vec[hs, hp, 7:8], op0=ALU.mult,
                                                 op1=ALU.add), [pr(3), "vec"], [R("ysb")])
                dv(lambda e: e.tensor_tensor(out=ysb[:], in0=ysb[:], in1=tA[:], op=ALU.add), [R("ysb"), R("tA")], [R("ysb")], S.pool)
                ob = Bf["osb"][j % 2]
                dv(lambda e: e.tensor_tensor(out=ob[:], in0=ysb[:], in1=g_[:], op=ALU.mult), [R("ysb"), F], [R(("osb", j % 2))], S.pool)
                S.dma("sp", lambda e: e.dma_start(out=self.oT_d[hp][:, tok], in_=ob[:]), reads=[R(("osb", j % 2))],
                      writes=[("oTd", hp)])
                yield

            def lane_gen(l):
                Bf = lanes[l]
                for hp in range(l, 8, 2):
                    cols = slice(hp * 128, (hp + 1) * 128)
                    for jj in range(3):
                        scaled_weights(wsrc[jj][:, :, cols], 128, [Bf["wa"][jj]], [Bf["wb"][jj]], [(0, 128, jj)], (l, "wsc"))
                    S.op(S.pool, lambda e: e.memset(Bf["Abd"][:], 0.0), reads=[(l, "Abd")], writes=[(l, "Abd")])
                    load_xs(l, 0)
                    yield
                    for j in range(NT_):
                        yield from unit(l, hp, j)

            gens = [lane_gen(0), lane_gen(1)]
            alive = [True, True]
            for _ in range(int(os.environ.get("RW_OFF", "14"))):
                next(gens[0])
            while any(alive):
                for l in range(2):
                    if alive[l]:
                        try:
                            next(gens[l])
                        except StopIteration:
                            alive[l] = False
            S.barrier()
            self.load_oT()
            S.barrier()
        self.out_proj_phase(L, self.rw_wo)

def host_layout(inp):
    out = {}
    cw = np.zeros((DEPTH, NCH * 128, 4), np.float32)
    cw[:, :D_FF, 0:3] = np.transpose(inp["ffn_conv_w"], (0, 2, 1))
    cw[:, :D_FF, 3] = inp["ffn_conv_b"]
    out["ffn_cw"] = np.ascontiguousarray(cw.reshape(DEPTH, NCH, 128, 4).transpose(0, 2, 1, 3))
    for k in ("ln_g", "ln_b", "ffn_w_in", "ffn_w_out"):
        out[k] = np.ascontiguousarray(inp[k], dtype=np.float32)
    for k in ("dil_w_qkv", "dil_w_o"):
        out[k] = np.ascontiguousarray(inp[k][0], dtype=np.float32)
    fm = lambda v: np.ascontiguousarray(np.asarray(v, np.float32).reshape(8, 128).T)
    out["rw_mu"] = np.ascontiguousarray(inp["rwkv_mu"][0].reshape(6, 8, 128).transpose(2, 1, 0))
    ka = inp["rwkv_k_a"][0]
    vecs = [inp["rwkv_w0"][0], inp["rwkv_a0"][0], inp["rwkv_k_k"][0], ka, None, inp["rwkv_r_k"][0].reshape(-1),
            inp["rwkv_ln_w"][0], inp["rwkv_ln_b"][0]]
    rv = np.zeros((128, 8, 8), np.float32)
    for i, v in enumerate(vecs):
        if v is not None:
            rv[:, :, i] = fm(v)
    out["rw_vec"] = rv
    out["rwkv_w_rkv"] = np.ascontiguousarray(inp["rwkv_w_rkv"][0], dtype=np.float32)
    out["rw_l1"] = np.ascontiguousarray(np.concatenate([inp["rwkv_w1"][0], inp["rwkv_a1"][0], inp["rwkv_g1"][0]], axis=1))
    for k in ("rwkv_w2", "rwkv_a2", "rwkv_g2", "rwkv_w_o"):
        out[k] = np.ascontiguousarray(inp[k][0], dtype=np.float32)
    wd = inp["mla_w_down"]
    out["mla_wd"] = np.ascontiguousarray(np.concatenate(
        [wd[:, :, 0:640], wd[:, :, 0:64], wd[:, :, 640:672], wd[:, :, 656:672], wd[:, :, 640:656]], axis=2))
    out["mla_qn"] = np.ascontiguousarray(inp["mla_q_norm"].reshape(-1, 3, 128).transpose(0, 2, 1))
    out["mla_kvn"] = np.ascontiguousarray(inp["mla_kv_norm"].reshape(-1, 2, 128).transpose(0, 2, 1))
    wq = inp["mla_w_uq"].reshape(-1, 384, 16, 96)
    out["mla_wuq"] = np.ascontiguousarray(np.concatenate(
        [wq[..., 0:96], wq[..., 80:96], wq[..., 64:80]], axis=3).reshape(-1, 384, 2048))
    wkv = inp["mla_w_ukv"].reshape(-1, 256, 16, 128)
    out["mla_wukv"] = np.ascontiguousarray(np.concatenate(
        [wkv[..., 0:64].reshape(-1, 256, 1024), wkv[..., 64:128].reshape(-1, 256, 1024)], axis=2))
    out["mla_wo"] = np.ascontiguousarray(inp["mla_w_o"], dtype=np.float32)
    rc = np.zeros((96, 2), np.float32)
    invf = (10000.0 ** (-np.arange(0, 32, 2, dtype=np.float32) / np.float32(32))).astype(np.float32)
    rc[64:80, 0] = invf / np.float32(2 * np.pi)
    rc[80:96, 0] = invf / np.float32(2 * np.pi)
    rc[64:80, 1] = -1.0
    rc[80:96, 1] = 1.0
    out["rope_c"] = rc
    return out


DEFAULT_PLAN = [("mla", 0), ("ffn", 0), ("dil", 1), ("ffn", 1), ("rwkv", 2), ("ffn", 2), ("mla", 3), ("ffn", 3)]
_CACHE = {}


def run(inputs, plan, n_cores=8, trace=False):
    key = tuple(plan)
    if key not in _CACHE:
        b = Builder(plan)
        nc = b.build()
        _CACHE[key] = (b, nc)
    b, nc = _CACHE[key]
    shared = host_layout(inputs)
    in_maps = []
    for c in range(n_cores):
        d = {"x": np.ascontiguousarray(inputs["x"][c], dtype=np.float32),
             "positions": np.ascontiguousarray(inputs["positions"][c], dtype=np.int32)}
        d.update(shared)
        d = {k: v for k, v in d.items() if k in b.din}
        in_maps.append(d)
    res = run_bass_kernel_spmd(nc, in_maps, core_ids=list(range(n_cores)), trace=trace)
    return np.stack([r["out"] for r in res.results], axis=0), res


def kernel(**inputs):
    out, _ = run(inputs, DEFAULT_PLAN)
    return out.astype(np.float32)
```

```python
import numpy as np
from contextlib import ExitStack
import concourse.bass as bass
import concourse.mybir as mybir
from concourse.bass_utils import run_bass_kernel_spmd

F32 = mybir.dt.float32
BF16 = mybir.dt.bfloat16
I32 = mybir.dt.int32
AF = mybir.ActivationFunctionType
ALU = mybir.AluOpType

T = 4096
D = 1024
DEPTH = 4
NB = T // 128
ALPHA = (2 * DEPTH) ** 0.25
LN_EPS = 1e-5
RMS_EPS = 1e-6
D_FF = 2752
NCH = 22


class _Eng:
    def __init__(self, name, eng, sem):
        self.name = name
        self.eng = eng
        self.sem = sem
        self.count = 0
        self.waited = {}


class Sched:
    def __init__(self, nc, stack, n_dma_sems=16):
        self.nc = nc
        mk = lambda n: stack.enter_context(nc.semaphore(n))
        self.pe = _Eng("pe", nc.tensor, mk("s_pe"))
        self.act = _Eng("act", nc.scalar, mk("s_act"))
        self.dve = _Eng("dve", nc.vector, mk("s_dve"))
        self.pool = _Eng("pool", nc.gpsimd, mk("s_pool"))
        self.sp = _Eng("sp", nc.sync, None)
        self.q = {"sp": [mk(f"s_dsp{i}") for i in range(n_dma_sems)],
                  "pool": [mk(f"s_dpl{i}") for i in range(n_dma_sems)]}
        self.qeng = {"sp": self.sp, "pool": self.pool}
        self.dma_cnt = {"sp": 0, "pool": 0}
        self.dma_last = {}
        self.last_write = {}
        self.readers = {}
        self.n_ops = 0
        self.n_waits = 0

    def _wait(self, E, tok):
        sem, val, src = tok
        if src == "pe" and E.name == "pe":
            return
        k = id(sem)
        if E.waited.get(k, 0) >= val:
            return
        E.eng.wait_ge(sem, val)
        E.waited[k] = val
        self.n_waits += 1

    def _deps(self, E, reads, writes):
        for r in reads:
            t = self.last_write.get(r)
            if t is not None:
                self._wait(E, t)
        for w in writes:
            t = self.last_write.get(w)
            if t is not None:
                self._wait(E, t)
            for t in self.readers.get(w, ()):
                self._wait(E, t)

    def _commit(self, tok, reads, writes):
        for r in reads:
            self.readers.setdefault(r, []).append(tok)
        for w in writes:
            self.last_write[w] = tok
            self.readers[w] = []

    def op(self, E, fn, reads=(), writes=()):
        self._deps(E, reads, writes)
        ins = fn(E.eng)
        E.count += 1
        ins.then_inc(E.sem, 1)
        tok = (E.sem, E.count, E.name)
        self._commit(tok, reads, writes)
        self.n_ops += 1
        return tok

    def dma(self, qname, fn, reads=(), writes=()):
        E = self.qeng[qname]
        pool = self.q[qname]
        i = self.dma_cnt[qname]
        self.dma_cnt[qname] = i + 1
        slot = i % len(pool)
        prev = self.dma_last.get((qname, slot))
        if prev is not None:
            self._wait(E, prev)
        self._deps(E, reads, writes)
        ins = fn(E.eng)
        ins.then_inc(pool[slot], 16)
        tok = (pool[slot], 16 * (i // len(pool) + 1), "dma_" + qname)
        self.dma_last[(qname, slot)] = tok
        self._commit(tok, reads, writes)
        self.n_ops += 1
        return tok

    def barrier(self):
        toks = [(E.sem, E.count, E.name) for E in (self.pe, self.act, self.dve, self.pool) if E.count]
        toks += list(self.dma_last.values())
        for E in (self.pe, self.act, self.dve, self.pool, self.sp):
            for t in toks:
                if t[2] == E.name:
                    continue
                self._wait(E, t)
        self.last_write = {}
        self.readers = {}


class Builder:
    def __init__(self, plan, debug_out=False):
        self.plan = plan
        nc = bass.Bass("TRN2", target_bir_lowering=False)
        self.nc = nc
        self.din = {}

    def nm(self, n):
        self._uid = getattr(self, "_uid", 0) + 1
        return f"{n}_{self._uid}"

    def dram_in(self, name, shape, dt=F32):
        t = self.nc.dram_tensor(name, list(shape), dt, kind="ExternalInput").ap()
        self.din[name] = t
        return t

    def build(self):
        nc = self.nc
        plan = self.plan
        self.x_in = self.dram_in("x", [T, D])
        self.out = nc.dram_tensor("out", [T, D], F32, kind="ExternalOutput").ap()
        self.ln_g = self.dram_in("ln_g", [DEPTH, 2, D])
        self.ln_b = self.dram_in("ln_b", [DEPTH, 2, D])
        self.ffn_w_in = self.dram_in("ffn_w_in", [DEPTH, D, 2 * D_FF])
        self.ffn_w_out = self.dram_in("ffn_w_out", [DEPTH, D_FF, D])
        self.ffn_cw = self.dram_in("ffn_cw", [DEPTH, 128, NCH, 4])
        kinds = {k for k, _ in plan}
        if "dil" in kinds or "rwkv" in kinds:
            self.oT_d = nc.dram_tensor("oT_d", [8, 128, T], BF16, kind="Internal").ap()
            self.xT_d = nc.dram_tensor("xT_d", [8, 128, T], BF16, kind="Internal").ap()
        if "dil" in kinds:
            self.dil_wqkv = self.dram_in("dil_w_qkv", [D, 9216])
            self.dil_wo = self.dram_in("dil_w_o", [D, D])
        if "rwkv" in kinds:
            self.rw_mu = self.dram_in("rw_mu", [128, 8, 6])
            self.rw_wrkv = self.dram_in("rwkv_w_rkv", [3, D, D])
            self.rw_l1 = self.dram_in("rw_l1", [D, 288])
            self.rw_w2 = self.dram_in("rwkv_w2", [64, D])
            self.rw_a2 = self.dram_in("rwkv_a2", [64, D])
            self.rw_g2 = self.dram_in("rwkv_g2", [160, D])
            self.rw_vec = self.dram_in("rw_vec", [128, 8, 8])
            self.rw_wo = self.dram_in("rwkv_w_o", [D, D])
        if "mla" in kinds:
            self.pos = self.dram_in("positions", [T], I32)
            self.rope_c = self.dram_in("rope_c", [96, 2])
            self.mla_wd = self.dram_in("mla_wd", [2, D, 768])
            self.mla_qn = self.dram_in("mla_qn", [2, 128, 3])
            self.mla_kvn = self.dram_in("mla_kvn", [2, 128, 2])
            self.mla_wuq = self.dram_in("mla_wuq", [2, 384, 2048])
            self.mla_wukv = self.dram_in("mla_wukv", [2, 256, 2048])
            self.mla_wo = self.dram_in("mla_wo", [2, D, D])

        with ExitStack() as st:
            self.st = st
            S = self.S = Sched(nc, st)
            gsb = lambda n, shp, dt: st.enter_context(nc.sbuf_tensor(self.nm(n), shp, dt))
            self.actT = gsb("actT", [128, 8, T], BF16)
            self.ident = gsb("ident", [128, 128], BF16)
            self.lng = gsb("lng", [128, D], F32)
            self.lnb = gsb("lnb", [128, D], F32)
            self.ones_bf = gsb("ones_bf", [128, 128], BF16)
            self.ones_f = gsb("ones_f", [128, 128], F32)
            self.tri = gsb("tri", [128, 128], BF16)
            self.ep_idx = 0
            self.cur_src = self.x_in

            S.op(S.pool, lambda e: e.memset(self.ident[:], 1.0), writes=["ident"])
            S.op(S.pool, lambda e: e.affine_select(out=self.ident[:], in_=self.ident[:], pattern=[[1, 128]],
                                                   compare_op=ALU.is_equal, fill=0.0, base=0,
                                                   channel_multiplier=-1),
                 reads=["ident"], writes=["ident"])
            S.op(S.pool, lambda e: e.memset(self.ones_bf[:], 1.0), writes=["ones_bf"])
            S.op(S.pool, lambda e: e.memset(self.ones_f[:], 1.0), writes=["ones_f"])
            S.op(S.pool, lambda e: e.memset(self.tri[:], 1.0), writes=["tri"])
            S.op(S.pool, lambda e: e.affine_select(out=self.tri[:], in_=self.tri[:], pattern=[[1, 128]],
                                                   compare_op=ALU.is_ge, fill=0.0, base=0,
                                                   channel_multiplier=-1),
                 reads=["tri"], writes=["tri"])
            self.init_phase()
            for step in plan:
                kind, L = step
                if kind == "ffn":
                    self.ffn_phase(L)
                elif kind == "mla":
                    self.mla_phase(L)
                elif kind == "dil":
                    self.dil_phase(L)
                elif kind == "rwkv":
                    self.rwkv_phase2(L)
                elif kind == "copy":
                    self.copy_phase()
                else:
                    raise ValueError(kind)
            S.barrier()
        return nc

    def transposes_to_actT(self, m, xb, pT, res_xb):
        S = self.S
        for k in range(8):
            S.op(S.pe, lambda e, k=k: e.transpose(out=pT[:, k, :], in_=xb[:, k * 128:(k + 1) * 128],
                                                  identity=self.ident[:]),
                 reads=[res_xb, "ident"], writes=["pT"])
        S.op(S.dve, lambda e: e.tensor_copy(out=self.actT[:, :, m * 128:(m + 1) * 128], in_=pT[:]),
             reads=["pT"], writes=[("actT", m)])

    def alloc_epi(self, ph):
        nc = self.nc
        sb = lambda n, shp, dt: ph.enter_context(nc.sbuf_tensor(self.nm(n), shp, dt))
        self.xr = [sb(f"xr{i}", [128, D], F32) for i in range(2)]
        self.z = [sb(f"z{i}", [128, D], F32) for i in range(2)]
        self.xb = [sb(f"xb{i}", [128, D], BF16) for i in range(2)]
        self.st6 = [sb(f"st6{i}", [128, 2, 6], F32) for i in range(2)]
        self.mv = [sb(f"mv{i}", [128, 8], F32) for i in range(2)]

    def init_phase(self):
        nc, S = self.nc, self.S
        with ExitStack() as ph:
            self.alloc_epi(ph)
            pT = ph.enter_context(nc.psum_tensor(self.nm("pT_i"), [128, 8, 128], BF16))
            for m in range(NB):
                b = m % 2
                S.dma("sp", lambda e: e.dma_start(out=self.xr[b][:], in_=self.x_in[m * 128:(m + 1) * 128, :]),
                      writes=[("xr", b)])
                S.op(S.act, lambda e: e.copy(out=self.xb[b][:], in_=self.xr[b][:]),
                     reads=[("xr", b)], writes=[("xb", b)])
                self.transposes_to_actT(m, self.xb[b], pT, ("xb", b))
            S.barrier()

    def copy_phase(self):
        S = self.S
        ph = ExitStack()
        self.alloc_epi(ph)
        for m in range(NB):
            b = m % 2
            S.dma("sp", lambda e: e.dma_start(out=self.xr[b][:], in_=self.cur_src[m * 128:(m + 1) * 128, :]),
                  reads=[("xres", m)], writes=[("xr", b)])
            S.dma("sp", lambda e: e.dma_start(out=self.out[m * 128:(m + 1) * 128, :], in_=self.xr[b][:]),
                  reads=[("xr", b)], writes=[("xres", m)])
        S.barrier()
        ph.close()
        self.cur_src = self.out

    def load_ln(self, L, which):
        S = self.S
        S.dma("sp", lambda e: e.dma_start(out=self.lng[:], in_=self.ln_g[L, which, :].partition_broadcast(128)),
              writes=["lng"])
        S.dma("sp", lambda e: e.dma_start(out=self.lnb[:], in_=self.ln_b[L, which, :].partition_broadcast(128)),
              writes=["lnb"])

    def prefetch_xr(self, m):
        S = self.S
        b = self.ep_idx % 2
        src = self.cur_src
        S.dma("sp", lambda e: e.dma_start(out=self.xr[b][:], in_=src[m * 128:(m + 1) * 128, :]),
              reads=[("xres", m)], writes=[("xr", b)])

    def epilogue(self, m, py, py_res, pT):
        S = self.S
        prev_tr = getattr(self, "pending_tr", None)
        self.pending_tr = None
        b = self.ep_idx % 2
        self.ep_idx += 1
        xr, z, xb, st6, mv = self.xr[b], self.z[b], self.xb[b], self.st6[b], self.mv[b]
        rz, rmv = ("z", b), ("mv", b)
        S.op(S.dve, lambda e: e.scalar_tensor_tensor(out=z[:], in0=xr[:], scalar=float(ALPHA), in1=py,
                                                     op0=ALU.mult, op1=ALU.add),
             reads=[("xr", b), py_res], writes=[rz])
        if prev_tr is not None:
            self.transposes_to_actT(*prev_tr)
        for c in range(2):
            S.op(S.dve, lambda e, c=c: e.bn_stats(out=st6[:, c, :], in_=z[:, c * 512:(c + 1) * 512]),
                 reads=[rz], writes=[("st6", b, c)])
        S.op(S.dve, lambda e: e.bn_aggr(out=mv[:, 0:2], in_=st6[:].rearrange("p a b -> p (a b)")),
             reads=[("st6", b, 0), ("st6", b, 1)], writes=[rmv])
        S.op(S.dve, lambda e: e.tensor_scalar(out=mv[:, 2:3], in0=mv[:, 1:2], scalar1=float(LN_EPS), scalar2=None,
                                              op0=ALU.add), reads=[rmv], writes=[rmv])
        S.op(S.act, lambda e: e.activation(out=mv[:, 3:4], in_=mv[:, 2:3], func=AF.Sqrt), reads=[rmv], writes=[rmv])
        S.op(S.dve, lambda e: e.reciprocal(out=mv[:, 4:5], in_=mv[:, 3:4]), reads=[rmv], writes=[rmv])
        S.op(S.dve, lambda e: e.scalar_tensor_tensor(out=mv[:, 5:6], in0=mv[:, 0:1], scalar=-1.0, in1=mv[:, 4:5],
                                                     op0=ALU.mult, op1=ALU.mult), reads=[rmv], writes=[rmv])
        S.op(S.act, lambda e: e.activation(out=z[:], in_=z[:], func=AF.Identity, bias=mv[:, 5:6], scale=mv[:, 4:5]),
             reads=[rmv, rz], writes=[rz])
        S.op(S.pool, lambda e: e.tensor_tensor(out=z[:], in0=z[:], in1=self.lng[:], op=ALU.mult),
             reads=[rz, "lng"], writes=[rz])
        S.op(S.pool, lambda e: e.tensor_tensor(out=z[:], in0=z[:], in1=self.lnb[:], op=ALU.add),
             reads=[rz, "lnb"], writes=[rz])
        S.dma("pool", lambda e: e.dma_start(out=self.out[m * 128:(m + 1) * 128, :], in_=z[:]),
              reads=[rz], writes=[("xres", m)])
        S.op(S.act, lambda e: e.copy(out=xb[:], in_=z[:]), reads=[rz], writes=[("xb", b)])
        self.pending_tr = (m, xb, pT, ("xb", b))

    def flush_tr(self):
        if getattr(self, "pending_tr", None) is not None:
            self.transposes_to_actT(*self.pending_tr)
            self.pending_tr = None

    def ffn_phase(self, L):
        nc, S = self.nc, self.S
        actT = self.actT
        with ExitStack() as ph:
            sb = lambda n, shp, dt: ph.enter_context(nc.sbuf_tensor(self.nm(n), shp, dt))
            ps = lambda n, shp, dt: ph.enter_context(nc.psum_tensor(self.nm(n), shp, dt))
            self.alloc_epi(ph)
            w_out = sb("f_wout", [128, NCH, D], BF16)
            cw = sb("f_cw", [128, NCH, 4], F32)
            halo = sb("f_halo", [128, NCH, 2], F32)
            g = sb("f_g", [128, NCH, 512], BF16)
            wab = [sb(f"f_wab{i}", [128, 8, 256], BF16) for i in range(3)]
            asb = [sb(f"f_a{i}", [128, 514], F32) for i in range(3)]
            tt = [sb(f"f_t{i}", [128, 512], F32) for i in range(3)]
            pab = [ps(f"f_pab{i}", [128, 512], F32) for i in range(3)]
            py = [ps(f"f_py{i}", [128, D], F32) for i in range(2)]
            pT = ps("f_pT", [128, 8, 128], BF16)

            self.load_ln(L, 1)
            S.dma("sp", lambda e: e.dma_start(out=cw[:], in_=self.ffn_cw[L]), writes=["cw"])
            S.dma("pool", lambda e: e.dma_start(
                out=w_out[:, 0:21, :], in_=self.ffn_w_out[L, 0:21 * 128, :].rearrange("(c p) n -> p c n", p=128)),
                writes=["wout"])
            S.dma("pool", lambda e: e.dma_start(out=w_out[0:64, 21, :], in_=self.ffn_w_out[L, 21 * 128:D_FF, :]),
                  writes=["wout21"])
            S.op(S.pool, lambda e: e.memset(halo[:], 0.0), writes=[("halo", c) for c in range(NCH)])
            w_in = self.ffn_w_in[L].rearrange("(k p) n -> p k n", p=128)
            NJ = T // 512
            NIT = NJ * NCH

            def load_w(i):
                if i >= NIT:
                    return
                c = i % NCH
                wc = 128 if c < NCH - 1 else 64
                r3 = i % 3
                wb_ = wab[r3]
                S.dma("pool", lambda e: e.dma_start(out=wb_[:, :, 0:wc], in_=w_in[:, :, c * 128:c * 128 + wc]),
                      writes=[("wa", r3)])
                S.dma("pool", lambda e: e.dma_start(out=wb_[:, :, 128:128 + wc],
                                                    in_=w_in[:, :, D_FF + c * 128:D_FF + c * 128 + wc]),
                      writes=[("wb", r3)])

            load_w(0)
            load_w(1)
            for i in range(NIT):
                j, c = i // NCH, i % NCH
                tok = slice(j * 512, (j + 1) * 512)
                act_res = [("actT", 4 * j + q) for q in range(4)]
                wc = 128 if c < NCH - 1 else 64
                r3 = i % 3
                ia_, ib_ = (2 * i) % 3, (2 * i + 1) % 3
                pa_, pb_ = pab[ia_], pab[ib_]
                rpa, rpb = ("pab", ia_), ("pab", ib_)
                wb_, a_, t_ = wab[r3], asb[r3], tt[r3]
                load_w(i + 2)
                for k in range(8):
                    S.op(S.pe, lambda e, k=k: e.matmul(pa_[0:wc, :], lhsT=wb_[:, k, 0:wc], rhs=actT[:, k, tok],
                                                       start=(k == 0), stop=(k == 7)),
                         reads=[("wa", r3)] + act_res, writes=[rpa])
                for k in range(8):
                    S.op(S.pe, lambda e, k=k: e.matmul(pb_[0:wc, :], lhsT=wb_[:, k, 128:128 + wc],
                                                       rhs=actT[:, k, tok], start=(k == 0), stop=(k == 7)),
                         reads=[("wb", r3)] + act_res, writes=[rpb])
                if c == 1:
                    self.flush_tr()
                ra, rt = ("a", r3), ("t", r3)
                S.op(S.act, lambda e: e.copy(out=a_[0:wc, 0:2], in_=halo[0:wc, c, :]),
                     reads=[("halo", c)], writes=[ra])
                S.op(S.act, lambda e: e.copy(out=a_[0:wc, 2:514], in_=pa_[0:wc, :]),
                     reads=[rpa, ra], writes=[ra])
                S.op(S.act, lambda e: e.copy(out=halo[0:wc, c, :], in_=a_[0:wc, 512:514]),
                     reads=[ra], writes=[("halo", c)])
                S.op(S.dve, lambda e: e.tensor_scalar(out=t_[0:wc, :], in0=a_[0:wc, 2:514],
                                                      scalar1=cw[0:wc, c, 2:3], scalar2=cw[0:wc, c, 3:4],
                                                      op0=ALU.mult, op1=ALU.add),
                     reads=[ra, "cw"], writes=[rt])
                S.op(S.dve, lambda e: e.scalar_tensor_tensor(out=t_[0:wc, :], in0=a_[0:wc, 1:513],
                                                             scalar=cw[0:wc, c, 1:2], in1=t_[0:wc, :],
                                                             op0=ALU.mult, op1=ALU.add),
                     reads=[ra, rt], writes=[rt])
                S.op(S.dve, lambda e: e.scalar_tensor_tensor(out=t_[0:wc, :], in0=a_[0:wc, 0:512],
                                                             scalar=cw[0:wc, c, 0:1], in1=t_[0:wc, :],
                                                             op0=ALU.mult, op1=ALU.add),
                     reads=[ra, rt], writes=[rt])
                S.op(S.act, lambda e: e.activation(out=t_[0:wc, :], in_=t_[0:wc, :], func=AF.Silu),
                     reads=[rt], writes=[rt])
                S.op(S.dve, lambda e: e.tensor_tensor(out=g[0:wc, c, :], in0=t_[0:wc, :], in1=pb_[0:wc, :],
                                                      op=ALU.mult),
                     reads=[rt, rpb], writes=[("g", c)])
                if c < NCH - 1:
                    continue
                for mm in range(4):
                    m = 4 * j + mm
                    self.prefetch_xr(m)
                    p_ = py[m % 2]
                    for n in range(2):
                        for cc in range(NCH):
                            wcc = 128 if cc < NCH - 1 else 64
                            S.op(S.pe, lambda e, n=n, cc=cc, wcc=wcc: e.matmul(
                                p_[:, n * 512:(n + 1) * 512], lhsT=g[0:wcc, cc, mm * 128:(mm + 1) * 128],
                                rhs=w_out[0:wcc, cc, n * 512:(n + 1) * 512], start=(cc == 0), stop=(cc == NCH - 1)),
                                 reads=[("g", cc), "wout", "wout21"], writes=[("py", m % 2)])
                    self.epilogue(m, p_[:], ("py", m % 2), pT)
            self.flush_tr()
            S.barrier()
        self.cur_src = self.out


    def out_proj_phase(self, L, w_dram):
        nc, S = self.nc, self.S
        with ExitStack() as ph:
            sb = lambda n, shp, dt: ph.enter_context(nc.sbuf_tensor(self.nm(n), shp, dt))
            ps = lambda n, shp, dt: ph.enter_context(nc.psum_tensor(self.nm(n), shp, dt))
            self.alloc_epi(ph)
            wo = sb("o_w", [128, 8, D], BF16)
            py = [ps(f"o_py{i}", [128, D], F32) for i in range(2)]
            pT = ps("o_pT", [128, 8, 128], BF16)
            self.load_ln(L, 0)
            S.dma("pool", lambda e: e.dma_start(out=wo[:], in_=w_dram.rearrange("(k p) n -> p k n", p=128)),
                  writes=["wo"])
            for m in range(NB):
                self.prefetch_xr(m)
                p_ = py[m % 2]
                for n in range(2):
                    for k in range(8):
                        S.op(S.pe, lambda e, n=n, k=k: e.matmul(
                            p_[:, n * 512:(n + 1) * 512], lhsT=self.actT[:, k, m * 128:(m + 1) * 128],
                            rhs=wo[:, k, n * 512:(n + 1) * 512], start=(k == 0), stop=(k == 7)),
                             reads=[("actT", m), "wo"], writes=[("py", m % 2)])
                self.epilogue(m, p_[:], ("py", m % 2), pT)
            self.flush_tr()
            S.barrier()
        self.cur_src = self.out

    def mla_phase(self, L):
        nc, S = self.nc, self.S
        actT = self.actT
        ia = L // 3
        SCALE = 96.0 ** -0.5
        TWO_PI = 2.0 * np.pi
        with ExitStack() as ml:
            msb = lambda n, shp, dt: ml.enter_context(nc.sbuf_tensor(self.nm(n), shp, dt))
            cqn = msb("m_cqn", [128, 3, T], BF16)
            ckvn = msb("m_ckvn", [128, 2, T], BF16)
            KT = msb("m_KT", [96, T], BF16)
            cosT = msb("m_cos", [96, T], BF16)
            sinS = msb("m_sin", [96, T], BF16)
            rc = msb("m_rc", [96, 2], F32)
            with ExitStack() as ph:
                sb = lambda n, shp, dt: ph.enter_context(nc.sbuf_tensor(self.nm(n), shp, dt))
                HT = T // 2
                posi = sb("m_posi", [96, HT], I32)
                ang = sb("m_ang", [96, HT], F32)
                tmp = sb("m_tmp", [96, HT], F32)
                yi = sb("m_yi", [96, HT], I32)
                msk = sb("m_msk", [96, HT], F32)
                S.dma("sp", lambda e: e.dma_start(out=rc[:], in_=self.rope_c), writes=["rc"])

                def sin_turns():
                    S.op(S.dve, lambda e: e.tensor_copy(out=yi[:], in_=tmp[:]), reads=["tmp"], writes=["yi"])
                    S.op(S.dve, lambda e: e.tensor_copy(out=msk[:], in_=yi[:]), reads=["yi"], writes=["msk"])
                    S.op(S.dve, lambda e: e.tensor_tensor(out=tmp[:], in0=tmp[:], in1=msk[:], op=ALU.subtract),
                         reads=["tmp", "msk"], writes=["tmp"])
                    S.op(S.dve, lambda e: e.tensor_scalar(out=msk[:], in0=tmp[:], scalar1=0.5, scalar2=None,
                                                          op0=ALU.is_gt), reads=["tmp"], writes=["msk"])
                    S.op(S.dve, lambda e: e.tensor_tensor(out=tmp[:], in0=tmp[:], in1=msk[:], op=ALU.subtract),
                         reads=["tmp", "msk"], writes=["tmp"])
                    S.op(S.dve, lambda e: e.tensor_scalar(out=msk[:], in0=tmp[:], scalar1=-0.5, scalar2=None,
                                                          op0=ALU.is_lt), reads=["tmp"], writes=["msk"])
                    S.op(S.dve, lambda e: e.tensor_tensor(out=tmp[:], in0=tmp[:], in1=msk[:], op=ALU.add),
                         reads=["tmp", "msk"], writes=["tmp"])
                    S.op(S.act, lambda e: e.activation(out=tmp[:], in_=tmp[:], func=AF.Sin, scale=6.28318),
                         reads=["tmp"], writes=["tmp"])

                for hh in range(2):
                    cs = slice(hh * HT, (hh + 1) * HT)
                    S.dma("sp", lambda e: e.dma_start(out=posi[:], in_=self.pos[cs].partition_broadcast(96)),
                          writes=["posi"])
                    S.op(S.dve, lambda e: e.tensor_copy(out=ang[:], in_=posi[:]), reads=["posi"], writes=["ang"])
                    S.op(S.dve, lambda e: e.tensor_scalar(out=ang[:], in0=ang[:], scalar1=rc[:, 0:1], scalar2=None,
                                                          op0=ALU.mult), reads=["ang", "rc"], writes=["ang"])
                    S.op(S.dve, lambda e: e.tensor_copy(out=tmp[:], in_=ang[:]), reads=["ang"], writes=["tmp"])
                    sin_turns()
                    S.op(S.dve, lambda e: e.tensor_scalar(out=sinS[64:96, cs], in0=tmp[64:96, :],
                                                          scalar1=rc[64:96, 1:2], scalar2=None, op0=ALU.mult),
                         reads=["tmp", "rc"], writes=["sinS"])
                    S.op(S.dve, lambda e: e.tensor_scalar(out=tmp[:], in0=ang[:], scalar1=0.25, scalar2=None,
                                                          op0=ALU.add), reads=["ang", "sinS"], writes=["tmp"])
                    sin_turns()
                    S.op(S.dve, lambda e: e.tensor_copy(out=cosT[64:96, cs], in_=tmp[64:96, :]),
                         reads=["tmp"], writes=["cosT"])
                S.barrier()
            with ExitStack() as ph:
                sb = lambda n, shp, dt: ph.enter_context(nc.sbuf_tensor(self.nm(n), shp, dt))
                ps = lambda n, shp, dt: ph.enter_context(nc.psum_tensor(self.nm(n), shp, dt))
                wd = sb("m_wd", [128, 8, 768], BF16)
                gq = sb("m_gq", [128, 3], F32)
                gkv = sb("m_gkv", [128, 2], F32)
                raw = [sb(f"m_raw{i}", [128, 5, 512], F32) for i in range(2)]
                sq = [sb(f"m_sq{i}", [128, 5, 512], BF16) for i in range(2)]
                rs = [sb(f"m_rs{i}", [128, 2, 512], F32) for i in range(2)]
                t1 = [sb(f"m_t1{i}", [96, 512], F32) for i in range(2)]
                t2 = [sb(f"m_t2{i}", [96, 512], F32) for i in range(2)]
                p_lat = [ps(f"m_plat{i}", [128, 512], F32) for i in range(2)]
                p_ss = [ps(f"m_pss{i}", [128, 512], F32) for i in range(2)]
                p_kA = ps("m_pkA", [96, 512], F32)
                p_kB = ps("m_pkB", [96, 512], F32)
                S.dma("pool", lambda e: e.dma_start(out=wd[:], in_=self.mla_wd[ia].rearrange("(k p) n -> p k n", p=128)),
                      writes=["wd"])
                S.dma("sp", lambda e: e.dma_start(out=gq[:], in_=self.mla_qn[ia]), writes=["gq"])
                S.dma("sp", lambda e: e.dma_start(out=gkv[:], in_=self.mla_kvn[ia]), writes=["gkv"])
                it = 0
                for j in range(T // 512):
                    tok = slice(j * 512, (j + 1) * 512)
                    ares = [("actT", 4 * j + q) for q in range(4)]
                    b = j % 2
                    for c in range(5):
                        pl = p_lat[it % 2]
                        rpl = ("plat", it % 2)
                        it += 1
                        for k in range(8):
                            S.op(S.pe, lambda e, k=k: e.matmul(pl[:], lhsT=wd[:, k, c * 128:(c + 1) * 128],
                                                               rhs=actT[:, k, tok], start=(k == 0), stop=(k == 7)),
                                 reads=["wd"] + ares, writes=[rpl])
                        S.op(S.act, lambda e: e.copy(out=raw[b][:, c, :], in_=pl[:]), reads=[rpl], writes=[("raw", b, c)])
                        S.op(S.act, lambda e: e.activation(out=sq[b][:, c, :], in_=pl[:], func=AF.Square),
                             reads=[rpl], writes=[("sq", b, c)])
                    for which, (c0, c1, dim) in enumerate([(0, 3, 384.0), (3, 5, 256.0)]):
                        for c in range(c0, c1):
                            S.op(S.pe, lambda e, c=c: e.matmul(p_ss[which][:], lhsT=self.ones_bf[:], rhs=sq[b][:, c, :],
                                                               start=(c == c0), stop=(c == c1 - 1)),
                                 reads=[("sq", b, c), "ones_bf"], writes=[("pss", which)])
                        rr = ("rs", b, which)
                        S.op(S.dve, lambda e: e.tensor_scalar(out=rs[b][:, which, :], in0=p_ss[which][:],
                                                              scalar1=1.0 / dim, scalar2=float(RMS_EPS),
                                                              op0=ALU.mult, op1=ALU.add),
                             reads=[("pss", which)], writes=[rr])
                        S.op(S.act, lambda e: e.activation(out=rs[b][:, which, :], in_=rs[b][:, which, :], func=AF.Sqrt),
                             reads=[rr], writes=[rr])
                        S.op(S.dve, lambda e: e.reciprocal(out=rs[b][:, which, :], in_=rs[b][:, which, :]),
                             reads=[rr], writes=[rr])
                        for c in range(c0, c1):
                            dst = cqn[:, c, tok] if which == 0 else ckvn[:, c - 3, tok]
                            gsc = gq[:, c:c + 1] if which == 0 else gkv[:, c - 3:c - 2]
                            S.op(S.dve, lambda e: e.scalar_tensor_tensor(out=dst, in0=raw[b][:, c, :], scalar=gsc,
                                                                         in1=rs[b][:, which, :], op0=ALU.mult,
                                                                         op1=ALU.mult),
                                 reads=[("raw", b, c), rr, "gq", "gkv"], writes=[("cn", c, j)])
                    for k in range(8):
                        S.op(S.pe, lambda e, k=k: e.matmul(p_kA[:], lhsT=wd[:, k, 640:736], rhs=actT[:, k, tok],
                                                           start=(k == 0), stop=(k == 7)),
                             reads=["wd"] + ares, writes=["pkA"])
                    for k in range(8):
                        S.op(S.pe, lambda e, k=k: e.matmul(p_kB[:], lhsT=wd[:, k, 672:768], rhs=actT[:, k, tok],
                                                           start=(k == 0), stop=(k == 7)),
                             reads=["wd"] + ares, writes=["pkB"])
                    S.op(S.dve, lambda e: e.tensor_tensor(out=t1[b][64:96, :], in0=p_kA[64:96, :], in1=cosT[64:96, tok],
                                                          op=ALU.mult), reads=["pkA"], writes=[("t1", b)])
                    S.op(S.dve, lambda e: e.tensor_tensor(out=t2[b][64:96, :], in0=p_kB[64:96, :], in1=sinS[64:96, tok],
                                                          op=ALU.mult), reads=["pkB"], writes=[("t2", b)])
                    S.op(S.pool, lambda e: e.tensor_tensor(out=KT[64:96, tok], in0=t1[b][64:96, :], in1=t2[b][64:96, :],
                                                           op=ALU.add), reads=[("t1", b), ("t2", b)], writes=[("KTpe", j)])
                S.barrier()
            with ExitStack() as ph:
                sb = lambda n, shp, dt: ph.enter_context(nc.sbuf_tensor(self.nm(n), shp, dt))
                ps = lambda n, shp, dt: ph.enter_context(nc.psum_tensor(self.nm(n), shp, dt))
                wuq = sb("m_wuq", [128, 3, 2048], BF16)
                wukv = sb("m_wukv", [128, 2, 2048], BF16)
                Vx = [sb(f"m_Vx{i}", [128, 32, 128], BF16) for i in range(2)]
                QT = [sb(f"m_QT{i}", [96, 512], BF16) for i in range(2)]
                pt = [sb(f"m_pt{i}", [128, 512], BF16) for i in range(4)]
                t1 = [sb(f"m_u1{i}", [96, 512], F32) for i in range(2)]
                t2 = [sb(f"m_u2{i}", [96, 512], F32) for i in range(2)]
                rec = sb("m_rec", [128, 512], F32)
                bcs = sb("m_bcs", [128, 512], F32)
                p_p = [ps(f"m_pp{i}", [128, 512], F32) for i in range(2)]
                p_s = [ps(f"m_ps{i}", [128, 512], F32) for i in range(3)]
                p_o = [ps(f"m_po{i}", [128, 512], F32) for i in range(2)]
                p_bc = ps("m_pbc", [128, 512], F32)
                KTb = sb("m_KTb", [96, T], BF16)
                KTs = [KT, KTb]
                S.dma("pool", lambda e: e.dma_start(out=wuq[:], in_=self.mla_wuq[ia].rearrange("(k p) n -> p k n", p=128)),
                      writes=["wuq"])
                S.dma("pool", lambda e: e.dma_start(out=wukv[:], in_=self.mla_wukv[ia].rearrange("(k p) n -> p k n", p=128)),
                      writes=["wukv"])
                S.op(S.pool, lambda e: e.memset(Vx[0][:], 1.0), writes=[("Vx", 0)])
                S.op(S.pool, lambda e: e.memset(Vx[1][:], 1.0), writes=[("Vx", 1)])
                S.op(S.pool, lambda e: e.tensor_copy(out=KTb[64:96, :], in_=KT[64:96, :]), writes=["KTb_pe"])
                cnt = {"pp": 0}

                def kv_proj(h):
                    hl = h % 2
                    kt = KTs[hl]
                    vx = Vx[hl]
                    r0 = hl * 64
                    for j in range(T // 512):
                        tok = slice(j * 512, (j + 1) * 512)
                        pp, rpp = p_p[cnt["pp"] % 2], ("pp", cnt["pp"] % 2)
                        cnt["pp"] += 1
                        for k in range(2):
                            S.op(S.pe, lambda e, k=k: e.matmul(pp[0:64, :], lhsT=wukv[:, k, h * 64:(h + 1) * 64],
                                                               rhs=ckvn[:, k, tok], start=(k == 0), stop=(k == 1)),
                                 reads=["wukv"], writes=[rpp])
                        S.op(S.dve, lambda e: e.tensor_copy(out=kt[0:64, tok], in_=pp[0:64, :]), reads=[rpp],
                             writes=[("KT", hl, j)])
                    for j in range(4):
                        pp, rpp = p_p[cnt["pp"] % 2], ("pp", cnt["pp"] % 2)
                        cnt["pp"] += 1
                        ppv = pp[:].rearrange("p (b d) -> p b d", d=64)
                        for bb in range(8):
                            blk = j * 8 + bb
                            for k in range(2):
                                S.op(S.pe, lambda e, k=k: e.matmul(
                                    ppv[:, bb, :], lhsT=ckvn[:, k, blk * 128:(blk + 1) * 128],
                                    rhs=wukv[:, k, 1024 + h * 64:1024 + (h + 1) * 64], start=(k == 0), stop=(k == 1)),
                                     reads=["wukv"], writes=[rpp])
                        S.op(S.dve, lambda e: e.tensor_copy(out=vx[:, j * 8:(j + 1) * 8, r0:r0 + 64], in_=ppv),
                             reads=[rpp], writes=[("Vx", hl, j)])

                def prep_q(h, qt, iq):
                    tok = slice(qt * 512, (qt + 1) * 512)
                    qT, rq = QT[iq % 2], ("QT", iq % 2)
                    u1, u2 = t1[iq % 2], t2[iq % 2]
                    ru1, ru2 = ("u1", iq % 2), ("u2", iq % 2)
                    pA, pB = p_p[0], p_p[1]
                    for k in range(3):
                        S.op(S.pe, lambda e, k=k: e.matmul(pA[0:96, :], lhsT=wuq[:, k, h * 128:h * 128 + 96],
                                                           rhs=cqn[:, k, tok], start=(k == 0), stop=(k == 2)),
                             reads=["wuq"], writes=[("pp", 0)])
                    for k in range(3):
                        S.op(S.pe, lambda e, k=k: e.matmul(pB[0:96, :], lhsT=wuq[:, k, h * 128 + 32:h * 128 + 128],
                                                           rhs=cqn[:, k, tok], start=(k == 0), stop=(k == 2)),
                             reads=["wuq"], writes=[("pp", 1)])
                    S.op(S.dve, lambda e: e.tensor_copy(out=qT[0:64, :], in_=pA[0:64, :]), reads=[("pp", 0)], writes=[rq])
                    S.op(S.dve, lambda e: e.tensor_tensor(out=u1[64:96, :], in0=pA[64:96, :], in1=cosT[64:96, tok],
                                                          op=ALU.mult), reads=[("pp", 0)], writes=[ru1])
                    S.op(S.dve, lambda e: e.tensor_tensor(out=u2[64:96, :], in0=pB[64:96, :], in1=sinS[64:96, tok],
                                                          op=ALU.mult), reads=[("pp", 1)], writes=[ru2])
                    S.op(S.dve, lambda e: e.tensor_tensor(out=qT[64:96, :], in0=u1[64:96, :], in1=u2[64:96, :],
                                                          op=ALU.add), reads=[ru1, ru2], writes=[rq])

                def fin_q1(h, qt, iq):
                    hl = h % 2
                    d0 = 64 - hl * 64
                    po, rpo = p_o[iq % 2], ("po", iq % 2)
                    S.op(S.act, lambda e: e.copy(out=rec[d0:d0 + 1, :], in_=po[d0:d0 + 1, :]), reads=[rpo], writes=["rec"])

                def fin_q2(h, qt, iq):
                    hl, ch = h % 2, h // 2
                    r0, d0 = hl * 64, 64 - hl * 64
                    tok = slice(qt * 512, (qt + 1) * 512)
                    po, rpo = p_o[iq % 2], ("po", iq % 2)
                    S.op(S.pe, lambda e: e.matmul(p_bc[:], lhsT=self.ones_f[d0:d0 + 1, :], rhs=rec[d0:d0 + 1, :],
                                                  start=True, stop=True), reads=["rec", "ones_f"], writes=["pbc"])
                    S.op(S.dve, lambda e: e.reciprocal(out=bcs[r0:r0 + 64, :], in_=p_bc[r0:r0 + 64, :]),
                         reads=["pbc"], writes=["bcs"])
                    S.op(S.dve, lambda e: e.tensor_tensor(out=actT[r0:r0 + 64, ch, tok], in0=po[r0:r0 + 64, :],
                                                          in1=bcs[r0:r0 + 64, :], op=ALU.mult),
                         reads=[rpo, "bcs"], writes=[("actT", 4 * qt + q) for q in range(4)])

                items = []
                iq = 0
                for h in range(16):
                    for qt in range(T // 512):
                        nkb = 4 * qt + 4
                        for kb in range(nkb):
                            items.append(dict(h=h, qt=qt, kb=kb, nkb=nkb, iq=iq, pre=[], post=[]))
                        iq += 1
                first = {}
                for i, it in enumerate(items):
                    first.setdefault((it["h"], it["qt"]), i)
                for (h, qt), i in first.items():
                    lo = 0 if i == 0 else i - items[i - 1]["nkb"]
                    items[max(lo, i - 6)]["pre"].append(lambda h=h, qt=qt, iq=items[i]["iq"]: prep_q(h, qt, iq))
                    if qt == 0 and h > 0:
                        items[first[(h - 1, 5)]]["pre"].insert(0, lambda h=h: kv_proj(h))
                for i, it in enumerate(items):
                    if it["kb"] == it["nkb"] - 1:
                        it["post"].append(lambda it=it: fin_q1(it["h"], it["qt"], it["iq"]))
                        items[min(len(items) - 1, i + 3)]["post"].append(
                            lambda it=it: fin_q2(it["h"], it["qt"], it["iq"]))
                kv_proj(0)

                def stA(i, it):
                    h, qt, kb = it["h"], it["qt"], it["kb"]
                    hl = h % 2
                    n0 = max(0, kb - 4 * qt) * 128
                    qT, rq = QT[it["iq"] % 2], ("QT", it["iq"] % 2)
                    S.op(S.pe, lambda e: e.matmul(p_s[i % 3][:, n0:512], lhsT=KTs[hl][0:96, kb * 128:(kb + 1) * 128],
                                                  rhs=qT[0:96, n0:512], start=True, stop=True),
                         reads=[("KT", hl, kb // 4), rq, "KTb_pe"], writes=[("ps", i % 3)])

                def stB(i, it):
                    qt, kb = it["qt"], it["kb"]
                    n0 = max(0, kb - 4 * qt) * 128
                    ptb, rpt = pt[i % 4], ("pt", i % 4)
                    S.op(S.act, lambda e: e.activation(out=ptb[:, n0:512], in_=p_s[i % 3][:, n0:512], func=AF.Exp,
                                                       scale=float(SCALE)), reads=[("ps", i % 3)], writes=[rpt])
                    if kb >= 4 * qt:
                        S.op(S.dve, lambda e: e.tensor_tensor(out=ptb[:, n0:n0 + 128], in0=ptb[:, n0:n0 + 128],
                                                              in1=self.tri[:], op=ALU.mult),
                             reads=[rpt, "tri"], writes=[rpt])

                def stC(i, it):
                    h, qt, kb, nkb = it["h"], it["qt"], it["kb"], it["nkb"]
                    hl = h % 2
                    n0 = max(0, kb - 4 * qt) * 128
                    po, rpo = p_o[it["iq"] % 2], ("po", it["iq"] % 2)
                    S.op(S.pe, lambda e: e.matmul(po[:, n0:512], lhsT=Vx[hl][:, kb, :], rhs=pt[i % 4][:, n0:512],
                                                  start=(kb == 0), stop=(kb == nkb - 1)),
                         reads=[("Vx", hl, kb // 8), ("pt", i % 4)], writes=[rpo])

                n = len(items)
                for t_ in range(n + 3):
                    if t_ < n:
                        for f in items[t_]["pre"]:
                            f()
                        stA(t_, items[t_])
                    if 0 <= t_ - 1 < n:
                        stB(t_ - 1, items[t_ - 1])
                    if 0 <= t_ - 3 < n:
                        stC(t_ - 3, items[t_ - 3])
                        for f in items[t_ - 3]["post"]:
                            f()
                S.barrier()
        self.out_proj_phase(L, self.mla_wo[ia])


    def load_oT(self):
        S = self.S
        for c in range(8):
            S.dma("sp", lambda e, c=c: e.dma_start(out=self.actT[:, c, :], in_=self.oT_d[c]),
                  reads=[("oTd", c)], writes=[("actT", m) for m in range(NB)])

    def dil_phase(self, L):
        nc, S = self.nc, self.S
        actT = self.actT
        DIL = (1, 4, 16)
        with ExitStack() as ph:
            sb = lambda n, shp, dt: ph.enter_context(nc.sbuf_tensor(self.nm(n), shp, dt))
            ps = lambda n, shp, dt: ph.enter_context(nc.psum_tensor(self.nm(n), shp, dt))
            wd = sb("d_w", [128, 8, 9, 128], BF16)
            QK = [[sb(f"d_qk{g}{i}", [128, T], BF16) for i in range(2)] for g in range(3)]
            Vx = [sb(f"d_vx{g}", [128, 32, 192], BF16) for g in range(3)]
            osb = sb("d_osb", [128, T], BF16)
            mask2 = sb("d_mask2", [128, 256], BF16)
            pt = [sb(f"d_pt{i}", [128, 256], BF16) for i in range(4)]
            rec = sb("d_rec", [128, 512], F32)
            bcs = sb("d_bcs", [128, 512], F32)
            p_o = ps("d_po", [128, 2048], F32)
            p_s = [ps(f"d_ps{i}", [128, 512], F32) for i in range(2)]
            p_bc = ps("d_pbc", [128, 512], F32)
            p_p = ps("d_pp", [128, 512], F32)
            S.op(S.pool, lambda e: e.memset(mask2[:], 1.0), writes=["mask2"])
            S.op(S.pool, lambda e: e.affine_select(out=mask2[:, 0:128], in_=mask2[:, 0:128], pattern=[[-1, 128]],
                                                   compare_op=ALU.is_ge, fill=0.0, base=0, channel_multiplier=1),
                 reads=["mask2"], writes=["mask2"])
            S.op(S.pool, lambda e: e.affine_select(out=mask2[:, 128:256], in_=mask2[:, 128:256], pattern=[[1, 128]],
                                                   compare_op=ALU.is_ge, fill=0.0, base=0, channel_multiplier=-1),
                 reads=["mask2"], writes=["mask2"])
            for g in range(3):
                S.op(S.pool, lambda e, g=g: e.memset(Vx[g][:], 1.0), writes=[("Vx", g)])
            wsrc = self.dil_wqkv.rearrange("(k p) (c h d) -> p k c (h d)", p=128, c=9, h=16)
            pbank = [p_p, p_s[0], p_s[1]]
            cnt = {"pp": 0, "it": 0}

            def next_pp():
                i = cnt["pp"] % 3
                cnt["pp"] += 1
                return pbank[i], ("pb", i)

            def fin(hl, U, qq):
                r0, d0 = hl * 64, 64 - hl * 64
                rows = slice(r0, r0 + 64)
                cs = slice(qq * 512, (qq + 1) * 512)
                tok = slice(U * 2048 + qq * 512, U * 2048 + (qq + 1) * 512)
                S.op(S.act, lambda e: e.copy(out=rec[d0:d0 + 1, :], in_=p_o[d0:d0 + 1, cs]), reads=["po"], writes=["rec"])
                S.op(S.pe, lambda e: e.matmul(p_bc[:], lhsT=self.ones_f[d0:d0 + 1, :], rhs=rec[d0:d0 + 1, :],
                                              start=True, stop=True), reads=["rec", "ones_f"], writes=["pbc"])
                S.op(S.dve, lambda e: e.reciprocal(out=bcs[rows, :], in_=p_bc[rows, :]), reads=["pbc"], writes=["bcs"])
                S.op(S.dve, lambda e: e.tensor_tensor(out=osb[rows, tok], in0=p_o[rows, cs], in1=bcs[rows, :],
                                                      op=ALU.mult), reads=["po", "bcs"], writes=["osb"])

            for hp in range(8):
                for c9 in range(9):
                    S.dma("pool", lambda e: e.dma_start(out=wd[:, :, c9, :], in_=wsrc[:, :, c9, hp * 128:(hp + 1) * 128]),
                          writes=[("wd", c9)])
                for j in range(T // 512):
                    tok = slice(j * 512, (j + 1) * 512)
                    ares = [("actT", 4 * j + q) for q in range(4)]
                    for g in range(3):
                        for qk in range(2):
                            pp, rpp = next_pp()
                            for k in range(8):
                                S.op(S.pe, lambda e, k=k: e.matmul(pp[:], lhsT=wd[:, k, g * 3 + qk, :],
                                                                   rhs=actT[:, k, tok], start=(k == 0), stop=(k == 7)),
                                     reads=[("wd", g * 3 + qk)] + ares, writes=[rpp])
                            if (g * 2 + qk) % 2 == 0:
                                S.op(S.act, lambda e: e.copy(out=QK[g][qk][:, tok], in_=pp[:]),
                                     reads=[rpp], writes=[("QK", g, qk, j)])
                            else:
                                S.op(S.dve, lambda e: e.tensor_copy(out=QK[g][qk][:, tok], in_=pp[:]),
                                     reads=[rpp], writes=[("QK", g, qk, j)])
                for g in range(3):
                    d = DIL[g]
                    for b4 in range(8):
                        pp, rpp = next_pp()
                        ppv = pp[:].rearrange("p (b d) -> p b d", d=128)
                        for bb in range(4):
                            blk = b4 * 4 + bb
                            n, r = blk // d, blk % d
                            t0 = n * 128 * d + r
                            for k in range(8):
                                S.op(S.pe, lambda e, k=k: e.matmul(
                                    ppv[:, bb, :], lhsT=actT[:, k, t0:t0 + 127 * d + 1:d], rhs=wd[:, k, g * 3 + 2, :],
                                    start=(k == 0), stop=(k == 7)),
                                     reads=[("wd", g * 3 + 2)] + [("actT", m) for m in range(n * d, (n + 1) * d)], writes=[rpp])
                        S.op(S.act, lambda e: e.copy(out=Vx[g][:, b4 * 4:(b4 + 1) * 4, 0:64], in_=ppv[:, :, 0:64]),
                             reads=[rpp], writes=[("Vx", g)])
                        S.op(S.dve, lambda e: e.tensor_copy(out=Vx[g][:, b4 * 4:(b4 + 1) * 4, 128:192],
                                                            in_=ppv[:, :, 64:128]),
                             reads=[rpp], writes=[("Vx", g)])
                items = []
                for hl in range(2):
                    for U in range(2):
                        for g in range(3):
                            for qb in range(16):
                                items.append(dict(hl=hl, U=U, g=g, qb=qb, pre=[], preC=[], post=[]))
                        items[-48]["preC"].append(lambda: S.op(S.dve, lambda e: e.memset(p_o[:], 0.0), writes=["po"]))
                        for qq in range(4):
                            items[-1]["post"].append(lambda hl=hl, U=U, qq=qq: fin(hl, U, qq))

                def geom(it):
                    hl, U, g, qb = it["hl"], it["U"], it["g"], it["qb"]
                    d = DIL[g]
                    blk = U * 16 + qb
                    n, r = blk // d, blk % d
                    t0 = n * 128 * d + r
                    return hl, U, g, d, blk, n, t0

                def stA(i, it):
                    hl, U, g, d, blk, n, t0 = geom(it)
                    rows = slice(hl * 64, hl * 64 + 64)
                    Kt, Qt = QK[g][1], QK[g][0]
                    qsl = slice(t0, t0 + 127 * d + 1, d)
                    psb = p_s[i % 2]
                    half = 0
                    rps = ("ps", i % 2)
                    if n > 0:
                        tp = t0 - 128 * d
                        S.op(S.pe, lambda e: e.matmul(psb[:, half:half + 128], lhsT=Kt[rows, tp:tp + 127 * d + 1:d],
                                                      rhs=Qt[rows, qsl], start=True, stop=True),
                             reads=[("QKall",)], writes=[rps, ("pb", i % 2 + 1)])
                    S.op(S.pe, lambda e: e.matmul(psb[:, half + 128:half + 256], lhsT=Kt[rows, qsl],
                                                  rhs=Qt[rows, qsl], start=True, stop=True),
                         reads=[("QKall",)], writes=[rps, ("pb", i % 2 + 1)])

                def stB(i, it):
                    hl, U, g, d, blk, n, t0 = geom(it)
                    psb = p_s[i % 2]
                    half = 0
                    rps = ("ps", i % 2)
                    ptb, rpt = pt[i % 4], ("pt", i % 4)
                    c0 = 0 if n > 0 else 128
                    S.op(S.act, lambda e: e.activation(out=ptb[:, c0:256], in_=psb[:, half + c0:half + 256],
                                                       func=AF.Exp, scale=0.125), reads=[rps], writes=[rpt])
                    S.op(S.dve, lambda e: e.tensor_tensor(out=ptb[:, c0:256], in0=ptb[:, c0:256],
                                                          in1=mask2[:, c0:256], op=ALU.mult),
                         reads=[rpt, "mask2"], writes=[rpt])

                def stC(i, it):
                    hl, U, g, d, blk, n, t0 = geom(it)
                    vsl = slice(0, 128) if hl == 0 else slice(64, 192)
                    osl = slice(t0 - U * 2048, t0 - U * 2048 + 127 * d + 1, d)
                    ptb, rpt = pt[i % 4], ("pt", i % 4)
                    if n > 0:
                        S.op(S.pe, lambda e: e.matmul(p_o[:, osl], lhsT=Vx[g][:, blk - d, vsl], rhs=ptb[:, 0:128],
                                                      start=False, stop=False, skip_group_check=True),
                             reads=[("Vx", g), rpt, "po"], writes=["po"])
                    S.op(S.pe, lambda e: e.matmul(p_o[:, osl], lhsT=Vx[g][:, blk, vsl], rhs=ptb[:, 128:256],
                                                  start=False, stop=False, skip_group_check=True),
                         reads=[("Vx", g), rpt, "po"], writes=["po"])

                n_it = len(items)
                for t_ in range(n_it + 3):
                    if t_ < n_it:
                        for f in items[t_]["pre"]:
                            f()
                        stA(t_, items[t_])
                    if 0 <= t_ - 1 < n_it:
                        stB(t_ - 1, items[t_ - 1])
                    if 0 <= t_ - 3 < n_it:
                        for f in items[t_ - 3]["preC"]:
                            f()
                        stC(t_ - 3, items[t_ - 3])
                        for f in items[t_ - 3]["post"]:
                            f()
                S.dma("sp", lambda e: e.dma_start(out=self.oT_d[hp], in_=osb[:]), reads=["osb"], writes=[("oTd", hp)])
            S.barrier()
            self.load_oT()
            S.barrier()
        self.out_proj_phase(L, self.dil_wo)


    def rwkv_phase(self, L):
        nc, S = self.nc, self.S
        actT = self.actT
        RT = 256
        NT_ = T // RT
        GN_EPS = 64e-5
        DEC = -float(np.exp(-0.5))
        with ExitStack() as ph:
            sb = lambda n, shp, dt: ph.enter_context(nc.sbuf_tensor(self.nm(n), shp, dt))
            ps = lambda n, shp, dt: ph.enter_context(nc.psum_tensor(self.nm(n), shp, dt))
            mu = sb("r_mu", [128, 8, 6], F32)
            omu = sb("r_omu", [128, 8, 6], F32)
            vec = sb("r_vec", [128, 8, 8], F32)
            stage = sb("r_stage", [128, 8, 160], F32)
            l1a = sb("r_l1a", [128, 8, 288], BF16)
            l1b = sb("r_l1b", [128, 8, 288], BF16)
            hwa = sb("r_hwa", [128, T], BF16)
            hg = sb("r_hg", [128, 2, T], BF16)
            w2 = sb("r_w2", [128, D], BF16)
            g2 = sb("r_g2", [128, 2, D], BF16)
            wa = [sb(f"r_wa{i}", [128, 8, 128], BF16) for i in range(3)]
            wb = [sb(f"r_wb{i}", [128, 8, 128], BF16) for i in range(3)]
            f32t = lambda n: sb(n, [128, RT], F32)
            r_, k_, v_, lw, a_, g_, kk, km, be, Lc, eL, eLn, eLp, eD, tA, tB = [f32t(f"r_f{i}") for i in range(16)]
            LC = sb("r_LC", [128, 4], F32)
            eLC = sb("r_eLC", [128, 4], F32)
            rmask = sb("r_rmask", [128, RT], F32)
            blk1 = sb("r_blk1", [128, 128], F32)
            bdn = ["kap", "rt", "kt", "bt", "vf", "kb", "bb"]
            BD = {n: sb("r_bd_" + n, [128, 4, 128], F32) for n in bdn}
            MkvT, AkrT, AbrT, Y = [sb(f"r_m{i}", [128, 4, 128], F32) for i in range(4)]
            X = [sb(f"r_X{i}", [128, 4, 128], F32) for i in range(2)]
            XT = [sb(f"r_XT{i}", [128, 4, 128], F32) for i in range(2)]
            Vtok, Ktok, Btok = [sb(f"r_tk{i}", [128, 4, 128], F32) for i in range(3)]
            Wsb = sb("r_Wsb", [128, 128], F32)
            nU = sb("r_nU", [128, 128], F32)
            Abd = sb("r_Abd", [128, 128], F32)
            Osb4 = sb("r_Osb4", [128, 4, 128], F32)
            st64 = sb("r_st64", [128, 4, 6], F32)
            mv4 = sb("r_mv4", [128, 4, 8], F32)
            On = sb("r_On", [128, 4, 128], F32)
            ysb = sb("r_ysb", [128, RT], F32)
            osb = [sb(f"r_osb{i}", [128, RT], BF16) for i in range(2)]
            SU4, UI4, SL4, ID4 = [sb(f"r_msk{i}", [128, 4, 128], F32) for i in range(4)]
            identf = sb("r_identf", [128, 128], F32)
            st6 = sb("r_st6", [128, 6], F32)
            mv = sb("r_mv", [128, 8], F32)
            pP = [ps(f"r_pP{i}", [128, 512], F32) for i in range(2)]
            pX = [ps(f"r_pX{i}", [128, 512], F32) for i in range(5)]
            pS = ps("r_pS", [128, 512], F32)

            def aff(t, pattern, cm, op, base=0):
                S.op(S.pool, lambda e: e.affine_select(out=t, in_=t, pattern=pattern, compare_op=op, fill=0.0,
                                                       base=base, channel_multiplier=cm),
                     reads=["cst"], writes=["cst"])
            S.op(S.pool, lambda e: e.memset(identf[:], 1.0), writes=["cst"])
            aff(identf[:], [[1, 128]], -1, ALU.is_equal)
            for t4 in (SU4, UI4, SL4, ID4):
                S.op(S.pool, lambda e, t4=t4: e.memset(t4[:], 0.0), reads=["cst"], writes=["cst"])
            S.op(S.pool, lambda e: e.memset(blk1[:], 0.0), reads=["cst"], writes=["cst"])
            for hb in range(2):
                rs_ = slice(hb * 64, hb * 64 + 64)
                S.op(S.pool, lambda e: e.memset(blk1[rs_, rs_], 1.0), reads=["cst"], writes=["cst"])
                for c in range(4):
                    for t4 in (SU4, UI4, SL4, ID4):
                        S.op(S.pool, lambda e, t4=t4: e.memset(t4[rs_, c, rs_], 1.0), reads=["cst"], writes=["cst"])
                    aff(SU4[rs_, c, rs_], [[1, 64]], -1, ALU.is_gt)
                    aff(UI4[rs_, c, rs_], [[1, 64]], -1, ALU.is_ge)
                    aff(SL4[rs_, c, rs_], [[-1, 64]], 1, ALU.is_gt)
                    aff(ID4[rs_, c, rs_], [[1, 64]], -1, ALU.is_equal)
            S.op(S.pool, lambda e: e.memset(rmask[:], 1.0), reads=["cst"], writes=["cst"])
            for c in range(4):
                S.op(S.pool, lambda e, c=c: e.memset(rmask[:, c * 64:c * 64 + 1], 0.0), reads=["cst"], writes=["cst"])
            for n in bdn:
                S.op(S.pool, lambda e, n=n: e.memset(BD[n][:], 0.0), writes=[("bd", n)])
            S.op(S.pool, lambda e: e.memset(On[:], 0.0), writes=["On"])

            S.dma("sp", lambda e: e.dma_start(out=mu[:], in_=self.rw_mu), writes=["mu"])
            S.dma("sp", lambda e: e.dma_start(out=vec[:], in_=self.rw_vec), writes=["vec"])
            S.op(S.dve, lambda e: e.tensor_scalar(out=omu[:], in0=mu[:], scalar1=-1.0, scalar2=1.0, op0=ALU.mult,
                                                  op1=ALU.add), reads=["mu"], writes=["omu"])
            S.op(S.dve, lambda e: e.tensor_scalar(out=vec[:, :, 4:5], in0=vec[:, :, 3:4], scalar1=-1.0, scalar2=1.0,
                                                  op0=ALU.mult, op1=ALU.add), reads=["vec"], writes=["vec"])
            S.dma("pool", lambda e: e.dma_start(out=w2[0:64, :], in_=self.rw_w2), writes=["w2"])
            S.dma("pool", lambda e: e.dma_start(out=w2[64:128, :], in_=self.rw_a2), writes=["a2"])
            S.dma("pool", lambda e: e.dma_start(out=g2[:, 0, :], in_=self.rw_g2[0:128, :]), writes=["g2a"])
            S.dma("pool", lambda e: e.dma_start(out=g2[0:32, 1, :], in_=self.rw_g2[128:160, :]), writes=["g2b"])

            def scaled_weights(src3, ncols, dsts_a, dsts_b, jcols):
                S.dma("sp", lambda e: e.dma_start(out=stage[:, :, 0:ncols], in_=src3), writes=["stage"])
                for (c0, c1, j), da, db in zip(jcols, dsts_a, dsts_b):
                    for k in range(8):
                        S.op(S.pool, lambda e, k=k: e.tensor_scalar(out=da[:, k, :], in0=stage[:, k, c0:c1],
                                                                    scalar1=omu[:, k, j:j + 1], scalar2=None,
                                                                    op0=ALU.mult),
                             reads=["stage", "omu"], writes=["wsc"])
                        S.op(S.pool, lambda e, k=k: e.tensor_scalar(out=db[:, k, :], in0=stage[:, k, c0:c1],
                                                                    scalar1=mu[:, k, j:j + 1], scalar2=None,
                                                                    op0=ALU.mult),
                             reads=["stage", "mu"], writes=["wsc"])

            def proj(pout, la, lb, j, M=128, r0=0):
                t0 = j * RT
                ares = [("actT", m) for m in range(max(0, 2 * j - 1), 2 * j + 2)]
                for k in range(8):
                    S.op(S.pe, lambda e, k=k: e.matmul(pout[r0:r0 + M, 0:RT], lhsT=la(k), rhs=actT[:, k, t0:t0 + RT],
                                                       start=(k == 0), stop=False),
                         reads=["wsc"] + ares, writes=["pP"])
                c0 = 1 if j == 0 else 0
                for k in range(8):
                    S.op(S.pe, lambda e, k=k: e.matmul(pout[r0:r0 + M, c0:RT], lhsT=lb(k),
                                                       rhs=actT[:, k, t0 - 1 + c0:t0 + RT - 1],
                                                       start=False, stop=(k == 7)),
                         reads=["wsc"] + ares, writes=["pP"])

            l1src = self.rw_l1.rearrange("(k p) n -> p k n", p=128)
            scaled_weights(l1src[:, :, 0:128], 128, [l1a[:, :, 0:64], l1a[:, :, 64:128]],
                           [l1b[:, :, 0:64], l1b[:, :, 64:128]], [(0, 64, 3), (64, 128, 4)])
            scaled_weights(l1src[:, :, 128:288], 160, [l1a[:, :, 128:288]], [l1b[:, :, 128:288]], [(0, 160, 5)])
            for j in range(NT_):
                tok = slice(j * RT, (j + 1) * RT)
                p0 = pP[j % 2]
                proj(p0, lambda k: l1a[:, k, 0:128], lambda k: l1b[:, k, 0:128], j)
                S.op(S.act, lambda e: e.activation(out=hwa[0:64, tok], in_=p0[0:64, 0:RT], func=AF.Tanh),
                     reads=["pP"], writes=["hwa"])
                S.op(S.act, lambda e: e.copy(out=hwa[64:128, tok], in_=p0[64:128, 0:RT]), reads=["pP"], writes=["hwa"])
                proj(p0, lambda k: l1a[:, k, 128:256], lambda k: l1b[:, k, 128:256], j)
                S.op(S.act, lambda e: e.activation(out=hg[:, 0, tok], in_=p0[:, 0:RT], func=AF.Sigmoid),
                     reads=["pP"], writes=["hg"])
                proj(p0, lambda k: l1a[:, k, 256:288], lambda k: l1b[:, k, 256:288], j, M=32)
                S.op(S.act, lambda e: e.activation(out=hg[0:32, 1, tok], in_=p0[0:32, 0:RT], func=AF.Sigmoid),
                     reads=["pP"], writes=["hg"])

            wsrc = self.rw_wrkv.rearrange("j (k p) n -> j p k n", p=128)
            H0, H1 = slice(0, 64), slice(64, 128)

            def dv(fn, reads, writes, eng=None):
                S.op(eng or S.dve, fn, reads=reads, writes=writes)

            import os
            STOP = int(os.environ.get("RW_STOP", "9"))
            for hp in range(8 if STOP > 0 else 0):
                cols = slice(hp * 128, (hp + 1) * 128)
                for jj in range(3):
                    scaled_weights(wsrc[jj][:, :, cols], 128, [wa[jj]], [wb[jj]], [(0, 128, jj)])
                S.op(S.pool, lambda e: e.memset(Abd[:], 0.0), reads=["Abd"], writes=["Abd"])
                vcol = lambda i: vec[:, hp, i:i + 1]
                for j in range(NT_):
                    tok = slice(j * RT, (j + 1) * RT)
                    F = "F"
                    for jj, dst in enumerate((r_, k_, v_)):
                        p0 = pP[jj % 2]
                        proj(p0, lambda k: wa[jj][:, k, :], lambda k: wb[jj][:, k, :], j)
                        S.op(S.act, lambda e: e.copy(out=dst[:], in_=p0[:, 0:RT]), reads=["pP"], writes=[F])
                    p0 = pP[1]
                    S.op(S.pe, lambda e: e.matmul(p0[:, 0:RT], lhsT=w2[0:64, cols], rhs=hwa[0:64, tok], start=True,
                                                  stop=True), reads=["w2", "hwa"], writes=["pP"])
                    S.op(S.act, lambda e: e.activation(out=lw[:], in_=p0[:, 0:RT], func=AF.Sigmoid, bias=vcol(0)),
                         reads=["pP", "vec"], writes=[F])
                    S.op(S.pe, lambda e: e.matmul(p0[:, 0:RT], lhsT=w2[64:128, cols], rhs=hwa[64:128, tok], start=True,
                                                  stop=True), reads=["a2", "hwa"], writes=["pP"])
                    S.op(S.act, lambda e: e.activation(out=a_[:], in_=p0[:, 0:RT], func=AF.Sigmoid, bias=vcol(1)),
                         reads=["pP", "vec"], writes=[F])
                    S.op(S.pe, lambda e: e.matmul(p0[:, 0:RT], lhsT=g2[:, 0, cols], rhs=hg[:, 0, tok], start=True,
                                                  stop=False), reads=["g2a", "hg"], writes=["pP"])
                    S.op(S.pe, lambda e: e.matmul(p0[:, 0:RT], lhsT=g2[0:32, 1, cols], rhs=hg[0:32, 1, tok], start=False,
                                                  stop=True), reads=["g2b", "hg"], writes=["pP"])
                    S.op(S.act, lambda e: e.copy(out=g_[:], in_=p0[:, 0:RT]), reads=["pP"], writes=[F])
                    dv(lambda e: e.tensor_scalar(out=lw[:], in0=lw[:], scalar1=DEC, scalar2=None, op0=ALU.mult), [F], [F])
                    dv(lambda e: e.tensor_scalar(out=kk[:], in0=k_[:], scalar1=vcol(2), scalar2=None, op0=ALU.mult),
                       [F, "vec"], [F])
                    dv(lambda e: e.tensor_tensor(out=tA[:], in0=kk[:], in1=kk[:], op=ALU.mult), [F], [F], S.pool)
                    S.op(S.pe, lambda e: e.matmul(pP[0][:, 0:RT], lhsT=blk1[:], rhs=tA[:], start=True, stop=True),
                         reads=[F, "cst"], writes=["pP"])
                    S.op(S.act, lambda e: e.activation(out=tA[:], in_=pP[0][:, 0:RT], func=AF.Sqrt), reads=["pP"], writes=[F])
                    dv(lambda e: e.tensor_scalar(out=tA[:], in0=tA[:], scalar1=1e-12, scalar2=None, op0=ALU.max), [F], [F])
                    dv(lambda e: e.reciprocal(out=tA[:], in_=tA[:]), [F], [F])
                    dv(lambda e: e.tensor_tensor(out=kk[:], in0=kk[:], in1=tA[:], op=ALU.mult), [F], [F])
                    dv(lambda e: e.tensor_scalar(out=tB[:], in0=a_[:], scalar1=vcol(3), scalar2=vcol(4), op0=ALU.mult,
                                                 op1=ALU.add), [F, "vec"], [F])
                    dv(lambda e: e.tensor_tensor(out=km[:], in0=k_[:], in1=tB[:], op=ALU.mult), [F], [F])
                    dv(lambda e: e.tensor_tensor(out=be[:], in0=kk[:], in1=a_[:], op=ALU.mult), [F], [F], S.pool)
                    dv(lambda e: e.scalar_tensor_tensor(out=tB[:], in0=r_[:], scalar=vcol(5), in1=km[:], op0=ALU.mult,
                                                        op1=ALU.mult), [F, "vec"], [F])
                    S.op(S.pe, lambda e: e.matmul(pP[0][:, 0:RT], lhsT=blk1[:], rhs=tB[:], start=True, stop=True),
                         reads=[F, "cst"], writes=["pP"])
                    dv(lambda e: e.tensor_tensor(out=tA[:], in0=pP[0][:, 0:RT], in1=v_[:], op=ALU.mult), ["pP", F], [F])
                    dv(lambda e: e.tensor_tensor_scan(out=Lc[:], data0=rmask[:], data1=lw[:], initial=0.0,
                                                      op0=ALU.mult, op1=ALU.add), [F, "cst"], [F])
                    S.op(S.act, lambda e: e.activation(out=eL[:], in_=Lc[:], func=AF.Exp), reads=[F], writes=[F])
                    S.op(S.act, lambda e: e.activation(out=eLn[:], in_=Lc[:], func=AF.Exp, scale=-1.0), reads=[F], writes=[F])
                    dv(lambda e: e.tensor_tensor(out=eLp[:], in0=Lc[:], in1=lw[:], op=ALU.subtract), [F], [F], S.pool)
                    S.op(S.act, lambda e: e.activation(out=eLp[:], in_=eLp[:], func=AF.Exp), reads=[F], writes=[F])
                    L3 = Lc[:].rearrange("p (c t) -> p c t", t=64)
                    dv(lambda e: e.tensor_copy(out=LC[:], in_=L3[:, :, 63]), [F], [F])
                    S.op(S.act, lambda e: e.activation(out=eLC[:], in_=LC[:], func=AF.Exp), reads=[F], writes=[F])
                    for c in range(4):
                        S.op(S.act, lambda e, c=c: e.activation(out=eD[:, c * 64:(c + 1) * 64], in_=Lc[:, c * 64:(c + 1) * 64],
                                                                func=AF.Exp, scale=-1.0, bias=LC[:, c:c + 1]),
                             reads=[F], writes=[F])
                    prods = [("kap", kk, eLp), ("rt", r_, eL), ("kt", km, eLn), ("bt", be, eLn), ("kb", km, eD),
                             ("bb", be, eD)]
                    ie = 0
                    for n, x0, x1 in prods:
                        for hs in (H0, H1):
                            eng = S.dve if ie % 2 == 0 else S.pool
                            ie += 1
                            dv(lambda e: e.tensor_tensor(out=BD[n][hs, :, hs],
                                                         in0=x0[hs, :].rearrange("p (c t) -> p c t", t=64),
                                                         in1=x1[hs, :].rearrange("p (c t) -> p c t", t=64), op=ALU.mult),
                               [F], [("bd", n)], eng)
                    for hs in (H0, H1):
                        S.op(S.act, lambda e: e.copy(out=BD["vf"][hs, :, hs], in_=v_[hs, :].rearrange("p (c t) -> p c t", t=64)),
                             reads=[F], writes=[("bd", "vf")])
                    if STOP < 2:
                        continue
                    for c in range(4):
                        cs = slice(c * 128, (c + 1) * 128)
                        gm = [(0, "kt", "kap"), (1, "kt", "rt"), (2, "bt", "kap"), (3, "bt", "rt"), (4, "kap", "bt")]
                        for pi, ln_, rn_ in gm:
                            S.op(S.pe, lambda e: e.matmul(pX[pi][:, cs], lhsT=BD[ln_][:, c, :], rhs=BD[rn_][:, c, :],
                                                          start=True, stop=True),
                                 reads=[("bd", ln_), ("bd", rn_)], writes=[("pX", pi)])
                    f4 = lambda t: t[:].rearrange("p c t -> p (c t)")
                    dv(lambda e: e.tensor_tensor(out=f4(MkvT), in0=pX[0][:], in1=f4(SU4), op=ALU.mult), [("pX", 0), "cst"], ["MkvT"])
                    dv(lambda e: e.tensor_tensor(out=f4(AkrT), in0=pX[1][:], in1=f4(UI4), op=ALU.mult), [("pX", 1), "cst"], ["AkrT"])
                    dv(lambda e: e.tensor_tensor(out=f4(X[0]), in0=pX[2][:], in1=f4(SU4), op=ALU.mult), [("pX", 2), "cst"], [("X", 0)])
                    dv(lambda e: e.tensor_tensor(out=f4(AbrT), in0=pX[3][:], in1=f4(UI4), op=ALU.mult), [("pX", 3), "cst"], ["AbrT"])
                    dv(lambda e: e.tensor_tensor(out=f4(XT[0]), in0=pX[4][:], in1=f4(SL4), op=ALU.mult), [("pX", 4), "cst"], [("XT", 0)])
                    if STOP < 3:
                        continue
                    dv(lambda e: e.tensor_tensor(out=f4(Y), in0=f4(ID4), in1=f4(X[0]), op=ALU.subtract),
                       [("X", 0), "cst"], ["Y"], S.pool)
                    cur = 0
                    for lvl in range(int(os.environ.get('RW_LVL', '5'))):
                        nxt = 1 - cur
                        last = (lvl == 4)
                        for c in range(4):
                            cs = slice(c * 128, (c + 1) * 128)
                            if not last:
                                S.op(S.pe, lambda e: e.matmul(pX[0][:, cs], lhsT=XT[cur][:, c, :], rhs=X[cur][:, c, :],
                                                              start=True, stop=True),
                                     reads=[("X", cur), ("XT", cur)], writes=[("pX", 0)])
                            S.op(S.pe, lambda e: e.matmul(pX[2][:, cs], lhsT=X[cur][:, c, :], rhs=XT[cur][:, c, :],
                                                          start=True, stop=True),
                                 reads=[("X", cur), ("XT", cur)], writes=[("pX", 2)])
                        if not last:
                            dv(lambda e: e.tensor_copy(out=f4(X[nxt]), in_=pX[0][:]), [("pX", 0)], [("X", nxt)])
                        S.op(S.act, lambda e: e.copy(out=f4(XT[nxt]), in_=pX[2][:]), reads=[("pX", 2)], writes=[("XT", nxt)])
                        for c in range(4):
                            cs = slice(c * 128, (c + 1) * 128)
                            S.op(S.pe, lambda e: e.matmul(pX[1][:, cs], lhsT=XT[nxt][:, c, :],
                                                          rhs=Y[:, c, :], start=True, stop=True),
                                 reads=[("XT", nxt), "Y"], writes=[("pX", 1)])
                        dv(lambda e: e.tensor_tensor(out=f4(Y), in0=f4(Y), in1=pX[1][:], op=ALU.add),
                           [("pX", 1), "Y"], ["Y"])
                        cur = nxt
                    for pi, n, dst in ((3, "vf", Vtok), (4, "kb", Ktok), (1, "bb", Btok)):
                        for c in range(4):
                            S.op(S.pe, lambda e: e.transpose(out=pX[pi][:, c * 128:(c + 1) * 128], in_=BD[n][:, c, :],
                                                             identity=identf[:]),
                                 reads=[("bd", n), "cst"], writes=[("pX", pi)])
                        S.op(S.act, lambda e: e.copy(out=f4(dst), in_=pX[pi][:]), reads=[("pX", pi)], writes=[("tok", n)])
                    if STOP < 4:
                        continue
                    for c in range(4):
                        S.op(S.pe, lambda e: e.matmul(pX[0][:, 0:128], lhsT=BD["kap"][:, c, :], rhs=Abd[:], start=True, stop=False),
                             reads=[("bd", "kap"), "Abd"], writes=["pW"])
                        S.op(S.pe, lambda e: e.matmul(pX[0][:, 0:128], lhsT=MkvT[:, c, :], rhs=Vtok[:, c, :], start=False, stop=True),
                             reads=["MkvT", ("tok", "vf")], writes=["pW"])
                        S.op(S.act, lambda e: e.copy(out=Wsb[:], in_=pX[0][:, 0:128]), reads=["pW"], writes=["Wsb"])
                        S.op(S.pe, lambda e: e.matmul(pX[1][:, 0:128], lhsT=Y[:, c, :], rhs=Wsb[:], start=True, stop=True),
                             reads=["Y", "Wsb"], writes=["pU"])
                        dv(lambda e: e.tensor_scalar(out=nU[:], in0=pX[1][:, 0:128], scalar1=-1.0, scalar2=None, op0=ALU.mult),
                           ["pU"], ["nU"])
                        S.op(S.pe, lambda e: e.matmul(pX[3][:, 0:128], lhsT=Ktok[:, c, :], rhs=Vtok[:, c, :], start=True, stop=False),
                             reads=[("tok", "kb"), ("tok", "vf")], writes=["pA"])
                        S.op(S.pe, lambda e: e.matmul(pX[3][:, 0:128], lhsT=Btok[:, c, :], rhs=nU[:], start=False, stop=True),
                             reads=[("tok", "bb"), "nU"], writes=["pA"])
                        oc = pX[2][:, c * 128:(c + 1) * 128]
                        S.op(S.pe, lambda e: e.matmul(oc, lhsT=BD["rt"][:, c, :], rhs=Abd[:], start=True, stop=False),
                             reads=[("bd", "rt"), "Abd"], writes=["pO"])
                        S.op(S.pe, lambda e: e.matmul(oc, lhsT=AkrT[:, c, :], rhs=Vtok[:, c, :], start=False, stop=False),
                             reads=["AkrT", ("tok", "vf")], writes=["pO"])
                        S.op(S.pe, lambda e: e.matmul(oc, lhsT=AbrT[:, c, :], rhs=nU[:], start=False, stop=True),
                             reads=["AbrT", "nU"], writes=["pO"])
                        dv(lambda e: e.scalar_tensor_tensor(out=Abd[:], in0=Abd[:], scalar=eLC[:, c:c + 1], in1=pX[3][:, 0:128],
                                                            op0=ALU.mult, op1=ALU.add), ["pA", "Abd", F], ["Abd"])
                    S.op(S.act, lambda e: e.copy(out=f4(Osb4), in_=pX[2][:]), reads=["pO"], writes=["Osb"])
                    for c in range(4):
                        for hs in (H0, H1):
                            dv(lambda e: e.bn_stats(out=st64[hs, c, :], in_=Osb4[hs, c, hs]), ["Osb"], ["st6"])
                        dv(lambda e: e.bn_aggr(out=mv4[:, c, 0:2], in_=st64[:, c, :]), ["st6"], ["mv"])
                    dv(lambda e: e.tensor_scalar(out=mv4[:, :, 2:3], in0=mv4[:, :, 1:2], scalar1=GN_EPS, scalar2=None, op0=ALU.add),
                       ["mv"], ["mv"])
                    S.op(S.act, lambda e: e.activation(out=mv4[:, :, 3:4], in_=mv4[:, :, 2:3], func=AF.Sqrt), reads=["mv"], writes=["mv"])
                    dv(lambda e: e.reciprocal(out=mv4[:, :, 4:5], in_=mv4[:, :, 3:4]), ["mv"], ["mv"])
                    dv(lambda e: e.scalar_tensor_tensor(out=mv4[:, :, 5:6], in0=mv4[:, :, 0:1], scalar=-1.0, in1=mv4[:, :, 4:5],
                                                        op0=ALU.mult, op1=ALU.mult), ["mv"], ["mv"])
                    for c in range(4):
                        for hs in (H0, H1):
                            S.op(S.act, lambda e: e.activation(out=On[hs, c, hs], in_=Osb4[hs, c, hs], func=AF.Identity,
                                                               bias=mv4[hs, c, 5:6], scale=mv4[hs, c, 4:5]),
                                 reads=["mv", "Osb"], writes=["On"])
                    for c in range(4):
                        S.op(S.pe, lambda e: e.transpose(out=pP[1][:, c * 128:(c + 1) * 128], in_=On[:, c, :], identity=identf[:]),
                             reads=["On", "cst"], writes=["pP"])
                    for hs in (H0, H1):
                        dv(lambda e: e.tensor_scalar(out=ysb[hs, :].rearrange("p (c t) -> p c t", t=64),
                                                     in0=pP[1][:].rearrange("p (c t) -> p c t", t=128)[hs, :, hs],
                                                     scalar1=vec[hs, hp, 6:7], scalar2=vec[hs, hp, 7:8], op0=ALU.mult,
                                                     op1=ALU.add), ["pP", "vec"], ["ysb"])
                    dv(lambda e: e.tensor_tensor(out=ysb[:], in0=ysb[:], in1=tA[:], op=ALU.add), ["ysb", F], ["ysb"], S.pool)
                    ob = osb[j % 2]
                    dv(lambda e: e.tensor_tensor(out=ob[:], in0=ysb[:], in1=g_[:], op=ALU.mult), ["ysb", F], [("osb", j % 2)], S.pool)
                    S.dma("sp", lambda e: e.dma_start(out=self.oT_d[hp][:, tok], in_=ob[:]), reads=[("osb", j % 2)],
                          writes=[("oTd", hp)])
            S.barrier()
            self.load_oT()
            S.barrier()
        self.out_proj_phase(L, self.rw_wo)


    def rwkv_phase2(self, L):
        import os
        nc, S = self.nc, self.S
        actT = self.actT
        RT = 256
        NCk = RT // 64
        NT_ = T // RT
        GN_EPS = 64e-5
        DEC = -float(np.exp(-0.5))
        H0, H1 = slice(0, 64), slice(64, 128)
        with ExitStack() as ph:
            sb = lambda n, shp, dt: ph.enter_context(nc.sbuf_tensor(self.nm(n), shp, dt))
            ps = lambda n, shp, dt: ph.enter_context(nc.psum_tensor(self.nm(n), shp, dt))
            PB = [[ps(f"r_P{l}{i}", [128, 512], F32) for i in range(4)] for l in range(2)]
            mu = sb("r_mu", [128, 8, 6], F32)
            omu = sb("r_omu", [128, 8, 6], F32)
            vec = sb("r_vec", [128, 8, 8], F32)
            hwa = sb("r_hwa", [128, T], BF16)
            hg = sb("r_hg", [128, 2, T], BF16)
            w2 = sb("r_w2", [128, D], BF16)
            g2 = sb("r_g2", [128, 2, D], BF16)
            stage = sb("r_stage", [128, 8, 128], F32)
            rmask = sb("r_rmask", [128, RT], BF16)
            blk1 = sb("r_blk1", [128, 128], F32)
            identf = sb("r_identf", [128, 128], F32)
            SUm, UIm, SLm, IDm = [sb(f"r_msk{i}", [128, NCk, 128], BF16) for i in range(4)]

            def dv(fn, reads, writes, eng=None):
                S.op(eng or S.dve, fn, reads=reads, writes=writes)

            def aff(t, pattern, cm, op):
                S.op(S.pool, lambda e: e.affine_select(out=t, in_=t, pattern=pattern, compare_op=op, fill=0.0,
                                                       base=0, channel_multiplier=cm), reads=["cst"], writes=["cst"])
            S.op(S.pool, lambda e: e.memset(identf[:], 1.0), writes=["cst"])
            aff(identf[:], [[1, 128]], -1, ALU.is_equal)
            for t4 in (SUm, UIm, SLm, IDm):
                S.op(S.pool, lambda e, t4=t4: e.memset(t4[:], 0.0), reads=["cst"], writes=["cst"])
            S.op(S.pool, lambda e: e.memset(blk1[:], 0.0), reads=["cst"], writes=["cst"])
            for hb in range(2):
                rs_ = slice(hb * 64, hb * 64 + 64)
                S.op(S.pool, lambda e: e.memset(blk1[rs_, rs_], 1.0), reads=["cst"], writes=["cst"])
                for c in range(NCk):
                    for t4 in (SUm, UIm, SLm, IDm):
                        S.op(S.pool, lambda e, t4=t4: e.memset(t4[rs_, c, rs_], 1.0), reads=["cst"], writes=["cst"])
                    aff(SUm[rs_, c, rs_], [[1, 64]], -1, ALU.is_gt)
                    aff(UIm[rs_, c, rs_], [[1, 64]], -1, ALU.is_ge)
                    aff(SLm[rs_, c, rs_], [[-1, 64]], 1, ALU.is_gt)
                    aff(IDm[rs_, c, rs_], [[1, 64]], -1, ALU.is_equal)
            S.op(S.pool, lambda e: e.memset(rmask[:], 1.0), reads=["cst"], writes=["cst"])
            for c in range(NCk):
                S.op(S.pool, lambda e, c=c: e.memset(rmask[:, c * 64:c * 64 + 1], 0.0), reads=["cst"], writes=["cst"])

            S.dma("sp", lambda e: e.dma_start(out=mu[:], in_=self.rw_mu), writes=["mu"])
            S.dma("sp", lambda e: e.dma_start(out=vec[:], in_=self.rw_vec), writes=["vec"])
            dv(lambda e: e.tensor_scalar(out=omu[:], in0=mu[:], scalar1=-1.0, scalar2=1.0, op0=ALU.mult, op1=ALU.add),
               ["mu"], ["omu"])
            dv(lambda e: e.tensor_scalar(out=vec[:, :, 4:5], in0=vec[:, :, 3:4], scalar1=-1.0, scalar2=1.0,
                                         op0=ALU.mult, op1=ALU.add), ["vec"], ["vec"])
            S.dma("pool", lambda e: e.dma_start(out=w2[0:64, :], in_=self.rw_w2), writes=["w2"])
            S.dma("pool", lambda e: e.dma_start(out=w2[64:128, :], in_=self.rw_a2), writes=["a2"])
            S.dma("pool", lambda e: e.dma_start(out=g2[:, 0, :], in_=self.rw_g2[0:128, :]), writes=["g2a"])
            S.dma("pool", lambda e: e.dma_start(out=g2[0:32, 1, :], in_=self.rw_g2[128:160, :]), writes=["g2b"])

            def scaled_weights(src3, ncols, dsts_a, dsts_b, jcols, wres, stage=stage):
                S.dma("sp", lambda e: e.dma_start(out=stage[:, :, 0:ncols], in_=src3), writes=["stage"])
                for (c0, c1, j), da, db in zip(jcols, dsts_a, dsts_b):
                    for k in range(8):
                        S.op(S.pool, lambda e, k=k: e.tensor_scalar(out=da[:, k, :], in0=stage[:, k, c0:c1],
                                                                    scalar1=omu[:, k, j:j + 1], scalar2=None,
                                                                    op0=ALU.mult),
                             reads=["stage", "omu"], writes=[wres])
                        S.op(S.pool, lambda e, k=k: e.tensor_scalar(out=db[:, k, :], in0=stage[:, k, c0:c1],
                                                                    scalar1=mu[:, k, j:j + 1], scalar2=None,
                                                                    op0=ALU.mult),
                             reads=["stage", "mu"], writes=[wres])

            def proj(pout, pres, la, lb, t0, n_tok, wres, M=128, xs=None, xres=None):
                if xs is not None:
                    for k in range(8):
                        S.op(S.pe, lambda e, k=k: e.matmul(pout[0:M, 0:n_tok], lhsT=la(k), rhs=xs[:, k, 1:n_tok + 1],
                                                           start=(k == 0), stop=False),
                             reads=[wres, xres], writes=[pres])
                    for k in range(8):
                        S.op(S.pe, lambda e, k=k: e.matmul(pout[0:M, 0:n_tok], lhsT=lb(k), rhs=xs[:, k, 0:n_tok],
                                                           start=False, stop=(k == 7)),
                             reads=[wres, xres], writes=[pres])
                    return
                ares = [("actT", m) for m in range(max(0, t0 // 128 - 1), (t0 + n_tok - 1) // 128 + 1)]
                for k in range(8):
                    S.op(S.pe, lambda e, k=k: e.matmul(pout[0:M, 0:n_tok], lhsT=la(k), rhs=actT[:, k, t0:t0 + n_tok],
                                                       start=(k == 0), stop=False),
                         reads=[wres] + ares, writes=[pres])
                c0 = 1 if t0 == 0 else 0
                for k in range(8):
                    S.op(S.pe, lambda e, k=k: e.matmul(pout[0:M, c0:n_tok], lhsT=lb(k),
                                                       rhs=actT[:, k, t0 - 1 + c0:t0 + n_tok - 1],
                                                       start=False, stop=(k == 7)),
                         reads=[wres] + ares, writes=[pres])

            with ExitStack() as ph1:
                sb1 = lambda n, shp, dt: ph1.enter_context(nc.sbuf_tensor(self.nm(n), shp, dt))
                l1a = sb1("r_l1a", [128, 8, 288], BF16)
                l1b = sb1("r_l1b", [128, 8, 288], BF16)
                stage1 = sb1("r_stage1", [128, 8, 160], F32)
                l1src = self.rw_l1.rearrange("(k p) n -> p k n", p=128)
                scaled_weights(l1src[:, :, 0:128], 128, [l1a[:, :, 0:64], l1a[:, :, 64:128]],
                               [l1b[:, :, 0:64], l1b[:, :, 64:128]], [(0, 64, 3), (64, 128, 4)], "l1", stage=stage1)
                scaled_weights(l1src[:, :, 128:288], 160, [l1a[:, :, 128:288]], [l1b[:, :, 128:288]], [(0, 160, 5)], "l1",
                               stage=stage1)
                R1T = 256
                for j in range(T // R1T):
                    tok = slice(j * R1T, (j + 1) * R1T)
                    b0, b1, b2 = PB[0][j % 2], PB[0][2 + j % 2], PB[1][j % 2]
                    r0_, r1_, r2_ = ("P", 0, j % 2), ("P", 0, 2 + j % 2), ("P", 1, j % 2)
                    proj(b0, r0_, lambda k: l1a[:, k, 0:128], lambda k: l1b[:, k, 0:128], j * R1T, R1T, "l1")
                    S.op(S.act, lambda e: e.activation(out=hwa[0:64, tok], in_=b0[0:64, 0:R1T], func=AF.Tanh),
                         reads=[r0_], writes=["hwa"])
                    S.op(S.act, lambda e: e.copy(out=hwa[64:128, tok], in_=b0[64:128, 0:R1T]), reads=[r0_], writes=["hwa"])
                    proj(b1, r1_, lambda k: l1a[:, k, 128:256], lambda k: l1b[:, k, 128:256], j * R1T, R1T, "l1")
                    S.op(S.act, lambda e: e.activation(out=hg[:, 0, tok], in_=b1[:, 0:R1T], func=AF.Sigmoid),
                         reads=[r1_], writes=["hg"])
                    proj(b2, r2_, lambda k: l1a[:, k, 256:288], lambda k: l1b[:, k, 256:288], j * R1T, R1T, "l1", M=32)
                    S.op(S.act, lambda e: e.activation(out=hg[0:32, 1, tok], in_=b2[0:32, 0:R1T], func=AF.Sigmoid),
                         reads=[r2_], writes=["hg"])
                for c in range(8):
                    S.dma("sp", lambda e, c=c: e.dma_start(out=self.xT_d[c], in_=actT[:, c, :]), writes=[("xTd", c)])
                S.barrier()

            flat = actT[:].rearrange("p k t -> p (k t)")
            arena = {"off": 0}

            def carve(shape, dt):
                n = int(np.prod(shape[1:]))
                nb = n * (2 if dt == F32 else 1)
                assert arena["off"] + nb <= 8 * T, "lane-1 arena overflow"
                v = flat[:, arena["off"]:arena["off"] + nb]
                arena["off"] += nb
                if dt == F32:
                    v = v.bitcast(F32)
                if len(shape) == 3:
                    v = v.rearrange("p (a b) -> p a b", b=shape[2])
                return v

            class _T:
                def __init__(self, ap):
                    self.ap = ap

                def __getitem__(self, key):
                    return self.ap[key]

            bdn = ["kap", "rt", "kt", "bt", "vf", "kb", "bb"]
            lanes = []
            for l in range(2):
                Bf = {}

                def lb_(name, shape, dt, l=l, small=False):
                    if l == 0 or small:
                        return sb(f"r{l}_{name}", shape, dt)
                    return _T(carve(shape, dt))
                for n in ["r_", "k_", "v_", "lw", "a_", "g_", "kk", "km", "be", "Lc", "eL", "eLn", "eLp", "eD", "tA", "tB", "ysb"]:
                    Bf[n] = lb_(n, [128, RT], F32)
                Bf["LC"] = lb_("LC", [128, NCk], F32, small=True)
                Bf["eLC"] = lb_("eLC", [128, NCk], F32, small=True)
                Bf["BD"] = {n: lb_("bd_" + n, [128, NCk, 128], F32) for n in bdn}
                for n in ["MkvT", "AkrT", "AbrT", "Y", "X0", "X1", "XT0", "XT1", "Vtok", "Ktok", "Btok", "Osb4", "On"]:
                    Bf[n] = lb_(n, [128, NCk, 128], F32, small=(n in ("Osb4", "On", "Btok")))
                for n in ["Wsb", "nU", "Abd"]:
                    Bf[n] = lb_(n, [128, 128], F32, small=True)
                Bf["osb"] = [lb_(f"osb{i}", [128, RT], BF16, small=True) for i in range(2)]
                Bf["st64"] = lb_("st64", [128, NCk, 6], F32, small=True)
                Bf["mv4"] = lb_("mv4", [128, NCk, 8], F32, small=True)
                Bf["wa"] = [lb_(f"wa{i}", [128, 8, 128], BF16) for i in range(3)]
                Bf["wb"] = [lb_(f"wb{i}", [128, 8, 128], BF16) for i in range(3)]
                Bf["xs"] = lb_("xs", [128, 8, RT + 1], BF16, small=True)
                for n in bdn:
                    S.op(S.pool, lambda e, n=n: e.memset(Bf["BD"][n][:], 0.0), writes=[(l, "bd", n)])
                S.op(S.pool, lambda e: e.memset(Bf["On"][:], 0.0), writes=[(l, "On")])
                lanes.append(Bf)

            wsrc = self.rw_wrkv.rearrange("j (k p) n -> j p k n", p=128)
            f4 = lambda t: t[:].rearrange("p c t -> p (c t)")
            W_ = NCk * 128

            xsrc = self.xT_d.rearrange("c p t -> p c t")

            def load_xs(l, j):
                xs = lanes[l]["xs"]
                t0 = j * RT
                if j == 0:
                    S.op(S.pool, lambda e: e.memset(xs[:, :, 0:1], 0.0), writes=[(l, "xs")])
                    S.dma("sp", lambda e: e.dma_start(out=xs[:, :, 1:RT + 1], in_=xsrc[:, :, 0:RT]),
                          reads=[("xTd", c) for c in range(8)], writes=[(l, "xs")])
                else:
                    S.dma("sp", lambda e: e.dma_start(out=xs[:, :, :], in_=xsrc[:, :, t0 - 1:t0 + RT]),
                          reads=[("xTd", c) for c in range(8)], writes=[(l, "xs")])

            def unit(l, hp, j):
                Bf = lanes[l]
                P = PB[l]
                pr = lambda i: ("P", l, i)
                R = lambda n: (l, n)
                F = (l, "F")
                cols = slice(hp * 128, (hp + 1) * 128)
                tok = slice(j * RT, (j + 1) * RT)
                vcol = lambda i: vec[:, hp, i:i + 1]
                r_, k_, v_, lw, a_, g_, kk, km, be, Lc, eL, eLn, eLp, eD, tA, tB, ysb = [
                    Bf[n] for n in ["r_", "k_", "v_", "lw", "a_", "g_", "kk", "km", "be", "Lc", "eL", "eLn", "eLp", "eD", "tA", "tB", "ysb"]]
                LC, eLC, BD = Bf["LC"], Bf["eLC"], Bf["BD"]
                MkvT, AkrT, AbrT, Y, Vtok, Ktok, Btok, Osb4, On = [Bf[n] for n in ["MkvT", "AkrT", "AbrT", "Y", "Vtok", "Ktok", "Btok", "Osb4", "On"]]
                X, XT = [Bf["X0"], Bf["X1"]], [Bf["XT0"], Bf["XT1"]]
                Wsb, nU, Abd, st64, mv4 = Bf["Wsb"], Bf["nU"], Bf["Abd"], Bf["st64"], Bf["mv4"]
                wa, wb = Bf["wa"], Bf["wb"]
                wres = R("wsc")
                for jj, dst in enumerate((r_, k_, v_)):
                    proj(P[jj], pr(jj), lambda k: wa[jj][:, k, :], lambda k: wb[jj][:, k, :], j * RT, RT, wres,
                         xs=Bf["xs"], xres=R("xs"))
                    S.op(S.act, lambda e: e.copy(out=dst[:], in_=P[jj][:, 0:RT]), reads=[pr(jj)], writes=[F])
                if j + 1 < NT_:
                    load_xs(l, j + 1)
                yield
                S.op(S.pe, lambda e: e.matmul(P[3][:, 0:RT], lhsT=w2[0:64, cols], rhs=hwa[0:64, tok], start=True, stop=True),
                     reads=["w2", "hwa"], writes=[pr(3)])
                S.op(S.act, lambda e: e.activation(out=lw[:], in_=P[3][:, 0:RT], func=AF.Sigmoid, bias=vcol(0)),
                     reads=[pr(3), "vec"], writes=[F])
                S.op(S.pe, lambda e: e.matmul(P[0][:, 0:RT], lhsT=w2[64:128, cols], rhs=hwa[64:128, tok], start=True, stop=True),
                     reads=["a2", "hwa"], writes=[pr(0)])
                S.op(S.act, lambda e: e.activation(out=a_[:], in_=P[0][:, 0:RT], func=AF.Sigmoid, bias=vcol(1)),
                     reads=[pr(0), "vec"], writes=[F])
                S.op(S.pe, lambda e: e.matmul(P[1][:, 0:RT], lhsT=g2[:, 0, cols], rhs=hg[:, 0, tok], start=True, stop=False),
                     reads=["g2a", "hg"], writes=[pr(1)])
                S.op(S.pe, lambda e: e.matmul(P[1][:, 0:RT], lhsT=g2[0:32, 1, cols], rhs=hg[0:32, 1, tok], start=False, stop=True),
                     reads=["g2b", "hg"], writes=[pr(1)])
                S.op(S.act, lambda e: e.copy(out=g_[:], in_=P[1][:, 0:RT]), reads=[pr(1)], writes=[F])
                yield
                dv(lambda e: e.tensor_scalar(out=lw[:], in0=lw[:], scalar1=DEC, scalar2=None, op0=ALU.mult), [F], [F])
                dv(lambda e: e.tensor_scalar(out=kk[:], in0=k_[:], scalar1=vcol(2), scalar2=None, op0=ALU.mult), [F, "vec"], [F])
                dv(lambda e: e.tensor_tensor(out=tA[:], in0=kk[:], in1=kk[:], op=ALU.mult), [F], [F], S.pool)
                S.op(S.pe, lambda e: e.matmul(P[2][:, 0:RT], lhsT=blk1[:], rhs=tA[:], start=True, stop=True),
                     reads=[F, "cst"], writes=[pr(2)])
                yield
                S.op(S.act, lambda e: e.activation(out=tA[:], in_=P[2][:, 0:RT], func=AF.Sqrt), reads=[pr(2)], writes=[F])
                dv(lambda e: e.tensor_scalar(out=tA[:], in0=tA[:], scalar1=1e-12, scalar2=None, op0=ALU.max), [F], [F])
                dv(lambda e: e.reciprocal(out=tA[:], in_=tA[:]), [F], [F])
                dv(lambda e: e.tensor_tensor(out=kk[:], in0=kk[:], in1=tA[:], op=ALU.mult), [F], [F])
                yield
                dv(lambda e: e.tensor_scalar(out=tB[:], in0=a_[:], scalar1=vcol(3), scalar2=vcol(4), op0=ALU.mult, op1=ALU.add),
                   [F, "vec"], [F])
                dv(lambda e: e.tensor_tensor(out=km[:], in0=k_[:], in1=tB[:], op=ALU.mult), [F], [F])
                dv(lambda e: e.tensor_tensor(out=be[:], in0=kk[:], in1=a_[:], op=ALU.mult), [F], [F], S.pool)
                dv(lambda e: e.scalar_tensor_tensor(out=tB[:], in0=r_[:], scalar=vcol(5), in1=km[:], op0=ALU.mult, op1=ALU.mult),
                   [F, "vec"], [F])
                S.op(S.pe, lambda e: e.matmul(P[3][:, 0:RT], lhsT=blk1[:], rhs=tB[:], start=True, stop=True),
                     reads=[F, "cst"], writes=[pr(3)])
                yield
                dv(lambda e: e.tensor_tensor(out=tA[:], in0=P[3][:, 0:RT], in1=v_[:], op=ALU.mult), [pr(3), F], [R("tA")])
                dv(lambda e: e.tensor_tensor_scan(out=Lc[:], data0=rmask[:], data1=lw[:], initial=0.0, op0=ALU.mult, op1=ALU.add),
                   [F, "cst"], [F])
                S.op(S.act, lambda e: e.activation(out=eL[:], in_=Lc[:], func=AF.Exp), reads=[F], writes=[F])
                S.op(S.act, lambda e: e.activation(out=eLn[:], in_=Lc[:], func=AF.Exp, scale=-1.0), reads=[F], writes=[F])
                dv(lambda e: e.tensor_tensor(out=eLp[:], in0=Lc[:], in1=lw[:], op=ALU.subtract), [F], [F], S.pool)
                S.op(S.act, lambda e: e.activation(out=eLp[:], in_=eLp[:], func=AF.Exp), reads=[F], writes=[F])
                L3 = Lc[:].rearrange("p (c t) -> p c t", t=64)
                dv(lambda e: e.tensor_copy(out=LC[:], in_=L3[:, :, 63]), [F], [F])
                S.op(S.act, lambda e: e.activation(out=eLC[:], in_=LC[:], func=AF.Exp), reads=[F], writes=[R("eLC")])
                for c in range(NCk):
                    S.op(S.act, lambda e, c=c: e.activation(out=eD[:, c * 64:(c + 1) * 64], in_=Lc[:, c * 64:(c + 1) * 64],
                                                            func=AF.Exp, scale=-1.0, bias=LC[:, c:c + 1]), reads=[F], writes=[F])
                yield
                prods = [("kap", kk, eLp), ("rt", r_, eL), ("kt", km, eLn), ("bt", be, eLn), ("kb", km, eD), ("bb", be, eD)]
                ie = 0
                for n, x0, x1 in prods:
                    for hs in (H0, H1):
                        eng = S.dve if ie % 2 == 0 else S.pool
                        ie += 1
                        dv(lambda e: e.tensor_tensor(out=BD[n][hs, :, hs], in0=x0[hs, :].rearrange("p (c t) -> p c t", t=64),
                                                     in1=x1[hs, :].rearrange("p (c t) -> p c t", t=64), op=ALU.mult),
                           [F], [R(("bd", n))], eng)
                for hs in (H0, H1):
                    S.op(S.act, lambda e: e.copy(out=BD["vf"][hs, :, hs], in_=v_[hs, :].rearrange("p (c t) -> p c t", t=64)),
                         reads=[F], writes=[R(("bd", "vf"))])
                yield
                gm = [(0, "kt", "kap"), (1, "kt", "rt"), (2, "bt", "kap"), (3, "bt", "rt")]
                for c in range(NCk):
                    for pi, ln_, rn_ in gm:
                        S.op(S.pe, lambda e: e.matmul(P[pi][:, c * 128:(c + 1) * 128], lhsT=BD[ln_][:, c, :],
                                                      rhs=BD[rn_][:, c, :], start=True, stop=True),
                             reads=[R(("bd", ln_)), R(("bd", rn_))], writes=[pr(pi)])
                yield
                dv(lambda e: e.tensor_tensor(out=f4(MkvT), in0=P[0][:, 0:W_], in1=f4(SUm), op=ALU.mult), [pr(0), "cst"], [R("MkvT")])
                dv(lambda e: e.tensor_tensor(out=f4(X[0]), in0=P[2][:, 0:W_], in1=f4(SUm), op=ALU.mult), [pr(2), "cst"], [R(("X", 0))])
                for c in range(NCk):
                    S.op(S.pe, lambda e: e.matmul(P[0][:, c * 128:(c + 1) * 128], lhsT=BD["kap"][:, c, :],
                                                  rhs=BD["bt"][:, c, :], start=True, stop=True),
                         reads=[R(("bd", "kap")), R(("bd", "bt"))], writes=[pr(0)])
                dv(lambda e: e.tensor_tensor(out=f4(AkrT), in0=P[1][:, 0:W_], in1=f4(UIm), op=ALU.mult), [pr(1), "cst"], [R("AkrT")])
                dv(lambda e: e.tensor_tensor(out=f4(AbrT), in0=P[3][:, 0:W_], in1=f4(UIm), op=ALU.mult), [pr(3), "cst"], [R("AbrT")])
                dv(lambda e: e.tensor_tensor(out=f4(Y), in0=f4(IDm), in1=f4(X[0]), op=ALU.subtract), [R(("X", 0)), "cst"], [R("Y")], S.pool)
                yield
                dv(lambda e: e.tensor_tensor(out=f4(XT[0]), in0=P[0][:, 0:W_], in1=f4(SLm), op=ALU.mult), [pr(0), "cst"], [R(("XT", 0))])
                yield
                cur = 0
                for lvl in range(5):
                    nxt = 1 - cur
                    last = (lvl == 4)
                    for c in range(NCk):
                        cs = slice(c * 128, (c + 1) * 128)
                        if not last:
                            S.op(S.pe, lambda e: e.matmul(P[0][:, cs], lhsT=XT[cur][:, c, :], rhs=X[cur][:, c, :], start=True, stop=True),
                                 reads=[R(("X", cur)), R(("XT", cur))], writes=[pr(0)])
                        S.op(S.pe, lambda e: e.matmul(P[1][:, cs], lhsT=X[cur][:, c, :], rhs=XT[cur][:, c, :], start=True, stop=True),
                             reads=[R(("X", cur)), R(("XT", cur))], writes=[pr(1)])
                    yield
                    if not last:
                        dv(lambda e: e.tensor_copy(out=f4(X[nxt]), in_=P[0][:, 0:W_]), [pr(0)], [R(("X", nxt))])
                    S.op(S.act, lambda e: e.copy(out=f4(XT[nxt]), in_=P[1][:, 0:W_]), reads=[pr(1)], writes=[R(("XT", nxt))])
                    for c in range(NCk):
                        cs = slice(c * 128, (c + 1) * 128)
                        S.op(S.pe, lambda e: e.matmul(P[2][:, cs], lhsT=XT[nxt][:, c, :], rhs=Y[:, c, :], start=True, stop=True),
                             reads=[R(("XT", nxt)), R("Y")], writes=[pr(2)])
                    yield
                    dv(lambda e: e.tensor_tensor(out=f4(Y), in0=f4(Y), in1=P[2][:, 0:W_], op=ALU.add), [pr(2), R("Y")], [R("Y")])
                    cur = nxt
                for pi, n, dst in ((3, "vf", Vtok), (0, "kb", Ktok), (1, "bb", Btok)):
                    for c in range(NCk):
                        S.op(S.pe, lambda e: e.transpose(out=P[pi][:, c * 128:(c + 1) * 128], in_=BD[n][:, c, :], identity=identf[:]),
                             reads=[R(("bd", n)), "cst"], writes=[pr(pi)])
                    S.op(S.act, lambda e: e.copy(out=f4(dst), in_=P[pi][:, 0:W_]), reads=[pr(pi)], writes=[R(("tok", n))])
                yield
                for c in range(NCk):
                    S.op(S.pe, lambda e: e.matmul(P[3][:, 0:128], lhsT=BD["kap"][:, c, :], rhs=Abd[:], start=True, stop=False),
                         reads=[R(("bd", "kap")), R("Abd")], writes=[pr(3)])
                    S.op(S.pe, lambda e: e.matmul(P[3][:, 0:128], lhsT=MkvT[:, c, :], rhs=Vtok[:, c, :], start=False, stop=True),
                         reads=[R("MkvT"), R(("tok", "vf"))], writes=[pr(3)])
                    S.op(S.act, lambda e: e.copy(out=Wsb[:], in_=P[3][:, 0:128]), reads=[pr(3)], writes=[R("Wsb")])
                    yield
                    S.op(S.pe, lambda e: e.matmul(P[0][:, 0:128], lhsT=Y[:, c, :], rhs=Wsb[:], start=True, stop=True),
                         reads=[R("Y"), R("Wsb")], writes=[pr(0)])
                    dv(lambda e: e.tensor_scalar(out=nU[:], in0=P[0][:, 0:128], scalar1=-1.0, scalar2=None, op0=ALU.mult),
                       [pr(0)], [R("nU")])
                    yield
                    S.op(S.pe, lambda e: e.matmul(P[2][:, 0:128], lhsT=Ktok[:, c, :], rhs=Vtok[:, c, :], start=True, stop=False),
                         reads=[R(("tok", "kb")), R(("tok", "vf"))], writes=[pr(2)])
                    S.op(S.pe, lambda e: e.matmul(P[2][:, 0:128], lhsT=Btok[:, c, :], rhs=nU[:], start=False, stop=True),
                         reads=[R(("tok", "bb")), R("nU")], writes=[pr(2)])
                    oc = P[1][:, c * 128:(c + 1) * 128]
                    S.op(S.pe, lambda e: e.matmul(oc, lhsT=BD["rt"][:, c, :], rhs=Abd[:], start=True, stop=False),
                         reads=[R(("bd", "rt")), R("Abd")], writes=[pr(1)])
                    S.op(S.pe, lambda e: e.matmul(oc, lhsT=AkrT[:, c, :], rhs=Vtok[:, c, :], start=False, stop=False),
                         reads=[R("AkrT"), R(("tok", "vf"))], writes=[pr(1)])
                    S.op(S.pe, lambda e: e.matmul(oc, lhsT=AbrT[:, c, :], rhs=nU[:], start=False, stop=True),
                         reads=[R("AbrT"), R("nU")], writes=[pr(1)])
                    dv(lambda e: e.scalar_tensor_tensor(out=Abd[:], in0=Abd[:], scalar=eLC[:, c:c + 1], in1=P[2][:, 0:128],
                                                        op0=ALU.mult, op1=ALU.add), [pr(2), R("Abd"), R("eLC")], [R("Abd")])
                    yield
                S.op(S.act, lambda e: e.copy(out=f4(Osb4), in_=P[1][:, 0:W_]), reads=[pr(1)], writes=[R("Osb")])
                for c in range(NCk):
                    for hs in (H0, H1):
                        dv(lambda e: e.bn_stats(out=st64[hs, c, :], in_=Osb4[hs, c, hs]), [R("Osb")], [R("st6")])
                    dv(lambda e: e.bn_aggr(out=mv4[:, c, 0:2], in_=st64[:, c, :]), [R("st6")], [R("mv")])
                dv(lambda e: e.tensor_scalar(out=mv4[:, :, 2:3], in0=mv4[:, :, 1:2], scalar1=GN_EPS, scalar2=None, op0=ALU.add),
                   [R("mv")], [R("mv")])
                yield
                S.op(S.act, lambda e: e.activation(out=mv4[:, :, 3:4], in_=mv4[:, :, 2:3], func=AF.Sqrt), reads=[R("mv")], writes=[R("mv")])
                dv(lambda e: e.reciprocal(out=mv4[:, :, 4:5], in_=mv4[:, :, 3:4]), [R("mv")], [R("mv")])
                dv(lambda e: e.scalar_tensor_tensor(out=mv4[:, :, 5:6], in0=mv4[:, :, 0:1], scalar=-1.0, in1=mv4[:, :, 4:5],
                                                    op0=ALU.mult, op1=ALU.mult), [R("mv")], [R("mv")])
                yield
                for c in range(NCk):
                    for hs in (H0, H1):
                        S.op(S.act, lambda e: e.activation(out=On[hs, c, hs], in_=Osb4[hs, c, hs], func=AF.Identity,
                                                           bias=mv4[hs, c, 5:6], scale=mv4[hs, c, 4:5]),
                             reads=[R("mv"), R("Osb")], writes=[R("On")])
                for c in range(NCk):
                    S.op(S.pe, lambda e: e.transpose(out=P[3][:, c * 128:(c + 1) * 128], in_=On[:, c, :], identity=identf[:]),
                         reads=[R("On"), "cst"], writes=[pr(3)])
                yield
                for hs in (H0, H1):
                    dv(lambda e: e.tensor_scalar(out=ysb[hs, :].rearrange("p (c t) -> p c t", t=64),
                                                 in0=P[3][:, 0:W_].rearrange("p (c t) -> p c t", t=128)[hs, :, hs],
                                                 scalar1=vec[hs, hp, 6:7], scalar2=vec[hs, hp, 7:8], op0=ALU.mult,
                                                 op1=ALU.add), [pr(3), "vec"], [R("ysb")])
                dv(lambda e: e.tensor_tensor(out=ysb[:], in0=ysb[:], in1=tA[:], op=ALU.add), [R("ysb"), R("tA")], [R("ysb")], S.pool)
                ob = Bf["osb"][j % 2]
                dv(lambda e: e.tensor_tensor(out=ob[:], in0=ysb[:], in1=g_[:], op=ALU.mult), [R("ysb"), F], [R(("osb", j % 2))], S.pool)
                S.dma("sp", lambda e: e.dma_start(out=self.oT_d[hp][:, tok], in_=ob[:]), reads=[R(("osb", j % 2))],
                      writes=[("oTd", hp)])
                yield

            def lane_gen(l):
                Bf = lanes[l]
                for hp in range(l, 8, 2):
                    cols = slice(hp * 128, (hp + 1) * 128)
                    for jj in range(3):
                        scaled_weights(wsrc[jj][:, :, cols], 128, [Bf["wa"][jj]], [Bf["wb"][jj]], [(0, 128, jj)], (l, "wsc"))
                    S.op(S.pool, lambda e: e.memset(Bf["Abd"][:], 0.0), reads=[(l, "Abd")], writes=[(l, "Abd")])
                    load_xs(l, 0)
                    yield
                    for j in range(NT_):
                        yield from unit(l, hp, j)

            gens = [lane_gen(0), lane_gen(1)]
            alive = [True, True]
            for _ in range(int(os.environ.get("RW_OFF", "0"))):
                next(gens[0])
            while any(alive):
                for l in range(2):
                    if alive[l]:
                        try:
                            next(gens[l])
                        except StopIteration:
                            alive[l] = False
            S.barrier()
            self.load_oT()
            S.barrier()
        self.out_proj_phase(L, self.rw_wo)

def host_layout(inp):
    out = {}
    cw = np.zeros((DEPTH, NCH * 128, 4), np.float32)
    cw[:, :D_FF, 0:3] = np.transpose(inp["ffn_conv_w"], (0, 2, 1))
    cw[:, :D_FF, 3] = inp["ffn_conv_b"]
    out["ffn_cw"] = np.ascontiguousarray(cw.reshape(DEPTH, NCH, 128, 4).transpose(0, 2, 1, 3))
    for k in ("ln_g", "ln_b", "ffn_w_in", "ffn_w_out"):
        out[k] = np.ascontiguousarray(inp[k], dtype=np.float32)
    for k in ("dil_w_qkv", "dil_w_o"):
        out[k] = np.ascontiguousarray(inp[k][0], dtype=np.float32)
    fm = lambda v: np.ascontiguousarray(np.asarray(v, np.float32).reshape(8, 128).T)
    out["rw_mu"] = np.ascontiguousarray(inp["rwkv_mu"][0].reshape(6, 8, 128).transpose(2, 1, 0))
    ka = inp["rwkv_k_a"][0]
    vecs = [inp["rwkv_w0"][0], inp["rwkv_a0"][0], inp["rwkv_k_k"][0], ka, None, inp["rwkv_r_k"][0].reshape(-1),
            inp["rwkv_ln_w"][0], inp["rwkv_ln_b"][0]]
    rv = np.zeros((128, 8, 8), np.float32)
    for i, v in enumerate(vecs):
        if v is not None:
            rv[:, :, i] = fm(v)
    out["rw_vec"] = rv
    out["rwkv_w_rkv"] = np.ascontiguousarray(inp["rwkv_w_rkv"][0], dtype=np.float32)
    out["rw_l1"] = np.ascontiguousarray(np.concatenate([inp["rwkv_w1"][0], inp["rwkv_a1"][0], inp["rwkv_g1"][0]], axis=1))
    for k in ("rwkv_w2", "rwkv_a2", "rwkv_g2", "rwkv_w_o"):
        out[k] = np.ascontiguousarray(inp[k][0], dtype=np.float32)
    wd = inp["mla_w_down"]
    out["mla_wd"] = np.ascontiguousarray(np.concatenate(
        [wd[:, :, 0:640], wd[:, :, 0:64], wd[:, :, 640:672], wd[:, :, 656:672], wd[:, :, 640:656]], axis=2))
    out["mla_qn"] = np.ascontiguousarray(inp["mla_q_norm"].reshape(-1, 3, 128).transpose(0, 2, 1))
    out["mla_kvn"] = np.ascontiguousarray(inp["mla_kv_norm"].reshape(-1, 2, 128).transpose(0, 2, 1))
    wq = inp["mla_w_uq"].reshape(-1, 384, 16, 96)
    out["mla_wuq"] = np.ascontiguousarray(np.concatenate(
        [wq[..., 0:96], wq[..., 80:96], wq[..., 64:80]], axis=3).reshape(-1, 384, 2048))
    wkv = inp["mla_w_ukv"].reshape(-1, 256, 16, 128)
    out["mla_wukv"] = np.ascontiguousarray(np.concatenate(
        [wkv[..., 0:64].reshape(-1, 256, 1024), wkv[..., 64:128].reshape(-1, 256, 1024)], axis=2))
    out["mla_wo"] = np.ascontiguousarray(inp["mla_w_o"], dtype=np.float32)
    rc = np.zeros((96, 2), np.float32)
    invf = (10000.0 ** (-np.arange(0, 32, 2, dtype=np.float32) / np.float32(32))).astype(np.float32)
    rc[64:80, 0] = invf / np.float32(2 * np.pi)
    rc[80:96, 0] = invf / np.float32(2 * np.pi)
    rc[64:80, 1] = -1.0
    rc[80:96, 1] = 1.0
    out["rope_c"] = rc
    return out


DEFAULT_PLAN = [("mla", 0), ("ffn", 0), ("dil", 1), ("ffn", 1), ("rwkv", 2), ("ffn", 2), ("mla", 3), ("ffn", 3)]
_CACHE = {}


def run(inputs, plan, n_cores=8, trace=False):
    key = tuple(plan)
    if key not in _CACHE:
        b = Builder(plan)
        nc = b.build()
        _CACHE[key] = (b, nc)
    b, nc = _CACHE[key]
    shared = host_layout(inputs)
    in_maps = []
    for c in range(n_cores):
        d = {"x": np.ascontiguousarray(inputs["x"][c], dtype=np.float32),
             "positions": np.ascontiguousarray(inputs["positions"][c], dtype=np.int32)}
        d.update(shared)
        d = {k: v for k, v in d.items() if k in b.din}
        in_maps.append(d)
    res = run_bass_kernel_spmd(nc, in_maps, core_ids=list(range(n_cores)), trace=trace)
    return np.stack([r["out"] for r in res.results], axis=0), res


def kernel(**inputs):
    out, _ = run(inputs, DEFAULT_PLAN)
    return out.astype(np.float32)
```

```python
import numpy as np
from contextlib import ExitStack
import concourse.bass as bass
import concourse.mybir as mybir
from concourse.bass_utils import run_bass_kernel_spmd

F32 = mybir.dt.float32
BF16 = mybir.dt.bfloat16
I32 = mybir.dt.int32
AF = mybir.ActivationFunctionType
ALU = mybir.AluOpType

T = 4096
D = 1024
DEPTH = 4
NB = T // 128
ALPHA = (2 * DEPTH) ** 0.25
LN_EPS = 1e-5
RMS_EPS = 1e-6
D_FF = 2752
NCH = 22


class _Eng:
    def __init__(self, name, eng, sem):
        self.name = name
        self.eng = eng
        self.sem = sem
        self.count = 0
        self.waited = {}


class Sched:
    def __init__(self, nc, stack, n_dma_sems=16):
        self.nc = nc
        mk = lambda n: stack.enter_context(nc.semaphore(n))
        self.pe = _Eng("pe", nc.tensor, mk("s_pe"))
        self.act = _Eng("act", nc.scalar, mk("s_act"))
        self.dve = _Eng("dve", nc.vector, mk("s_dve"))
        self.pool = _Eng("pool", nc.gpsimd, mk("s_pool"))
        self.sp = _Eng("sp", nc.sync, None)
        self.q = {"sp": [mk(f"s_dsp{i}") for i in range(n_dma_sems)],
                  "pool": [mk(f"s_dpl{i}") for i in range(n_dma_sems)]}
        self.qeng = {"sp": self.sp, "pool": self.pool}
        self.dma_cnt = {"sp": 0, "pool": 0}
        self.dma_last = {}
        self.last_write = {}
        self.readers = {}
        self.n_ops = 0
        self.n_waits = 0

    def _wait(self, E, tok):
        sem, val, src = tok
        if src == "pe" and E.name == "pe":
            return
        k = id(sem)
        if E.waited.get(k, 0) >= val:
            return
        E.eng.wait_ge(sem, val)
        E.waited[k] = val
        self.n_waits += 1

    def _deps(self, E, reads, writes):
        for r in reads:
            t = self.last_write.get(r)
            if t is not None:
                self._wait(E, t)
        for w in writes:
            t = self.last_write.get(w)
            if t is not None:
                self._wait(E, t)
            for t in self.readers.get(w, ()):
                self._wait(E, t)

    def _commit(self, tok, reads, writes):
        for r in reads:
            self.readers.setdefault(r, []).append(tok)
        for w in writes:
            self.last_write[w] = tok
            self.readers[w] = []

    def op(self, E, fn, reads=(), writes=()):
        self._deps(E, reads, writes)
        ins = fn(E.eng)
        E.count += 1
        ins.then_inc(E.sem, 1)
        tok = (E.sem, E.count, E.name)
        self._commit(tok, reads, writes)
        self.n_ops += 1
        return tok

    def dma(self, qname, fn, reads=(), writes=()):
        E = self.qeng[qname]
        pool = self.q[qname]
        i = self.dma_cnt[qname]
        self.dma_cnt[qname] = i + 1
        slot = i % len(pool)
        prev = self.dma_last.get((qname, slot))
        if prev is not None:
            self._wait(E, prev)
        self._deps(E, reads, writes)
        ins = fn(E.eng)
        ins.then_inc(pool[slot], 16)
        tok = (pool[slot], 16 * (i // len(pool) + 1), "dma_" + qname)
        self.dma_last[(qname, slot)] = tok
        self._commit(tok, reads, writes)
        self.n_ops += 1
        return tok

    def barrier(self):
        toks = [(E.sem, E.count, E.name) for E in (self.pe, self.act, self.dve, self.pool) if E.count]
        toks += list(self.dma_last.values())
        for E in (self.pe, self.act, self.dve, self.pool, self.sp):
            for t in toks:
                if t[2] == E.name:
                    continue
                self._wait(E, t)
        self.last_write = {}
        self.readers = {}


class Builder:
    def __init__(self, plan, debug_out=False):
        self.plan = plan
        nc = bass.Bass("TRN2", target_bir_lowering=False)
        self.nc = nc
        self.din = {}

    def nm(self, n):
        self._uid = getattr(self, "_uid", 0) + 1
        return f"{n}_{self._uid}"

    def dram_in(self, name, shape, dt=F32):
        t = self.nc.dram_tensor(name, list(shape), dt, kind="ExternalInput").ap()
        self.din[name] = t
        return t

    def build(self):
        nc = self.nc
        plan = self.plan
        self.x_in = self.dram_in("x", [T, D])
        self.out = nc.dram_tensor("out", [T, D], F32, kind="ExternalOutput").ap()
        self.ln_g = self.dram_in("ln_g", [DEPTH, 2, D])
        self.ln_b = self.dram_in("ln_b", [DEPTH, 2, D])
        self.ffn_w_in = self.dram_in("ffn_w_in", [DEPTH, D, 2 * D_FF])
        self.ffn_w_out = self.dram_in("ffn_w_out", [DEPTH, D_FF, D])
        self.ffn_cw = self.dram_in("ffn_cw", [DEPTH, 128, NCH, 4])
        kinds = {k for k, _ in plan}
        if "dil" in kinds or "rwkv" in kinds:
            self.oT_d = nc.dram_tensor("oT_d", [8, 128, T], BF16, kind="Internal").ap()
            self.xT_d = nc.dram_tensor("xT_d", [8, 128, T], BF16, kind="Internal").ap()
        if "dil" in kinds:
            self.dil_wqkv = self.dram_in("dil_w_qkv", [D, 9216])
            self.dil_wo = self.dram_in("dil_w_o", [D, D])
        if "rwkv" in kinds:
            self.rw_mu = self.dram_in("rw_mu", [128, 8, 6])
            self.rw_wrkv = self.dram_in("rwkv_w_rkv", [3, D, D])
            self.rw_l1 = self.dram_in("rw_l1", [D, 288])
            self.rw_w2 = self.dram_in("rwkv_w2", [64, D])
            self.rw_a2 = self.dram_in("rwkv_a2", [64, D])
            self.rw_g2 = self.dram_in("rwkv_g2", [160, D])
            self.rw_vec = self.dram_in("rw_vec", [128, 8, 8])
            self.rw_wo = self.dram_in("rwkv_w_o", [D, D])
        if "mla" in kinds:
            self.pos = self.dram_in("positions", [T], I32)
            self.rope_c = self.dram_in("rope_c", [96, 2])
            self.mla_wd = self.dram_in("mla_wd", [2, D, 768])
            self.mla_qn = self.dram_in("mla_qn", [2, 128, 3])
            self.mla_kvn = self.dram_in("mla_kvn", [2, 128, 2])
            self.mla_wuq = self.dram_in("mla_wuq", [2, 384, 2048])
            self.mla_wukv = self.dram_in("mla_wukv", [2, 256, 2048])
            self.mla_wo = self.dram_in("mla_wo", [2, D, D])

        with ExitStack() as st:
            self.st = st
            S = self.S = Sched(nc, st)
            gsb = lambda n, shp, dt: st.enter_context(nc.sbuf_tensor(self.nm(n), shp, dt))
            self.actT = gsb("actT", [128, 8, T], BF16)
            self.ident = gsb("ident", [128, 128], BF16)
            self.lng = gsb("lng", [128, D], F32)
            self.lnb = gsb("lnb", [128, D], F32)
            self.ones_bf = gsb("ones_bf", [128, 128], BF16)
            self.ones_f = gsb("ones_f", [128, 128], F32)
            self.tri = gsb("tri", [128, 128], BF16)
            self.ep_idx = 0
            self.cur_src = self.x_in

            S.op(S.pool, lambda e: e.memset(self.ident[:], 1.0), writes=["ident"])
            S.op(S.pool, lambda e: e.affine_select(out=self.ident[:], in_=self.ident[:], pattern=[[1, 128]],
                                                   compare_op=ALU.is_equal, fill=0.0, base=0,
                                                   channel_multiplier=-1),
                 reads=["ident"], writes=["ident"])
            S.op(S.pool, lambda e: e.memset(self.ones_bf[:], 1.0), writes=["ones_bf"])
            S.op(S.pool, lambda e: e.memset(self.ones_f[:], 1.0), writes=["ones_f"])
            S.op(S.pool, lambda e: e.memset(self.tri[:], 1.0), writes=["tri"])
            S.op(S.pool, lambda e: e.affine_select(out=self.tri[:], in_=self.tri[:], pattern=[[1, 128]],
                                                   compare_op=ALU.is_ge, fill=0.0, base=0,
                                                   channel_multiplier=-1),
                 reads=["tri"], writes=["tri"])
            self.init_phase()
            for step in plan:
                kind, L = step
                if kind == "ffn":
                    self.ffn_phase(L)
                elif kind == "mla":
                    self.mla_phase(L)
                elif kind == "dil":
                    self.dil_phase(L)
                elif kind == "rwkv":
                    self.rwkv_phase2(L)
                elif kind == "copy":
                    self.copy_phase()
                else:
                    raise ValueError(kind)
            S.barrier()
        return nc

    def transposes_to_actT(self, m, xb, pT, res_xb):
        S = self.S
        for k in range(8):
            S.op(S.pe, lambda e, k=k: e.transpose(out=pT[:, k, :], in_=xb[:, k * 128:(k + 1) * 128],
                                                  identity=self.ident[:]),
                 reads=[res_xb, "ident"], writes=["pT"])
        S.op(S.dve, lambda e: e.tensor_copy(out=self.actT[:, :, m * 128:(m + 1) * 128], in_=pT[:]),
             reads=["pT"], writes=[("actT", m)])

    def alloc_epi(self, ph):
        nc = self.nc
        sb = lambda n, shp, dt: ph.enter_context(nc.sbuf_tensor(self.nm(n), shp, dt))
        self.xr = [sb(f"xr{i}", [128, D], F32) for i in range(2)]
        self.z = [sb(f"z{i}", [128, D], F32) for i in range(2)]
        self.xb = [sb(f"xb{i}", [128, D], BF16) for i in range(2)]
        self.st6 = [sb(f"st6{i}", [128, 2, 6], F32) for i in range(2)]
        self.mv = [sb(f"mv{i}", [128, 8], F32) for i in range(2)]

    def init_phase(self):
        nc, S = self.nc, self.S
        with ExitStack() as ph:
            self.alloc_epi(ph)
            pT = ph.enter_context(nc.psum_tensor(self.nm("pT_i"), [128, 8, 128], BF16))
            for m in range(NB):
                b = m % 2
                S.dma("sp", lambda e: e.dma_start(out=self.xr[b][:], in_=self.x_in[m * 128:(m + 1) * 128, :]),
                      writes=[("xr", b)])
                S.op(S.act, lambda e: e.copy(out=self.xb[b][:], in_=self.xr[b][:]),
                     reads=[("xr", b)], writes=[("xb", b)])
                self.transposes_to_actT(m, self.xb[b], pT, ("xb", b))
            S.barrier()

    def copy_phase(self):
        S = self.S
        ph = ExitStack()
        self.alloc_epi(ph)
        for m in range(NB):
            b = m % 2
            S.dma("sp", lambda e: e.dma_start(out=self.xr[b][:], in_=self.cur_src[m * 128:(m + 1) * 128, :]),
                  reads=[("xres", m)], writes=[("xr", b)])
            S.dma("sp", lambda e: e.dma_start(out=self.out[m * 128:(m + 1) * 128, :], in_=self.xr[b][:]),
                  reads=[("xr", b)], writes=[("xres", m)])
        S.barrier()
        ph.close()
        self.cur_src = self.out

    def load_ln(self, L, which):
        S = self.S
        S.dma("sp", lambda e: e.dma_start(out=self.lng[:], in_=self.ln_g[L, which, :].partition_broadcast(128)),
              writes=["lng"])
        S.dma("sp", lambda e: e.dma_start(out=self.lnb[:], in_=self.ln_b[L, which, :].partition_broadcast(128)),
              writes=["lnb"])

    def prefetch_xr(self, m):
        S = self.S
        b = self.ep_idx % 2
        src = self.cur_src
        S.dma("sp", lambda e: e.dma_start(out=self.xr[b][:], in_=src[m * 128:(m + 1) * 128, :]),
              reads=[("xres", m)], writes=[("xr", b)])

    def epilogue(self, m, py, py_res, pT):
        S = self.S
        prev_tr = getattr(self, "pending_tr", None)
        self.pending_tr = None
        b = self.ep_idx % 2
        self.ep_idx += 1
        xr, z, xb, st6, mv = self.xr[b], self.z[b], self.xb[b], self.st6[b], self.mv[b]
        rz, rmv = ("z", b), ("mv", b)
        S.op(S.dve, lambda e: e.scalar_tensor_tensor(out=z[:], in0=xr[:], scalar=float(ALPHA), in1=py,
                                                     op0=ALU.mult, op1=ALU.add),
             reads=[("xr", b), py_res], writes=[rz])
        if prev_tr is not None:
            self.transposes_to_actT(*prev_tr)
        for c in range(2):
            S.op(S.dve, lambda e, c=c: e.bn_stats(out=st6[:, c, :], in_=z[:, c * 512:(c + 1) * 512]),
                 reads=[rz], writes=[("st6", b, c)])
        S.op(S.dve, lambda e: e.bn_aggr(out=mv[:, 0:2], in_=st6[:].rearrange("p a b -> p (a b)")),
             reads=[("st6", b, 0), ("st6", b, 1)], writes=[rmv])
        S.op(S.dve, lambda e: e.tensor_scalar(out=mv[:, 2:3], in0=mv[:, 1:2], scalar1=float(LN_EPS), scalar2=None,
                                              op0=ALU.add), reads=[rmv], writes=[rmv])
        S.op(S.act, lambda e: e.activation(out=mv[:, 3:4], in_=mv[:, 2:3], func=AF.Sqrt), reads=[rmv], writes=[rmv])
        S.op(S.dve, lambda e: e.reciprocal(out=mv[:, 4:5], in_=mv[:, 3:4]), reads=[rmv], writes=[rmv])
        S.op(S.dve, lambda e: e.scalar_tensor_tensor(out=mv[:, 5:6], in0=mv[:, 0:1], scalar=-1.0, in1=mv[:, 4:5],
                                                     op0=ALU.mult, op1=ALU.mult), reads=[rmv], writes=[rmv])
        S.op(S.act, lambda e: e.activation(out=z[:], in_=z[:], func=AF.Identity, bias=mv[:, 5:6], scale=mv[:, 4:5]),
             reads=[rmv, rz], writes=[rz])
        S.op(S.pool, lambda e: e.tensor_tensor(out=z[:], in0=z[:], in1=self.lng[:], op=ALU.mult),
             reads=[rz, "lng"], writes=[rz])
        S.op(S.pool, lambda e: e.tensor_tensor(out=z[:], in0=z[:], in1=self.lnb[:], op=ALU.add),
             reads=[rz, "lnb"], writes=[rz])
        S.dma("pool", lambda e: e.dma_start(out=self.out[m * 128:(m + 1) * 128, :], in_=z[:]),
              reads=[rz], writes=[("xres", m)])
        S.op(S.act, lambda e: e.copy(out=xb[:], in_=z[:]), reads=[rz], writes=[("xb", b)])
        self.pending_tr = (m, xb, pT, ("xb", b))

    def flush_tr(self):
        if getattr(self, "pending_tr", None) is not None:
            self.transposes_to_actT(*self.pending_tr)
            self.pending_tr = None

    def ffn_phase(self, L):
        nc, S = self.nc, self.S
        actT = self.actT
        with ExitStack() as ph:
            sb = lambda n, shp, dt: ph.enter_context(nc.sbuf_tensor(self.nm(n), shp, dt))
            ps = lambda n, shp, dt: ph.enter_context(nc.psum_tensor(self.nm(n), shp, dt))
            self.alloc_epi(ph)
            w_out = sb("f_wout", [128, NCH, D], BF16)
            cw = sb("f_cw", [128, NCH, 4], F32)
            halo = sb("f_halo", [128, NCH, 2], F32)
            g = sb("f_g", [128, NCH, 512], BF16)
            wab = [sb(f"f_wab{i}", [128, 8, 256], BF16) for i in range(3)]
            asb = [sb(f"f_a{i}", [128, 514], F32) for i in range(3)]
            tt = [sb(f"f_t{i}", [128, 512], F32) for i in range(3)]
            pab = [ps(f"f_pab{i}", [128, 512], F32) for i in range(3)]
            py = [ps(f"f_py{i}", [128, D], F32) for i in range(2)]
            pT = ps("f_pT", [128, 8, 128], BF16)

            self.load_ln(L, 1)
            S.dma("sp", lambda e: e.dma_start(out=cw[:], in_=self.ffn_cw[L]), writes=["cw"])
            S.op(S.pool, lambda e: e.memset(halo[:], 0.0), writes=[("halo", c) for c in range(NCH)])
            w_in = self.ffn_w_in[L].rearrange("(k p) n -> p k n", p=128)
            NJ = T // 512
            NIT = NJ * NCH

            def load_wout(c):
                rows = 128 if c < NCH - 1 else 64
                S.dma("pool", lambda e: e.dma_start(out=w_out[0:rows, c, :],
                                                    in_=self.ffn_w_out[L, c * 128:c * 128 + rows, :]),
                      writes=[("wout", c)])

            def load_w(i):
                if i >= NIT:
                    return
                c = i % NCH
                wc = 128 if c < NCH - 1 else 64
                r3 = i % 3
                wb_ = wab[r3]
                S.dma("pool", lambda e: e.dma_start(out=wb_[:, :, 0:wc], in_=w_in[:, :, c * 128:c * 128 + wc]),
                      writes=[("wa", r3)])
                S.dma("pool", lambda e: e.dma_start(out=wb_[:, :, 128:128 + wc],
                                                    in_=w_in[:, :, D_FF + c * 128:D_FF + c * 128 + wc]),
                      writes=[("wb", r3)])

            load_w(0)
            load_w(1)
            for i in range(NIT):
                j, c = i // NCH, i % NCH
                tok = slice(j * 512, (j + 1) * 512)
                act_res = [("actT", 4 * j + q) for q in range(4)]
                wc = 128 if c < NCH - 1 else 64
                r3 = i % 3
                ia_, ib_ = (2 * i) % 3, (2 * i + 1) % 3
                pa_, pb_ = pab[ia_], pab[ib_]
                rpa, rpb = ("pab", ia_), ("pab", ib_)
                wb_, a_, t_ = wab[r3], asb[r3], tt[r3]
                load_w(i + 2)
                if i < NCH:
                    load_wout(i)
                for k in range(8):
                    S.op(S.pe, lambda e, k=k: e.matmul(pa_[0:wc, :], lhsT=wb_[:, k, 0:wc], rhs=actT[:, k, tok],
                                                       start=(k == 0), stop=(k == 7)),
                         reads=[("wa", r3)] + act_res, writes=[rpa])
                for k in range(8):
                    S.op(S.pe, lambda e, k=k: e.matmul(pb_[0:wc, :], lhsT=wb_[:, k, 128:128 + wc],
                                                       rhs=actT[:, k, tok], start=(k == 0), stop=(k == 7)),
                         reads=[("wb", r3)] + act_res, writes=[rpb])
                if c == 1:
                    self.flush_tr()
                ra, rt = ("a", r3), ("t", r3)
                S.op(S.act, lambda e: e.copy(out=a_[0:wc, 0:2], in_=halo[0:wc, c, :]),
                     reads=[("halo", c)], writes=[ra])
                S.op(S.act, lambda e: e.copy(out=a_[0:wc, 2:514], in_=pa_[0:wc, :]),
                     reads=[rpa, ra], writes=[ra])
                S.op(S.act, lambda e: e.copy(out=halo[0:wc, c, :], in_=a_[0:wc, 512:514]),
                     reads=[ra], writes=[("halo", c)])
                S.op(S.dve, lambda e: e.tensor_scalar(out=t_[0:wc, :], in0=a_[0:wc, 2:514],
                                                      scalar1=cw[0:wc, c, 2:3], scalar2=cw[0:wc, c, 3:4],
                                                      op0=ALU.mult, op1=ALU.add),
                     reads=[ra, "cw"], writes=[rt])
                S.op(S.dve, lambda e: e.scalar_tensor_tensor(out=t_[0:wc, :], in0=a_[0:wc, 1:513],
                                                             scalar=cw[0:wc, c, 1:2], in1=t_[0:wc, :],
                                                             op0=ALU.mult, op1=ALU.add),
                     reads=[ra, rt], writes=[rt])
                S.op(S.dve, lambda e: e.scalar_tensor_tensor(out=t_[0:wc, :], in0=a_[0:wc, 0:512],
                                                             scalar=cw[0:wc, c, 0:1], in1=t_[0:wc, :],
                                                             op0=ALU.mult, op1=ALU.add),
                     reads=[ra, rt], writes=[rt])
                S.op(S.act, lambda e: e.activation(out=t_[0:wc, :], in_=t_[0:wc, :], func=AF.Silu),
                     reads=[rt], writes=[rt])
                S.op(S.dve, lambda e: e.tensor_tensor(out=g[0:wc, c, :], in0=t_[0:wc, :], in1=pb_[0:wc, :],
                                                      op=ALU.mult),
                     reads=[rt, rpb], writes=[("g", c)])
                if c < NCH - 1:
                    continue
                for mm in range(4):
                    m = 4 * j + mm
                    self.prefetch_xr(m)
                    p_ = py[m % 2]
                    for n in range(2):
                        for cc in range(NCH):
                            wcc = 128 if cc < NCH - 1 else 64
                            S.op(S.pe, lambda e, n=n, cc=cc, wcc=wcc: e.matmul(
                                p_[:, n * 512:(n + 1) * 512], lhsT=g[0:wcc, cc, mm * 128:(mm + 1) * 128],
                                rhs=w_out[0:wcc, cc, n * 512:(n + 1) * 512], start=(cc == 0), stop=(cc == NCH - 1)),
                                 reads=[("g", cc), ("wout", cc)], writes=[("py", m % 2)])
                    self.epilogue(m, p_[:], ("py", m % 2), pT)
            self.flush_tr()
            S.barrier()
        self.cur_src = self.out


    def out_proj_phase(self, L, w_dram):
        nc, S = self.nc, self.S
        with ExitStack() as ph:
            sb = lambda n, shp, dt: ph.enter_context(nc.sbuf_tensor(self.nm(n), shp, dt))
            ps = lambda n, shp, dt: ph.enter_context(nc.psum_tensor(self.nm(n), shp, dt))
            self.alloc_epi(ph)
            wo = sb("o_w", [128, 8, D], BF16)
            py = [ps(f"o_py{i}", [128, D], F32) for i in range(2)]
            pT = ps("o_pT", [128, 8, 128], BF16)
            self.load_ln(L, 0)
            S.dma("pool", lambda e: e.dma_start(out=wo[:], in_=w_dram.rearrange("(k p) n -> p k n", p=128)),
                  writes=["wo"])
            for m in range(NB):
                self.prefetch_xr(m)
                p_ = py[m % 2]
                for n in range(2):
                    for k in range(8):
                        S.op(S.pe, lambda e, n=n, k=k: e.matmul(
                            p_[:, n * 512:(n + 1) * 512], lhsT=self.actT[:, k, m * 128:(m + 1) * 128],
                            rhs=wo[:, k, n * 512:(n + 1) * 512], start=(k == 0), stop=(k == 7)),
                             reads=[("actT", m), "wo"], writes=[("py", m % 2)])
                self.epilogue(m, p_[:], ("py", m % 2), pT)
            self.flush_tr()
            S.barrier()
        self.cur_src = self.out

    def mla_phase(self, L):
        nc, S = self.nc, self.S
        actT = self.actT
        ia = L // 3
        SCALE = 96.0 ** -0.5
        TWO_PI = 2.0 * np.pi
        with ExitStack() as ml:
            msb = lambda n, shp, dt: ml.enter_context(nc.sbuf_tensor(self.nm(n), shp, dt))
            cqn = msb("m_cqn", [128, 3, T], BF16)
            ckvn = msb("m_ckvn", [128, 2, T], BF16)
            KT = msb("m_KT", [96, T], BF16)
            cosT = msb("m_cos", [96, T], BF16)
            sinS = msb("m_sin", [96, T], BF16)
            rc = msb("m_rc", [96, 2], F32)
            with ExitStack() as ph:
                sb = lambda n, shp, dt: ph.enter_context(nc.sbuf_tensor(self.nm(n), shp, dt))
                HT = T // 2
                posi = sb("m_posi", [96, HT], I32)
                ang = sb("m_ang", [96, HT], F32)
                tmp = sb("m_tmp", [96, HT], F32)
                yi = sb("m_yi", [96, HT], I32)
                msk = sb("m_msk", [96, HT], F32)
                S.dma("sp", lambda e: e.dma_start(out=rc[:], in_=self.rope_c), writes=["rc"])

                def sin_turns():
                    S.op(S.dve, lambda e: e.tensor_copy(out=yi[:], in_=tmp[:]), reads=["tmp"], writes=["yi"])
                    S.op(S.dve, lambda e: e.tensor_copy(out=msk[:], in_=yi[:]), reads=["yi"], writes=["msk"])
                    S.op(S.dve, lambda e: e.tensor_tensor(out=tmp[:], in0=tmp[:], in1=msk[:], op=ALU.subtract),
                         reads=["tmp", "msk"], writes=["tmp"])
                    S.op(S.dve, lambda e: e.tensor_scalar(out=msk[:], in0=tmp[:], scalar1=0.5, scalar2=None,
                                                          op0=ALU.is_gt), reads=["tmp"], writes=["msk"])
                    S.op(S.dve, lambda e: e.tensor_tensor(out=tmp[:], in0=tmp[:], in1=msk[:], op=ALU.subtract),
                         reads=["tmp", "msk"], writes=["tmp"])
                    S.op(S.dve, lambda e: e.tensor_scalar(out=msk[:], in0=tmp[:], scalar1=-0.5, scalar2=None,
                                                          op0=ALU.is_lt), reads=["tmp"], writes=["msk"])
                    S.op(S.dve, lambda e: e.tensor_tensor(out=tmp[:], in0=tmp[:], in1=msk[:], op=ALU.add),
                         reads=["tmp", "msk"], writes=["tmp"])
                    S.op(S.act, lambda e: e.activation(out=tmp[:], in_=tmp[:], func=AF.Sin, scale=6.28318),
                         reads=["tmp"], writes=["tmp"])

                for hh in range(2):
                    cs = slice(hh * HT, (hh + 1) * HT)
                    S.dma("sp", lambda e: e.dma_start(out=posi[:], in_=self.pos[cs].partition_broadcast(96)),
                          writes=["posi"])
                    S.op(S.dve, lambda e: e.tensor_copy(out=ang[:], in_=posi[:]), reads=["posi"], writes=["ang"])
                    S.op(S.dve, lambda e: e.tensor_scalar(out=ang[:], in0=ang[:], scalar1=rc[:, 0:1], scalar2=None,
                                                          op0=ALU.mult), reads=["ang", "rc"], writes=["ang"])
                    S.op(S.dve, lambda e: e.tensor_copy(out=tmp[:], in_=ang[:]), reads=["ang"], writes=["tmp"])
                    sin_turns()
                    S.op(S.dve, lambda e: e.tensor_scalar(out=sinS[64:96, cs], in0=tmp[64:96, :],
                                                          scalar1=rc[64:96, 1:2], scalar2=None, op0=ALU.mult),
                         reads=["tmp", "rc"], writes=["sinS"])
                    S.op(S.dve, lambda e: e.tensor_scalar(out=tmp[:], in0=ang[:], scalar1=0.25, scalar2=None,
                                                          op0=ALU.add), reads=["ang", "sinS"], writes=["tmp"])
                    sin_turns()
                    S.op(S.dve, lambda e: e.tensor_copy(out=cosT[64:96, cs], in_=tmp[64:96, :]),
                         reads=["tmp"], writes=["cosT"])
                S.barrier()
            with ExitStack() as ph:
                sb = lambda n, shp, dt: ph.enter_context(nc.sbuf_tensor(self.nm(n), shp, dt))
                ps = lambda n, shp, dt: ph.enter_context(nc.psum_tensor(self.nm(n), shp, dt))
                wd = sb("m_wd", [128, 8, 768], BF16)
                gq = sb("m_gq", [128, 3], F32)
                gkv = sb("m_gkv", [128, 2], F32)
                raw = [sb(f"m_raw{i}", [128, 5, 512], F32) for i in range(2)]
                sq = [sb(f"m_sq{i}", [128, 5, 512], BF16) for i in range(2)]
                rs = [sb(f"m_rs{i}", [128, 2, 512], F32) for i in range(2)]
                t1 = [sb(f"m_t1{i}", [96, 512], F32) for i in range(2)]
                t2 = [sb(f"m_t2{i}", [96, 512], F32) for i in range(2)]
                p_lat = [ps(f"m_plat{i}", [128, 512], F32) for i in range(2)]
                p_ss = [ps(f"m_pss{i}", [128, 512], F32) for i in range(2)]
                p_kA = ps("m_pkA", [96, 512], F32)
                p_kB = ps("m_pkB", [96, 512], F32)
                S.dma("pool", lambda e: e.dma_start(out=wd[:], in_=self.mla_wd[ia].rearrange("(k p) n -> p k n", p=128)),
                      writes=["wd"])
                S.dma("sp", lambda e: e.dma_start(out=gq[:], in_=self.mla_qn[ia]), writes=["gq"])
                S.dma("sp", lambda e: e.dma_start(out=gkv[:], in_=self.mla_kvn[ia]), writes=["gkv"])
                it = 0
                for j in range(T // 512):
                    tok = slice(j * 512, (j + 1) * 512)
                    ares = [("actT", 4 * j + q) for q in range(4)]
                    b = j % 2
                    for c in range(5):
                        pl = p_lat[it % 2]
                        rpl = ("plat", it % 2)
                        it += 1
                        for k in range(8):
                            S.op(S.pe, lambda e, k=k: e.matmul(pl[:], lhsT=wd[:, k, c * 128:(c + 1) * 128],
                                                               rhs=actT[:, k, tok], start=(k == 0), stop=(k == 7)),
                                 reads=["wd"] + ares, writes=[rpl])
                        S.op(S.act, lambda e: e.copy(out=raw[b][:, c, :], in_=pl[:]), reads=[rpl], writes=[("raw", b, c)])
                        S.op(S.act, lambda e: e.activation(out=sq[b][:, c, :], in_=pl[:], func=AF.Square),
                             reads=[rpl], writes=[("sq", b, c)])
                    for which, (c0, c1, dim) in enumerate([(0, 3, 384.0), (3, 5, 256.0)]):
                        for c in range(c0, c1):
                            S.op(S.pe, lambda e, c=c: e.matmul(p_ss[which][:], lhsT=self.ones_bf[:], rhs=sq[b][:, c, :],
                                                               start=(c == c0), stop=(c == c1 - 1)),
                                 reads=[("sq", b, c), "ones_bf"], writes=[("pss", which)])
                        rr = ("rs", b, which)
                        S.op(S.dve, lambda e: e.tensor_scalar(out=rs[b][:, which, :], in0=p_ss[which][:],
                                                              scalar1=1.0 / dim, scalar2=float(RMS_EPS),
                                                              op0=ALU.mult, op1=ALU.add),
                             reads=[("pss", which)], writes=[rr])
                        S.op(S.act, lambda e: e.activation(out=rs[b][:, which, :], in_=rs[b][:, which, :], func=AF.Sqrt),
                             reads=[rr], writes=[rr])
                        S.op(S.dve, lambda e: e.reciprocal(out=rs[b][:, which, :], in_=rs[b][:, which, :]),
                             reads=[rr], writes=[rr])
                        for c in range(c0, c1):
                            dst = cqn[:, c, tok] if which == 0 else ckvn[:, c - 3, tok]
                            gsc = gq[:, c:c + 1] if which == 0 else gkv[:, c - 3:c - 2]
                            S.op(S.dve, lambda e: e.scalar_tensor_tensor(out=dst, in0=raw[b][:, c, :], scalar=gsc,
                                                                         in1=rs[b][:, which, :], op0=ALU.mult,
                                                                         op1=ALU.mult),
                                 reads=[("raw", b, c), rr, "gq", "gkv"], writes=[("cn", c, j)])
                    for k in range(8):
                        S.op(S.pe, lambda e, k=k: e.matmul(p_kA[:], lhsT=wd[:, k, 640:736], rhs=actT[:, k, tok],
                                                           start=(k == 0), stop=(k == 7)),
                             reads=["wd"] + ares, writes=["pkA"])
                    for k in range(8):
                        S.op(S.pe, lambda e, k=k: e.matmul(p_kB[:], lhsT=wd[:, k, 672:768], rhs=actT[:, k, tok],
                                                           start=(k == 0), stop=(k == 7)),
                             reads=["wd"] + ares, writes=["pkB"])
                    S.op(S.dve, lambda e: e.tensor_tensor(out=t1[b][64:96, :], in0=p_kA[64:96, :], in1=cosT[64:96, tok],
                                                          op=ALU.mult), reads=["pkA"], writes=[("t1", b)])
                    S.op(S.dve, lambda e: e.tensor_tensor(out=t2[b][64:96, :], in0=p_kB[64:96, :], in1=sinS[64:96, tok],
                                                          op=ALU.mult), reads=["pkB"], writes=[("t2", b)])
                    S.op(S.pool, lambda e: e.tensor_tensor(out=KT[64:96, tok], in0=t1[b][64:96, :], in1=t2[b][64:96, :],
                                                           op=ALU.add), reads=[("t1", b), ("t2", b)], writes=[("KTpe", j)])
                S.barrier()
            with ExitStack() as ph:
                sb = lambda n, shp, dt: ph.enter_context(nc.sbuf_tensor(self.nm(n), shp, dt))
                ps = lambda n, shp, dt: ph.enter_context(nc.psum_tensor(self.nm(n), shp, dt))
                wuq = sb("m_wuq", [128, 3, 2048], BF16)
                wukv = sb("m_wukv", [128, 2, 2048], BF16)
                Vx = [sb(f"m_Vx{i}", [128, 32, 128], BF16) for i in range(2)]
                QT = [sb(f"m_QT{i}", [96, 512], BF16) for i in range(2)]
                pt = [sb(f"m_pt{i}", [128, 512], BF16) for i in range(4)]
                t1 = [sb(f"m_u1{i}", [96, 512], F32) for i in range(2)]
                t2 = [sb(f"m_u2{i}", [96, 512], F32) for i in range(2)]
                rec = sb("m_rec", [128, 512], F32)
                bcs = sb("m_bcs", [128, 512], F32)
                p_p = [ps(f"m_pp{i}", [128, 512], F32) for i in range(2)]
                p_s = [ps(f"m_ps{i}", [128, 512], F32) for i in range(3)]
                p_o = [ps(f"m_po{i}", [128, 512], F32) for i in range(2)]
                p_bc = ps("m_pbc", [128, 512], F32)
                KTb = sb("m_KTb", [96, T], BF16)
                KTs = [KT, KTb]
                S.dma("pool", lambda e: e.dma_start(out=wuq[:], in_=self.mla_wuq[ia].rearrange("(k p) n -> p k n", p=128)),
                      writes=["wuq"])
                S.dma("pool", lambda e: e.dma_start(out=wukv[:], in_=self.mla_wukv[ia].rearrange("(k p) n -> p k n", p=128)),
                      writes=["wukv"])
                S.op(S.pool, lambda e: e.memset(Vx[0][:], 1.0), writes=[("Vx", 0)])
                S.op(S.pool, lambda e: e.memset(Vx[1][:], 1.0), writes=[("Vx", 1)])
                S.op(S.pool, lambda e: e.tensor_copy(out=KTb[64:96, :], in_=KT[64:96, :]), writes=["KTb_pe"])
                cnt = {"pp": 0}

                def kv_proj(h):
                    hl = h % 2
                    kt = KTs[hl]
                    vx = Vx[hl]
                    r0 = hl * 64
                    for j in range(T // 512):
                        tok = slice(j * 512, (j + 1) * 512)
                        pp, rpp = p_p[cnt["pp"] % 2], ("pp", cnt["pp"] % 2)
                        cnt["pp"] += 1
                        for k in range(2):
                            S.op(S.pe, lambda e, k=k: e.matmul(pp[0:64, :], lhsT=wukv[:, k, h * 64:(h + 1) * 64],
                                                               rhs=ckvn[:, k, tok], start=(k == 0), stop=(k == 1)),
                                 reads=["wukv"], writes=[rpp])
                        S.op(S.dve, lambda e: e.tensor_copy(out=kt[0:64, tok], in_=pp[0:64, :]), reads=[rpp],
                             writes=[("KT", hl, j)])
                    for j in range(4):
                        pp, rpp = p_p[cnt["pp"] % 2], ("pp", cnt["pp"] % 2)
                        cnt["pp"] += 1
                        ppv = pp[:].rearrange("p (b d) -> p b d", d=64)
                        for bb in range(8):
                            blk = j * 8 + bb
                            for k in range(2):
                                S.op(S.pe, lambda e, k=k: e.matmul(
                                    ppv[:, bb, :], lhsT=ckvn[:, k, blk * 128:(blk + 1) * 128],
                                    rhs=wukv[:, k, 1024 + h * 64:1024 + (h + 1) * 64], start=(k == 0), stop=(k == 1)),
                                     reads=["wukv"], writes=[rpp])
                        S.op(S.dve, lambda e: e.tensor_copy(out=vx[:, j * 8:(j + 1) * 8, r0:r0 + 64], in_=ppv),
                             reads=[rpp], writes=[("Vx", hl, j)])

                def prep_q(h, qt, iq):
                    tok = slice(qt * 512, (qt + 1) * 512)
                    qT, rq = QT[iq % 2], ("QT", iq % 2)
                    u1, u2 = t1[iq % 2], t2[iq % 2]
                    ru1, ru2 = ("u1", iq % 2), ("u2", iq % 2)
                    pA, pB = p_p[0], p_p[1]
                    for k in range(3):
                        S.op(S.pe, lambda e, k=k: e.matmul(pA[0:96, :], lhsT=wuq[:, k, h * 128:h * 128 + 96],
                                                           rhs=cqn[:, k, tok], start=(k == 0), stop=(k == 2)),
                             reads=["wuq"], writes=[("pp", 0)])
                    for k in range(3):
                        S.op(S.pe, lambda e, k=k: e.matmul(pB[0:96, :], lhsT=wuq[:, k, h * 128 + 32:h * 128 + 128],
                                                           rhs=cqn[:, k, tok], start=(k == 0), stop=(k == 2)),
                             reads=["wuq"], writes=[("pp", 1)])
                    S.op(S.dve, lambda e: e.tensor_copy(out=qT[0:64, :], in_=pA[0:64, :]), reads=[("pp", 0)], writes=[rq])
                    S.op(S.dve, lambda e: e.tensor_tensor(out=u1[64:96, :], in0=pA[64:96, :], in1=cosT[64:96, tok],
                                                          op=ALU.mult), reads=[("pp", 0)], writes=[ru1])
                    S.op(S.dve, lambda e: e.tensor_tensor(out=u2[64:96, :], in0=pB[64:96, :], in1=sinS[64:96, tok],
                                                          op=ALU.mult), reads=[("pp", 1)], writes=[ru2])
                    S.op(S.dve, lambda e: e.tensor_tensor(out=qT[64:96, :], in0=u1[64:96, :], in1=u2[64:96, :],
                                                          op=ALU.add), reads=[ru1, ru2], writes=[rq])

                def fin_q1(h, qt, iq):
                    hl = h % 2
                    d0 = 64 - hl * 64
                    po, rpo = p_o[iq % 2], ("po", iq % 2)
                    S.op(S.act, lambda e: e.copy(out=rec[d0:d0 + 1, :], in_=po[d0:d0 + 1, :]), reads=[rpo], writes=["rec"])

                def fin_q2(h, qt, iq):
                    hl, ch = h % 2, h // 2
                    r0, d0 = hl * 64, 64 - hl * 64
                    tok = slice(qt * 512, (qt + 1) * 512)
                    po, rpo = p_o[iq % 2], ("po", iq % 2)
                    S.op(S.pe, lambda e: e.matmul(p_bc[:], lhsT=self.ones_f[d0:d0 + 1, :], rhs=rec[d0:d0 + 1, :],
                                                  start=True, stop=True), reads=["rec", "ones_f"], writes=["pbc"])
                    S.op(S.dve, lambda e: e.reciprocal(out=bcs[r0:r0 + 64, :], in_=p_bc[r0:r0 + 64, :]),
                         reads=["pbc"], writes=["bcs"])
                    S.op(S.dve, lambda e: e.tensor_tensor(out=actT[r0:r0 + 64, ch, tok], in0=po[r0:r0 + 64, :],
                                                          in1=bcs[r0:r0 + 64, :], op=ALU.mult),
                         reads=[rpo, "bcs"], writes=[("actT", 4 * qt + q) for q in range(4)])

                items = []
                iq = 0
                for h in range(16):
                    for qt in range(T // 512):
                        nkb = 4 * qt + 4
                        for kb in range(nkb):
                            items.append(dict(h=h, qt=qt, kb=kb, nkb=nkb, iq=iq, pre=[], post=[]))
                        iq += 1
                first = {}
                for i, it in enumerate(items):
                    first.setdefault((it["h"], it["qt"]), i)
                for (h, qt), i in first.items():
                    lo = 0 if i == 0 else i - items[i - 1]["nkb"]
                    items[max(lo, i - 6)]["pre"].append(lambda h=h, qt=qt, iq=items[i]["iq"]: prep_q(h, qt, iq))
                    if qt == 0 and h > 0:
                        items[first[(h - 1, 5)]]["pre"].insert(0, lambda h=h: kv_proj(h))
                for i, it in enumerate(items):
                    if it["kb"] == it["nkb"] - 1:
                        it["post"].append(lambda it=it: fin_q1(it["h"], it["qt"], it["iq"]))
                        items[min(len(items) - 1, i + 3)]["post"].append(
                            lambda it=it: fin_q2(it["h"], it["qt"], it["iq"]))
                kv_proj(0)

                def stA(i, it):
                    h, qt, kb = it["h"], it["qt"], it["kb"]
                    hl = h % 2
                    n0 = max(0, kb - 4 * qt) * 128
                    qT, rq = QT[it["iq"] % 2], ("QT", it["iq"] % 2)
                    S.op(S.pe, lambda e: e.matmul(p_s[i % 3][:, n0:512], lhsT=KTs[hl][0:96, kb * 128:(kb + 1) * 128],
                                                  rhs=qT[0:96, n0:512], start=True, stop=True),
                         reads=[("KT", hl, kb // 4), rq, "KTb_pe"], writes=[("ps", i % 3)])

                def stB(i, it):
                    qt, kb = it["qt"], it["kb"]
                    n0 = max(0, kb - 4 * qt) * 128
                    ptb, rpt = pt[i % 4], ("pt", i % 4)
                    S.op(S.act, lambda e: e.activation(out=ptb[:, n0:512], in_=p_s[i % 3][:, n0:512], func=AF.Exp,
                                                       scale=float(SCALE)), reads=[("ps", i % 3)], writes=[rpt])
                    if kb >= 4 * qt:
                        S.op(S.dve, lambda e: e.tensor_tensor(out=ptb[:, n0:n0 + 128], in0=ptb[:, n0:n0 + 128],
                                                              in1=self.tri[:], op=ALU.mult),
                             reads=[rpt, "tri"], writes=[rpt])

                def stC(i, it):
                    h, qt, kb, nkb = it["h"], it["qt"], it["kb"], it["nkb"]
                    hl = h % 2
                    n0 = max(0, kb - 4 * qt) * 128
                    po, rpo = p_o[it["iq"] % 2], ("po", it["iq"] % 2)
                    S.op(S.pe, lambda e: e.matmul(po[:, n0:512], lhsT=Vx[hl][:, kb, :], rhs=pt[i % 4][:, n0:512],
                                                  start=(kb == 0), stop=(kb == nkb - 1)),
                         reads=[("Vx", hl, kb // 8), ("pt", i % 4)], writes=[rpo])

                n = len(items)
                for t_ in range(n + 3):
                    if t_ < n:
                        for f in items[t_]["pre"]:
                            f()
                        stA(t_, items[t_])
                    if 0 <= t_ - 1 < n:
                        stB(t_ - 1, items[t_ - 1])
                    if 0 <= t_ - 3 < n:
                        stC(t_ - 3, items[t_ - 3])
                        for f in items[t_ - 3]["post"]:
                            f()
                S.barrier()
        self.out_proj_phase(L, self.mla_wo[ia])


    def load_oT(self):
        S = self.S
        for c in range(8):
            S.dma("sp", lambda e, c=c: e.dma_start(out=self.actT[:, c, :], in_=self.oT_d[c]),
                  reads=[("oTd", c)], writes=[("actT", m) for m in range(NB)])

    def dil_phase(self, L):
        nc, S = self.nc, self.S
        actT = self.actT
        DIL = (1, 4, 16)
        with ExitStack() as ph:
            sb = lambda n, shp, dt: ph.enter_context(nc.sbuf_tensor(self.nm(n), shp, dt))
            ps = lambda n, shp, dt: ph.enter_context(nc.psum_tensor(self.nm(n), shp, dt))
            wd = sb("d_w", [128, 8, 9, 128], BF16)
            QK = [[sb(f"d_qk{g}{i}", [128, T], BF16) for i in range(2)] for g in range(3)]
            Vx = [sb(f"d_vx{g}", [128, 32, 192], BF16) for g in range(3)]
            osb = sb("d_osb", [128, T], BF16)
            mask2 = sb("d_mask2", [128, 256], BF16)
            pt = [sb(f"d_pt{i}", [128, 256], BF16) for i in range(4)]
            rec = sb("d_rec", [128, 512], F32)
            bcs = sb("d_bcs", [128, 512], F32)
            p_o = ps("d_po", [128, 2048], F32)
            p_s = [ps(f"d_ps{i}", [128, 512], F32) for i in range(2)]
            p_bc = ps("d_pbc", [128, 512], F32)
            p_p = ps("d_pp", [128, 512], F32)
            S.op(S.pool, lambda e: e.memset(mask2[:], 1.0), writes=["mask2"])
            S.op(S.pool, lambda e: e.affine_select(out=mask2[:, 0:128], in_=mask2[:, 0:128], pattern=[[-1, 128]],
                                                   compare_op=ALU.is_ge, fill=0.0, base=0, channel_multiplier=1),
                 reads=["mask2"], writes=["mask2"])
            S.op(S.pool, lambda e: e.affine_select(out=mask2[:, 128:256], in_=mask2[:, 128:256], pattern=[[1, 128]],
                                                   compare_op=ALU.is_ge, fill=0.0, base=0, channel_multiplier=-1),
                 reads=["mask2"], writes=["mask2"])
            for g in range(3):
                S.op(S.pool, lambda e, g=g: e.memset(Vx[g][:], 1.0), writes=[("Vx", g)])
            wsrc = self.dil_wqkv.rearrange("(k p) (c h d) -> p k c (h d)", p=128, c=9, h=16)
            pbank = [p_p, p_s[0], p_s[1]]
            cnt = {"pp": 0, "it": 0}

            def next_pp():
                i = cnt["pp"] % 3
                cnt["pp"] += 1
                return pbank[i], ("pb", i)

            def fin(hl, U, qq):
                r0, d0 = hl * 64, 64 - hl * 64
                rows = slice(r0, r0 + 64)
                cs = slice(qq * 512, (qq + 1) * 512)
                tok = slice(U * 2048 + qq * 512, U * 2048 + (qq + 1) * 512)
                S.op(S.act, lambda e: e.copy(out=rec[d0:d0 + 1, :], in_=p_o[d0:d0 + 1, cs]), reads=["po"], writes=["rec"])
                S.op(S.pe, lambda e: e.matmul(p_bc[:], lhsT=self.ones_f[d0:d0 + 1, :], rhs=rec[d0:d0 + 1, :],
                                              start=True, stop=True), reads=["rec", "ones_f"], writes=["pbc"])
                S.op(S.dve, lambda e: e.reciprocal(out=bcs[rows, :], in_=p_bc[rows, :]), reads=["pbc"], writes=["bcs"])
                S.op(S.dve, lambda e: e.tensor_tensor(out=osb[rows, tok], in0=p_o[rows, cs], in1=bcs[rows, :],
                                                      op=ALU.mult), reads=["po", "bcs"], writes=["osb"])

            for hp in range(8):
                for c9 in range(9):
                    S.dma("pool", lambda e: e.dma_start(out=wd[:, :, c9, :], in_=wsrc[:, :, c9, hp * 128:(hp + 1) * 128]),
                          writes=[("wd", c9)])
                for j in range(T // 512):
                    tok = slice(j * 512, (j + 1) * 512)
                    ares = [("actT", 4 * j + q) for q in range(4)]
                    for g in range(3):
                        for qk in range(2):
                            pp, rpp = next_pp()
                            for k in range(8):
                                S.op(S.pe, lambda e, k=k: e.matmul(pp[:], lhsT=wd[:, k, g * 3 + qk, :],
                                                                   rhs=actT[:, k, tok], start=(k == 0), stop=(k == 7)),
                                     reads=[("wd", g * 3 + qk)] + ares, writes=[rpp])
                            if (g * 2 + qk) % 2 == 0:
                                S.op(S.act, lambda e: e.copy(out=QK[g][qk][:, tok], in_=pp[:]),
                                     reads=[rpp], writes=[("QK", g, qk, j)])
                            else:
                                S.op(S.dve, lambda e: e.tensor_copy(out=QK[g][qk][:, tok], in_=pp[:]),
                                     reads=[rpp], writes=[("QK", g, qk, j)])
                for g in range(3):
                    d = DIL[g]
                    for b4 in range(8):
                        pp, rpp = next_pp()
                        ppv = pp[:].rearrange("p (b d) -> p b d", d=128)
                        for bb in range(4):
                            blk = b4 * 4 + bb
                            n, r = blk // d, blk % d
                            t0 = n * 128 * d + r
                            for k in range(8):
                                S.op(S.pe, lambda e, k=k: e.matmul(
                                    ppv[:, bb, :], lhsT=actT[:, k, t0:t0 + 127 * d + 1:d], rhs=wd[:, k, g * 3 + 2, :],
                                    start=(k == 0), stop=(k == 7)),
                                     reads=[("wd", g * 3 + 2)] + [("actT", m) for m in range(n * d, (n + 1) * d)], writes=[rpp])
                        S.op(S.act, lambda e: e.copy(out=Vx[g][:, b4 * 4:(b4 + 1) * 4, 0:64], in_=ppv[:, :, 0:64]),
                             reads=[rpp], writes=[("Vx", g)])
                        S.op(S.dve, lambda e: e.tensor_copy(out=Vx[g][:, b4 * 4:(b4 + 1) * 4, 128:192],
                                                            in_=ppv[:, :, 64:128]),
                             reads=[rpp], writes=[("Vx", g)])
                items = []
                for hl in range(2):
                    for U in range(2):
                        for g in range(3):
                            for qb in range(16):
                                items.append(dict(hl=hl, U=U, g=g, qb=qb, pre=[], preC=[], post=[]))
                        items[-48]["preC"].append(lambda: S.op(S.dve, lambda e: e.memset(p_o[:], 0.0), writes=["po"]))
                        for qq in range(4):
                            items[-1]["post"].append(lambda hl=hl, U=U, qq=qq: fin(hl, U, qq))

                def geom(it):
                    hl, U, g, qb = it["hl"], it["U"], it["g"], it["qb"]
                    d = DIL[g]
                    blk = U * 16 + qb
                    n, r = blk // d, blk % d
                    t0 = n * 128 * d + r
                    return hl, U, g, d, blk, n, t0

                def stA(i, it):
                    hl, U, g, d, blk, n, t0 = geom(it)
                    rows = slice(hl * 64, hl * 64 + 64)
                    Kt, Qt = QK[g][1], QK[g][0]
                    qsl = slice(t0, t0 + 127 * d + 1, d)
                    psb = p_s[i % 2]
                    half = 0
                    rps = ("ps", i % 2)
                    if n > 0:
                        tp = t0 - 128 * d
                        S.op(S.pe, lambda e: e.matmul(psb[:, half:half + 128], lhsT=Kt[rows, tp:tp + 127 * d + 1:d],
                                                      rhs=Qt[rows, qsl], start=True, stop=True),
                             reads=[("QKall",)], writes=[rps, ("pb", i % 2 + 1)])
                    S.op(S.pe, lambda e: e.matmul(psb[:, half + 128:half + 256], lhsT=Kt[rows, qsl],
                                                  rhs=Qt[rows, qsl], start=True, stop=True),
                         reads=[("QKall",)], writes=[rps, ("pb", i % 2 + 1)])

                def stB(i, it):
                    hl, U, g, d, blk, n, t0 = geom(it)
                    psb = p_s[i % 2]
                    half = 0
                    rps = ("ps", i % 2)
                    ptb, rpt = pt[i % 4], ("pt", i % 4)
                    c0 = 0 if n > 0 else 128
                    S.op(S.act, lambda e: e.activation(out=ptb[:, c0:256], in_=psb[:, half + c0:half + 256],
                                                       func=AF.Exp, scale=0.125), reads=[rps], writes=[rpt])
                    S.op(S.dve, lambda e: e.tensor_tensor(out=ptb[:, c0:256], in0=ptb[:, c0:256],
                                                          in1=mask2[:, c0:256], op=ALU.mult),
                         reads=[rpt, "mask2"], writes=[rpt])

                def stC(i, it):
                    hl, U, g, d, blk, n, t0 = geom(it)
                    vsl = slice(0, 128) if hl == 0 else slice(64, 192)
                    osl = slice(t0 - U * 2048, t0 - U * 2048 + 127 * d + 1, d)
                    ptb, rpt = pt[i % 4], ("pt", i % 4)
                    if n > 0:
                        S.op(S.pe, lambda e: e.matmul(p_o[:, osl], lhsT=Vx[g][:, blk - d, vsl], rhs=ptb[:, 0:128],
                                                      start=False, stop=False, skip_group_check=True),
                             reads=[("Vx", g), rpt, "po"], writes=["po"])
                    S.op(S.pe, lambda e: e.matmul(p_o[:, osl], lhsT=Vx[g][:, blk, vsl], rhs=ptb[:, 128:256],
                                                  start=False, stop=False, skip_group_check=True),
                         reads=[("Vx", g), rpt, "po"], writes=["po"])

                n_it = len(items)
                for t_ in range(n_it + 3):
                    if t_ < n_it:
                        for f in items[t_]["pre"]:
                            f()
                        stA(t_, items[t_])
                    if 0 <= t_ - 1 < n_it:
                        stB(t_ - 1, items[t_ - 1])
                    if 0 <= t_ - 3 < n_it:
                        for f in items[t_ - 3]["preC"]:
                            f()
                        stC(t_ - 3, items[t_ - 3])
                        for f in items[t_ - 3]["post"]:
                            f()
                S.dma("sp", lambda e: e.dma_start(out=self.oT_d[hp], in_=osb[:]), reads=["osb"], writes=[("oTd", hp)])
            S.barrier()
            self.load_oT()
            S.barrier()
        self.out_proj_phase(L, self.dil_wo)


    def rwkv_phase(self, L):
        nc, S = self.nc, self.S
        actT = self.actT
        RT = 256
        NT_ = T // RT
        GN_EPS = 64e-5
        DEC = -float(np.exp(-0.5))
        with ExitStack() as ph:
            sb = lambda n, shp, dt: ph.enter_context(nc.sbuf_tensor(self.nm(n), shp, dt))
            ps = lambda n, shp, dt: ph.enter_context(nc.psum_tensor(self.nm(n), shp, dt))
            mu = sb("r_mu", [128, 8, 6], F32)
            omu = sb("r_omu", [128, 8, 6], F32)
            vec = sb("r_vec", [128, 8, 8], F32)
            stage = sb("r_stage", [128, 8, 160], F32)
            l1a = sb("r_l1a", [128, 8, 288], BF16)
            l1b = sb("r_l1b", [128, 8, 288], BF16)
            hwa = sb("r_hwa", [128, T], BF16)
            hg = sb("r_hg", [128, 2, T], BF16)
            w2 = sb("r_w2", [128, D], BF16)
            g2 = sb("r_g2", [128, 2, D], BF16)
            wa = [sb(f"r_wa{i}", [128, 8, 128], BF16) for i in range(3)]
            wb = [sb(f"r_wb{i}", [128, 8, 128], BF16) for i in range(3)]
            f32t = lambda n: sb(n, [128, RT], F32)
            r_, k_, v_, lw, a_, g_, kk, km, be, Lc, eL, eLn, eLp, eD, tA, tB = [f32t(f"r_f{i}") for i in range(16)]
            LC = sb("r_LC", [128, 4], F32)
            eLC = sb("r_eLC", [128, 4], F32)
            rmask = sb("r_rmask", [128, RT], F32)
            blk1 = sb("r_blk1", [128, 128], F32)
            bdn = ["kap", "rt", "kt", "bt", "vf", "kb", "bb"]
            BD = {n: sb("r_bd_" + n, [128, 4, 128], F32) for n in bdn}
            MkvT, AkrT, AbrT, Y = [sb(f"r_m{i}", [128, 4, 128], F32) for i in range(4)]
            X = [sb(f"r_X{i}", [128, 4, 128], F32) for i in range(2)]
            XT = [sb(f"r_XT{i}", [128, 4, 128], F32) for i in range(2)]
            Vtok, Ktok, Btok = [sb(f"r_tk{i}", [128, 4, 128], F32) for i in range(3)]
            Wsb = sb("r_Wsb", [128, 128], F32)
            nU = sb("r_nU", [128, 128], F32)
            Abd = sb("r_Abd", [128, 128], F32)
            Osb4 = sb("r_Osb4", [128, 4, 128], F32)
            st64 = sb("r_st64", [128, 4, 6], F32)
            mv4 = sb("r_mv4", [128, 4, 8], F32)
            On = sb("r_On", [128, 4, 128], F32)
            ysb = sb("r_ysb", [128, RT], F32)
            osb = [sb(f"r_osb{i}", [128, RT], BF16) for i in range(2)]
            SU4, UI4, SL4, ID4 = [sb(f"r_msk{i}", [128, 4, 128], F32) for i in range(4)]
            identf = sb("r_identf", [128, 128], F32)
            st6 = sb("r_st6", [128, 6], F32)
            mv = sb("r_mv", [128, 8], F32)
            pP = [ps(f"r_pP{i}", [128, 512], F32) for i in range(2)]
            pX = [ps(f"r_pX{i}", [128, 512], F32) for i in range(5)]
            pS = ps("r_pS", [128, 512], F32)

            def aff(t, pattern, cm, op, base=0):
                S.op(S.pool, lambda e: e.affine_select(out=t, in_=t, pattern=pattern, compare_op=op, fill=0.0,
                                                       base=base, channel_multiplier=cm),
                     reads=["cst"], writes=["cst"])
            S.op(S.pool, lambda e: e.memset(identf[:], 1.0), writes=["cst"])
            aff(identf[:], [[1, 128]], -1, ALU.is_equal)
            for t4 in (SU4, UI4, SL4, ID4):
                S.op(S.pool, lambda e, t4=t4: e.memset(t4[:], 0.0), reads=["cst"], writes=["cst"])
            S.op(S.pool, lambda e: e.memset(blk1[:], 0.0), reads=["cst"], writes=["cst"])
            for hb in range(2):
                rs_ = slice(hb * 64, hb * 64 + 64)
                S.op(S.pool, lambda e: e.memset(blk1[rs_, rs_], 1.0), reads=["cst"], writes=["cst"])
                for c in range(4):
                    for t4 in (SU4, UI4, SL4, ID4):
                        S.op(S.pool, lambda e, t4=t4: e.memset(t4[rs_, c, rs_], 1.0), reads=["cst"], writes=["cst"])
                    aff(SU4[rs_, c, rs_], [[1, 64]], -1, ALU.is_gt)
                    aff(UI4[rs_, c, rs_], [[1, 64]], -1, ALU.is_ge)
                    aff(SL4[rs_, c, rs_], [[-1, 64]], 1, ALU.is_gt)
                    aff(ID4[rs_, c, rs_], [[1, 64]], -1, ALU.is_equal)
            S.op(S.pool, lambda e: e.memset(rmask[:], 1.0), reads=["cst"], writes=["cst"])
            for c in range(4):
                S.op(S.pool, lambda e, c=c: e.memset(rmask[:, c * 64:c * 64 + 1], 0.0), reads=["cst"], writes=["cst"])
            for n in bdn:
                S.op(S.pool, lambda e, n=n: e.memset(BD[n][:], 0.0), writes=[("bd", n)])
            S.op(S.pool, lambda e: e.memset(On[:], 0.0), writes=["On"])

            S.dma("sp", lambda e: e.dma_start(out=mu[:], in_=self.rw_mu), writes=["mu"])
            S.dma("sp", lambda e: e.dma_start(out=vec[:], in_=self.rw_vec), writes=["vec"])
            S.op(S.dve, lambda e: e.tensor_scalar(out=omu[:], in0=mu[:], scalar1=-1.0, scalar2=1.0, op0=ALU.mult,
                                                  op1=ALU.add), reads=["mu"], writes=["omu"])
            S.op(S.dve, lambda e: e.tensor_scalar(out=vec[:, :, 4:5], in0=vec[:, :, 3:4], scalar1=-1.0, scalar2=1.0,
                                                  op0=ALU.mult, op1=ALU.add), reads=["vec"], writes=["vec"])
            S.dma("pool", lambda e: e.dma_start(out=w2[0:64, :], in_=self.rw_w2), writes=["w2"])
            S.dma("pool", lambda e: e.dma_start(out=w2[64:128, :], in_=self.rw_a2), writes=["a2"])
            S.dma("pool", lambda e: e.dma_start(out=g2[:, 0, :], in_=self.rw_g2[0:128, :]), writes=["g2a"])
            S.dma("pool", lambda e: e.dma_start(out=g2[0:32, 1, :], in_=self.rw_g2[128:160, :]), writes=["g2b"])

            def scaled_weights(src3, ncols, dsts_a, dsts_b, jcols):
                S.dma("sp", lambda e: e.dma_start(out=stage[:, :, 0:ncols], in_=src3), writes=["stage"])
                for (c0, c1, j), da, db in zip(jcols, dsts_a, dsts_b):
                    for k in range(8):
                        S.op(S.pool, lambda e, k=k: e.tensor_scalar(out=da[:, k, :], in0=stage[:, k, c0:c1],
                                                                    scalar1=omu[:, k, j:j + 1], scalar2=None,
                                                                    op0=ALU.mult),
                             reads=["stage", "omu"], writes=["wsc"])
                        S.op(S.pool, lambda e, k=k: e.tensor_scalar(out=db[:, k, :], in0=stage[:, k, c0:c1],
                                                                    scalar1=mu[:, k, j:j + 1], scalar2=None,
                                                                    op0=ALU.mult),
                             reads=["stage", "mu"], writes=["wsc"])

            def proj(pout, la, lb, j, M=128, r0=0):
                t0 = j * RT
                ares = [("actT", m) for m in range(max(0, 2 * j - 1), 2 * j + 2)]
                for k in range(8):
                    S.op(S.pe, lambda e, k=k: e.matmul(pout[r0:r0 + M, 0:RT], lhsT=la(k), rhs=actT[:, k, t0:t0 + RT],
                                                       start=(k == 0), stop=False),
                         reads=["wsc"] + ares, writes=["pP"])
                c0 = 1 if j == 0 else 0
                for k in range(8):
                    S.op(S.pe, lambda e, k=k: e.matmul(pout[r0:r0 + M, c0:RT], lhsT=lb(k),
                                                       rhs=actT[:, k, t0 - 1 + c0:t0 + RT - 1],
                                                       start=False, stop=(k == 7)),
                         reads=["wsc"] + ares, writes=["pP"])

            l1src = self.rw_l1.rearrange("(k p) n -> p k n", p=128)
            scaled_weights(l1src[:, :, 0:128], 128, [l1a[:, :, 0:64], l1a[:, :, 64:128]],
                           [l1b[:, :, 0:64], l1b[:, :, 64:128]], [(0, 64, 3), (64, 128, 4)])
            scaled_weights(l1src[:, :, 128:288], 160, [l1a[:, :, 128:288]], [l1b[:, :, 128:288]], [(0, 160, 5)])
            for j in range(NT_):
                tok = slice(j * RT, (j + 1) * RT)
                p0 = pP[j % 2]
                proj(p0, lambda k: l1a[:, k, 0:128], lambda k: l1b[:, k, 0:128], j)
                S.op(S.act, lambda e: e.activation(out=hwa[0:64, tok], in_=p0[0:64, 0:RT], func=AF.Tanh),
                     reads=["pP"], writes=["hwa"])
                S.op(S.act, lambda e: e.copy(out=hwa[64:128, tok], in_=p0[64:128, 0:RT]), reads=["pP"], writes=["hwa"])
                proj(p0, lambda k: l1a[:, k, 128:256], lambda k: l1b[:, k, 128:256], j)
                S.op(S.act, lambda e: e.activation(out=hg[:, 0, tok], in_=p0[:, 0:RT], func=AF.Sigmoid),
                     reads=["pP"], writes=["hg"])
                proj(p0, lambda k: l1a[:, k, 256:288], lambda k: l1b[:, k, 256:288], j, M=32)
                S.op(S.act, lambda e: e.activation(out=hg[0:32, 1, tok], in_=p0[0:32, 0:RT], func=AF.Sigmoid),
                     reads=["pP"], writes=["hg"])

            wsrc = self.rw_wrkv.rearrange("j (k p) n -> j p k n", p=128)
            H0, H1 = slice(0, 64), slice(64, 128)

            def dv(fn, reads, writes, eng=None):
                S.op(eng or S.dve, fn, reads=reads, writes=writes)

            import os
            STOP = int(os.environ.get("RW_STOP", "9"))
            for hp in range(8 if STOP > 0 else 0):
                cols = slice(hp * 128, (hp + 1) * 128)
                for jj in range(3):
                    scaled_weights(wsrc[jj][:, :, cols], 128, [wa[jj]], [wb[jj]], [(0, 128, jj)])
                S.op(S.pool, lambda e: e.memset(Abd[:], 0.0), reads=["Abd"], writes=["Abd"])
                vcol = lambda i: vec[:, hp, i:i + 1]
                for j in range(NT_):
                    tok = slice(j * RT, (j + 1) * RT)
                    F = "F"
                    for jj, dst in enumerate((r_, k_, v_)):
                        p0 = pP[jj % 2]
                        proj(p0, lambda k: wa[jj][:, k, :], lambda k: wb[jj][:, k, :], j)
                        S.op(S.act, lambda e: e.copy(out=dst[:], in_=p0[:, 0:RT]), reads=["pP"], writes=[F])
                    p0 = pP[1]
                    S.op(S.pe, lambda e: e.matmul(p0[:, 0:RT], lhsT=w2[0:64, cols], rhs=hwa[0:64, tok], start=True,
                                                  stop=True), reads=["w2", "hwa"], writes=["pP"])
                    S.op(S.act, lambda e: e.activation(out=lw[:], in_=p0[:, 0:RT], func=AF.Sigmoid, bias=vcol(0)),
                         reads=["pP", "vec"], writes=[F])
                    S.op(S.pe, lambda e: e.matmul(p0[:, 0:RT], lhsT=w2[64:128, cols], rhs=hwa[64:128, tok], start=True,
                                                  stop=True), reads=["a2", "hwa"], writes=["pP"])
                    S.op(S.act, lambda e: e.activation(out=a_[:], in_=p0[:, 0:RT], func=AF.Sigmoid, bias=vcol(1)),
                         reads=["pP", "vec"], writes=[F])
                    S.op(S.pe, lambda e: e.matmul(p0[:, 0:RT], lhsT=g2[:, 0, cols], rhs=hg[:, 0, tok], start=True,
                                                  stop=False), reads=["g2a", "hg"], writes=["pP"])
                    S.op(S.pe, lambda e: e.matmul(p0[:, 0:RT], lhsT=g2[0:32, 1, cols], rhs=hg[0:32, 1, tok], start=False,
                                                  stop=True), reads=["g2b", "hg"], writes=["pP"])
                    S.op(S.act, lambda e: e.copy(out=g_[:], in_=p0[:, 0:RT]), reads=["pP"], writes=[F])
                    dv(lambda e: e.tensor_scalar(out=lw[:], in0=lw[:], scalar1=DEC, scalar2=None, op0=ALU.mult), [F], [F])
                    dv(lambda e: e.tensor_scalar(out=kk[:], in0=k_[:], scalar1=vcol(2), scalar2=None, op0=ALU.mult),
                       [F, "vec"], [F])
                    dv(lambda e: e.tensor_tensor(out=tA[:], in0=kk[:], in1=kk[:], op=ALU.mult), [F], [F], S.pool)
                    S.op(S.pe, lambda e: e.matmul(pP[0][:, 0:RT], lhsT=blk1[:], rhs=tA[:], start=True, stop=True),
                         reads=[F, "cst"], writes=["pP"])
                    S.op(S.act, lambda e: e.activation(out=tA[:], in_=pP[0][:, 0:RT], func=AF.Sqrt), reads=["pP"], writes=[F])
                    dv(lambda e: e.tensor_scalar(out=tA[:], in0=tA[:], scalar1=1e-12, scalar2=None, op0=ALU.max), [F], [F])
                    dv(lambda e: e.reciprocal(out=tA[:], in_=tA[:]), [F], [F])
                    dv(lambda e: e.tensor_tensor(out=kk[:], in0=kk[:], in1=tA[:], op=ALU.mult), [F], [F])
                    dv(lambda e: e.tensor_scalar(out=tB[:], in0=a_[:], scalar1=vcol(3), scalar2=vcol(4), op0=ALU.mult,
                                                 op1=ALU.add), [F, "vec"], [F])
                    dv(lambda e: e.tensor_tensor(out=km[:], in0=k_[:], in1=tB[:], op=ALU.mult), [F], [F])
                    dv(lambda e: e.tensor_tensor(out=be[:], in0=kk[:], in1=a_[:], op=ALU.mult), [F], [F], S.pool)
                    dv(lambda e: e.scalar_tensor_tensor(out=tB[:], in0=r_[:], scalar=vcol(5), in1=km[:], op0=ALU.mult,
                                                        op1=ALU.mult), [F, "vec"], [F])
                    S.op(S.pe, lambda e: e.matmul(pP[0][:, 0:RT], lhsT=blk1[:], rhs=tB[:], start=True, stop=True),
                         reads=[F, "cst"], writes=["pP"])
                    dv(lambda e: e.tensor_tensor(out=tA[:], in0=pP[0][:, 0:RT], in1=v_[:], op=ALU.mult), ["pP", F], [F])
                    dv(lambda e: e.tensor_tensor_scan(out=Lc[:], data0=rmask[:], data1=lw[:], initial=0.0,
                                                      op0=ALU.mult, op1=ALU.add), [F, "cst"], [F])
                    S.op(S.act, lambda e: e.activation(out=eL[:], in_=Lc[:], func=AF.Exp), reads=[F], writes=[F])
                    S.op(S.act, lambda e: e.activation(out=eLn[:], in_=Lc[:], func=AF.Exp, scale=-1.0), reads=[F], writes=[F])
                    dv(lambda e: e.tensor_tensor(out=eLp[:], in0=Lc[:], in1=lw[:], op=ALU.subtract), [F], [F], S.pool)
                    S.op(S.act, lambda e: e.activation(out=eLp[:], in_=eLp[:], func=AF.Exp), reads=[F], writes=[F])
                    L3 = Lc[:].rearrange("p (c t) -> p c t", t=64)
                    dv(lambda e: e.tensor_copy(out=LC[:], in_=L3[:, :, 63]), [F], [F])
                    S.op(S.act, lambda e: e.activation(out=eLC[:], in_=LC[:], func=AF.Exp), reads=[F], writes=[F])
                    for c in range(4):
                        S.op(S.act, lambda e, c=c: e.activation(out=eD[:, c * 64:(c + 1) * 64], in_=Lc[:, c * 64:(c + 1) * 64],
                                                                func=AF.Exp, scale=-1.0, bias=LC[:, c:c + 1]),
                             reads=[F], writes=[F])
                    prods = [("kap", kk, eLp), ("rt", r_, eL), ("kt", km, eLn), ("bt", be, eLn), ("kb", km, eD),
                             ("bb", be, eD)]
                    ie = 0
                    for n, x0, x1 in prods:
                        for hs in (H0, H1):
                            eng = S.dve if ie % 2 == 0 else S.pool
                            ie += 1
                            dv(lambda e: e.tensor_tensor(out=BD[n][hs, :, hs],
                                                         in0=x0[hs, :].rearrange("p (c t) -> p c t", t=64),
                                                         in1=x1[hs, :].rearrange("p (c t) -> p c t", t=64), op=ALU.mult),
                               [F], [("bd", n)], eng)
                    for hs in (H0, H1):
                        S.op(S.act, lambda e: e.copy(out=BD["vf"][hs, :, hs], in_=v_[hs, :].rearrange("p (c t) -> p c t", t=64)),
                             reads=[F], writes=[("bd", "vf")])
                    if STOP < 2:
                        continue
                    for c in range(4):
                        cs = slice(c * 128, (c + 1) * 128)
                        gm = [(0, "kt", "kap"), (1, "kt", "rt"), (2, "bt", "kap"), (3, "bt", "rt"), (4, "kap", "bt")]
                        for pi, ln_, rn_ in gm:
                            S.op(S.pe, lambda e: e.matmul(pX[pi][:, cs], lhsT=BD[ln_][:, c, :], rhs=BD[rn_][:, c, :],
                                                          start=True, stop=True),
                                 reads=[("bd", ln_), ("bd", rn_)], writes=[("pX", pi)])
                    f4 = lambda t: t[:].rearrange("p c t -> p (c t)")
                    dv(lambda e: e.tensor_tensor(out=f4(MkvT), in0=pX[0][:], in1=f4(SU4), op=ALU.mult), [("pX", 0), "cst"], ["MkvT"])
                    dv(lambda e: e.tensor_tensor(out=f4(AkrT), in0=pX[1][:], in1=f4(UI4), op=ALU.mult), [("pX", 1), "cst"], ["AkrT"])
                    dv(lambda e: e.tensor_tensor(out=f4(X[0]), in0=pX[2][:], in1=f4(SU4), op=ALU.mult), [("pX", 2), "cst"], [("X", 0)])
                    dv(lambda e: e.tensor_tensor(out=f4(AbrT), in0=pX[3][:], in1=f4(UI4), op=ALU.mult), [("pX", 3), "cst"], ["AbrT"])
                    dv(lambda e: e.tensor_tensor(out=f4(XT[0]), in0=pX[4][:], in1=f4(SL4), op=ALU.mult), [("pX", 4), "cst"], [("XT", 0)])
                    if STOP < 3:
                        continue
                    dv(lambda e: e.tensor_tensor(out=f4(Y), in0=f4(ID4), in1=f4(X[0]), op=ALU.subtract),
                       [("X", 0), "cst"], ["Y"], S.pool)
                    cur = 0
                    for lvl in range(int(os.environ.get('RW_LVL', '5'))):
                        nxt = 1 - cur
                        last = (lvl == 4)
                        for c in range(4):
                            cs = slice(c * 128, (c + 1) * 128)
                            if not last:
                                S.op(S.pe, lambda e: e.matmul(pX[0][:, cs], lhsT=XT[cur][:, c, :], rhs=X[cur][:, c, :],
                                                              start=True, stop=True),
                                     reads=[("X", cur), ("XT", cur)], writes=[("pX", 0)])
                            S.op(S.pe, lambda e: e.matmul(pX[2][:, cs], lhsT=X[cur][:, c, :], rhs=XT[cur][:, c, :],
                                                          start=True, stop=True),
                                 reads=[("X", cur), ("XT", cur)], writes=[("pX", 2)])
                        if not last:
                            dv(lambda e: e.tensor_copy(out=f4(X[nxt]), in_=pX[0][:]), [("pX", 0)], [("X", nxt)])
                        S.op(S.act, lambda e: e.copy(out=f4(XT[nxt]), in_=pX[2][:]), reads=[("pX", 2)], writes=[("XT", nxt)])
                        for c in range(4):
                            cs = slice(c * 128, (c + 1) * 128)
                            S.op(S.pe, lambda e: e.matmul(pX[1][:, cs], lhsT=XT[nxt][:, c, :],
                                                          rhs=Y[:, c, :], start=True, stop=True),
                                 reads=[("XT", nxt), "Y"], writes=[("pX", 1)])
                        dv(lambda e: e.tensor_tensor(out=f4(Y), in0=f4(Y), in1=pX[1][:], op=ALU.add),
                           [("pX", 1), "Y"], ["Y"])
                        cur = nxt
                    for pi, n, dst in ((3, "vf", Vtok), (4, "kb", Ktok), (1, "bb", Btok)):
                        for c in range(4):
                            S.op(S.pe, lambda e: e.transpose(out=pX[pi][:, c * 128:(c + 1) * 128], in_=BD[n][:, c, :],
                                                             identity=identf[:]),
                                 reads=[("bd", n), "cst"], writes=[("pX", pi)])
                        S.op(S.act, lambda e: e.copy(out=f4(dst), in_=pX[pi][:]), reads=[("pX", pi)], writes=[("tok", n)])
                    if STOP < 4:
                        continue
                    for c in range(4):
                        S.op(S.pe, lambda e: e.matmul(pX[0][:, 0:128], lhsT=BD["kap"][:, c, :], rhs=Abd[:], start=True, stop=False),
                             reads=[("bd", "kap"), "Abd"], writes=["pW"])
                        S.op(S.pe, lambda e: e.matmul(pX[0][:, 0:128], lhsT=MkvT[:, c, :], rhs=Vtok[:, c, :], start=False, stop=True),
                             reads=["MkvT", ("tok", "vf")], writes=["pW"])
                        S.op(S.act, lambda e: e.copy(out=Wsb[:], in_=pX[0][:, 0:128]), reads=["pW"], writes=["Wsb"])
                        S.op(S.pe, lambda e: e.matmul(pX[1][:, 0:128], lhsT=Y[:, c, :], rhs=Wsb[:], start=True, stop=True),
                             reads=["Y", "Wsb"], writes=["pU"])
                        dv(lambda e: e.tensor_scalar(out=nU[:], in0=pX[1][:, 0:128], scalar1=-1.0, scalar2=None, op0=ALU.mult),
                           ["pU"], ["nU"])
                        S.op(S.pe, lambda e: e.matmul(pX[3][:, 0:128], lhsT=Ktok[:, c, :], rhs=Vtok[:, c, :], start=True, stop=False),
                             reads=[("tok", "kb"), ("tok", "vf")], writes=["pA"])
                        S.op(S.pe, lambda e: e.matmul(pX[3][:, 0:128], lhsT=Btok[:, c, :], rhs=nU[:], start=False, stop=True),
                             reads=[("tok", "bb"), "nU"], writes=["pA"])
                        oc = pX[2][:, c * 128:(c + 1) * 128]
                        S.op(S.pe, lambda e: e.matmul(oc, lhsT=BD["rt"][:, c, :], rhs=Abd[:], start=True, stop=False),
                             reads=[("bd", "rt"), "Abd"], writes=["pO"])
                        S.op(S.pe, lambda e: e.matmul(oc, lhsT=AkrT[:, c, :], rhs=Vtok[:, c, :], start=False, stop=False),
                             reads=["AkrT", ("tok", "vf")], writes=["pO"])
                        S.op(S.pe, lambda e: e.matmul(oc, lhsT=AbrT[:, c, :], rhs=nU[:], start=False, stop=True),
                             reads=["AbrT", "nU"], writes=["pO"])
                        dv(lambda e: e.scalar_tensor_tensor(out=Abd[:], in0=Abd[:], scalar=eLC[:, c:c + 1], in1=pX[3][:, 0:128],
                                                            op0=ALU.mult, op1=ALU.add), ["pA", "Abd", F], ["Abd"])
                    S.op(S.act, lambda e: e.copy(out=f4(Osb4), in_=pX[2][:]), reads=["pO"], writes=["Osb"])
                    for c in range(4):
                        for hs in (H0, H1):
                            dv(lambda e: e.bn_stats(out=st64[hs, c, :], in_=Osb4[hs, c, hs]), ["Osb"], ["st6"])
                        dv(lambda e: e.bn_aggr(out=mv4[:, c, 0:2], in_=st64[:, c, :]), ["st6"], ["mv"])
                    dv(lambda e: e.tensor_scalar(out=mv4[:, :, 2:3], in0=mv4[:, :, 1:2], scalar1=GN_EPS, scalar2=None, op0=ALU.add),
                       ["mv"], ["mv"])
                    S.op(S.act, lambda e: e.activation(out=mv4[:, :, 3:4], in_=mv4[:, :, 2:3], func=AF.Sqrt), reads=["mv"], writes=["mv"])
                    dv(lambda e: e.reciprocal(out=mv4[:, :, 4:5], in_=mv4[:, :, 3:4]), ["mv"], ["mv"])
                    dv(lambda e: e.scalar_tensor_tensor(out=mv4[:, :, 5:6], in0=mv4[:, :, 0:1], scalar=-1.0, in1=mv4[:, :, 4:5],
                                                        op0=ALU.mult, op1=ALU.mult), ["mv"], ["mv"])
                    for c in range(4):
                        for hs in (H0, H1):
                            S.op(S.act, lambda e: e.activation(out=On[hs, c, hs], in_=Osb4[hs, c, hs], func=AF.Identity,
                                                               bias=mv4[hs, c, 5:6], scale=mv4[hs, c, 4:5]),
                                 reads=["mv", "Osb"], writes=["On"])
                    for c in range(4):
                        S.op(S.pe, lambda e: e.transpose(out=pP[1][:, c * 128:(c + 1) * 128], in_=On[:, c, :], identity=identf[:]),
                             reads=["On", "cst"], writes=["pP"])
                    for hs in (H0, H1):
                        dv(lambda e: e.tensor_scalar(out=ysb[hs, :].rearrange("p (c t) -> p c t", t=64),
                                                     in0=pP[1][:].rearrange("p (c t) -> p c t", t=128)[hs, :, hs],
                                                     scalar1=vec[hs, hp, 6:7], scalar2=vec[hs, hp, 7:8], op0=ALU.mult,
                                                     op1=ALU.add), ["pP", "vec"], ["ysb"])
                    dv(lambda e: e.tensor_tensor(out=ysb[:], in0=ysb[:], in1=tA[:], op=ALU.add), ["ysb", F], ["ysb"], S.pool)
                    ob = osb[j % 2]
                    dv(lambda e: e.tensor_tensor(out=ob[:], in0=ysb[:], in1=g_[:], op=ALU.mult), ["ysb", F], [("osb", j % 2)], S.pool)
                    S.dma("sp", lambda e: e.dma_start(out=self.oT_d[hp][:, tok], in_=ob[:]), reads=[("osb", j % 2)],
                          writes=[("oTd", hp)])
            S.barrier()
            self.load_oT()
            S.barrier()
        self.out_proj_phase(L, self.rw_wo)


    def rwkv_phase2(self, L):
        import os
        nc, S = self.nc, self.S
        actT = self.actT
        RT = 256
        NCk = RT // 64
        NT_ = T // RT
        GN_EPS = 64e-5
        DEC = -float(np.exp(-0.5))
        H0, H1 = slice(0, 64), slice(64, 128)
        with ExitStack() as ph:
            sb = lambda n, shp, dt: ph.enter_context(nc.sbuf_tensor(self.nm(n), shp, dt))
            ps = lambda n, shp, dt: ph.enter_context(nc.psum_tensor(self.nm(n), shp, dt))
            PB = [[ps(f"r_P{l}{i}", [128, 512], F32) for i in range(4)] for l in range(2)]
            mu = sb("r_mu", [128, 8, 6], F32)
            omu = sb("r_omu", [128, 8, 6], F32)
            vec = sb("r_vec", [128, 8, 8], F32)
            hwa = sb("r_hwa", [128, T], BF16)
            hg = sb("r_hg", [128, 2, T], BF16)
            w2 = sb("r_w2", [128, D], BF16)
            g2 = sb("r_g2", [128, 2, D], BF16)
            stage = sb("r_stage", [128, 8, 128], F32)
            rmask = sb("r_rmask", [128, RT], BF16)
            blk1 = sb("r_blk1", [128, 128], F32)
            identf = sb("r_identf", [128, 128], F32)
            SUm, UIm, SLm, IDm = [sb(f"r_msk{i}", [128, NCk, 128], BF16) for i in range(4)]

            def dv(fn, reads, writes, eng=None):
                S.op(eng or S.dve, fn, reads=reads, writes=writes)

            def aff(t, pattern, cm, op):
                S.op(S.pool, lambda e: e.affine_select(out=t, in_=t, pattern=pattern, compare_op=op, fill=0.0,
                                                       base=0, channel_multiplier=cm), reads=["cst"], writes=["cst"])
            S.op(S.pool, lambda e: e.memset(identf[:], 1.0), writes=["cst"])
            aff(identf[:], [[1, 128]], -1, ALU.is_equal)
            for t4 in (SUm, UIm, SLm, IDm):
                S.op(S.pool, lambda e, t4=t4: e.memset(t4[:], 0.0), reads=["cst"], writes=["cst"])
            S.op(S.pool, lambda e: e.memset(blk1[:], 0.0), reads=["cst"], writes=["cst"])
            for hb in range(2):
                rs_ = slice(hb * 64, hb * 64 + 64)
                S.op(S.pool, lambda e: e.memset(blk1[rs_, rs_], 1.0), reads=["cst"], writes=["cst"])
                for c in range(NCk):
                    for t4 in (SUm, UIm, SLm, IDm):
                        S.op(S.pool, lambda e, t4=t4: e.memset(t4[rs_, c, rs_], 1.0), reads=["cst"], writes=["cst"])
                    aff(SUm[rs_, c, rs_], [[1, 64]], -1, ALU.is_gt)
                    aff(UIm[rs_, c, rs_], [[1, 64]], -1, ALU.is_ge)
                    aff(SLm[rs_, c, rs_], [[-1, 64]], 1, ALU.is_gt)
                    aff(IDm[rs_, c, rs_], [[1, 64]], -1, ALU.is_equal)
            S.op(S.pool, lambda e: e.memset(rmask[:], 1.0), reads=["cst"], writes=["cst"])
            for c in range(NCk):
                S.op(S.pool, lambda e, c=c: e.memset(rmask[:, c * 64:c * 64 + 1], 0.0), reads=["cst"], writes=["cst"])

            S.dma("sp", lambda e: e.dma_start(out=mu[:], in_=self.rw_mu), writes=["mu"])
            S.dma("sp", lambda e: e.dma_start(out=vec[:], in_=self.rw_vec), writes=["vec"])
            dv(lambda e: e.tensor_scalar(out=omu[:], in0=mu[:], scalar1=-1.0, scalar2=1.0, op0=ALU.mult, op1=ALU.add),
               ["mu"], ["omu"])
            dv(lambda e: e.tensor_scalar(out=vec[:, :, 4:5], in0=vec[:, :, 3:4], scalar1=-1.0, scalar2=1.0,
                                         op0=ALU.mult, op1=ALU.add), ["vec"], ["vec"])
            S.dma("pool", lambda e: e.dma_start(out=w2[0:64, :], in_=self.rw_w2), writes=["w2"])
            S.dma("pool", lambda e: e.dma_start(out=w2[64:128, :], in_=self.rw_a2), writes=["a2"])
            S.dma("pool", lambda e: e.dma_start(out=g2[:, 0, :], in_=self.rw_g2[0:128, :]), writes=["g2a"])
            S.dma("pool", lambda e: e.dma_start(out=g2[0:32, 1, :], in_=self.rw_g2[128:160, :]), writes=["g2b"])

            def scaled_weights(src3, ncols, dsts_a, dsts_b, jcols, wres, stage=stage):
                S.dma("sp", lambda e: e.dma_start(out=stage[:, :, 0:ncols], in_=src3), writes=["stage"])
                for (c0, c1, j), da, db in zip(jcols, dsts_a, dsts_b):
                    for k in range(8):
                        S.op(S.act, lambda e, k=k: e.activation(out=da[:, k, :], in_=stage[:, k, c0:c1], func=AF.Identity,
                                                                scale=omu[:, k, j:j + 1]),
                             reads=["stage", "omu"], writes=[wres])
                        S.op(S.dve, lambda e, k=k: e.tensor_scalar(out=db[:, k, :], in0=stage[:, k, c0:c1],
                                                                   scalar1=mu[:, k, j:j + 1], scalar2=None,
                                                                   op0=ALU.mult),
                             reads=["stage", "mu"], writes=[wres])

            def proj(pout, pres, la, lb, t0, n_tok, wres, M=128, xs=None, xres=None):
                if xs is not None:
                    for k in range(8):
                        S.op(S.pe, lambda e, k=k: e.matmul(pout[0:M, 0:n_tok], lhsT=la(k), rhs=xs[:, k, 1:n_tok + 1],
                                                           start=(k == 0), stop=False),
                             reads=[wres, xres], writes=[pres])
                    for k in range(8):
                        S.op(S.pe, lambda e, k=k: e.matmul(pout[0:M, 0:n_tok], lhsT=lb(k), rhs=xs[:, k, 0:n_tok],
                                                           start=False, stop=(k == 7)),
                             reads=[wres, xres], writes=[pres])
                    return
                ares = [("actT", m) for m in range(max(0, t0 // 128 - 1), (t0 + n_tok - 1) // 128 + 1)]
                for k in range(8):
                    S.op(S.pe, lambda e, k=k: e.matmul(pout[0:M, 0:n_tok], lhsT=la(k), rhs=actT[:, k, t0:t0 + n_tok],
                                                       start=(k == 0), stop=False),
                         reads=[wres] + ares, writes=[pres])
                c0 = 1 if t0 == 0 else 0
                for k in range(8):
                    S.op(S.pe, lambda e, k=k: e.matmul(pout[0:M, c0:n_tok], lhsT=lb(k),
                                                       rhs=actT[:, k, t0 - 1 + c0:t0 + n_tok - 1],
                                                       start=False, stop=(k == 7)),
                         reads=[wres] + ares, writes=[pres])

            with ExitStack() as ph1:
                sb1 = lambda n, shp, dt: ph1.enter_context(nc.sbuf_tensor(self.nm(n), shp, dt))
                l1a = sb1("r_l1a", [128, 8, 288], BF16)
                l1b = sb1("r_l1b", [128, 8, 288], BF16)
                stage1 = sb1("r_stage1", [128, 8, 160], F32)
                l1src = self.rw_l1.rearrange("(k p) n -> p k n", p=128)
                scaled_weights(l1src[:, :, 0:128], 128, [l1a[:, :, 0:64], l1a[:, :, 64:128]],
                               [l1b[:, :, 0:64], l1b[:, :, 64:128]], [(0, 64, 3), (64, 128, 4)], "l1", stage=stage1)
                scaled_weights(l1src[:, :, 128:288], 160, [l1a[:, :, 128:288]], [l1b[:, :, 128:288]], [(0, 160, 5)], "l1",
                               stage=stage1)
                R1T = 256
                for j in range(T // R1T):
                    tok = slice(j * R1T, (j + 1) * R1T)
                    b0, b1, b2 = PB[0][j % 2], PB[0][2 + j % 2], PB[1][j % 2]
                    r0_, r1_, r2_ = ("P", 0, j % 2), ("P", 0, 2 + j % 2), ("P", 1, j % 2)
                    proj(b0, r0_, lambda k: l1a[:, k, 0:128], lambda k: l1b[:, k, 0:128], j * R1T, R1T, "l1")
                    S.op(S.act, lambda e: e.activation(out=hwa[0:64, tok], in_=b0[0:64, 0:R1T], func=AF.Tanh),
                         reads=[r0_], writes=["hwa"])
                    S.op(S.act, lambda e: e.copy(out=hwa[64:128, tok], in_=b0[64:128, 0:R1T]), reads=[r0_], writes=["hwa"])
                    proj(b1, r1_, lambda k: l1a[:, k, 128:256], lambda k: l1b[:, k, 128:256], j * R1T, R1T, "l1")
                    S.op(S.act, lambda e: e.activation(out=hg[:, 0, tok], in_=b1[:, 0:R1T], func=AF.Sigmoid),
                         reads=[r1_], writes=["hg"])
                    proj(b2, r2_, lambda k: l1a[:, k, 256:288], lambda k: l1b[:, k, 256:288], j * R1T, R1T, "l1", M=32)
                    S.op(S.act, lambda e: e.activation(out=hg[0:32, 1, tok], in_=b2[0:32, 0:R1T], func=AF.Sigmoid),
                         reads=[r2_], writes=["hg"])
                for c in range(8):
                    S.dma("sp", lambda e, c=c: e.dma_start(out=self.xT_d[c], in_=actT[:, c, :]), writes=[("xTd", c)])
                S.barrier()

            flat = actT[:].rearrange("p k t -> p (k t)")
            arena = {"off": 0}

            def carve(shape, dt):
                n = int(np.prod(shape[1:]))
                nb = n * (2 if dt == F32 else 1)
                assert arena["off"] + nb <= 8 * T, "lane-1 arena overflow"
                v = flat[:, arena["off"]:arena["off"] + nb]
                arena["off"] += nb
                if dt == F32:
                    v = v.bitcast(F32)
                if len(shape) == 3:
                    v = v.rearrange("p (a b) -> p a b", b=shape[2])
                return v

            class _T:
                def __init__(self, ap):
                    self.ap = ap

                def __getitem__(self, key):
                    return self.ap[key]

            bdn = ["kap", "rt", "kt", "bt", "vf", "kb", "bb"]
            lanes = []
            for l in range(2):
                Bf = {}

                def lb_(name, shape, dt, l=l, small=False):
                    if l == 0 or small:
                        return sb(f"r{l}_{name}", shape, dt)
                    return _T(carve(shape, dt))
                for n in ["r_", "k_", "v_", "lw", "a_", "g_", "kk", "km", "be", "Lc", "eL", "eLn", "eLp", "eD", "tA", "tB", "ysb"]:
                    Bf[n] = lb_(n, [128, RT], F32)
                Bf["LC"] = lb_("LC", [128, NCk], F32, small=True)
                Bf["eLC"] = lb_("eLC", [128, NCk], F32, small=True)
                Bf["BD"] = {n: lb_("bd_" + n, [128, NCk, 128], F32) for n in bdn}
                for n in ["MkvT", "AkrT", "AbrT", "Y", "X0", "X1", "XT0", "XT1", "Vtok", "Ktok", "Btok", "Osb4", "On"]:
                    Bf[n] = lb_(n, [128, NCk, 128], F32, small=(n in ("Osb4", "On", "Btok")))
                for n in ["Wsb", "nU", "Abd"]:
                    Bf[n] = lb_(n, [128, 128], F32, small=True)
                Bf["osb"] = [lb_(f"osb{i}", [128, RT], BF16, small=True) for i in range(2)]
                Bf["st64"] = lb_("st64", [128, NCk, 6], F32, small=True)
                Bf["mv4"] = lb_("mv4", [128, NCk, 8], F32, small=True)
                Bf["wa"] = [lb_(f"wa{i}", [128, 8, 128], BF16) for i in range(3)]
                Bf["wb"] = [lb_(f"wb{i}", [128, 8, 128], BF16) for i in range(3)]
                Bf["xs"] = lb_("xs", [128, 8, RT + 1], BF16, small=True)
                for n in bdn:
                    S.op(S.pool, lambda e, n=n: e.memset(Bf["BD"][n][:], 0.0), writes=[(l, "bd", n)])
                S.op(S.pool, lambda e: e.memset(Bf["On"][:], 0.0), writes=[(l, "On")])
                lanes.append(Bf)

            wsrc = self.rw_wrkv.rearrange("j (k p) n -> j p k n", p=128)
            f4 = lambda t: t[:].rearrange("p c t -> p (c t)")
            W_ = NCk * 128

            xsrc = self.xT_d.rearrange("c p t -> p c t")

            def prep_pair_weights(l, hp):
                cols = slice(hp * 128, (hp + 1) * 128)
                for jj in range(3):
                    scaled_weights(wsrc[jj][:, :, cols], 128, [lanes[l]["wa"][jj]], [lanes[l]["wb"][jj]], [(0, 128, jj)],
                                   (l, "wsc"))

            def load_xs(l, j):
                xs = lanes[l]["xs"]
                t0 = j * RT
                if j == 0:
                    S.op(S.pool, lambda e: e.memset(xs[:, :, 0:1], 0.0), writes=[(l, "xs")])
                    S.dma("sp", lambda e: e.dma_start(out=xs[:, :, 1:RT + 1], in_=xsrc[:, :, 0:RT]),
                          reads=[("xTd", c) for c in range(8)], writes=[(l, "xs")])
                else:
                    S.dma("sp", lambda e: e.dma_start(out=xs[:, :, :], in_=xsrc[:, :, t0 - 1:t0 + RT]),
                          reads=[("xTd", c) for c in range(8)], writes=[(l, "xs")])

            def unit(l, hp, j):
                Bf = lanes[l]
                P = PB[l]
                pr = lambda i: ("P", l, i)
                R = lambda n: (l, n)
                F = (l, "F")
                cols = slice(hp * 128, (hp + 1) * 128)
                tok = slice(j * RT, (j + 1) * RT)
                vcol = lambda i: vec[:, hp, i:i + 1]
                r_, k_, v_, lw, a_, g_, kk, km, be, Lc, eL, eLn, eLp, eD, tA, tB, ysb = [
                    Bf[n] for n in ["r_", "k_", "v_", "lw", "a_", "g_", "kk", "km", "be", "Lc", "eL", "eLn", "eLp", "eD", "tA", "tB", "ysb"]]
                LC, eLC, BD = Bf["LC"], Bf["eLC"], Bf["BD"]
                MkvT, AkrT, AbrT, Y, Vtok, Ktok, Btok, Osb4, On = [Bf[n] for n in ["MkvT", "AkrT", "AbrT", "Y", "Vtok", "Ktok", "Btok", "Osb4", "On"]]
                X, XT = [Bf["X0"], Bf["X1"]], [Bf["XT0"], Bf["XT1"]]
                Wsb, nU, Abd, st64, mv4 = Bf["Wsb"], Bf["nU"], Bf["Abd"], Bf["st64"], Bf["mv4"]
                wa, wb = Bf["wa"], Bf["wb"]
                wres = R("wsc")
                for jj, dst in enumerate((r_, k_, v_)):
                    proj(P[jj], pr(jj), lambda k: wa[jj][:, k, :], lambda k: wb[jj][:, k, :], j * RT, RT, wres,
                         xs=Bf["xs"], xres=R("xs"))
                    S.op(S.act, lambda e: e.copy(out=dst[:], in_=P[jj][:, 0:RT]), reads=[pr(jj)], writes=[F])
                if j + 1 < NT_:
                    load_xs(l, j + 1)
                elif hp + 2 < 8:
                    prep_pair_weights(l, hp + 2)
                yield
                S.op(S.pe, lambda e: e.matmul(P[3][:, 0:RT], lhsT=w2[0:64, cols], rhs=hwa[0:64, tok], start=True, stop=True),
                     reads=["w2", "hwa"], writes=[pr(3)])
                S.op(S.act, lambda e: e.activation(out=lw[:], in_=P[3][:, 0:RT], func=AF.Sigmoid, bias=vcol(0)),
                     reads=[pr(3), "vec"], writes=[F])
                S.op(S.pe, lambda e: e.matmul(P[0][:, 0:RT], lhsT=w2[64:128, cols], rhs=hwa[64:128, tok], start=True, stop=True),
                     reads=["a2", "hwa"], writes=[pr(0)])
                S.op(S.act, lambda e: e.activation(out=a_[:], in_=P[0][:, 0:RT], func=AF.Sigmoid, bias=vcol(1)),
                     reads=[pr(0), "vec"], writes=[F])
                S.op(S.pe, lambda e: e.matmul(P[1][:, 0:RT], lhsT=g2[:, 0, cols], rhs=hg[:, 0, tok], start=True, stop=False),
                     reads=["g2a", "hg"], writes=[pr(1)])
                S.op(S.pe, lambda e: e.matmul(P[1][:, 0:RT], lhsT=g2[0:32, 1, cols], rhs=hg[0:32, 1, tok], start=False, stop=True),
                     reads=["g2b", "hg"], writes=[pr(1)])
                S.op(S.act, lambda e: e.copy(out=g_[:], in_=P[1][:, 0:RT]), reads=[pr(1)], writes=[F])
                yield
                dv(lambda e: e.tensor_scalar(out=lw[:], in0=lw[:], scalar1=DEC, scalar2=None, op0=ALU.mult), [F], [F])
                dv(lambda e: e.tensor_scalar(out=kk[:], in0=k_[:], scalar1=vcol(2), scalar2=None, op0=ALU.mult), [F, "vec"], [F])
                S.op(S.act, lambda e: e.activation(out=tA[:], in_=k_[:], func=AF.Square, scale=vcol(2)),
                     reads=[F, "vec"], writes=[F])
                S.op(S.pe, lambda e: e.matmul(P[2][:, 0:RT], lhsT=blk1[:], rhs=tA[:], start=True, stop=True),
                     reads=[F, "cst"], writes=[pr(2)])
                yield
                S.op(S.act, lambda e: e.activation(out=tA[:], in_=P[2][:, 0:RT], func=AF.Sqrt), reads=[pr(2)], writes=[F])
                dv(lambda e: e.tensor_scalar(out=tA[:], in0=tA[:], scalar1=1e-12, scalar2=None, op0=ALU.max), [F], [F])
                dv(lambda e: e.reciprocal(out=tA[:], in_=tA[:]), [F], [F])
                dv(lambda e: e.tensor_tensor(out=kk[:], in0=kk[:], in1=tA[:], op=ALU.mult), [F], [F])
                yield
                dv(lambda e: e.tensor_scalar(out=tB[:], in0=a_[:], scalar1=vcol(3), scalar2=vcol(4), op0=ALU.mult, op1=ALU.add),
                   [F, "vec"], [F])
                dv(lambda e: e.tensor_tensor(out=km[:], in0=k_[:], in1=tB[:], op=ALU.mult), [F], [F])
                dv(lambda e: e.tensor_tensor(out=be[:], in0=kk[:], in1=a_[:], op=ALU.mult), [F], [F], S.pool)
                dv(lambda e: e.scalar_tensor_tensor(out=tB[:], in0=r_[:], scalar=vcol(5), in1=km[:], op0=ALU.mult, op1=ALU.mult),
                   [F, "vec"], [F])
                S.op(S.pe, lambda e: e.matmul(P[3][:, 0:RT], lhsT=blk1[:], rhs=tB[:], start=True, stop=True),
                     reads=[F, "cst"], writes=[pr(3)])
                yield
                dv(lambda e: e.tensor_tensor(out=tA[:], in0=P[3][:, 0:RT], in1=v_[:], op=ALU.mult), [pr(3), F], [R("tA")])
                dv(lambda e: e.tensor_tensor_scan(out=Lc[:], data0=rmask[:], data1=lw[:], initial=0.0, op0=ALU.mult, op1=ALU.add),
                   [F, "cst"], [F])
                S.op(S.act, lambda e: e.activation(out=eL[:], in_=Lc[:], func=AF.Exp), reads=[F], writes=[F])
                S.op(S.act, lambda e: e.activation(out=eLn[:], in_=Lc[:], func=AF.Exp, scale=-1.0), reads=[F], writes=[F])
                dv(lambda e: e.tensor_tensor(out=eLp[:], in0=Lc[:], in1=lw[:], op=ALU.subtract), [F], [F], S.pool)
                S.op(S.act, lambda e: e.activation(out=eLp[:], in_=eLp[:], func=AF.Exp), reads=[F], writes=[F])
                L3 = Lc[:].rearrange("p (c t) -> p c t", t=64)
                dv(lambda e: e.tensor_copy(out=LC[:], in_=L3[:, :, 63]), [F], [F])
                S.op(S.act, lambda e: e.activation(out=eLC[:], in_=LC[:], func=AF.Exp), reads=[F], writes=[R("eLC")])
                for c in range(NCk):
                    S.op(S.act, lambda e, c=c: e.activation(out=eD[:, c * 64:(c + 1) * 64], in_=Lc[:, c * 64:(c + 1) * 64],
                                                            func=AF.Exp, scale=-1.0, bias=LC[:, c:c + 1]), reads=[F], writes=[F])
                yield
                prods = [("kap", kk, eLp), ("rt", r_, eL), ("kt", km, eLn), ("bt", be, eLn), ("kb", km, eD), ("bb", be, eD)]
                ie = 0
                for n, x0, x1 in prods:
                    for hs in (H0, H1):
                        eng = S.dve if ie % 2 == 0 else S.pool
                        ie += 1
                        dv(lambda e: e.tensor_tensor(out=BD[n][hs, :, hs], in0=x0[hs, :].rearrange("p (c t) -> p c t", t=64),
                                                     in1=x1[hs, :].rearrange("p (c t) -> p c t", t=64), op=ALU.mult),
                           [F], [R(("bd", n))], eng)
                for hs in (H0, H1):
                    S.op(S.act, lambda e: e.copy(out=BD["vf"][hs, :, hs], in_=v_[hs, :].rearrange("p (c t) -> p c t", t=64)),
                         reads=[F], writes=[R(("bd", "vf"))])
                yield
                gm = [(0, "kt", "kap"), (1, "kt", "rt"), (2, "bt", "kap"), (3, "bt", "rt")]
                for c in range(NCk):
                    for pi, ln_, rn_ in gm:
                        S.op(S.pe, lambda e: e.matmul(P[pi][:, c * 128:(c + 1) * 128], lhsT=BD[ln_][:, c, :],
                                                      rhs=BD[rn_][:, c, :], start=True, stop=True),
                             reads=[R(("bd", ln_)), R(("bd", rn_))], writes=[pr(pi)])
                yield
                dv(lambda e: e.tensor_tensor(out=f4(MkvT), in0=P[0][:, 0:W_], in1=f4(SUm), op=ALU.mult), [pr(0), "cst"], [R("MkvT")])
                dv(lambda e: e.tensor_tensor(out=f4(X[0]), in0=P[2][:, 0:W_], in1=f4(SUm), op=ALU.mult), [pr(2), "cst"], [R(("X", 0))])
                for c in range(NCk):
                    S.op(S.pe, lambda e: e.matmul(P[0][:, c * 128:(c + 1) * 128], lhsT=BD["kap"][:, c, :],
                                                  rhs=BD["bt"][:, c, :], start=True, stop=True),
                         reads=[R(("bd", "kap")), R(("bd", "bt"))], writes=[pr(0)])
                dv(lambda e: e.tensor_tensor(out=f4(AkrT), in0=P[1][:, 0:W_], in1=f4(UIm), op=ALU.mult), [pr(1), "cst"], [R("AkrT")])
                dv(lambda e: e.tensor_tensor(out=f4(AbrT), in0=P[3][:, 0:W_], in1=f4(UIm), op=ALU.mult), [pr(3), "cst"], [R("AbrT")])
                dv(lambda e: e.tensor_tensor(out=f4(Y), in0=f4(IDm), in1=f4(X[0]), op=ALU.subtract), [R(("X", 0)), "cst"], [R("Y")], S.pool)
                yield
                dv(lambda e: e.tensor_tensor(out=f4(XT[0]), in0=P[0][:, 0:W_], in1=f4(SLm), op=ALU.mult), [pr(0), "cst"], [R(("XT", 0))])
                yield
                cur = 0
                for lvl in range(5):
                    nxt = 1 - cur
                    last = (lvl == 4)
                    for c in range(NCk):
                        cs = slice(c * 128, (c + 1) * 128)
                        if not last:
                            S.op(S.pe, lambda e: e.matmul(P[0][:, cs], lhsT=XT[cur][:, c, :], rhs=X[cur][:, c, :], start=True, stop=True),
                                 reads=[R(("X", cur)), R(("XT", cur))], writes=[pr(0)])
                        S.op(S.pe, lambda e: e.matmul(P[1][:, cs], lhsT=X[cur][:, c, :], rhs=XT[cur][:, c, :], start=True, stop=True),
                             reads=[R(("X", cur)), R(("XT", cur))], writes=[pr(1)])
                    yield
                    if not last:
                        dv(lambda e: e.tensor_copy(out=f4(X[nxt]), in_=P[0][:, 0:W_]), [pr(0)], [R(("X", nxt))])
                    S.op(S.act, lambda e: e.copy(out=f4(XT[nxt]), in_=P[1][:, 0:W_]), reads=[pr(1)], writes=[R(("XT", nxt))])
                    for c in range(NCk):
                        cs = slice(c * 128, (c + 1) * 128)
                        S.op(S.pe, lambda e: e.matmul(P[2][:, cs], lhsT=XT[nxt][:, c, :], rhs=Y[:, c, :], start=True, stop=True),
                             reads=[R(("XT", nxt)), R("Y")], writes=[pr(2)])
                    yield
                    dv(lambda e: e.tensor_tensor(out=f4(Y), in0=f4(Y), in1=P[2][:, 0:W_], op=ALU.add), [pr(2), R("Y")], [R("Y")])
                    cur = nxt
                for pi, n, dst in ((3, "vf", Vtok), (0, "kb", Ktok), (1, "bb", Btok)):
                    for c in range(NCk):
                        S.op(S.pe, lambda e: e.transpose(out=P[pi][:, c * 128:(c + 1) * 128], in_=BD[n][:, c, :], identity=identf[:]),
                             reads=[R(("bd", n)), "cst"], writes=[pr(pi)])
                    S.op(S.act, lambda e: e.copy(out=f4(dst), in_=P[pi][:, 0:W_]), reads=[pr(pi)], writes=[R(("tok", n))])
                yield
                for c in range(NCk):
                    S.op(S.pe, lambda e: e.matmul(P[3][:, 0:128], lhsT=BD["kap"][:, c, :], rhs=Abd[:], start=True, stop=False),
                         reads=[R(("bd", "kap")), R("Abd")], writes=[pr(3)])
                    S.op(S.pe, lambda e: e.matmul(P[3][:, 0:128], lhsT=MkvT[:, c, :], rhs=Vtok[:, c, :], start=False, stop=True),
                         reads=[R("MkvT"), R(("tok", "vf"))], writes=[pr(3)])
                    S.op(S.act, lambda e: e.copy(out=Wsb[:], in_=P[3][:, 0:128]), reads=[pr(3)], writes=[R("Wsb")])
                    yield
                    S.op(S.pe, lambda e: e.matmul(P[0][:, 0:128], lhsT=Y[:, c, :], rhs=Wsb[:], start=True, stop=True),
                         reads=[R("Y"), R("Wsb")], writes=[pr(0)])
                    dv(lambda e: e.tensor_scalar(out=nU[:], in0=P[0][:, 0:128], scalar1=-1.0, scalar2=None, op0=ALU.mult),
                       [pr(0)], [R("nU")])
                    yield
                    S.op(S.pe, lambda e: e.matmul(P[2][:, 0:128], lhsT=Ktok[:, c, :], rhs=Vtok[:, c, :], start=True, stop=False),
                         reads=[R(("tok", "kb")), R(("tok", "vf"))], writes=[pr(2)])
                    S.op(S.pe, lambda e: e.matmul(P[2][:, 0:128], lhsT=Btok[:, c, :], rhs=nU[:], start=False, stop=True),
                         reads=[R(("tok", "bb")), R("nU")], writes=[pr(2)])
                    oc = P[1][:, c * 128:(c + 1) * 128]
                    S.op(S.pe, lambda e: e.matmul(oc, lhsT=BD["rt"][:, c, :], rhs=Abd[:], start=True, stop=False),
                         reads=[R(("bd", "rt")), R("Abd")], writes=[pr(1)])
                    S.op(S.pe, lambda e: e.matmul(oc, lhsT=AkrT[:, c, :], rhs=Vtok[:, c, :], start=False, stop=False),
                         reads=[R("AkrT"), R(("tok", "vf"))], writes=[pr(1)])
                    S.op(S.pe, lambda e: e.matmul(oc, lhsT=AbrT[:, c, :], rhs=nU[:], start=False, stop=True),
                         reads=[R("AbrT"), R("nU")], writes=[pr(1)])
                    dv(lambda e: e.scalar_tensor_tensor(out=Abd[:], in0=Abd[:], scalar=eLC[:, c:c + 1], in1=P[2][:, 0:128],
                                                        op0=ALU.mult, op1=ALU.add), [pr(2), R("Abd"), R("eLC")], [R("Abd")])
                    yield
                S.op(S.act, lambda e: e.copy(out=f4(Osb4), in_=P[1][:, 0:W_]), reads=[pr(1)], writes=[R("Osb")])
                for c in range(NCk):
                    for hs in (H0, H1):
                        dv(lambda e: e.bn_stats(out=st64[hs, c, :], in_=Osb4[hs, c, hs]), [R("Osb")], [R("st6")])
                    dv(lambda e: e.bn_aggr(out=mv4[:, c, 0:2], in_=st64[:, c, :]), [R("st6")], [R("mv")])
                dv(lambda e: e.tensor_scalar(out=mv4[:, :, 2:3], in0=mv4[:, :, 1:2], scalar1=GN_EPS, scalar2=None, op0=ALU.add),
                   [R("mv")], [R("mv")])
                yield
                S.op(S.act, lambda e: e.activation(out=mv4[:, :, 3:4], in_=mv4[:, :, 2:3], func=AF.Sqrt), reads=[R("mv")], writes=[R("mv")])
                dv(lambda e: e.reciprocal(out=mv4[:, :, 4:5], in_=mv4[:, :, 3:4]), [R("mv")], [R("mv")])
                dv(lambda e: e.scalar_tensor_tensor(out=mv4[:, :, 5:6], in0=mv4[:, :, 0:1], scalar=-1.0, in1=mv4[:, :, 4:5],
                                                    op0=ALU.mult, op1=ALU.mult), [R("mv")], [R("mv")])
                yield
                for c in range(NCk):
                    for hs in (H0, H1):
                        S.op(S.act, lambda e: e.activation(out=On[hs, c, hs], in_=Osb4[hs, c, hs], func=AF.Identity,
                                                           bias=mv4[hs, c, 5:6], scale=mv4[hs, c, 4:5]),
                             reads=[R("mv"), R("Osb")], writes=[R("On")])
                for c in range(NCk):
                    S.op(S.pe, lambda e: e.transpose(out=P[3][:, c * 128:(c + 1) * 128], in_=On[:, c, :], identity=identf[:]),
                         reads=[R("On"), "cst"], writes=[pr(3)])
                yield
                for hs in (H0, H1):
                    dv(lambda e: e.tensor_scalar(out=ysb[hs, :].rearrange("p (c t) -> p c t", t=64),
                                                 in0=P[3][:, 0:W_].rearrange("p (c t) -> p c t", t=128)[hs, :, hs],
                                                 scalar1=vec[hs, hp, 6:7], scalar2=vec[hs, hp, 7:8], op0=ALU.mult,
                                                 op1=ALU.add), [pr(3), "vec"], [R("ysb")])
                dv(lambda e: e.tensor_tensor(out=ysb[:], in0=ysb[:], in1=tA[:], op=ALU.add), [R("ysb"), R("tA")], [R("ysb")])
                ob = Bf["osb"][j % 2]
                dv(lambda e: e.tensor_tensor(out=ob[:], in0=ysb[:], in1=g_[:], op=ALU.mult), [R("ysb"), F], [R(("osb", j % 2))])
                S.dma("sp", lambda e: e.dma_start(out=self.oT_d[hp][:, tok], in_=ob[:]), reads=[R(("osb", j % 2))],
                      writes=[("oTd", hp)])
                yield

            def lane_gen(l):
                Bf = lanes[l]
                for hp in range(l, 8, 2):
                    if hp == l:
                        prep_pair_weights(l, hp)
                    S.op(S.pool, lambda e: e.memset(Bf["Abd"][:], 0.0), reads=[(l, "Abd")], writes=[(l, "Abd")])
                    load_xs(l, 0)
                    yield
                    for j in range(NT_):
                        yield from unit(l, hp, j)

            gens = [lane_gen(0), lane_gen(1)]
            alive = [True, True]
            for _ in range(int(os.environ.get("RW_OFF", "0"))):
                next(gens[0])
            while any(alive):
                for l in range(2):
                    if alive[l]:
                        try:
                            next(gens[l])
                        except StopIteration:
                            alive[l] = False
            S.barrier()
            self.load_oT()
            S.barrier()
        self.out_proj_phase(L, self.rw_wo)

def host_layout(inp):
    out = {}
    cw = np.zeros((DEPTH, NCH * 128, 4), np.float32)
    cw[:, :D_FF, 0:3] = np.transpose(inp["ffn_conv_w"], (0, 2, 1))
    cw[:, :D_FF, 3] = inp["ffn_conv_b"]
    out["ffn_cw"] = np.ascontiguousarray(cw.reshape(DEPTH, NCH, 128, 4).transpose(0, 2, 1, 3))
    for k in ("ln_g", "ln_b", "ffn_w_in", "ffn_w_out"):
        out[k] = np.ascontiguousarray(inp[k], dtype=np.float32)
    for k in ("dil_w_qkv", "dil_w_o"):
        out[k] = np.ascontiguousarray(inp[k][0], dtype=np.float32)
    fm = lambda v: np.ascontiguousarray(np.asarray(v, np.float32).reshape(8, 128).T)
    out["rw_mu"] = np.ascontiguousarray(inp["rwkv_mu"][0].reshape(6, 8, 128).transpose(2, 1, 0))
    ka = inp["rwkv_k_a"][0]
    vecs = [inp["rwkv_w0"][0], inp["rwkv_a0"][0], inp["rwkv_k_k"][0], ka, None, inp["rwkv_r_k"][0].reshape(-1),
            inp["rwkv_ln_w"][0], inp["rwkv_ln_b"][0]]
    rv = np.zeros((128, 8, 8), np.float32)
    for i, v in enumerate(vecs):
        if v is not None:
            rv[:, :, i] = fm(v)
    out["rw_vec"] = rv
    out["rwkv_w_rkv"] = np.ascontiguousarray(inp["rwkv_w_rkv"][0], dtype=np.float32)
    out["rw_l1"] = np.ascontiguousarray(np.concatenate([inp["rwkv_w1"][0], inp["rwkv_a1"][0], inp["rwkv_g1"][0]], axis=1))
    for k in ("rwkv_w2", "rwkv_a2", "rwkv_g2", "rwkv_w_o"):
        out[k] = np.ascontiguousarray(inp[k][0], dtype=np.float32)
    wd = inp["mla_w_down"]
    out["mla_wd"] = np.ascontiguousarray(np.concatenate(
        [wd[:, :, 0:640], wd[:, :, 0:64], wd[:, :, 640:672], wd[:, :, 656:672], wd[:, :, 640:656]], axis=2))
    out["mla_qn"] = np.ascontiguousarray(inp["mla_q_norm"].reshape(-1, 3, 128).transpose(0, 2, 1))
    out["mla_kvn"] = np.ascontiguousarray(inp["mla_kv_norm"].reshape(-1, 2, 128).transpose(0, 2, 1))
    wq = inp["mla_w_uq"].reshape(-1, 384, 16, 96)
    out["mla_wuq"] = np.ascontiguousarray(np.concatenate(
        [wq[..., 0:96], wq[..., 80:96], wq[..., 64:80]], axis=3).reshape(-1, 384, 2048))
    wkv = inp["mla_w_ukv"].reshape(-1, 256, 16, 128)
    out["mla_wukv"] = np.ascontiguousarray(np.concatenate(
        [wkv[..., 0:64].reshape(-1, 256, 1024), wkv[..., 64:128].reshape(-1, 256, 1024)], axis=2))
    out["mla_wo"] = np.ascontiguousarray(inp["mla_w_o"], dtype=np.float32)
    rc = np.zeros((96, 2), np.float32)
    invf = (10000.0 ** (-np.arange(0, 32, 2, dtype=np.float32) / np.float32(32))).astype(np.float32)
    rc[64:80, 0] = invf / np.float32(2 * np.pi)
    rc[80:96, 0] = invf / np.float32(2 * np.pi)
    rc[64:80, 1] = -1.0
    rc[80:96, 1] = 1.0
    out["rope_c"] = rc
    return out


DEFAULT_PLAN = [("mla", 0), ("ffn", 0), ("dil", 1), ("ffn", 1), ("rwkv", 2), ("ffn", 2), ("mla", 3), ("ffn", 3)]
_CACHE = {}


def run(inputs, plan, n_cores=8, trace=False):
    key = tuple(plan)
    if key not in _CACHE:
        b = Builder(plan)
        nc = b.build()
        _CACHE[key] = (b, nc)
    b, nc = _CACHE[key]
    shared = host_layout(inputs)
    in_maps = []
    for c in range(n_cores):
        d = {"x": np.ascontiguousarray(inputs["x"][c], dtype=np.float32),
             "positions": np.ascontiguousarray(inputs["positions"][c], dtype=np.int32)}
        d.update(shared)
        d = {k: v for k, v in d.items() if k in b.din}
        in_maps.append(d)
    res = run_bass_kernel_spmd(nc, in_maps, core_ids=list(range(n_cores)), trace=trace)
    return np.stack([r["out"] for r in res.results], axis=0), res


def kernel(**inputs):
    out, _ = run(inputs, DEFAULT_PLAN)
    return out.astype(np.float32)
```
